# Optimizing a Trainium2 kernel written in Bass

```python
import math
import jax
import jax.numpy as jnp
from jax import lax
import numpy as np

D_MODEL = 1024
BATCH = 8
SEQ = 4096
DEPTH = 2

GRID_W = 64
CTX_LEN = 256
EPS = 1e-6
F32 = jnp.float32

MLA_HEADS = 8
MLA_NOPE = 64
MLA_ROPE = 32
MLA_V = 64
MLA_Q_LORA = 256
MLA_KV_LORA = 128
MLA_SCALE = (MLA_NOPE + MLA_ROPE) ** -0.5
ROPE_BASE = 10000.0
Q_BLOCK = 128

GLA_HEADS = 4
GLA_DK = 256
GLA_DV = 512
GLA_HK = GLA_DK // GLA_HEADS
GLA_HV = GLA_DV // GLA_HEADS
GLA_GATE_RANK = 16
GLA_TAU = 16.0
GLA_CHUNK = 64

HY_WIDTH = 512
HY_ORDER = 2
HY_SHORT = 3
HY_BANDS = 16
HY_POS_DIM = 1 + 2 * HY_BANDS
HY_FILTER_HIDDEN = 64
HY_FAST_DECAY = 0.3
HY_SLOW_DECAY = 1.5
HY_DECAY_TARGET = 1e-2

N_BRANCH = 3
D_FF = 4 * D_MODEL
MLA_OUT = MLA_HEADS * MLA_V
IN_SIZES = (MLA_Q_LORA, MLA_KV_LORA, MLA_ROPE, GLA_DK, GLA_DK, GLA_DV, GLA_DV, GLA_GATE_RANK, GLA_GATE_RANK,
            (HY_ORDER + 1) * HY_WIDTH, N_BRANCH * D_MODEL)
D_IN = sum(IN_SIZES)

kernel_name = 'hybrid_mla_gla_hyena_dit_block'


def _split(z, sizes):
    parts, start = [], 0
    for s in sizes:
        parts.append(z[..., start:start + s])
        start += s
    return parts


def rmsnorm(x, g):
    xf = x.astype(F32)
    y = xf * lax.rsqrt(jnp.mean(xf * xf, axis=-1, keepdims=True) + EPS)
    return (y * g.astype(F32)).astype(x.dtype)


def modulate(h, shift, scale):
    return h * (1 + scale) + shift


def axial_rope_tables(rows):
    row = jnp.repeat(jnp.arange(rows, dtype=F32), GRID_W)
    col = jnp.tile(jnp.arange(GRID_W, dtype=F32), rows)
    a = MLA_ROPE // 4
    inv = ROPE_BASE ** (-jnp.arange(a, dtype=F32) / a)
    ang = jnp.concatenate([row[:, None] * inv, col[:, None] * inv], axis=-1)
    return jnp.cos(ang), jnp.sin(ang)


def apply_axial_rope(x, cos, sin):
    a = MLA_ROPE // 4
    cos = cos.astype(x.dtype)
    sin = sin.astype(x.dtype)

    def rot(u, cs, sn):
        u1, u2 = u[..., :a], u[..., a:]
        return jnp.concatenate([u1 * cs - u2 * sn, u1 * sn + u2 * cs], axis=-1)

    return jnp.concatenate([rot(x[..., :2 * a], cos[..., :a], sin[..., :a]),
                            rot(x[..., 2 * a:], cos[..., a:], sin[..., a:])], axis=-1)


def mla_queries(zq, q_norm, w_uq, rope):
    B, L = zq.shape[:2]
    q = (rmsnorm(zq, q_norm) @ w_uq).reshape(B, L, MLA_HEADS, MLA_NOPE + MLA_ROPE)
    q_nope, q_rope = q[..., :MLA_NOPE], q[..., MLA_NOPE:]
    if rope is not None:
        q_rope = apply_axial_rope(q_rope, rope[0][:, None, :], rope[1][:, None, :])
    return q_nope, q_rope


def mla_keys_values(zkv, zkr, kv_norm, w_ukv, rope):
    B, L = zkv.shape[:2]
    kv = (rmsnorm(zkv, kv_norm) @ w_ukv).reshape(B, L, MLA_HEADS, MLA_NOPE + MLA_V)
    k_rope = zkr
    if rope is not None:
        k_rope = apply_axial_rope(k_rope, rope[0], rope[1])
    return kv[..., :MLA_NOPE], k_rope, kv[..., MLA_NOPE:]


def mla_attend(qn, qr, kn, kr, v):
    s = jnp.einsum('bqhd,bkhd->bhqk', qn, kn) + jnp.einsum('bqhr,bkr->bhqk', qr, kr)
    p = jax.nn.softmax(s.astype(F32) * MLA_SCALE, axis=-1).astype(v.dtype)
    return jnp.einsum('bhqk,bkhd->bqhd', p, v)


def mla_latent_attention(qn, qr, kn, kr, v):
    B, L = qn.shape[:2]
    nb = L // Q_BLOCK

    def blocks(t):
        return jnp.moveaxis(t.reshape(B, nb, Q_BLOCK, *t.shape[2:]), 1, 0)

    o = lax.map(lambda qb: mla_attend(qb[0], qb[1], kn, kr, v), (blocks(qn), blocks(qr)))
    return jnp.moveaxis(o, 0, 1).reshape(B, L, MLA_OUT)


def gla_heads(q, k, v, a_f, a_b, w_a2, b_a):
    def heads(t):
        B, L, W = t.shape
        return t.reshape(B, L, GLA_HEADS, W // GLA_HEADS).transpose(0, 2, 1, 3).astype(F32)

    def log_gate(a, d):
        return jax.nn.log_sigmoid((a @ w_a2[d] + b_a[d]).astype(F32)) / GLA_TAU

    return (heads(q) * GLA_HK ** -0.5, heads(k), heads(v), heads(log_gate(a_f, 0)), heads(log_gate(a_b, 1)))


def gla_chunked(q, k, v, g, s0):
    B, H, L, _ = q.shape
    C = min(GLA_CHUNK, L)
    n = L // C
    mask = jnp.tril(jnp.ones((C, C), dtype=bool))

    def to_chunks(t):
        return jnp.moveaxis(t.reshape(B, H, n, C, t.shape[-1]), 2, 0)

    def step(S, inp):
        qc, kc, vc, gc = inp
        G = jnp.cumsum(gc, axis=2)
        o_inter = jnp.einsum('bhck,bhkv->bhcv', qc * jnp.exp(G), S)
        diff = G[:, :, :, None, :] - G[:, :, None, :, :]
        decay = jnp.exp(jnp.where(mask[:, :, None], diff, -jnp.inf))
        A = jnp.einsum('bhik,bhjk,bhijk->bhij', qc, kc, decay)
        o = o_inter + jnp.einsum('bhij,bhjv->bhiv', A, vc)
        G_last = G[:, :, -1:, :]
        S_new = jnp.exp(G_last[:, :, 0, :])[..., None] * S + jnp.einsum('bhck,bhcv->bhkv', kc * jnp.exp(G_last - G), vc)
        return S_new, o

    S_fin, o = lax.scan(step, s0, (to_chunks(q), to_chunks(k), to_chunks(v), to_chunks(g)))
    return jnp.moveaxis(o, 0, 2).reshape(B, H, L, v.shape[-1]), S_fin


def gla_bidirectional(lat, ctx, with_ctx_out):
    q, k, v, gf, gb = lat
    qc, kc, vc, gfc, gbc = ctx
    s0 = jnp.zeros((q.shape[0], GLA_HEADS, GLA_HK, GLA_HV), F32)

    def flip(t):
        return jnp.flip(t, axis=2)

    oc_f, s_f = gla_chunked(qc, kc, vc, gfc, s0)
    oc_b, s_b = gla_chunked(flip(qc), flip(kc), flip(vc), flip(gbc), s0)
    o_f, _ = gla_chunked(q, k, v, gf, s_f)
    o_b, _ = gla_chunked(flip(q), flip(k), flip(v), flip(gb), s_b)
    o_lat = o_f + flip(o_b)
    o_ctx = oc_f + flip(oc_b) if with_ctx_out else None
    return o_lat, o_ctx


def gla_output(o, r, out_norm):
    B, H, L, V = o.shape
    o = rmsnorm(o, out_norm).transpose(0, 2, 1, 3).reshape(B, L, H * V).astype(r.dtype)
    return o * jax.nn.silu(r)


def hyena_filters(L, w1, b1, w2, b2, w3, b3):
    pos = jnp.arange(L, dtype=F32)
    t = pos / max(L - 1, 1)
    f = jnp.linspace(1e-4, HY_BANDS - 1, HY_BANDS, dtype=F32)
    ang = (2.0 * math.pi / L) * pos[:, None] * f
    feat = jnp.concatenate([t[:, None], jnp.cos(ang), jnp.sin(ang)], axis=-1)
    hdn = jnp.sin(feat @ w1.astype(F32) + b1.astype(F32))
    hdn = jnp.sin(hdn @ w2.astype(F32) + b2.astype(F32))
    h = (hdn @ w3.astype(F32) + b3.astype(F32)).reshape(L, 2, HY_ORDER, HY_WIDTH)
    deltas = jnp.linspace(math.log(HY_DECAY_TARGET) / HY_FAST_DECAY, math.log(HY_DECAY_TARGET) / HY_SLOW_DECAY,
                          HY_WIDTH, dtype=F32)
    window = jnp.exp(-t[:, None] * jnp.abs(deltas))
    h = h * window[:, None, None, :]
    return h / jnp.sum(jnp.abs(h), axis=(0, 1), keepdims=True)


def short_conv(u, w, b):
    L = u.shape[1]
    pad = HY_SHORT // 2
    up = jnp.pad(u, ((0, 0), (pad, pad), (0, 0)))
    y = b
    for j in range(HY_SHORT):
        y = y + up[:, j:j + L] * w[j]
    return y


def bidir_long_conv(u, h_fwd, h_bwd):
    L = u.shape[1]
    k = jnp.concatenate([h_fwd, jnp.zeros_like(h_fwd[:1]), h_bwd[:0:-1]], axis=0)
    spec = jnp.fft.rfft(u.astype(F32), n=2 * L, axis=1) * jnp.fft.rfft(k, axis=0)[None]
    return jnp.fft.irfft(spec, n=2 * L, axis=1)[:, :L].astype(u.dtype)


def hyena_branch(z, short_w, short_b, filt, hy_bias):
    x1, x2, v = _split(short_conv(z, short_w, short_b), (HY_WIDTH,) * 3)
    y = v
    for n, gate in enumerate((x1, x2)):
        y = gate * (bidir_long_conv(y, filt[:, 0, n], filt[:, 1, n]) + hy_bias[n] * y)
    return y


def merge_branches(z_gate, y_mla, y_gla, y_hy, w_o_mla, w_o_gla, w_o_hy, w_out):
    g_mla, g_gla, g_hy = _split(jax.nn.sigmoid(z_gate), (D_MODEL,) * N_BRANCH)
    m = g_mla * (y_mla @ w_o_mla) + g_gla * (y_gla @ w_o_gla) + g_hy * (y_hy @ w_o_hy)
    return m @ w_out


def sqrelu_mlp(h, w1, w2):
    return jnp.square(jax.nn.relu(h @ w1)) @ w2


def token_mixer(hx, hc, rope, with_ctx_out, w_in, mla_q_norm, mla_w_uq, mla_kv_norm, mla_w_ukv,
                gla_w_a2, gla_b_a, gla_out_norm, hy_short_w, hy_short_b, hy_filter_w, hy_bias,
                w_o_mla, w_o_gla, w_o_hy, w_out):
    L, Lc = hx.shape[1], hc.shape[1]
    xq, xkv, xkr, xgq, xgk, xgv, xgr, xaf, xab, xhy, xgate = _split(hx @ w_in, IN_SIZES)
    cq, ckv, ckr, cgq, cgk, cgv, cgr, caf, cab, chy, cgate = _split(hc @ w_in, IN_SIZES)

    qn, qr = mla_queries(xq, mla_q_norm, mla_w_uq, rope)
    kn, kr, v = mla_keys_values(xkv, xkr, mla_kv_norm, mla_w_ukv, rope)
    knc, krc, vc = mla_keys_values(ckv, ckr, mla_kv_norm, mla_w_ukv, None)
    y_mla = mla_latent_attention(qn, qr, jnp.concatenate([knc, kn], axis=1),
                                 jnp.concatenate([krc, kr], axis=1), jnp.concatenate([vc, v], axis=1))

    o_gla, o_gla_c = gla_bidirectional(gla_heads(xgq, xgk, xgv, xaf, xab, gla_w_a2, gla_b_a),
                                       gla_heads(cgq, cgk, cgv, caf, cab, gla_w_a2, gla_b_a), with_ctx_out)
    y_gla = gla_output(o_gla, xgr, gla_out_norm)

    y_hy = hyena_branch(xhy, hy_short_w, hy_short_b, hyena_filters(L, *hy_filter_w), hy_bias)

    out_x = merge_branches(xgate, y_mla, y_gla, y_hy, w_o_mla, w_o_gla, w_o_hy, w_out)
    if not with_ctx_out:
        return out_x, None

    qnc, qrc = mla_queries(cq, mla_q_norm, mla_w_uq, None)
    y_mla_c = mla_attend(qnc, qrc, knc, krc, vc).reshape(hc.shape[0], Lc, MLA_OUT)
    y_gla_c = gla_output(o_gla_c, cgr, gla_out_norm)
    y_hy_c = hyena_branch(chy, hy_short_w, hy_short_b, hyena_filters(Lc, *hy_filter_w), hy_bias)
    out_c = merge_branches(cgate, y_mla_c, y_gla_c, y_hy_c, w_o_mla, w_o_gla, w_o_hy, w_out)
    return out_x, out_c


def setup_inputs(seed: int = 0) -> dict:
    key = jax.random.key(seed)
    keys = jax.random.split(key, 32)

    def nrm(i, shape, scale):
        return jax.random.normal(keys[i], shape, F32) * scale

    def gain(i, shape):
        return 1.0 + 0.02 * jax.random.normal(keys[i], shape, F32)

    n_filt = 2 * HY_ORDER * HY_WIDTH
    return {
        'x': nrm(0, (BATCH, SEQ, D_MODEL), 1.0),
        'c': nrm(1, (BATCH, D_MODEL), 1.0),
        'ctx': nrm(2, (BATCH, CTX_LEN, D_MODEL), 1.0),
        'c_ctx': nrm(3, (D_MODEL,), 1.0),
        'ada_w': nrm(4, (DEPTH, D_MODEL, 6 * D_MODEL), 0.5 * D_MODEL ** -0.5),
        'ada_b': nrm(5, (DEPTH, 6 * D_MODEL), 0.02),
        'norm1_g': gain(6, (DEPTH, D_MODEL)),
        'norm2_g': gain(7, (DEPTH, D_MODEL)),
        'w_in': nrm(8, (DEPTH, D_MODEL, D_IN), D_MODEL ** -0.5),
        'mla_q_norm': gain(9, (DEPTH, MLA_Q_LORA)),
        'mla_w_uq': nrm(10, (DEPTH, MLA_Q_LORA, MLA_HEADS * (MLA_NOPE + MLA_ROPE)), MLA_Q_LORA ** -0.5),
        'mla_kv_norm': gain(11, (DEPTH, MLA_KV_LORA)),
        'mla_w_ukv': nrm(12, (DEPTH, MLA_KV_LORA, MLA_HEADS * (MLA_NOPE + MLA_V)), MLA_KV_LORA ** -0.5),
        'gla_w_a2': nrm(13, (DEPTH, 2, GLA_GATE_RANK, GLA_DK), GLA_GATE_RANK ** -0.5),
        'gla_b_a': nrm(14, (DEPTH, 2, GLA_DK), 0.1),
        'gla_out_norm': gain(15, (DEPTH, GLA_HV)),
        'hy_short_w': nrm(16, (DEPTH, HY_SHORT, (HY_ORDER + 1) * HY_WIDTH), HY_SHORT ** -0.5),
        'hy_short_b': nrm(17, (DEPTH, (HY_ORDER + 1) * HY_WIDTH), 0.02),
        'hy_f_w1': nrm(18, (DEPTH, HY_POS_DIM, HY_FILTER_HIDDEN), HY_POS_DIM ** -0.5),
        'hy_f_b1': nrm(19, (DEPTH, HY_FILTER_HIDDEN), 0.1),
        'hy_f_w2': nrm(20, (DEPTH, HY_FILTER_HIDDEN, HY_FILTER_HIDDEN), HY_FILTER_HIDDEN ** -0.5),
        'hy_f_b2': nrm(21, (DEPTH, HY_FILTER_HIDDEN), 0.1),
        'hy_f_w3': nrm(22, (DEPTH, HY_FILTER_HIDDEN, n_filt), HY_FILTER_HIDDEN ** -0.5),
        'hy_f_b3': nrm(23, (DEPTH, n_filt), 0.02),
        'hy_bias': nrm(24, (DEPTH, HY_ORDER, HY_WIDTH), 0.5),
        'w_o_mla': nrm(25, (DEPTH, MLA_OUT, D_MODEL), MLA_OUT ** -0.5),
        'w_o_gla': nrm(26, (DEPTH, GLA_DV, D_MODEL), GLA_DV ** -0.5),
        'w_o_hy': nrm(27, (DEPTH, HY_WIDTH, D_MODEL), HY_WIDTH ** -0.5),
        'w_out': nrm(28, (DEPTH, D_MODEL, D_MODEL), D_MODEL ** -0.5),
        'ff_w1': nrm(29, (DEPTH, D_MODEL, D_FF), D_MODEL ** -0.5),
        'ff_w2': nrm(30, (DEPTH, D_FF, D_MODEL), D_FF ** -0.5),
        'final_norm_g': gain(31, (D_MODEL,)),
    }


def reference(x, c, ctx, c_ctx, ada_w, ada_b, norm1_g, norm2_g, w_in, mla_q_norm, mla_w_uq, mla_kv_norm,
              mla_w_ukv, gla_w_a2, gla_b_a, gla_out_norm, hy_short_w, hy_short_b, hy_f_w1, hy_f_b1, hy_f_w2,
              hy_f_b2, hy_f_w3, hy_f_b3, hy_bias, w_o_mla, w_o_gla, w_o_hy, w_out, ff_w1, ff_w2, final_norm_g):
    rows = x.shape[1] // GRID_W
    rope = axial_rope_tables(rows)
    sc = jax.nn.silu(c)
    scc = jax.nn.silu(c_ctx)
    for l in range(DEPTH):
        with_ctx_out = l < DEPTH - 1
        mod_x = (sc @ ada_w[l] + ada_b[l])[:, None, :]
        mod_c = scc @ ada_w[l] + ada_b[l]
        shx1, scx1, gx1, shx2, scx2, gx2 = _split(mod_x, (D_MODEL,) * 6)
        shc1, scc1, gc1, shc2, scc2, gc2 = _split(mod_c, (D_MODEL,) * 6)

        hx = modulate(rmsnorm(x, norm1_g[l]), shx1, scx1)
        hc = modulate(rmsnorm(ctx, norm1_g[l]), shc1, scc1)
        mx, mc = token_mixer(hx, hc, rope, with_ctx_out, w_in[l], mla_q_norm[l], mla_w_uq[l], mla_kv_norm[l],
                             mla_w_ukv[l], gla_w_a2[l], gla_b_a[l], gla_out_norm[l], hy_short_w[l], hy_short_b[l],
                             (hy_f_w1[l], hy_f_b1[l], hy_f_w2[l], hy_f_b2[l], hy_f_w3[l], hy_f_b3[l]), hy_bias[l],
                             w_o_mla[l], w_o_gla[l], w_o_hy[l], w_out[l])
        x = x + gx1 * mx
        x = x + gx2 * sqrelu_mlp(modulate(rmsnorm(x, norm2_g[l]), shx2, scx2), ff_w1[l], ff_w2[l])
        if with_ctx_out:
            ctx = ctx + gc1 * mc
            ctx = ctx + gc2 * sqrelu_mlp(modulate(rmsnorm(ctx, norm2_g[l]), shc2, scc2), ff_w1[l], ff_w2[l])
    return rmsnorm(x, final_norm_g)
```

```python
import math
import numpy as np
import ml_dtypes
import concourse.bass as bass
import concourse.mybir as mybir
from concourse.bass_utils import run_bass_kernel_spmd

F32 = mybir.dt.float32
BF16 = mybir.dt.bfloat16
AF = mybir.ActivationFunctionType
ALU = mybir.AluOpType

D = 1024
L = 4096
LC = 256
T = L + LC
NT = T // 128
DEPTH = 2
DIN = 6592
EPS = 1e-6
MLA_SCALE = 96 ** -0.5
TB = [(0, 256)] + [(256 + 512 * i, 512) for i in range(8)]
NFFT = 8192


class V:
    __slots__ = ("b", "ap")

    def __init__(self, b, ap):
        self.b = b
        self.ap = ap

    def __getitem__(self, idx):
        return V(self.b, self.ap[idx])

    def re(self, pat, **kw):
        return V(self.b, self.ap.rearrange(pat, **kw))

    def bc(self, shape):
        return V(self.b, self.ap.broadcast_to(list(shape)))

    def un(self, axis):
        return V(self.b, self.ap.unsqueeze(axis))

    def pb(self, n):
        return V(self.b, self.ap.partition_broadcast(n))


class Buf:
    __slots__ = ("t", "name", "lw", "rd", "psum", "dram")

    def __init__(self, t, name, psum=False, dram=False):
        self.t = t
        self.name = name
        self.lw = None
        self.rd = []
        self.psum = psum
        self.dram = dram

    def __getitem__(self, idx):
        return V(self, self.t[idx])

    @property
    def v(self):
        return V(self, self.t[:] if not hasattr(self.t, "ap") or True else self.t)


class FW:
    NDMA_SEM = 36
    NDMA_HW = 24

    def __init__(self, nc):
        self.nc = nc
        self.eng = {"pe": nc.tensor, "act": nc.scalar, "dve": nc.vector, "pool": nc.gpsimd, "sp": nc.sync}
        self.sem = {}
        self.cnt = {}
        for e in self.eng:
            self.sem[e] = nc.alloc_semaphore("s_" + e)
            self.cnt[e] = 0
        self.dsem = [nc.alloc_semaphore("d%d" % i) for i in range(self.NDMA_SEM)]
        self.dcnt = [0] * self.NDMA_SEM
        self.dnext = 0
        self.dnext_sw = 0
        self.seen = {e: {} for e in self.eng}
        self.ninst = 0
        self._ctx = []
        self._uid = 0
        self.deferred = []

    def _nm(self, name):
        self._uid += 1
        return "%s_%d" % (name, self._uid)

    def sbuf(self, name, shape, dt):
        g = self.nc.sbuf_tensor(self._nm(name), list(shape), dt)
        t = g.__enter__()
        self._ctx.append(g)
        return Buf(t, name)

    def psum(self, name, shape, dt=F32):
        g = self.nc.psum_tensor(self._nm(name), list(shape), dt)
        t = g.__enter__()
        self._ctx.append(g)
        return Buf(t, name, psum=True)

    def dram(self, name, shape, dt, kind="Internal"):
        t = self.nc.dram_tensor(name, list(shape), dt, kind=kind)
        return Buf(t.ap(), name, dram=True)

    def _wait(self, e, tok):
        if tok is None:
            return
        key, val = tok
        if e == "pe" and key == "pe":
            return
        if self.seen[e].get(key, 0) >= val:
            return
        self.seen[e][key] = val
        sem = self.sem[key] if isinstance(key, str) else self.dsem[key]
        self.eng[e].wait_ge(sem, val)

    def _deps(self, e, reads, writes):
        for b in reads:
            self._wait(e, b.lw)
        for b in writes:
            self._wait(e, b.lw)
            for tok in b.rd:
                self._wait(e, tok)

    def _commit(self, tok, reads, writes):
        for b in reads:
            b.rd.append(tok)
            if len(b.rd) > 48:
                best = {}
                for k, v in b.rd:
                    if best.get(k, 0) < v:
                        best[k] = v
                b.rd = list(best.items())
        for b in writes:
            b.lw = tok
            b.rd = []

    def flush(self):
        d, self.deferred = self.deferred, []
        for (q, out, in_, kw) in d:
            self._dma_now(q, out, in_, **kw)

    def op(self, e, fn, reads=(), writes=()):
        if self.deferred:
            self.flush()
        rd = [b for b in reads if not b.psum]
        wr = list(writes) + [b for b in reads if b.psum]
        self._deps(e, rd, wr)
        ins = fn(self.eng[e])
        self.cnt[e] += 1
        ins.then_inc(self.sem[e], 1)
        self._commit((e, self.cnt[e]), rd, wr)
        self.ninst += 1
        return ins

    def dma(self, q, out, in_, **kw):
        if out.b.dram and not in_.b.dram:
            self.deferred.append((q, out, in_, kw))
            return
        for (_, so, si, _) in self.deferred:
            if so.b is in_.b or si.b is out.b or so.b is out.b:
                self.flush()
                break
        self._dma_now(q, out, in_, **kw)

    def _dma_now(self, q, out, in_, **kw):
        if q == "pool":
            slot = self.NDMA_HW + self.dnext_sw
            self.dnext_sw = (self.dnext_sw + 1) % (self.NDMA_SEM - self.NDMA_HW)
        else:
            slot = self.dnext
            self.dnext = (self.dnext + 1) % self.NDMA_HW
        if self.dcnt[slot] > 0:
            self._wait(q, (slot, self.dcnt[slot]))
        self._deps(q, [in_.b], [out.b])
        ins = self.eng[q].dma_start(out=out.ap, in_=in_.ap, **kw)
        self.dcnt[slot] += 16
        ins.then_inc(self.dsem[slot], 16)
        self._commit((slot, self.dcnt[slot]), [in_.b], [out.b])
        self.ninst += 1

    def barrier(self, engines=None):
        self.flush()
        for e in (engines or self.eng):
            for e2 in self.eng:
                if e2 != e and self.cnt[e2] > 0:
                    self._wait(e, (e2, self.cnt[e2]))
            for s in range(self.NDMA_SEM):
                if self.dcnt[s] > 0:
                    self._wait(e, (s, self.dcnt[s]))

    def mark(self):
        return len(self._ctx)

    def release(self, mark):
        self.barrier()
        while len(self._ctx) > mark:
            self._ctx.pop().__exit__(None, None, None)

    def mm(self, out, lhsT, rhs, start=True, stop=True):
        return self.op("pe", lambda e: e.matmul(out.ap, lhsT=lhsT.ap, rhs=rhs.ap, start=start, stop=stop),
                       reads=[lhsT.b, rhs.b], writes=[out.b])

    def tr(self, out, in_, ident):
        return self.op("pe", lambda e: e.transpose(out.ap, in_.ap, ident.ap), reads=[in_.b, ident.b], writes=[out.b])

    def act(self, out, in_, func, bias=None, scale=None, accum=None, eng="act"):
        kw = {}
        rd = [in_.b]
        wr = [out.b]
        if bias is not None:
            if isinstance(bias, V):
                kw["bias"] = bias.ap
                rd.append(bias.b)
            else:
                kw["bias"] = bias
        if scale is not None:
            if isinstance(scale, V):
                kw["scale"] = scale.ap
                rd.append(scale.b)
            else:
                kw["scale"] = scale
        if accum is not None:
            kw["accum_out"] = accum.ap
            wr.append(accum.b)
        return self.op("act", lambda e: e.activation(out=out.ap, in_=in_.ap, func=func, **kw), reads=rd, writes=wr)

    def tt(self, out, a, b, op, eng="dve"):
        return self.op(eng, lambda e: e.tensor_tensor(out=out.ap, in0=a.ap, in1=b.ap, op=op),
                       reads=[a.b, b.b], writes=[out.b])

    def ts(self, out, a, s1, op0, s2=None, op1=None, eng="dve"):
        rd = [a.b]
        s1a = s1
        s2a = s2
        if isinstance(s1, V):
            rd.append(s1.b)
            s1a = s1.ap
        if isinstance(s2, V):
            rd.append(s2.b)
            s2a = s2.ap
        kw = {}
        if op1 is not None:
            kw["op1"] = op1
        return self.op(eng, lambda e: e.tensor_scalar(out=out.ap, in0=a.ap, scalar1=s1a, scalar2=s2a, op0=op0, **kw),
                       reads=rd, writes=[out.b])

    def stt(self, out, a, s, b, op0, op1, eng="dve"):
        rd = [a.b, b.b]
        sa = s
        if isinstance(s, V):
            rd.append(s.b)
            sa = s.ap
        return self.op(eng, lambda e: e.scalar_tensor_tensor(out=out.ap, in0=a.ap, scalar=sa, in1=b.ap, op0=op0, op1=op1),
                       reads=rd, writes=[out.b])

    def cp(self, out, in_, eng="dve"):
        if eng == "act":
            return self.op("act", lambda e: e.copy(out=out.ap, in_=in_.ap), reads=[in_.b], writes=[out.b])
        return self.op(eng, lambda e: e.tensor_copy(out=out.ap, in_=in_.ap), reads=[in_.b], writes=[out.b])

    def memset(self, out, val, eng="pool"):
        return self.op(eng, lambda e: e.memset(out.ap, val), writes=[out.b])

    def recip(self, out, in_):
        return self.op("dve", lambda e: e.reciprocal(out=out.ap, in_=in_.ap), reads=[in_.b], writes=[out.b])


class Pool:
    def __init__(self, bufs):
        self.bufs = bufs
        self.i = 0

    def next(self):
        b = self.bufs[self.i % len(self.bufs)]
        self.i += 1
        return b


def _bf(a):
    return np.ascontiguousarray(a.astype(ml_dtypes.bfloat16))


def host_constants():
    c = {}
    c["ident_bf"] = _bf(np.eye(128, dtype=np.float32))
    c["ident_f"] = np.eye(128, dtype=np.float32)
    c["ones_f"] = np.ones((128, 128), np.float32)
    rows = L // 64
    row = np.repeat(np.arange(rows, dtype=np.float32), 64)
    col = np.tile(np.arange(64, dtype=np.float32), rows)
    inv = (10000.0 ** (-np.arange(8, dtype=np.float32) / 8)).astype(np.float32)
    ang = np.concatenate([row[:, None] * inv, col[:, None] * inv], axis=-1)
    cos, sin = np.cos(ang), np.sin(ang)
    cosT = np.ones((96, T), np.float32)
    sinT = np.zeros((96, T), np.float32)
    for r in range(32):
        g, j = r // 16, r % 16
        half, i = j // 8, j % 8
        cosT[64 + r, LC:] = cos[:, g * 8 + i]
        sinT[64 + r, LC:] = (-sin[:, g * 8 + i]) if half == 0 else sin[:, g * 8 + i]
    c["cosT"] = cosT
    c["sinT"] = sinT
    i_ = np.arange(128)[None, :]
    j_ = np.arange(128)[:, None]
    Mf = ((j_ >= 64) & (j_ <= i_)).astype(np.float32) - ((j_ > i_) & (j_ <= 63)).astype(np.float32)
    Mb = ((j_ >= i_) & (j_ <= 63)).astype(np.float32) - ((j_ >= 64) & (j_ < i_)).astype(np.float32)
    c["gla_M"] = np.stack([Mf, Mb], 1).astype(np.float32)
    c["gla_mask"] = np.stack([(j_ <= i_), (j_ >= i_)], 1).astype(np.float32)
    ind = np.zeros((128, 2), np.float32)
    ind[:64, 0] = 1
    ind[64:, 1] = 1
    c["gla_ind"] = ind
    deltas = np.linspace(math.log(1e-2) / 0.3, math.log(1e-2) / 1.5, 512)
    fb = np.linspace(1e-4, 15.0, 16)
    b_ = np.arange(64)
    fbb = np.arange(64)
    for sfx, Ls in (("", L), ("_c", LC)):
        NF = 2 * Ls
        NA = NF // 64
        NFA = NA // 2 + 1
        pos = np.arange(Ls, dtype=np.float64)
        tn = pos / max(Ls - 1, 1)
        ang = (2.0 * math.pi / Ls) * pos[:, None] * fb
        feat = np.concatenate([tn[:, None], np.cos(ang), np.sin(ang)], -1)
        win = np.exp(-tn[:, None] * np.abs(deltas))
        feat2 = np.zeros((NF, 33))
        win2 = np.zeros((NF, 512))
        feat2[:Ls] = feat
        win2[:Ls] = win
        idx = np.arange(NF - Ls + 1, NF)
        feat2[idx] = feat[NF - idx]
        win2[idx] = win[NF - idx]
        c["hy_featT" + sfx] = np.ascontiguousarray(feat2.T.astype(np.float32))
        c["hy_win" + sfx] = np.ascontiguousarray(win2.astype(np.float32))
        a_ = np.arange(NA)[:, None]
        fa = np.arange(NFA)[None, :]
        th = 2 * math.pi * ((fa * a_) % NA) / NA
        c["hy_F1" + sfx] = _bf(np.concatenate([np.cos(th), -np.sin(th), np.sin(th)], 1))
        E2r = np.zeros((64, 2, NFA, 64, 2))
        E2i = np.zeros((64, 2, NFA, 64, 2))
        ph = 2 * math.pi * (((np.arange(NFA)[None, :, None] + NA * fbb[None, None, :]) * b_[:, None, None]) % NF) / NF
        for cp in range(2):
            E2r[:, cp, :, :, cp] = np.cos(ph)
            E2i[:, cp, :, :, cp] = -np.sin(ph)
        c["hy_E2r" + sfx] = _bf(E2r.reshape(128, NFA, 128))
        c["hy_E2i" + sfx] = _bf(E2i.reshape(128, NFA, 128))
        tt_ = 64 * np.arange(NA // 2)[None, None, :] + b_[None, :, None]
        th2 = 2 * math.pi * ((np.arange(NFA)[:, None, None] * tt_) % NF) / NF
        wgt = np.full((NFA, 1, 1), 2.0 / NF)
        wgt[0] = wgt[NFA - 1] = 1.0 / NF
        c["hy_DBr" + sfx] = _bf(wgt * np.cos(th2))
        c["hy_DBni" + sfx] = _bf(-wgt * np.sin(th2))
    psi = 2 * math.pi * ((fbb[:, None] * b_[None, :]) % 64) / 64
    CA = np.zeros((64, 2, 3, 64, 2))
    for cp in range(2):
        CA[:, cp, 0, :, cp] = np.cos(psi)
        CA[:, cp, 1, :, cp] = np.sin(psi)
        CA[:, cp, 2, :, cp] = -np.sin(psi)
    c["hy_CA"] = _bf(CA.reshape(128, 3, 128))
    return c


def rope_swap_perm():
    p = np.zeros(32, np.int64)
    for r in range(32):
        g, j = r // 16, r % 16
        p[r] = g * 16 + (j + 8) % 16
    return p


def build(dbg=None):
    dbg = dbg or {}
    stop_after = dbg.get("stop_after", None)
    ext = dbg.get("ext", ())
    nc = bass.Bass("TRN2", target_bir_lowering=False)
    f = FW(nc)
    hc = host_constants()

    def inp(name, shape, dt=F32):
        return Buf(nc.dram_tensor(name, list(shape), dt, kind="ExternalInput").ap(), name, dram=True)

    def scratch(name, shape, dt):
        if name in dbg.get("inject", ()):
            return f.dram(name, shape, dt, kind="ExternalInput")
        return f.dram(name, shape, dt, kind="ExternalOutput" if name in ext else "Internal")

    I = {}
    I["xc"] = inp("xc", [T, D])
    I["cs"] = inp("cs", [128, 8, 2])
    I["ada_w"] = inp("ada_w", [DEPTH, D, 6 * D])
    I["ada_bf"] = inp("ada_bf", [128, DEPTH, 48])
    I["ada_br"] = inp("ada_br", [2, DEPTH, 6 * D])
    I["n1g"] = inp("n1g", [128, DEPTH, 8])
    I["n2g"] = inp("n2g", [128, DEPTH, 8])
    I["w_in"] = inp("w_in", [DEPTH, D, DIN])
    I["w_kr2"] = inp("w_kr2", [DEPTH, D, 2, 96])
    I["qng"] = inp("qng", [128, DEPTH, 2])
    I["kvng"] = inp("kvng", [128, DEPTH])
    I["w_uq"] = inp("w_uq", [DEPTH, 256, 768])
    I["w_uq_sw"] = inp("w_uq_sw", [DEPTH, 256, 768])
    I["w_ukv_k"] = inp("w_ukv_k", [DEPTH, 128, 512])
    I["w_ukv_v"] = inp("w_ukv_v", [DEPTH, 128, 512])
    I["w_a2"] = inp("w_a2", [DEPTH, 16, 512])
    I["b_a"] = inp("b_a", [DEPTH, 1, 512])
    I["gla_on"] = inp("gla_on", [DEPTH, 128, 128])
    for nm_ in ("w_o_mla", "w_o_gla", "w_o_hy"):
        I[nm_] = inp(nm_, [DEPTH, 512, D])
    I["w_out"] = inp("w_out", [DEPTH, D, D])
    I["ff_w1"] = inp("ff_w1", [DEPTH, D, 4 * D])
    I["ff_w2"] = inp("ff_w2", [DEPTH, 4 * D, D])
    I["fin_g"] = inp("fin_g", [128, D])
    I["hy_swb"] = inp("hy_swb", [DEPTH, 128, 12, 4])
    I["hy_f_w1"] = inp("hy_f_w1", [DEPTH, 33, 64])
    I["hy_f_w2"] = inp("hy_f_w2", [DEPTH, 64, 64])
    I["hy_f_w3"] = inp("hy_f_w3", [DEPTH, 64, 2048])
    I["hy_fb12"] = inp("hy_fb12", [DEPTH, 64, 2])
    I["hy_f_b3"] = inp("hy_f_b3", [DEPTH, 1, 2048])
    I["hy_bias"] = inp("hy_bias", [DEPTH, 1, 2, 512])
    for k, v in hc.items():
        I[k] = inp(k, list(v.shape), BF16 if v.dtype == ml_dtypes.bfloat16 else F32)
    out_y = Buf(nc.dram_tensor("y", [L, D], F32, kind="ExternalOutput").ap(), "y", dram=True)

    xres = scratch("xres", [T, D], F32)
    modrow = scratch("modrow", [2, DEPTH, 2, D], F32)
    qT_d = scratch("qT_d", [8, 96, T], BF16)
    kT_d = scratch("kT_d", [8, 96, T], BF16)
    V_d = scratch("V_d", [T, 520], BF16)
    gqk_d = scratch("gqk_d", [T, 512], F32)
    gv_d = scratch("gv_d", [T, 512], BF16)
    gr_d = scratch("gr_d", [T, 512], F32)
    gg_d = scratch("gg_d", [T, 512], F32)
    zhyT_d = scratch("zhyT_d", [1536, T], BF16)
    scT_d = scratch("scT_d", [1536, T], BF16)
    kf_d = scratch("kf_d", [2, NFFT, 512], F32)
    H_d = scratch("H_d", [2, 8, 128, 2 * 65 * 32], BF16)
    y1T_d = scratch("y1T_d", [512, T], BF16)
    gatesT_d = scratch("gatesT_d", [3072, T], BF16)
    hT_d = scratch("hT_d", [D, T], BF16) if "hT_d" in ext else None
    ymlaT_d = scratch("ymlaT_d", [512, T], BF16)
    yglaT_d = scratch("yglaT_d", [512, T], BF16)
    yhyT_d = scratch("yhyT_d", [512, T], BF16)
    og_d = scratch("og_d", [T, 512], F32)

    ident = f.sbuf("ident", [128, 128], BF16)
    ones_f = f.sbuf("ones_f", [128, 128], F32)
    modF = f.sbuf("modF", [128, DEPTH, 48, 2], F32)
    AB = f.sbuf("AB", [128, DEPTH, 2, 2, 8, 2], F32)
    f.dma("sp", ident[:], I["ident_bf"][:])
    f.dma("sp", ones_f[:], I["ones_f"][:])
    xres_t = [Buf(xres.t[tt * 128:(tt + 1) * 128, :], "xres%d" % tt, dram=True) for tt in range(NT)]
    for tt in range(NT):
        f.dma("sp", xres_t[tt][:], I["xc"][tt * 128:(tt + 1) * 128, :])

    def phase_mod():
        m = f.mark()
        cs = f.sbuf("cs", [128, 8, 2], F32)
        scs = f.sbuf("scs", [128, 8, 2], F32)
        abf = f.sbuf("abf", [128, DEPTH, 48], F32)
        abr = f.sbuf("abr", [2, DEPTH, 6 * D], F32)
        g1 = f.sbuf("g1", [128, DEPTH, 8], F32)
        g2 = f.sbuf("g2", [128, DEPTH, 8], F32)
        wp = Pool([f.sbuf("adaw%d" % i, [128, 8, 512], F32) for i in range(2)])
        rowst = f.sbuf("rowst", [2, 512], F32)
        psF = f.psum("psF", [128, 512], F32)
        psR = Pool([f.psum("psR%d" % i, [128, 512], F32) for i in range(2)])
        f.dma("sp", cs[:], I["cs"][:])
        f.dma("sp", abf[:], I["ada_bf"][:])
        f.dma("sp", abr[:], I["ada_br"][:])
        f.dma("sp", g1[:], I["n1g"][:])
        f.dma("sp", g2[:], I["n2g"][:])
        f.act(scs[:], cs[:], AF.Silu)
        for l in range(DEPTH):
            wv = I["ada_w"][l].re("(kc p) j -> p kc j", p=128)
            for jb in range(12):
                w = wp.next()
                f.dma("sp", w[:], wv[:, :, jb * 512:(jb + 1) * 512])
                which = jb // 2
                if which in (2, 5):
                    pr = psR.next()
                    for kc in range(8):
                        f.mm(pr[0:2, :], scs[:, kc, :], w[:, kc, :], start=kc == 0, stop=kc == 7)
                    f.tt(rowst[:], pr[0:2, :], abr[:, l, jb * 512:(jb + 1) * 512], ALU.add)
                    f.dma("sp", modrow[:, l, 0 if which == 2 else 1, (jb % 2) * 512:(jb % 2 + 1) * 512], rowst[:])
                else:
                    for jc in range(4):
                        ch = jb * 4 + jc
                        for kc in range(8):
                            f.mm(psF[:, ch * 2:ch * 2 + 2], w[:, kc, jc * 128:(jc + 1) * 128], scs[:, kc, :],
                                 start=kc == 0, stop=kc == 7)
            for (c0, c1) in ((0, 16), (24, 40)):
                pv = psF[:, c0 * 2:c1 * 2].re("p (c s) -> p c s", s=2)
                f.tt(modF[:, l, c0:c1, :], pv, abf[:, l, c0:c1].un(2).bc([128, c1 - c0, 2]), ALU.add)
            for n_i, (sh0, sc0, g) in enumerate(((0, 8, g1), (24, 32, g2))):
                f.stt(AB[:, l, n_i, 0], modF[:, l, sc0:sc0 + 8, :], 1.0, g[:, l, :].un(2).bc([128, 8, 2]), ALU.add, ALU.mult)
                f.cp(AB[:, l, n_i, 1], modF[:, l, sh0:sh0 + 8, :])
        f.release(m)

    class NormCtx:
        def __init__(self):
            self.xp = Pool([f.sbuf("nx%d" % i, [128, D], F32) for i in range(2)])
            self.xnp = Pool([f.sbuf("nxn%d" % i, [128, D], BF16) for i in range(2)])
            self.stp = Pool([f.sbuf("nst%d" % i, [128, 4], F32) for i in range(3)])
            self.pp = Pool([f.psum("nps%d" % i, [128, 8, 128], BF16) for i in range(2)])

        def emit(self, l, n_i, hT, tt, c0):
            s = 0 if tt >= 2 else 1
            x = self.xp.next()
            st = self.stp.next()
            xn = self.xnp.next()
            f.dma("sp", x[:], xres_t[tt][:])
            f.act(xn[:], x[:], AF.Square, accum=st[:, 0:1])
            f.act(st[:, 1:2], st[:, 0:1], AF.Sqrt, bias=EPS_T[:, 0:1], scale=1.0 / D)
            f.recip(st[:, 2:3], st[:, 1:2])
            f.act(xn[:], x[:], AF.Identity, scale=st[:, 2:3])
            ps = self.pp.next()
            for kc in range(8):
                f.tr(ps[:, kc, :], xn[:, kc * 128:(kc + 1) * 128], ident[:])
            for kc in range(8):
                f.act(hT[:, kc, c0:c0 + 128], ps[:, kc, :], AF.Identity,
                      bias=AB[:, l, n_i, 1, kc, s:s + 1], scale=AB[:, l, n_i, 0, kc, s:s + 1])

    def norm_tiles(l, n_i, hT, tiles):
        m = f.mark()
        nctx = NormCtx()
        for tt in tiles:
            nctx.emit(l, n_i, hT, tt, tt * 128)
        f.release(m)

    EPS_T = f.sbuf("eps_t", [128, 1], F32)
    f.memset(EPS_T[:], EPS)

    def phase_proj(l, hT):
        m = f.mark()
        wp = Pool([f.sbuf("pw%d" % i, [128, 8, 512], BF16) for i in range(3)])
        psp = Pool([f.psum("pps%d" % i, [128, 512], F32) for i in range(4)])
        psq = Pool([f.psum("ppq%d" % i, [128, 512], F32) for i in range(3)])
        w_l = I["w_in"][l].re("(kc p) n -> p kc n", p=128)

        def loadw(c0, n):
            w = wp.next()
            f.dma("pool", w[:, :, 0:n], w_l[:, :, c0:c0 + n])
            return w

        def fm(w, wc0, ncol, evac):
            for bi, (t0, n) in enumerate(TB):
                ps = psp.next()
                for kc in range(8):
                    f.mm(ps[0:ncol, 0:n], w[:, kc, wc0:wc0 + ncol], hT[:, kc, t0:t0 + n], start=kc == 0, stop=kc == 7)
                evac(ps, bi, t0, n)

        def tm(w, ncol, evac):
            for tt in range(NT):
                ps = psp.next()
                for kc in range(8):
                    f.mm(ps[:, 0:ncol], hT[:, kc, tt * 128:(tt + 1) * 128], w[:, kc, 0:ncol], start=kc == 0, stop=kc == 7)
                evac(ps, tt)

        w0 = loadw(0, 384)
        wk = wp.next()
        f.dma("pool", wk[:, :, 0:192], I["w_kr2"][l].re("(kc p) a n -> p kc (a n)", p=128))
        qng = f.sbuf("qng", [128, DEPTH, 2], F32)
        kvng = f.sbuf("kvng", [128, DEPTH], F32)
        f.dma("sp", qng[:], I["qng"][:])
        f.dma("sp", kvng[:], I["kvng"][:])
        wuq = f.sbuf("wuq", [128, 2, 768], BF16)
        wuqs = f.sbuf("wuqs", [128, 2, 768], BF16)
        wk_k = f.sbuf("wukv_k", [128, 512], BF16)
        wk_v = f.sbuf("wukv_v", [128, 512], BF16)
        f.dma("pool", wuq[:], I["w_uq"][l].re("(kc p) n -> p kc n", p=128))
        f.dma("pool", wuqs[:], I["w_uq_sw"][l].re("(kc p) n -> p kc n", p=128))
        f.dma("pool", wk_k[:], I["w_ukv_k"][l])
        f.dma("pool", wk_v[:], I["w_ukv_v"][l])
        cosp = Pool([f.sbuf("cosb%d" % i, [96, 512], F32) for i in range(2)])
        sinp = Pool([f.sbuf("sinb%d" % i, [96, 512], F32) for i in range(2)])
        cqp = Pool([f.sbuf("cqb%d" % i, [128, 2, 512], F32) for i in range(2)])
        ckvp = Pool([f.sbuf("ckvb%d" % i, [128, 512], F32) for i in range(2)])
        cqnp = Pool([f.sbuf("cqnb%d" % i, [128, 2, 512], BF16) for i in range(2)])
        ckvnp = Pool([f.sbuf("ckvnb%d" % i, [128, 512], BF16) for i in range(2)])
        krp = Pool([f.sbuf("krb%d" % i, [96, 512], BF16) for i in range(2)])
        tmpa = Pool([f.sbuf("ptmpa%d" % i, [128, 512], F32) for i in range(2)])
        tmpb = Pool([f.sbuf("ptmpb%d" % i, [128, 512], F32) for i in range(2)])
        sqp = Pool([f.sbuf("psq%d" % i, [128, 2, 512], F32) for i in range(2)])
        rsp = Pool([f.sbuf("prs%d" % i, [128, 512], F32) for i in range(2)])
        qst = Pool([f.sbuf("qst%d" % i, [96, 512], BF16) for i in range(3)])
        kst = Pool([f.sbuf("kst%d" % i, [96, 512], BF16) for i in range(3)])
        vst = Pool([f.sbuf("vst%d" % i, [128, 8, 65], BF16) for i in range(2)])
        for vb in vst.bufs:
            f.memset(vb[:], 1.0)
        for bi, (t0, n) in enumerate(TB):
            cosb = cosp.next()
            sinb = sinp.next()
            f.dma("sp", cosb[64:96, 0:n], I["cosT"][64:96, t0:t0 + n])
            f.dma("sp", sinb[64:96, 0:n], I["sinT"][64:96, t0:t0 + n])
            cq = cqp.next()
            ckv = ckvp.next()
            for c in range(3):
                ps = psp.next()
                for kc in range(8):
                    f.mm(ps[:, 0:n], w0[:, kc, c * 128:(c + 1) * 128], hT[:, kc, t0:t0 + n], start=kc == 0, stop=kc == 7)
                f.cp(cq[:, c, 0:n] if c < 2 else ckv[:, 0:n], ps[:, 0:n], eng="act")
            pa = psp.next()
            pb = psp.next()
            for kc in range(8):
                f.mm(pa[0:96, 0:n], wk[:, kc, 0:96], hT[:, kc, t0:t0 + n], start=kc == 0, stop=kc == 7)
            for kc in range(8):
                f.mm(pb[0:96, 0:n], wk[:, kc, 96:192], hT[:, kc, t0:t0 + n], start=kc == 0, stop=kc == 7)
            ta = tmpa.next()
            tb_ = tmpb.next()
            krb = krp.next()
            f.tt(ta[64:96, 0:n], pa[64:96, 0:n], cosb[64:96, 0:n], ALU.mult)
            f.tt(tb_[64:96, 0:n], pb[64:96, 0:n], sinb[64:96, 0:n], ALU.mult)
            f.tt(krb[64:96, 0:n], ta[64:96, 0:n], tb_[64:96, 0:n], ALU.add, eng="pool")
            cqn = cqnp.next()
            ckvn = ckvnp.next()
            for (nchunk, gains) in ((2, qng), (1, kvng)):
                sq = sqp.next()
                ps = psp.next()
                for c in range(nchunk):
                    sv = cq[:, c, 0:n] if nchunk == 2 else ckv[:, 0:n]
                    f.tt(sq[:, c, 0:n], sv, sv, ALU.mult)
                for c in range(nchunk):
                    f.mm(ps[:, 0:n], ones_f[:], sq[:, c, 0:n], start=c == 0, stop=c == nchunk - 1)
                rs = rsp.next()
                f.act(rs[:, 0:n], ps[:, 0:n], AF.Sqrt, bias=EPS_T[:, 0:1], scale=1.0 / (128 * nchunk))
                f.recip(rs[:, 0:n], rs[:, 0:n])
                for c in range(nchunk):
                    sv = cq[:, c, 0:n] if nchunk == 2 else ckv[:, 0:n]
                    dv = cqn[:, c, 0:n] if nchunk == 2 else ckvn[:, 0:n]
                    gv = gains[:, l, c:c + 1] if nchunk == 2 else gains[:, l:l + 1]
                    f.stt(dv, sv, gv, rs[:, 0:n], ALU.mult, ALU.mult)
            for h in range(8):
                pa = psq.next()
                pb = psq.next()
                for kc in range(2):
                    f.mm(pa[0:96, 0:n], wuq[:, kc, h * 96:(h + 1) * 96], cqn[:, kc, 0:n], start=kc == 0, stop=kc == 1)
                for kc in range(2):
                    f.mm(pb[0:96, 0:n], wuqs[:, kc, h * 96:(h + 1) * 96], cqn[:, kc, 0:n], start=kc == 0, stop=kc == 1)
                q = qst.next()
                ta = tmpa.next()
                tb_ = tmpb.next()
                f.cp(q[0:64, 0:n], pa[0:64, 0:n], eng="act")
                f.tt(ta[64:96, 0:n], pa[64:96, 0:n], cosb[64:96, 0:n], ALU.mult)
                f.tt(tb_[64:96, 0:n], pb[64:96, 0:n], sinb[64:96, 0:n], ALU.mult)
                f.tt(q[64:96, 0:n], ta[64:96, 0:n], tb_[64:96, 0:n], ALU.add, eng="pool")
                f.dma("sp", qT_d[h, :, t0:t0 + n], q[:, 0:n])
                pk = psq.next()
                f.mm(pk[0:64, 0:n], wk_k[:, h * 64:(h + 1) * 64], ckvn[:, 0:n])
                k = kst.next()
                f.cp(k[0:64, 0:n], pk[0:64, 0:n], eng="act")
                f.cp(k[64:96, 0:n], krb[64:96, 0:n], eng="pool")
                f.dma("sp", kT_d[h, :, t0:t0 + n], k[:, 0:n])
            for ti in range(n // 128):
                tt = t0 // 128 + ti
                pv = psq.next()
                f.mm(pv[:, :], ckvn[:, ti * 128:(ti + 1) * 128], wk_v[:, :])
                vs = vst.next()
                f.cp(vs[:, :, 0:64], pv[:, :].re("p (h e) -> p h e", e=64), eng="act")
                f.dma("sp", V_d[tt * 128:(tt + 1) * 128, :], vs[:].re("p h e -> p (h e)"))

        wa = loadw(1952, 32)
        wa2 = f.sbuf("wa2", [16, 512], F32)
        ba = f.sbuf("ba", [1, 512], F32)
        f.dma("sp", wa2[:], I["w_a2"][l])
        f.dma("sp", ba[:], I["b_a"][l])
        aft = Pool([f.sbuf("aft%d" % i, [16, 512], F32) for i in range(2)])
        abt = Pool([f.sbuf("abt%d" % i, [16, 512], F32) for i in range(2)])
        ggp = Pool([f.sbuf("ggs%d" % i, [128, 512], F32) for i in range(2)])
        for bi, (t0, n) in enumerate(TB):
            pa = psp.next()
            pb = psp.next()
            for kc in range(8):
                f.mm(pa[0:16, 0:n], wa[:, kc, 0:16], hT[:, kc, t0:t0 + n], start=kc == 0, stop=kc == 7)
            for kc in range(8):
                f.mm(pb[0:16, 0:n], wa[:, kc, 16:32], hT[:, kc, t0:t0 + n], start=kc == 0, stop=kc == 7)
            af = aft.next()
            ab = abt.next()
            f.cp(af[:, 0:n], pa[0:16, 0:n], eng="act")
            f.cp(ab[:, 0:n], pb[0:16, 0:n], eng="act")
            for ti in range(n // 128):
                pg = psq.next()
                f.mm(pg[:, 0:256], af[:, ti * 128:(ti + 1) * 128], wa2[:, 0:256], start=True, stop=False)
                f.mm(pg[:, 0:256], ones_f[0:1, :], ba[:, 0:256], start=False, stop=True)
                f.mm(pg[:, 256:512], ab[:, ti * 128:(ti + 1) * 128], wa2[:, 256:512], start=True, stop=False)
                f.mm(pg[:, 256:512], ones_f[0:1, :], ba[:, 256:512], start=False, stop=True)
                gs = ggp.next()
                f.act(gs[:], pg[:], AF.Exp, scale=-1.0)
                f.act(gs[:], gs[:], AF.Ln, bias=ONE_T[:, 0:1])
                f.ts(gs[:], gs[:], -1.0 / 16.0, ALU.mult)
                tt = t0 // 128 + ti
                f.dma("sp", gg_d[tt * 128:(tt + 1) * 128, :], gs[:])

        st32 = Pool([f.sbuf("pst32_%d" % i, [128, 512], F32) for i in range(3)])
        st16 = Pool([f.sbuf("pst16_%d" % i, [128, 512], BF16) for i in range(3)])

        def ev_qk(ps, tt):
            s = st32.next()
            f.act(s[:, 0:256], ps[:, 0:256], AF.Copy, scale=0.125)
            f.cp(s[:, 256:512], ps[:, 256:512], eng="dve")
            f.dma("sp", gqk_d[tt * 128:(tt + 1) * 128, :], s[:])

        def ev_to(dst, c0, dt16):
            def ev(ps, tt):
                s = (st16 if dt16 else st32).next()
                f.cp(s[:], ps[:], eng="act" if tt % 2 else "dve")
                f.dma("sp", dst[tt * 128:(tt + 1) * 128, c0:c0 + 512], s[:])
            return ev

        tm(loadw(416, 512), 512, ev_qk)
        tm(loadw(928, 512), 512, ev_to(gv_d, 0, True))
        tm(loadw(1440, 512), 512, ev_to(gr_d, 0, False))
        for sc in range(3):
            w = loadw(1984 + sc * 512, 512)
            for c in range(4):
                ch = sc * 4 + c

                def ev_h(ps, bi, t0, n, ch=ch):
                    s = st16.next()
                    f.cp(s[:, 0:n], ps[:, 0:n], eng="act" if (bi + ch) % 2 else "dve")
                    f.dma("sp", zhyT_d[ch * 128:(ch + 1) * 128, t0:t0 + n], s[:, 0:n])
                fm(w, c * 128, 128, ev_h)

        for sc in range(6):
            w = loadw(3520 + sc * 512, 512)
            for c in range(4):
                ch = sc * 4 + c

                def ev_g(ps, bi, t0, n, ch=ch):
                    s = st16.next()
                    f.act(s[:, 0:n], ps[:, 0:n], AF.Sigmoid)
                    f.dma("sp", gatesT_d[ch * 128:(ch + 1) * 128, t0:t0 + n], s[:, 0:n])
                fm(w, c * 128, 128, ev_g)
        f.release(m)

    def phase_att(l):
        m = f.mark()
        Vaug = f.sbuf("Vaug", [128, NT, 520], BF16)
        f.dma("sp", Vaug[:], V_d[:].re("(t p) e -> p t e", p=128))
        qp = Pool([f.sbuf("aq%d" % i, [96, T], BF16) for i in range(2)])
        kp = Pool([f.sbuf("ak%d" % i, [96, T], BF16) for i in range(2)])
        pp = Pool([f.sbuf("ap%d" % i, [128, 512], BF16) for i in range(4)])
        sps = Pool([f.psum("as%d" % i, [128, 512], F32) for i in range(4)])
        ops = Pool([f.psum("ao%d" % i, [128, 512], F32) for i in range(2)])
        bps = f.psum("abc", [128, 512], F32)
        rec = Pool([f.sbuf("arec%d" % i, [65, 512], F32) for i in range(2)])
        osb = Pool([f.sbuf("aosb%d" % i, [64, 512], F32) for i in range(2)])
        yst = Pool([f.sbuf("ayst%d" % i, [64, 512], BF16) for i in range(3)])
        for h in range(8):
            q = qp.next()
            k = kp.next()
            f.dma("sp", q[:], qT_d[h])
            f.dma("sp", k[:], kT_d[h])
            for bi, (t0, n) in enumerate(TB):
                if bi == 0 and l == DEPTH - 1:
                    continue
                nk = 2 if bi == 0 else NT
                o = ops.next()
                pend = None
                for j in range(nk + 1):
                    p = None
                    if j < nk:
                        s = sps.next()
                        f.mm(s[:, 0:n], k[:, j * 128:(j + 1) * 128], q[:, t0:t0 + n])
                        p = pp.next()
                        f.act(p[:, 0:n], s[:, 0:n], AF.Exp, scale=MLA_SCALE)
                    if pend is not None:
                        jj, pv = pend
                        f.mm(o[0:65, 0:n], Vaug[:, jj, h * 65:(h + 1) * 65], pv[:, 0:n], start=jj == 0, stop=jj == nk - 1)
                    pend = (j, p) if j < nk else None
                r = rec.next()
                f.recip(r[64:65, 0:n], o[64:65, 0:n])
                f.mm(bps[0:64, 0:n], ones_f[64:65, 0:64], r[64:65, 0:n])
                os_ = osb.next()
                f.cp(os_[:, 0:n], o[0:64, 0:n], eng="act")
                y = yst.next()
                f.tt(y[:, 0:n], os_[:, 0:n], bps[0:64, 0:n], ALU.mult)
                f.dma("sp", ymlaT_d[h * 64:(h + 1) * 64, t0:t0 + n], y[:, 0:n])
        f.release(m)

    def phase_gla(l):
        m = f.mark()
        Mm = f.sbuf("glaM", [128, 2, 128], F32)
        mask = f.sbuf("glamask", [128, 2, 128], F32)
        ind = f.sbuf("glaind", [128, 2], F32)
        gon = f.sbuf("gon", [128, 128], F32)
        f.dma("sp", Mm[:], I["gla_M"][:])
        f.dma("sp", mask[:], I["gla_mask"][:])
        f.dma("sp", ind[:], I["gla_ind"][:])
        f.dma("sp", gon[:], I["gla_on"][l])
        S = f.sbuf("glaS", [64, 4, 128], F32)
        qkp = Pool([f.sbuf("gqk%d" % i, [128, 512], F32) for i in range(2)])
        gp = Pool([f.sbuf("gg%d" % i, [128, 256], F32) for i in range(2)])
        vp = Pool([f.sbuf("gv%d" % i, [128, 512], BF16) for i in range(2)])
        ePp = Pool([f.sbuf("geP%d" % i, [128, 256], F32) for i in range(2)])
        eNp = Pool([f.sbuf("geN%d" % i, [128, 256], F32) for i in range(2)])
        qpp = Pool([f.sbuf("gqp%d" % i, [128, 256], BF16) for i in range(2)])
        kpp = Pool([f.sbuf("gkp%d" % i, [128, 256], BF16) for i in range(2)])
        decp = Pool([f.sbuf("gdec%d" % i, [64, 4, 2], F32) for i in range(2)])
        qkTp = Pool([f.sbuf("gqkT%d" % i, [64, 8, 128], BF16) for i in range(2)])
        Amp = Pool([f.sbuf("gAm%d" % i, [128, 4, 128], BF16) for i in range(2)])
        Smidp = Pool([f.sbuf("gSm%d" % i, [64, 4, 128], F32) for i in range(2)])
        Smbp = Pool([f.sbuf("gSb%d" % i, [64, 4, 128], BF16) for i in range(2)])
        osp = Pool([f.sbuf("gos%d" % i, [128, 4, 128], F32) for i in range(2)])
        opp = Pool([f.sbuf("gop%d" % i, [128, 4, 128], F32) for i in range(2)])
        sqp = Pool([f.sbuf("gsq%d" % i, [128, 4, 128], F32) for i in range(2)])
        stp = Pool([f.sbuf("gst%d" % i, [128, 8], F32) for i in range(2)])
        rp = Pool([f.sbuf("gr%d" % i, [128, 512], F32) for i in range(2)])
        yp = Pool([f.sbuf("gy%d" % i, [128, 512], BF16) for i in range(2)])
        yTp = Pool([f.sbuf("gyT%d" % i, [128, 4, 128], BF16) for i in range(2)])
        pE = f.psum("gpE", [128, 512], F32)
        pcs = f.psum("gpcs", [128, 512], F32)
        ptr = f.psum("gptr", [128, 8, 128], BF16)
        pA = f.psum("gpA", [128, 4, 128], F32)
        po = Pool([f.psum("gpo%d" % i, [128, 4, 128], F32) for i in range(2)])
        pU = f.psum("gpU", [128, 4, 128], F32)
        pT = f.psum("gpT", [128, 8, 128], BF16)
        for d in (0, 1):
            order = list(range(NT)) if d == 0 else [1, 0] + list(range(NT - 1, 1, -1))
            f.memset(S[:], 0.0)
            for tt in order:
                r0 = tt * 128
                qk = qkp.next()
                g = gp.next()
                v = vp.next()
                f.dma("sp", qk[:], gqk_d[r0:r0 + 128, :])
                f.dma("sp", g[:], gg_d[r0:r0 + 128, d * 256:(d + 1) * 256])
                f.dma("sp", v[:], gv_d[r0:r0 + 128, :])
                f.mm(pE[:, 0:256], Mm[:, d, :], g[:])
                eP = ePp.next()
                eN = eNp.next()
                f.act(eP[:], pE[:, 0:256], AF.Exp)
                f.act(eN[:], pE[:, 0:256], AF.Exp, scale=-1.0)
                for h in range(4):
                    f.mm(pcs[0:64, h * 2:(h + 1) * 2], g[:, h * 64:(h + 1) * 64], ind[:])
                dec = decp.next()
                f.act(dec[:].re("p h s -> p (h s)"), pcs[0:64, 0:8], AF.Exp)
                dmid = dec[:, :, d:d + 1]
                dend = dec[:, :, 1 - d:2 - d]
                qp_ = qpp.next()
                kp_ = kpp.next()
                f.tt(qp_[:], qk[:, 0:256], eP[:], ALU.mult)
                f.tt(kp_[:], qk[:, 256:512], eN[:], ALU.mult, eng="pool")
                for h in range(4):
                    f.tr(ptr[0:64, h, :], qp_[:, h * 64:(h + 1) * 64], ident[:])
                for h in range(4):
                    f.tr(ptr[0:64, 4 + h, :], kp_[:, h * 64:(h + 1) * 64], ident[:])
                qkT = qkTp.next()
                f.cp(qkT[:], ptr[0:64], eng="act")
                for h in range(4):
                    f.mm(pA[:, h, :], qkT[:, 4 + h, :], qkT[:, h, :])
                Am = Amp.next()
                f.tt(Am[:], pA[:], mask[:, d:d + 1, :].bc([128, 4, 128]), ALU.mult)
                Smid = Smidp.next()
                f.tt(Smid[:], S[:], dmid.bc([64, 4, 128]), ALU.mult)
                Smb = Smbp.next()
                f.cp(Smb[:], Smid[:], eng="act")
                o_ps = po.next()
                for h in range(4):
                    f.mm(o_ps[:, h, :], qkT[:, h, :], Smb[:, h, :], start=True, stop=False)
                    f.mm(o_ps[:, h, :], Am[:, h, :], v[:, h * 128:(h + 1) * 128], start=False, stop=True)
                for h in range(4):
                    f.mm(pU[0:64, h, :], kp_[:, h * 64:(h + 1) * 64], v[:, h * 128:(h + 1) * 128])
                f.tt(S[:], Smid[:], pU[0:64], ALU.add)
                f.tt(S[:], S[:], dend.bc([64, 4, 128]), ALU.mult)
                if d == 0:
                    os_ = osp.next()
                    f.cp(os_[:], o_ps[:], eng="act")
                    f.dma("sp", og_d[r0:r0 + 128, :], os_[:].re("p h v -> p (h v)"))
                else:
                    if tt < 2 and l == DEPTH - 1:
                        continue
                    op_ = opp.next()
                    f.dma("sp", op_[:].re("p h v -> p (h v)"), og_d[r0:r0 + 128, :])
                    rr = rp.next()
                    f.dma("sp", rr[:], gr_d[r0:r0 + 128, :])
                    os_ = osp.next()
                    f.tt(os_[:], o_ps[:], op_[:], ALU.add)
                    sq = sqp.next()
                    f.tt(sq[:], os_[:], os_[:], ALU.mult, eng="pool")
                    st = stp.next()
                    f.op("dve", lambda e: e.tensor_reduce(out=st[:, 0:4].ap, in_=sq[:].ap, axis=mybir.AxisListType.X, op=ALU.add),
                         reads=[sq], writes=[st])
                    f.act(st[:, 4:8], st[:, 0:4], AF.Sqrt, bias=EPS_T[:, 0:1], scale=1.0 / 128)
                    f.recip(st[:, 4:8], st[:, 4:8])
                    f.tt(os_[:], os_[:], st[:, 4:8].un(2).bc([128, 4, 128]), ALU.mult)
                    f.tt(os_[:], os_[:], gon[:].un(1).bc([128, 4, 128]), ALU.mult, eng="pool")
                    f.act(rr[:], rr[:], AF.Silu)
                    y = yp.next()
                    f.tt(y[:], os_[:].re("p h v -> p (h v)"), rr[:], ALU.mult)
                    for c in range(4):
                        f.tr(pT[:, c, :], y[:, c * 128:(c + 1) * 128], ident[:])
                    yT = yTp.next()
                    f.cp(yT[:], pT[:, 0:4, :], eng="act")
                    f.dma("sp", yglaT_d[:, r0:r0 + 128].re("(c p) t -> p c t", p=128), yT[:])
        f.release(m)

    def phase_merge(l):
        m = f.mark()
        wo = [f.sbuf("wo%d" % i, [128, 4, D], BF16) for i in range(3)]
        for i, nm in enumerate(("w_o_mla", "w_o_gla", "w_o_hy")):
            f.dma("pool", wo[i][:], I[nm][l].re("(kc p) n -> p kc n", p=128))
        wout = f.sbuf("wout", [128, 8, D], BF16)
        for c in range(2):
            f.dma("pool", wout[:, :, c * 512:(c + 1) * 512], I["w_out"][l].re("(kc p) n -> p kc n", p=128)[:, :, c * 512:(c + 1) * 512])
        gx = f.sbuf("gx", [128, 2, D], F32)
        f.dma("sp", gx[:, 0, :], modrow[0, l, 0].pb(128))
        f.dma("sp", gx[:, 1, :], modrow[1, l, 0].pb(128))
        ybp = Pool([f.sbuf("mby%d" % i, [128, 3, 4, 512], BF16) for i in range(2)])
        gtp = Pool([f.sbuf("mgt%d" % i, [128, 3, 512], BF16) for i in range(3)])
        mTp = Pool([f.sbuf("mT%d" % i, [128, 8, 512], BF16) for i in range(2)])
        accp = Pool([f.sbuf("macc%d" % i, [128, 512], F32) for i in range(2)])
        tmpp = Pool([f.sbuf("mtmp%d" % i, [128, 512], F32) for i in range(3)])
        xp = Pool([f.sbuf("mx%d" % i, [128, D], F32) for i in range(3)])
        ps3 = [Pool([f.psum("mps%d_%d" % (i, j), [128, 512], F32) for j in range(2)]) for i in range(3)]
        pso = Pool([f.psum("mpo%d" % i, [128, 512], F32) for i in range(2)])
        gview = gatesT_d[:].re("(b c p) t -> p b c t", b=3, c=8, p=128)
        for bi, (t0, n) in enumerate(TB):
            if bi == 0 and l == DEPTH - 1:
                continue
            yb = ybp.next()
            for i, srcT in enumerate((ymlaT_d, yglaT_d, yhyT_d)):
                f.dma("sp", yb[:, i, :, 0:n], srcT[:, t0:t0 + n].re("(kc p) t -> p kc t", p=128))
            mT = mTp.next()
            for oc in range(8):
                gt = gtp.next()
                f.dma("sp", gt[:, :, 0:n], gview[:, :, oc, t0:t0 + n])
                pss = []
                for i in range(3):
                    ps = ps3[i].next()
                    for kc in range(4):
                        f.mm(ps[:, 0:n], wo[i][:, kc, oc * 128:(oc + 1) * 128], yb[:, i, kc, 0:n], start=kc == 0, stop=kc == 3)
                    pss.append(ps)
                acc = accp.next()
                t1 = tmpp.next()
                t2 = tmpp.next()
                f.tt(acc[:, 0:n], pss[0][:, 0:n], gt[:, 0, 0:n], ALU.mult)
                f.tt(t1[:, 0:n], pss[1][:, 0:n], gt[:, 1, 0:n], ALU.mult)
                f.tt(t2[:, 0:n], pss[2][:, 0:n], gt[:, 2, 0:n], ALU.mult)
                f.tt(acc[:, 0:n], acc[:, 0:n], t1[:, 0:n], ALU.add, eng="pool")
                f.tt(mT[:, oc, 0:n], acc[:, 0:n], t2[:, 0:n], ALU.add, eng="pool")
            s = 1 if bi == 0 else 0
            for ti in range(n // 128):
                tt = t0 // 128 + ti
                x = xp.next()
                f.dma("sp", x[:], xres_t[tt][:])
                for half in range(2):
                    ps = pso.next()
                    for kc in range(8):
                        f.mm(ps[:, :], mT[:, kc, ti * 128:(ti + 1) * 128], wout[:, kc, half * 512:(half + 1) * 512],
                             start=kc == 0, stop=kc == 7)
                    t1 = tmpp.next()
                    f.tt(t1[:], ps[:], gx[:, s, half * 512:(half + 1) * 512], ALU.mult)
                    f.tt(x[:, half * 512:(half + 1) * 512], x[:, half * 512:(half + 1) * 512], t1[:], ALU.add, eng="pool")
                f.dma("sp", xres_t[tt][:], x[:])
        f.release(m)

    def phase_ffn(l):
        m = f.mark()
        w1 = f.sbuf("ffw1", [128, 8, 4096], BF16)
        w2 = f.sbuf("ffw2", [128, 32, D], BF16)
        w1v = I["ff_w1"][l].re("(kc p) n -> p kc n", p=128)
        w2v = I["ff_w2"][l].re("(kc p) n -> p kc n", p=128)
        for c in range(8):
            f.dma("pool", w1[:, :, c * 512:(c + 1) * 512], w1v[:, :, c * 512:(c + 1) * 512])
        for c in range(8):
            f.dma("pool", w2[:, c * 4:(c + 1) * 4, :], w2v[:, c * 4:(c + 1) * 4, :])
        gx = f.sbuf("fgx", [128, 2, D], F32)
        f.dma("sp", gx[:, 0, :], modrow[0, l, 1].pb(128))
        f.dma("sp", gx[:, 1, :], modrow[1, l, 1].pb(128))
        nctx = NormCtx()
        hTp = Pool([f.sbuf("fhT%d" % i, [128, 8, 256], BF16) for i in range(2)])
        aTp = Pool([f.sbuf("faT%d" % i, [128, 32, 256], BF16) for i in range(1)])
        rp = Pool([f.sbuf("fr%d" % i, [128, 256], F32) for i in range(2)])
        tmpp = Pool([f.sbuf("ftmp%d" % i, [128, 512], F32) for i in range(2)])
        xp = Pool([f.sbuf("fx%d" % i, [128, D], F32) for i in range(2)])
        psA = Pool([f.psum("fpa%d" % i, [128, 512], F32) for i in range(3)])
        pso = Pool([f.psum("fpo%d" % i, [128, 512], F32) for i in range(3)])
        for blk in range(T // 256):
            if blk == 0 and l == DEPTH - 1:
                continue
            s = 1 if blk == 0 else 0
            hTb = hTp.next()
            for ti in range(2):
                nctx.emit(l, 1, hTb, blk * 2 + ti, ti * 128)
            aT = aTp.next()
            for fc in range(32):
                ps = psA.next()
                for kc in range(8):
                    f.mm(ps[:, 0:256], w1[:, kc, fc * 128:(fc + 1) * 128], hTb[:, kc, :], start=kc == 0, stop=kc == 7)
                r = rp.next()
                f.act(r[:], ps[:, 0:256], AF.Relu)
                f.tt(aT[:, fc, :], r[:], r[:], ALU.mult, eng="pool" if fc % 2 else "dve")
            for ti in range(2):
                tt = blk * 2 + ti
                x = xp.next()
                f.dma("sp", x[:], xres_t[tt][:])
                for half in range(2):
                    ps = pso.next()
                    for fc in range(32):
                        f.mm(ps[:, :], aT[:, fc, ti * 128:(ti + 1) * 128], w2[:, fc, half * 512:(half + 1) * 512],
                             start=fc == 0, stop=fc == 31)
                    t1 = tmpp.next()
                    f.tt(t1[:], ps[:], gx[:, s, half * 512:(half + 1) * 512], ALU.mult)
                    f.tt(x[:, half * 512:(half + 1) * 512], x[:, half * 512:(half + 1) * 512], t1[:], ALU.add, eng="pool")
                f.dma("sp", xres_t[tt][:], x[:])
        f.release(m)

    def phase_final():
        m = f.mark()
        fg = f.sbuf("fing", [128, D], F32)
        f.dma("sp", fg[:], I["fin_g"][:])
        xp = Pool([f.sbuf("zx%d" % i, [128, D], F32) for i in range(3)])
        jp = Pool([f.sbuf("zj%d" % i, [128, D], F32) for i in range(2)])
        stp = Pool([f.sbuf("zst%d" % i, [128, 4], F32) for i in range(3)])
        for tt in range(2, NT):
            x = xp.next()
            j = jp.next()
            st = stp.next()
            f.dma("sp", x[:], xres_t[tt][:])
            f.act(j[:], x[:], AF.Square, accum=st[:, 0:1])
            f.act(st[:, 1:2], st[:, 0:1], AF.Sqrt, bias=EPS_T[:, 0:1], scale=1.0 / D)
            f.recip(st[:, 2:3], st[:, 1:2])
            f.act(j[:], x[:], AF.Identity, scale=st[:, 2:3])
            f.tt(x[:], j[:], fg[:], ALU.mult)
            f.dma("sp", out_y[(tt - 2) * 128:(tt - 1) * 128, :], x[:])
        f.release(m)

    def phase_hy(l, ctx_seg):
        m = f.mark()
        r0, nrow = (0, LC) if ctx_seg else (LC, L)
        na = nrow // 64
        NA = 2 * na
        NF = NA * 64
        NFA = NA // 2 + 1
        sfx = "_c" if ctx_seg else ""
        rn = f.sbuf("hrn", [128, 2, 512], F32)

        mA = f.mark()
        swt = f.sbuf("hsw", [128, 12, 4], F32)
        f.dma("sp", swt[:], I["hy_swb"][l])
        zp = Pool([f.sbuf("hz%d" % i, [128, L], BF16) for i in range(2)])
        accp = Pool([f.sbuf("hacc%d" % i, [128, L], F32) for i in range(2)])
        op_ = Pool([f.sbuf("hso%d" % i, [128, L], BF16) for i in range(2)])
        for ch in range(12):
            z = zp.next()
            acc = accp.next()
            o = op_.next()
            f.dma("sp", z[:, 0:nrow], zhyT_d[ch * 128:(ch + 1) * 128, r0:r0 + nrow])
            f.act(acc[:, 0:nrow], z[:, 0:nrow], AF.Identity, bias=swt[:, ch, 3:4], scale=swt[:, ch, 1:2])
            f.stt(acc[:, 1:nrow], z[:, 0:nrow - 1], swt[:, ch, 0:1], acc[:, 1:nrow], ALU.mult, ALU.add)
            f.stt(acc[:, 0:nrow - 1], z[:, 1:nrow], swt[:, ch, 2:3], acc[:, 0:nrow - 1], ALU.mult, ALU.add)
            f.cp(o[:, 0:nrow], acc[:, 0:nrow], eng="act")
            f.dma("sp", scT_d[ch * 128:(ch + 1) * 128, r0:r0 + nrow], o[:, 0:nrow])
        f.release(mA)

        mB = f.mark()
        featT = f.sbuf("hfeat", [33, NF], F32)
        f.dma("sp", featT[:], I["hy_featT" + sfx][:])
        fw1 = f.sbuf("hfw1", [33, 64], F32)
        fw2 = f.sbuf("hfw2", [64, 64], F32)
        fw3 = f.sbuf("hfw3", [64, 2048], F32)
        fb12 = f.sbuf("hfb12", [64, 2], F32)
        fb3 = f.sbuf("hfb3", [1, 2048], F32)
        f.dma("sp", fw1[:], I["hy_f_w1"][l])
        f.dma("sp", fw2[:], I["hy_f_w2"][l])
        f.dma("sp", fw3[:], I["hy_f_w3"][l])
        f.dma("sp", fb12[:], I["hy_fb12"][l])
        f.dma("sp", fb3[:], I["hy_f_b3"][l])
        hd2 = f.sbuf("hhd2", [64, NF], F32)
        hd1p = Pool([f.sbuf("hhd1_%d" % i, [64, 512], F32) for i in range(2)])
        ap_ = Pool([f.sbuf("ha%d" % i, [64, 512], F32) for i in range(2)])
        m1p = Pool([f.sbuf("hm1_%d" % i, [64, 512], F32) for i in range(2)])
        m2p = Pool([f.sbuf("hm2_%d" % i, [64, 512], F32) for i in range(2)])
        winp = Pool([f.sbuf("hwin%d" % i, [128, 512], F32) for i in range(2)])
        kp = Pool([f.sbuf("hk%d" % i, [128, 512], F32) for i in range(3)])
        kap = Pool([f.sbuf("hka%d" % i, [128, 512], F32) for i in range(2)])
        psm = Pool([f.psum("hpm%d" % i, [128, 512], F32) for i in range(3)])
        pn = [f.psum("hpn%d" % i, [128, 512], F32) for i in range(2)]
        tiles = list(range(NF // 128))
        blocks = list(range(NF // 512))

        def sin_wrap(dst, ps, bias):
            a = ap_.next()
            m1 = m1p.next()
            m2 = m2p.next()
            f.act(a[:], ps[0:64, :], AF.Identity, bias=bias)
            f.ts(m1[:], a[:], math.pi, ALU.is_gt, 2 * math.pi, ALU.mult)
            f.ts(m2[:], a[:], -math.pi, ALU.is_lt, 2 * math.pi, ALU.mult, eng="pool")
            f.tt(a[:], a[:], m1[:], ALU.subtract)
            f.tt(a[:], a[:], m2[:], ALU.add)
            f.act(dst, a[:], AF.Sin)

        for blk in blocks:
            ps = psm.next()
            f.mm(ps[0:64, :], fw1[:], featT[:, blk * 512:(blk + 1) * 512])
            hd1 = hd1p.next()
            sin_wrap(hd1[:], ps, fb12[:, 0:1])
            ps2 = psm.next()
            f.mm(ps2[0:64, :], fw2[:], hd1[:])
            sin_wrap(hd2[:, blk * 512:(blk + 1) * 512], ps2, fb12[:, 1:2])
        for ti in tiles:
            dr = 0 if ti < len(tiles) // 2 else 1
            win = winp.next()
            f.dma("sp", win[:], I["hy_win" + sfx][ti * 128:(ti + 1) * 128, :])
            for n_ in range(2):
                c0 = (dr * 2 + n_) * 512
                ps = psm.next()
                f.mm(ps[:], hd2[:, ti * 128:(ti + 1) * 128], fw3[:, c0:c0 + 512], start=True, stop=False)
                f.mm(ps[:], ones_f[0:1, :], fb3[:, c0:c0 + 512], start=False, stop=True)
                k = kp.next()
                f.tt(k[:], ps[:], win[:], ALU.mult)
                f.dma("sp", kf_d[n_, ti * 128:(ti + 1) * 128, :], k[:])
                ka = kap.next()
                f.act(ka[:], k[:], AF.Abs)
                f.mm(pn[n_][:], ones_f[:], ka[:], start=ti == tiles[0], stop=ti == tiles[-1])
        for n_ in range(2):
            f.recip(rn[:, n_, :], pn[n_][:])
        f.release(mB)

        F1 = f.sbuf("hF1", [NA, 3 * NFA], BF16)
        E2r = f.sbuf("hE2r", [128, NFA, 128], BF16)
        E2i = f.sbuf("hE2i", [128, NFA, 128], BF16)
        f.dma("sp", F1[:], I["hy_F1" + sfx][:])
        f.dma("sp", E2r[:], I["hy_E2r" + sfx][:])
        f.dma("sp", E2i[:], I["hy_E2i" + sfx][:])
        Yp = Pool([f.sbuf("hY%d" % i, [128, 32, 3 * NFA], BF16) for i in range(2)])
        psY = Pool([f.psum("hpY%d" % i, [128, 512], F32) for i in range(2)])
        psXr = Pool([f.psum("hpXr%d" % i, [128, 8, 32], F32) for i in range(1)])
        psXi = Pool([f.psum("hpXi%d" % i, [128, 8, 32], F32) for i in range(1)])

        def spectrum(ut, Kp, consume):
            Y = Yp.next()
            for q in range(32):
                ps = psY.next()
                f.mm(ps[:, 0:3 * NFA], ut[0:Kp, q, :], F1[0:Kp, :])
                f.cp(Y[:, q, :], ps[:, 0:3 * NFA], eng="act" if q % 2 else "dve")
            for fa0 in range(0, NFA, 8):
                nfa = min(8, NFA - fa0)
                pr = psXr.next()
                pi = psXi.next()
                for i in range(nfa):
                    fa = fa0 + i
                    f.mm(pr[:, i, :], E2r[:, fa, :], Y[:, :, fa], start=True, stop=False)
                    f.mm(pr[:, i, :], E2i[:, fa, :], Y[:, :, 2 * NFA + fa], start=False, stop=True)
                for i in range(nfa):
                    fa = fa0 + i
                    f.mm(pi[:, i, :], E2i[:, fa, :], Y[:, :, fa], start=True, stop=False)
                    f.mm(pi[:, i, :], E2r[:, fa, :], Y[:, :, NFA + fa], start=False, stop=True)
                consume(fa0, nfa, pr, pi)

        mC = f.mark()
        kg = f.sbuf("hkg", [128, 8, 32, 128], BF16)
        stg = Pool([f.sbuf("hstg%d" % i, [128, 8, 512], F32) for i in range(2)])
        hb = f.sbuf("hhb", [1, 2, 512], F32)
        f.dma("sp", hb[:], I["hy_bias"][l])
        k0 = f.sbuf("hk0", [1, 512], F32)
        Hst = Pool([f.sbuf("hHst%d" % i, [128, 2, NFA, 32], BF16) for i in range(2)])
        for n_ in range(2):
            kv = kf_d[n_, 0:NF].re("(a b) c -> a b c", b=64)
            for bc in range(8):
                s = stg.next()
                f.dma("sp", s[0:NA], kv[:, bc * 8:(bc + 1) * 8, :])
                for g in range(8):
                    f.tt(kg[0:NA, g].re("p q (b cp) -> p b q cp", cp=2)[:, bc * 8:(bc + 1) * 8],
                         s[0:NA, :, g * 64:(g + 1) * 64].re("p b (q cp) -> p b q cp", cp=2),
                         rn[0:NA, n_, g * 64:(g + 1) * 64].re("p (q cp) -> p q cp", cp=2).un(1).bc([NA, 8, 32, 2]),
                         ALU.mult, eng="pool" if g % 2 else "dve")
                if bc == 0:
                    f.tt(k0[:], s[0:1, 0, :], rn[0:1, n_, :], ALU.mult)
                    f.tt(kg[0:1].re("p g q (b cp) -> p g q b cp", cp=2)[:, :, :, 0, :], k0[:].re("p (g q cp) -> p g q cp", g=8, cp=2),
                         hb[:, n_, :].re("p (g q cp) -> p g q cp", g=8, cp=2), ALU.add)
            for g in range(8):
                hs = Hst.next()

                def cons(fa0, nfa, pr, pi, hs=hs):
                    f.cp(hs[:, 0, fa0:fa0 + nfa, :], pr[:, 0:nfa, :], eng="act")
                    f.cp(hs[:, 1, fa0:fa0 + nfa, :], pi[:, 0:nfa, :], eng="dve")
                spectrum(kg[:, g], NA, cons)
                f.dma("sp", H_d[n_, g, :, 0:2 * NFA * 32], hs[:].re("p r f q -> p (r f q)"))
        f.release(mC)

        CA = f.sbuf("hCA", [128, 3, 128], BF16)
        DBr = f.sbuf("hDBr", [NFA, 64, na], BF16)
        DBni = f.sbuf("hDBni", [NFA, 64, na], BF16)
        f.dma("sp", CA[:], I["hy_CA"][:])
        f.dma("sp", DBr[:], I["hy_DBr" + sfx][:])
        f.dma("sp", DBni[:], I["hy_DBni" + sfx][:])
        uTp = Pool([f.sbuf("huT%d" % i, [64, L], BF16) for i in range(1)])
        gTp = Pool([f.sbuf("hgT%d" % i, [64, L], BF16) for i in range(2)])
        yTp = Pool([f.sbuf("hyT%d" % i, [64, L], BF16) for i in range(1)])
        up = Pool([f.sbuf("hu%d" % i, [64, 32, 128], BF16) for i in range(2)])
        Hp = Pool([f.sbuf("hH%d" % i, [128, 2, NFA, 32], BF16) for i in range(2)])
        Pp = Pool([f.sbuf("hP%d" % i, [128, 2, 32, NFA], BF16) for i in range(2)])
        Z0p = Pool([f.sbuf("hZ0_%d" % i, [NFA, 2, 64, 64], BF16) for i in range(1)])
        tp = Pool([f.sbuf("ht%d" % i, [128, 8, 32], F32) for i in range(6)])
        pst = Pool([f.psum("hpt%d" % i, [128, 8, 64], BF16) for i in range(1)])
        psZ = Pool([f.psum("hpZ%d" % i, [128, 4, 128], F32) for i in range(2)])
        psO = Pool([f.psum("hpO%d" % i, [128, 8, 64], F32) for i in range(1)])
        for n_ in range(2):
            srcT = scT_d[1024:1536] if n_ == 0 else y1T_d
            gateT = scT_d[0:512] if n_ == 0 else scT_d[512:1024]
            dstT = y1T_d if n_ == 0 else yhyT_d
            for g in range(8):
                uT = uTp.next()
                gT = gTp.next()
                f.dma("sp", uT[:, 0:nrow], srcT[g * 64:(g + 1) * 64, r0:r0 + nrow])
                f.dma("sp", gT[:, 0:nrow], gateT[g * 64:(g + 1) * 64, r0:r0 + nrow])
                H = Hp.next()
                f.dma("sp", H[:].re("p r f q -> p (r f q)"), H_d[n_, g, :, 0:2 * NFA * 32])
                u = up.next()
                uv = uT[:, 0:nrow].re("c (a b) -> c b a", b=64)
                for b0 in range(0, 64, 8):
                    pt = pst.next()
                    for i in range(8):
                        f.tr(pt[0:na, i, :], uv[:, b0 + i, :], ident[0:64, 0:64])
                    f.cp(u[0:na].re("a q (b cp) -> a b q cp", cp=2)[:, b0:b0 + 8], pt[0:na, :, :].re("a b (q cp) -> a b q cp", cp=2),
                         eng="act" if (b0 // 8) % 2 else "dve")
                P = Pp.next()

                def cons(fa0, nfa, pr, pi, H=H, P=P):
                    t1, t2, t3, t4 = tp.next(), tp.next(), tp.next(), tp.next()
                    f.tt(t1[:, 0:nfa], pr[:, 0:nfa, :], H[:, 0, fa0:fa0 + nfa, :], ALU.mult)
                    f.tt(t2[:, 0:nfa], pi[:, 0:nfa, :], H[:, 1, fa0:fa0 + nfa, :], ALU.mult)
                    f.tt(t3[:, 0:nfa], pr[:, 0:nfa, :], H[:, 1, fa0:fa0 + nfa, :], ALU.mult)
                    f.tt(t4[:, 0:nfa], pi[:, 0:nfa, :], H[:, 0, fa0:fa0 + nfa, :], ALU.mult)
                    f.tt(P[:, 0, :, fa0:fa0 + nfa].re("p q f -> p f q"), t1[:, 0:nfa], t2[:, 0:nfa], ALU.subtract, eng="pool")
                    f.tt(P[:, 1, :, fa0:fa0 + nfa].re("p q f -> p f q"), t3[:, 0:nfa], t4[:, 0:nfa], ALU.add, eng="pool")
                spectrum(u, na, cons)
                Z0 = Z0p.next()
                for q0 in range(0, 32, 4):
                    zr = psZ.next()
                    zi = psZ.next()
                    for i in range(4):
                        q = q0 + i
                        f.mm(zr[0:NFA, i, :], P[:, 0, q, :], CA[:, 0, :], start=True, stop=False)
                        f.mm(zr[0:NFA, i, :], P[:, 1, q, :], CA[:, 2, :], start=False, stop=True)
                    for i in range(4):
                        q = q0 + i
                        f.mm(zi[0:NFA, i, :], P[:, 0, q, :], CA[:, 1, :], start=True, stop=False)
                        f.mm(zi[0:NFA, i, :], P[:, 1, q, :], CA[:, 0, :], start=False, stop=True)
                    f.cp(Z0[:, 0].re("f b (q cp) -> f q b cp", cp=2)[:, q0:q0 + 4], zr[0:NFA, :, :].re("f q (b cp) -> f q b cp", cp=2), eng="act")
                    f.cp(Z0[:, 1].re("f b (q cp) -> f q b cp", cp=2)[:, q0:q0 + 4], zi[0:NFA, :, :].re("f q (b cp) -> f q b cp", cp=2), eng="dve")
                yT = yTp.next()
                yv = yT[:, 0:nrow].re("c (a b) -> c a b", b=64)
                gv = gT[:, 0:nrow].re("c (a b) -> c a b", b=64)
                for b0 in range(0, 64, 8):
                    po_ = psO.next()
                    for i in range(8):
                        b = b0 + i
                        f.mm(po_[0:64, i, 0:na], Z0[:, 0, b, :], DBr[:, b, :], start=True, stop=False)
                        f.mm(po_[0:64, i, 0:na], Z0[:, 1, b, :], DBni[:, b, :], start=False, stop=True)
                    f.tt(yv[:, :, b0:b0 + 8], po_[0:64, :, 0:na].re("c b a -> c a b"), gv[:, :, b0:b0 + 8], ALU.mult)
                f.dma("sp", dstT[g * 64:(g + 1) * 64, r0:r0 + nrow], yT[:, 0:nrow])
            f.barrier()
        f.release(m)

    ONE_T = f.sbuf("one_t", [128, 1], F32)
    f.memset(ONE_T[:], 1.0)

    phase_mod()
    done = False
    if "only_hy" in dbg:
        phase_hy(0, False)
        phase_hy(0, True)
        f.barrier()
        f.barrier(["sp"])
        f.release(0)
        return nc
    for l in range(DEPTH):
        if "from_merge" not in dbg:
            mk = f.mark()
            hT = f.sbuf("hT", [128, 8, T], BF16)
            norm_tiles(l, 0, hT, range(NT))
            if hT_d is not None and l == 0:
                for kc in range(8):
                    f.dma("sp", hT_d[kc * 128:(kc + 1) * 128, :], hT[:, kc, :])
            if stop_after == "norm":
                f.release(mk)
                break
            phase_proj(l, hT)
            f.release(mk)
            if stop_after == "proj":
                break
            if "skip_att" not in dbg:
                phase_att(l)
            if stop_after == "att":
                break
            if "skip_gla" not in dbg:
                phase_gla(l)
            if stop_after == "gla":
                break
            if "skip_hy" not in dbg:
                phase_hy(l, False)
                if l < DEPTH - 1:
                    phase_hy(l, True)
            if stop_after == "hy":
                break
        phase_merge(l)
        if stop_after == "merge":
            break
        phase_ffn(l)
        if stop_after == "ffn":
            break
    else:
        done = True
    f.barrier()
    if done:
        phase_final()
    f.barrier(["sp"])
    f.release(0)
    return nc


def _fm(v, chunks):
    return np.ascontiguousarray(np.asarray(v, np.float32).reshape(chunks, 128).T)


def make_in_maps(inputs):
    g = {k: np.asarray(v) for k, v in inputs.items()}
    hc = host_constants()
    perm = rope_swap_perm()
    sh = {}
    sh["ada_w"] = np.ascontiguousarray(g["ada_w"], np.float32)
    sh["ada_bf"] = np.ascontiguousarray(np.stack([_fm(g["ada_b"][l], 48) for l in range(DEPTH)], 1))
    sh["ada_br"] = np.ascontiguousarray(np.broadcast_to(g["ada_b"][None], (2, DEPTH, 6 * D)), np.float32)
    sh["n1g"] = np.ascontiguousarray(np.stack([_fm(g["norm1_g"][l], 8) for l in range(DEPTH)], 1))
    sh["n2g"] = np.ascontiguousarray(np.stack([_fm(g["norm2_g"][l], 8) for l in range(DEPTH)], 1))
    sh["w_in"] = np.ascontiguousarray(g["w_in"], np.float32)
    wkr = np.zeros((DEPTH, D, 2, 96), np.float32)
    wkr[:, :, 0, 64:96] = g["w_in"][:, :, 384:416]
    wkr[:, :, 1, 64:96] = g["w_in"][:, :, 384:416][:, :, perm]
    sh["w_kr2"] = wkr
    sh["qng"] = np.ascontiguousarray(np.stack([_fm(g["mla_q_norm"][l], 2) for l in range(DEPTH)], 1))
    sh["kvng"] = np.ascontiguousarray(np.stack([g["mla_kv_norm"][l] for l in range(DEPTH)], 1), np.float32)
    sh["w_uq"] = np.ascontiguousarray(g["mla_w_uq"], np.float32)
    wsw = g["mla_w_uq"].reshape(DEPTH, 256, 8, 96).copy()
    wsw[:, :, :, 64:96] = wsw[:, :, :, 64:96][:, :, :, perm]
    sh["w_uq_sw"] = np.ascontiguousarray(wsw.reshape(DEPTH, 256, 768), np.float32)
    ukv = g["mla_w_ukv"].reshape(DEPTH, 128, 8, 128)
    sh["w_ukv_k"] = np.ascontiguousarray(ukv[:, :, :, 0:64].reshape(DEPTH, 128, 512), np.float32)
    sh["w_ukv_v"] = np.ascontiguousarray(ukv[:, :, :, 64:128].reshape(DEPTH, 128, 512), np.float32)
    sh["w_a2"] = np.ascontiguousarray(np.concatenate([g["gla_w_a2"][:, 0], g["gla_w_a2"][:, 1]], -1), np.float32)
    sh["b_a"] = np.ascontiguousarray(np.concatenate([g["gla_b_a"][:, 0], g["gla_b_a"][:, 1]], -1)[:, None, :], np.float32)
    for nm_ in ("w_o_mla", "w_o_gla", "w_o_hy", "w_out", "ff_w1", "ff_w2"):
        sh[nm_] = np.ascontiguousarray(g[nm_], np.float32)
    sw = g["hy_short_w"]
    swb = np.concatenate([sw, g["hy_short_b"][:, None, :]], 1)
    sh["hy_swb"] = np.ascontiguousarray(swb.reshape(DEPTH, 4, 12, 128).transpose(0, 3, 2, 1), np.float32)
    sh["hy_f_w1"] = np.ascontiguousarray(g["hy_f_w1"], np.float32)
    sh["hy_f_w2"] = np.ascontiguousarray(g["hy_f_w2"], np.float32)
    sh["hy_f_w3"] = np.ascontiguousarray(g["hy_f_w3"], np.float32)
    sh["hy_fb12"] = np.ascontiguousarray(np.stack([g["hy_f_b1"], g["hy_f_b2"]], -1), np.float32)
    sh["hy_f_b3"] = np.ascontiguousarray(g["hy_f_b3"][:, None, :], np.float32)
    sh["hy_bias"] = np.ascontiguousarray(g["hy_bias"][:, None], np.float32)
    sh["fin_g"] = np.ascontiguousarray(np.broadcast_to(g["final_norm_g"][None, :], (128, D)), np.float32)
    sh["gla_on"] = np.ascontiguousarray(np.broadcast_to(g["gla_out_norm"][:, None, :], (DEPTH, 128, 128)), np.float32)
    for k, v in hc.items():
        sh[k] = v
    maps = []
    for b in range(8):
        m = dict(sh)
        m["xc"] = np.ascontiguousarray(np.concatenate([g["ctx"][b], g["x"][b]], 0), np.float32)
        m["cs"] = np.ascontiguousarray(np.stack([_fm(g["c"][b], 8), _fm(g["c_ctx"], 8)], -1))
        maps.append(m)
    return maps


_NC_CACHE = {}


def kernel(**inputs):
    if "nc" not in _NC_CACHE:
        _NC_CACHE["nc"] = build()
    nc = _NC_CACHE["nc"]
    maps = make_in_maps(inputs)
    res = run_bass_kernel_spmd(nc, maps, core_ids=list(range(8)))
    return np.stack([np.asarray(r["y"], np.float32) for r in res.results], 0)
```

```python
import math
import numpy as np
import ml_dtypes
import concourse.bass as bass
import concourse.mybir as mybir
from concourse.bass_utils import run_bass_kernel_spmd

F32 = mybir.dt.float32
BF16 = mybir.dt.bfloat16
AF = mybir.ActivationFunctionType
ALU = mybir.AluOpType

D = 1024
L = 4096
LC = 256
T = L + LC
NT = T // 128
DEPTH = 2
DIN = 6592
EPS = 1e-6
MLA_SCALE = 96 ** -0.5
TB = [(0, 256)] + [(256 + 512 * i, 512) for i in range(8)]
NFFT = 8192


class V:
    __slots__ = ("b", "ap")

    def __init__(self, b, ap):
        self.b = b
        self.ap = ap

    def __getitem__(self, idx):
        return V(self.b, self.ap[idx])

    def re(self, pat, **kw):
        return V(self.b, self.ap.rearrange(pat, **kw))

    def bc(self, shape):
        return V(self.b, self.ap.broadcast_to(list(shape)))

    def un(self, axis):
        return V(self.b, self.ap.unsqueeze(axis))

    def pb(self, n):
        return V(self.b, self.ap.partition_broadcast(n))


class Buf:
    __slots__ = ("t", "name", "lw", "rd", "psum", "dram")

    def __init__(self, t, name, psum=False, dram=False):
        self.t = t
        self.name = name
        self.lw = []
        self.rd = []
        self.psum = psum
        self.dram = dram

    def __getitem__(self, idx):
        return V(self, self.t[idx])

    @property
    def v(self):
        return V(self, self.t[:] if not hasattr(self.t, "ap") or True else self.t)


class FW:
    NDMA_SEM = 36
    NDMA_HW = 24

    def __init__(self, nc):
        self.nc = nc
        self.eng = {"pe": nc.tensor, "act": nc.scalar, "dve": nc.vector, "pool": nc.gpsimd, "sp": nc.sync}
        self.sem = {}
        self.cnt = {}
        for e in self.eng:
            self.sem[e] = nc.alloc_semaphore("s_" + e)
            self.cnt[e] = 0
        self.dsem = [nc.alloc_semaphore("d%d" % i) for i in range(self.NDMA_SEM)]
        self.dcnt = [0] * self.NDMA_SEM
        self.dnext = 0
        self.dnext_sw = 0
        self.seen = {e: {} for e in self.eng}
        self.ninst = 0
        self._ctx = []
        self._uid = 0
        self.deferred = []

    def _nm(self, name):
        self._uid += 1
        return "%s_%d" % (name, self._uid)

    def sbuf(self, name, shape, dt):
        g = self.nc.sbuf_tensor(self._nm(name), list(shape), dt)
        t = g.__enter__()
        self._ctx.append(g)
        return Buf(t, name)

    def psum(self, name, shape, dt=F32):
        g = self.nc.psum_tensor(self._nm(name), list(shape), dt)
        t = g.__enter__()
        self._ctx.append(g)
        return Buf(t, name, psum=True)

    def dram(self, name, shape, dt, kind="Internal"):
        t = self.nc.dram_tensor(name, list(shape), dt, kind=kind)
        return Buf(t.ap(), name, dram=True)

    def _wait(self, e, tok):
        if tok is None:
            return
        key, val = tok
        if e == "pe" and key == "pe":
            return
        if self.seen[e].get(key, 0) >= val:
            return
        self.seen[e][key] = val
        sem = self.sem[key] if isinstance(key, str) else self.dsem[key]
        self.eng[e].wait_ge(sem, val)

    def _deps(self, e, reads, writes, dma_write=False):
        for b in reads:
            for tok in b.lw:
                self._wait(e, tok)
        for b in writes:
            if not (dma_write and all(isinstance(t[0], int) for t in b.lw)):
                for tok in b.lw:
                    self._wait(e, tok)
            for tok in b.rd:
                self._wait(e, tok)

    @staticmethod
    def _compact(toks):
        best = {}
        for k, v in toks:
            if best.get(k, 0) < v:
                best[k] = v
        return list(best.items())

    def _commit(self, tok, reads, writes, dma_write=False):
        for b in reads:
            b.rd.append(tok)
            if len(b.rd) > 48:
                b.rd = self._compact(b.rd)
        for b in writes:
            if dma_write and b.lw and all(isinstance(t[0], int) for t in b.lw):
                b.lw.append(tok)
                if len(b.lw) > 48:
                    b.lw = self._compact(b.lw)
            else:
                b.lw = [tok]
            b.rd = []

    def flush(self):
        d, self.deferred = self.deferred, []
        for (q, out, in_, kw) in d:
            self._dma_now(q, out, in_, **kw)

    def op(self, e, fn, reads=(), writes=()):
        if self.deferred:
            self.flush()
        rd = [b for b in reads if not b.psum]
        wr = list(writes) + [b for b in reads if b.psum]
        self._deps(e, rd, wr)
        ins = fn(self.eng[e])
        self.cnt[e] += 1
        ins.then_inc(self.sem[e], 1)
        self._commit((e, self.cnt[e]), rd, wr)
        self.ninst += 1
        return ins

    def dma(self, q, out, in_, **kw):
        if out.b.dram and not in_.b.dram:
            self.deferred.append((q, out, in_, kw))
            return
        for (_, so, si, _) in self.deferred:
            if so.b is in_.b or si.b is out.b or so.b is out.b:
                self.flush()
                break
        self._dma_now(q, out, in_, **kw)

    def _dma_now(self, q, out, in_, **kw):
        if q == "pool":
            slot = self.NDMA_HW + self.dnext_sw
            self.dnext_sw = (self.dnext_sw + 1) % (self.NDMA_SEM - self.NDMA_HW)
        else:
            slot = self.dnext
            self.dnext = (self.dnext + 1) % self.NDMA_HW
        if self.dcnt[slot] > 0:
            self._wait(q, (slot, self.dcnt[slot]))
        self._deps(q, [in_.b], [out.b], dma_write=True)
        ins = self.eng[q].dma_start(out=out.ap, in_=in_.ap, **kw)
        self.dcnt[slot] += 16
        ins.then_inc(self.dsem[slot], 16)
        self._commit((slot, self.dcnt[slot]), [in_.b], [out.b], dma_write=True)
        self.ninst += 1

    def barrier(self, engines=None):
        self.flush()
        for e in (engines or self.eng):
            for e2 in self.eng:
                if e2 != e and self.cnt[e2] > 0:
                    self._wait(e, (e2, self.cnt[e2]))
            for s in range(self.NDMA_SEM):
                if self.dcnt[s] > 0:
                    self._wait(e, (s, self.dcnt[s]))

    def mark(self):
        return len(self._ctx)

    def release(self, mark):
        self.barrier()
        while len(self._ctx) > mark:
            self._ctx.pop().__exit__(None, None, None)

    def mm(self, out, lhsT, rhs, start=True, stop=True):
        return self.op("pe", lambda e: e.matmul(out.ap, lhsT=lhsT.ap, rhs=rhs.ap, start=start, stop=stop),
                       reads=[lhsT.b, rhs.b], writes=[out.b])

    def tr(self, out, in_, ident):
        return self.op("pe", lambda e: e.transpose(out.ap, in_.ap, ident.ap), reads=[in_.b, ident.b], writes=[out.b])

    def act(self, out, in_, func, bias=None, scale=None, accum=None, eng="act"):
        kw = {}
        rd = [in_.b]
        wr = [out.b]
        if bias is not None:
            if isinstance(bias, V):
                kw["bias"] = bias.ap
                rd.append(bias.b)
            else:
                kw["bias"] = bias
        if scale is not None:
            if isinstance(scale, V):
                kw["scale"] = scale.ap
                rd.append(scale.b)
            else:
                kw["scale"] = scale
        if accum is not None:
            kw["accum_out"] = accum.ap
            wr.append(accum.b)
        return self.op("act", lambda e: e.activation(out=out.ap, in_=in_.ap, func=func, **kw), reads=rd, writes=wr)

    def tt(self, out, a, b, op, eng="dve"):
        return self.op(eng, lambda e: e.tensor_tensor(out=out.ap, in0=a.ap, in1=b.ap, op=op),
                       reads=[a.b, b.b], writes=[out.b])

    def ts(self, out, a, s1, op0, s2=None, op1=None, eng="dve"):
        rd = [a.b]
        s1a = s1
        s2a = s2
        if isinstance(s1, V):
            rd.append(s1.b)
            s1a = s1.ap
        if isinstance(s2, V):
            rd.append(s2.b)
            s2a = s2.ap
        kw = {}
        if op1 is not None:
            kw["op1"] = op1
        return self.op(eng, lambda e: e.tensor_scalar(out=out.ap, in0=a.ap, scalar1=s1a, scalar2=s2a, op0=op0, **kw),
                       reads=rd, writes=[out.b])

    def stt(self, out, a, s, b, op0, op1, eng="dve"):
        rd = [a.b, b.b]
        sa = s
        if isinstance(s, V):
            rd.append(s.b)
            sa = s.ap
        return self.op(eng, lambda e: e.scalar_tensor_tensor(out=out.ap, in0=a.ap, scalar=sa, in1=b.ap, op0=op0, op1=op1),
                       reads=rd, writes=[out.b])

    def cp(self, out, in_, eng="dve"):
        if eng == "act":
            return self.op("act", lambda e: e.copy(out=out.ap, in_=in_.ap), reads=[in_.b], writes=[out.b])
        return self.op(eng, lambda e: e.tensor_copy(out=out.ap, in_=in_.ap), reads=[in_.b], writes=[out.b])

    def memset(self, out, val, eng="pool"):
        return self.op(eng, lambda e: e.memset(out.ap, val), writes=[out.b])

    def recip(self, out, in_):
        return self.op("dve", lambda e: e.reciprocal(out=out.ap, in_=in_.ap), reads=[in_.b], writes=[out.b])


class Pool:
    def __init__(self, bufs):
        self.bufs = bufs
        self.i = 0

    def next(self):
        b = self.bufs[self.i % len(self.bufs)]
        self.i += 1
        return b


def _bf(a):
    return np.ascontiguousarray(a.astype(ml_dtypes.bfloat16))


def host_constants():
    c = {}
    c["ident_bf"] = _bf(np.eye(128, dtype=np.float32))
    c["ident_f"] = np.eye(128, dtype=np.float32)
    c["ones_f"] = np.ones((128, 128), np.float32)
    rows = L // 64
    row = np.repeat(np.arange(rows, dtype=np.float32), 64)
    col = np.tile(np.arange(64, dtype=np.float32), rows)
    inv = (10000.0 ** (-np.arange(8, dtype=np.float32) / 8)).astype(np.float32)
    ang = np.concatenate([row[:, None] * inv, col[:, None] * inv], axis=-1)
    cos, sin = np.cos(ang), np.sin(ang)
    cosT = np.ones((96, T), np.float32)
    sinT = np.zeros((96, T), np.float32)
    for r in range(32):
        g, j = r // 16, r % 16
        half, i = j // 8, j % 8
        cosT[64 + r, LC:] = cos[:, g * 8 + i]
        sinT[64 + r, LC:] = (-sin[:, g * 8 + i]) if half == 0 else sin[:, g * 8 + i]
    c["cosT"] = cosT
    c["sinT"] = sinT
    i_ = np.arange(128)[None, :]
    j_ = np.arange(128)[:, None]
    Mf = ((j_ >= 64) & (j_ <= i_)).astype(np.float32) - ((j_ > i_) & (j_ <= 63)).astype(np.float32)
    Mb = ((j_ >= i_) & (j_ <= 63)).astype(np.float32) - ((j_ >= 64) & (j_ < i_)).astype(np.float32)
    c["gla_M"] = np.stack([Mf, Mb], 1).astype(np.float32)
    c["gla_mask"] = np.stack([(j_ <= i_), (j_ >= i_)], 1).astype(np.float32)
    ind = np.zeros((128, 2), np.float32)
    ind[:64, 0] = 1
    ind[64:, 1] = 1
    c["gla_ind"] = ind
    deltas = np.linspace(math.log(1e-2) / 0.3, math.log(1e-2) / 1.5, 512)
    fb = np.linspace(1e-4, 15.0, 16)
    b_ = np.arange(64)
    fbb = np.arange(64)
    for sfx, Ls in (("", L), ("_c", LC)):
        NF = 2 * Ls
        NA = NF // 64
        NFA = NA // 2 + 1
        pos = np.arange(Ls, dtype=np.float64)
        tn = pos / max(Ls - 1, 1)
        ang = (2.0 * math.pi / Ls) * pos[:, None] * fb
        feat = np.concatenate([tn[:, None], np.cos(ang), np.sin(ang)], -1)
        win = np.exp(-tn[:, None] * np.abs(deltas))
        feat2 = np.zeros((NF, 33))
        win2 = np.zeros((NF, 512))
        feat2[:Ls] = feat
        win2[:Ls] = win
        idx = np.arange(NF - Ls + 1, NF)
        feat2[idx] = feat[NF - idx]
        win2[idx] = win[NF - idx]
        c["hy_featT" + sfx] = np.ascontiguousarray(feat2.T.astype(np.float32))
        tn2 = np.full((1, NF), 1.0e4)
        tn2[0, :Ls] = tn
        tn2[0, idx] = tn[NF - idx]
        c["hy_tn2" + sfx] = tn2.astype(np.float32)
        a_ = np.arange(NA)[:, None]
        fa = np.arange(NFA)[None, :]
        th = 2 * math.pi * ((fa * a_) % NA) / NA
        c["hy_F1" + sfx] = _bf(np.concatenate([np.cos(th), -np.sin(th), np.sin(th)], 1))
        E2r = np.zeros((64, 2, NFA, 64, 2))
        E2i = np.zeros((64, 2, NFA, 64, 2))
        ph = 2 * math.pi * (((np.arange(NFA)[None, :, None] + NA * fbb[None, None, :]) * b_[:, None, None]) % NF) / NF
        for cp in range(2):
            E2r[:, cp, :, :, cp] = np.cos(ph)
            E2i[:, cp, :, :, cp] = -np.sin(ph)
        c["hy_E2r" + sfx] = _bf(E2r.reshape(128, NFA, 128))
        c["hy_E2i" + sfx] = _bf(E2i.reshape(128, NFA, 128))
        tt_ = 64 * np.arange(NA // 2)[None, None, :] + b_[None, :, None]
        th2 = 2 * math.pi * ((np.arange(NFA)[:, None, None] * tt_) % NF) / NF
        wgt = np.full((NFA, 1, 1), 2.0 / NF)
        wgt[0] = wgt[NFA - 1] = 1.0 / NF
        c["hy_DBr" + sfx] = _bf(wgt * np.cos(th2))
        c["hy_DBni" + sfx] = _bf(-wgt * np.sin(th2))
    c["hy_negdelta"] = np.ascontiguousarray((-np.abs(deltas)).reshape(4, 128).T.astype(np.float32))
    psi = 2 * math.pi * ((fbb[:, None] * b_[None, :]) % 64) / 64
    CA = np.zeros((64, 2, 3, 64, 2))
    for cp in range(2):
        CA[:, cp, 0, :, cp] = np.cos(psi)
        CA[:, cp, 1, :, cp] = np.sin(psi)
        CA[:, cp, 2, :, cp] = -np.sin(psi)
    c["hy_CA"] = _bf(CA.reshape(128, 3, 128))
    return c


def rope_swap_perm():
    p = np.zeros(32, np.int64)
    for r in range(32):
        g, j = r // 16, r % 16
        p[r] = g * 16 + (j + 8) % 16
    return p


def build(dbg=None):
    dbg = dbg or {}
    stop_after = dbg.get("stop_after", None)
    ext = dbg.get("ext", ())
    nc = bass.Bass("TRN2", target_bir_lowering=False)
    f = FW(nc)
    hc = host_constants()

    def inp(name, shape, dt=F32):
        return Buf(nc.dram_tensor(name, list(shape), dt, kind="ExternalInput").ap(), name, dram=True)

    def scratch(name, shape, dt):
        if name in dbg.get("inject", ()):
            return f.dram(name, shape, dt, kind="ExternalInput")
        return f.dram(name, shape, dt, kind="ExternalOutput" if name in ext else "Internal")

    I = {}
    I["xc"] = inp("xc", [T, D])
    I["cs"] = inp("cs", [128, 8, 2])
    I["ada_w"] = inp("ada_w", [DEPTH, D, 6 * D])
    I["ada_bf"] = inp("ada_bf", [128, DEPTH, 48])
    I["ada_br"] = inp("ada_br", [2, DEPTH, 6 * D])
    I["n1g"] = inp("n1g", [128, DEPTH, 8])
    I["n2g"] = inp("n2g", [128, DEPTH, 8])
    I["w_in"] = inp("w_in", [DEPTH, D, DIN])
    I["w_kr2"] = inp("w_kr2", [DEPTH, D, 2, 96])
    I["qng"] = inp("qng", [128, DEPTH, 2])
    I["kvng"] = inp("kvng", [128, DEPTH])
    I["w_uq"] = inp("w_uq", [DEPTH, 256, 768])
    I["w_uq_sw"] = inp("w_uq_sw", [DEPTH, 256, 768])
    I["w_ukv_k"] = inp("w_ukv_k", [DEPTH, 128, 512])
    I["w_ukv_v"] = inp("w_ukv_v", [DEPTH, 128, 512])
    I["w_a2"] = inp("w_a2", [DEPTH, 16, 512])
    I["b_a"] = inp("b_a", [DEPTH, 1, 512])
    I["gla_on"] = inp("gla_on", [DEPTH, 128, 128])
    for nm_ in ("w_o_mla", "w_o_gla", "w_o_hy"):
        I[nm_] = inp(nm_, [DEPTH, 512, D])
    I["w_out"] = inp("w_out", [DEPTH, D, D])
    I["ff_w1"] = inp("ff_w1", [DEPTH, D, 4 * D])
    I["ff_w2"] = inp("ff_w2", [DEPTH, 4 * D, D])
    I["fin_g"] = inp("fin_g", [128, D])
    I["hy_swb"] = inp("hy_swb", [DEPTH, 128, 12, 4])
    I["hy_f_w1"] = inp("hy_f_w1", [DEPTH, 33, 64])
    I["hy_f_w2"] = inp("hy_f_w2", [DEPTH, 64, 64])
    I["hy_f_w3"] = inp("hy_f_w3", [DEPTH, 64, 2048])
    I["hy_fb12"] = inp("hy_fb12", [DEPTH, 64, 2])
    I["hy_fb3T"] = inp("hy_fb3T", [DEPTH, 128, 16])
    I["hy_biasT"] = inp("hy_biasT", [DEPTH, 128, 2, 4])
    for k, v in hc.items():
        I[k] = inp(k, list(v.shape), BF16 if v.dtype == ml_dtypes.bfloat16 else F32)
    out_y = Buf(nc.dram_tensor("y", [L, D], F32, kind="ExternalOutput").ap(), "y", dram=True)

    xres = scratch("xres", [T, D], F32)
    modrow = scratch("modrow", [2, DEPTH, 2, D], F32)
    qT_d = scratch("qT_d", [8, 96, T], BF16)
    kT_d = scratch("kT_d", [8, 96, T], BF16)
    V_d = scratch("V_d", [T, 520], BF16)
    gqk_d = scratch("gqk_d", [T, 512], F32)
    gv_d = scratch("gv_d", [T, 512], BF16)
    gr_d = scratch("gr_d", [T, 512], F32)
    gg_d = scratch("gg_d", [T, 512], F32)
    zhyT_d = scratch("zhyT_d", [1536, T], BF16)
    scT_d = scratch("scT_d", [1536, T], BF16)
    H_d = scratch("H_d", [2, 8, 128, 2 * 65 * 32], BF16)
    y1T_d = scratch("y1T_d", [512, T], BF16)
    gatesT_d = scratch("gatesT_d", [3072, T], BF16)
    hT_d = scratch("hT_d", [D, T], BF16) if "hT_d" in ext else None
    ymlaT_d = scratch("ymlaT_d", [512, T], BF16)
    yglaT_d = scratch("yglaT_d", [512, T], BF16)
    yhyT_d = scratch("yhyT_d", [512, T], BF16)
    og_d = scratch("og_d", [T, 512], F32)

    ident = f.sbuf("ident", [128, 128], BF16)
    ones_f = f.sbuf("ones_f", [128, 128], F32)
    modF = f.sbuf("modF", [128, DEPTH, 48, 2], F32)
    AB = f.sbuf("AB", [128, DEPTH, 2, 2, 8, 2], F32)
    f.dma("sp", ident[:], I["ident_bf"][:])
    f.dma("sp", ones_f[:], I["ones_f"][:])
    xres_t = [Buf(xres.t[tt * 128:(tt + 1) * 128, :], "xres%d" % tt, dram=True) for tt in range(NT)]
    for tt in range(NT):
        f.dma("sp", xres_t[tt][:], I["xc"][tt * 128:(tt + 1) * 128, :])

    def phase_mod():
        m = f.mark()
        cs = f.sbuf("cs", [128, 8, 2], F32)
        scs = f.sbuf("scs", [128, 8, 2], F32)
        abf = f.sbuf("abf", [128, DEPTH, 48], F32)
        abr = f.sbuf("abr", [2, DEPTH, 6 * D], F32)
        g1 = f.sbuf("g1", [128, DEPTH, 8], F32)
        g2 = f.sbuf("g2", [128, DEPTH, 8], F32)
        wp = Pool([f.sbuf("adaw%d" % i, [128, 8, 512], F32) for i in range(2)])
        rowst = f.sbuf("rowst", [2, 512], F32)
        psF = f.psum("psF", [128, 512], F32)
        psR = Pool([f.psum("psR%d" % i, [128, 512], F32) for i in range(2)])
        f.dma("sp", cs[:], I["cs"][:])
        f.dma("sp", abf[:], I["ada_bf"][:])
        f.dma("sp", abr[:], I["ada_br"][:])
        f.dma("sp", g1[:], I["n1g"][:])
        f.dma("sp", g2[:], I["n2g"][:])
        f.act(scs[:], cs[:], AF.Silu)
        for l in range(DEPTH):
            wv = I["ada_w"][l].re("(kc p) j -> p kc j", p=128)
            for jb in range(12):
                w = wp.next()
                f.dma("sp", w[:], wv[:, :, jb * 512:(jb + 1) * 512])
                which = jb // 2
                if which in (2, 5):
                    pr = psR.next()
                    for kc in range(8):
                        f.mm(pr[0:2, :], scs[:, kc, :], w[:, kc, :], start=kc == 0, stop=kc == 7)
                    f.tt(rowst[:], pr[0:2, :], abr[:, l, jb * 512:(jb + 1) * 512], ALU.add)
                    f.dma("sp", modrow[:, l, 0 if which == 2 else 1, (jb % 2) * 512:(jb % 2 + 1) * 512], rowst[:])
                else:
                    for jc in range(4):
                        ch = jb * 4 + jc
                        for kc in range(8):
                            f.mm(psF[:, ch * 2:ch * 2 + 2], w[:, kc, jc * 128:(jc + 1) * 128], scs[:, kc, :],
                                 start=kc == 0, stop=kc == 7)
            for (c0, c1) in ((0, 16), (24, 40)):
                pv = psF[:, c0 * 2:c1 * 2].re("p (c s) -> p c s", s=2)
                f.tt(modF[:, l, c0:c1, :], pv, abf[:, l, c0:c1].un(2).bc([128, c1 - c0, 2]), ALU.add)
            for n_i, (sh0, sc0, g) in enumerate(((0, 8, g1), (24, 32, g2))):
                f.stt(AB[:, l, n_i, 0], modF[:, l, sc0:sc0 + 8, :], 1.0, g[:, l, :].un(2).bc([128, 8, 2]), ALU.add, ALU.mult)
                f.cp(AB[:, l, n_i, 1], modF[:, l, sh0:sh0 + 8, :])
        f.release(m)

    class NormCtx:
        def __init__(self):
            self.xp = Pool([f.sbuf("nx%d" % i, [128, D], F32) for i in range(2)])
            self.xnp = Pool([f.sbuf("nxn%d" % i, [128, D], BF16) for i in range(2)])
            self.stp = Pool([f.sbuf("nst%d" % i, [128, 4], F32) for i in range(3)])
            self.pp = Pool([f.psum("nps%d" % i, [128, 8, 128], BF16) for i in range(2)])

        def emit(self, l, n_i, hT, tt, c0):
            s = 0 if tt >= 2 else 1
            x = self.xp.next()
            st = self.stp.next()
            xn = self.xnp.next()
            f.dma("sp", x[:], xres_t[tt][:])
            f.act(xn[:], x[:], AF.Square, accum=st[:, 0:1])
            f.act(st[:, 1:2], st[:, 0:1], AF.Sqrt, bias=EPS_T[:, 0:1], scale=1.0 / D)
            f.recip(st[:, 2:3], st[:, 1:2])
            f.act(xn[:], x[:], AF.Identity, scale=st[:, 2:3])
            ps = self.pp.next()
            for kc in range(8):
                f.tr(ps[:, kc, :], xn[:, kc * 128:(kc + 1) * 128], ident[:])
            for kc in range(8):
                f.act(hT[:, kc, c0:c0 + 128], ps[:, kc, :], AF.Identity,
                      bias=AB[:, l, n_i, 1, kc, s:s + 1], scale=AB[:, l, n_i, 0, kc, s:s + 1])

    def norm_tiles(l, n_i, hT, tiles):
        m = f.mark()
        nctx = NormCtx()
        for tt in tiles:
            nctx.emit(l, n_i, hT, tt, tt * 128)
        f.release(m)

    EPS_T = f.sbuf("eps_t", [128, 1], F32)
    f.memset(EPS_T[:], EPS)

    def phase_proj(l, hT):
        m = f.mark()
        wp = Pool([f.sbuf("pw%d" % i, [128, 8, 512], BF16) for i in range(3)])
        psp = Pool([f.psum("pps%d" % i, [128, 512], F32) for i in range(4)])
        psq = Pool([f.psum("ppq%d" % i, [128, 512], F32) for i in range(3)])
        w_l = I["w_in"][l].re("(kc p) n -> p kc n", p=128)

        def loadw(c0, n):
            w = wp.next()
            f.dma("pool", w[:, :, 0:n], w_l[:, :, c0:c0 + n])
            return w

        def fm(w, wc0, ncol, evac):
            for bi, (t0, n) in enumerate(TB):
                ps = psp.next()
                for kc in range(8):
                    f.mm(ps[0:ncol, 0:n], w[:, kc, wc0:wc0 + ncol], hT[:, kc, t0:t0 + n], start=kc == 0, stop=kc == 7)
                evac(ps, bi, t0, n)

        def tm(w, ncol, evac):
            for tt in range(NT):
                ps = psp.next()
                for kc in range(8):
                    f.mm(ps[:, 0:ncol], hT[:, kc, tt * 128:(tt + 1) * 128], w[:, kc, 0:ncol], start=kc == 0, stop=kc == 7)
                evac(ps, tt)

        w0 = loadw(0, 384)
        wk = wp.next()
        f.dma("pool", wk[:, :, 0:192], I["w_kr2"][l].re("(kc p) a n -> p kc (a n)", p=128))
        qng = f.sbuf("qng", [128, DEPTH, 2], F32)
        kvng = f.sbuf("kvng", [128, DEPTH], F32)
        f.dma("sp", qng[:], I["qng"][:])
        f.dma("sp", kvng[:], I["kvng"][:])
        wuq = f.sbuf("wuq", [128, 2, 768], BF16)
        wuqs = f.sbuf("wuqs", [128, 2, 768], BF16)
        wk_k = f.sbuf("wukv_k", [128, 512], BF16)
        wk_v = f.sbuf("wukv_v", [128, 512], BF16)
        f.dma("pool", wuq[:], I["w_uq"][l].re("(kc p) n -> p kc n", p=128))
        f.dma("pool", wuqs[:], I["w_uq_sw"][l].re("(kc p) n -> p kc n", p=128))
        f.dma("pool", wk_k[:], I["w_ukv_k"][l])
        f.dma("pool", wk_v[:], I["w_ukv_v"][l])
        cosp = Pool([f.sbuf("cosb%d" % i, [96, 512], F32) for i in range(2)])
        sinp = Pool([f.sbuf("sinb%d" % i, [96, 512], F32) for i in range(2)])
        cqp = Pool([f.sbuf("cqb%d" % i, [128, 2, 512], F32) for i in range(2)])
        ckvp = Pool([f.sbuf("ckvb%d" % i, [128, 512], F32) for i in range(2)])
        cqnp = Pool([f.sbuf("cqnb%d" % i, [128, 2, 512], BF16) for i in range(2)])
        ckvnp = Pool([f.sbuf("ckvnb%d" % i, [128, 512], BF16) for i in range(2)])
        krp = Pool([f.sbuf("krb%d" % i, [96, 512], BF16) for i in range(2)])
        tmpa = Pool([f.sbuf("ptmpa%d" % i, [128, 512], F32) for i in range(2)])
        tmpb = Pool([f.sbuf("ptmpb%d" % i, [128, 512], F32) for i in range(2)])
        sqp = Pool([f.sbuf("psq%d" % i, [128, 2, 512], F32) for i in range(2)])
        rsp = Pool([f.sbuf("prs%d" % i, [128, 512], F32) for i in range(2)])
        qst = Pool([f.sbuf("qst%d" % i, [96, 512], BF16) for i in range(3)])
        kst = Pool([f.sbuf("kst%d" % i, [96, 512], BF16) for i in range(3)])
        vst = Pool([f.sbuf("vst%d" % i, [128, 8, 65], BF16) for i in range(2)])
        for vb in vst.bufs:
            f.memset(vb[:], 1.0)
        for bi, (t0, n) in enumerate(TB):
            cosb = cosp.next()
            sinb = sinp.next()
            f.dma("sp", cosb[64:96, 0:n], I["cosT"][64:96, t0:t0 + n])
            f.dma("sp", sinb[64:96, 0:n], I["sinT"][64:96, t0:t0 + n])
            cq = cqp.next()
            ckv = ckvp.next()
            for c in range(3):
                ps = psp.next()
                for kc in range(8):
                    f.mm(ps[:, 0:n], w0[:, kc, c * 128:(c + 1) * 128], hT[:, kc, t0:t0 + n], start=kc == 0, stop=kc == 7)
                f.cp(cq[:, c, 0:n] if c < 2 else ckv[:, 0:n], ps[:, 0:n], eng="act")
            pa = psp.next()
            pb = psp.next()
            for kc in range(8):
                f.mm(pa[0:96, 0:n], wk[:, kc, 0:96], hT[:, kc, t0:t0 + n], start=kc == 0, stop=kc == 7)
            for kc in range(8):
                f.mm(pb[0:96, 0:n], wk[:, kc, 96:192], hT[:, kc, t0:t0 + n], start=kc == 0, stop=kc == 7)
            ta = tmpa.next()
            tb_ = tmpb.next()
            krb = krp.next()
            f.tt(ta[64:96, 0:n], pa[64:96, 0:n], cosb[64:96, 0:n], ALU.mult)
            f.tt(tb_[64:96, 0:n], pb[64:96, 0:n], sinb[64:96, 0:n], ALU.mult)
            f.tt(krb[64:96, 0:n], ta[64:96, 0:n], tb_[64:96, 0:n], ALU.add, eng="pool")
            cqn = cqnp.next()
            ckvn = ckvnp.next()
            for (nchunk, gains) in ((2, qng), (1, kvng)):
                sq = sqp.next()
                ps = psp.next()
                for c in range(nchunk):
                    sv = cq[:, c, 0:n] if nchunk == 2 else ckv[:, 0:n]
                    f.tt(sq[:, c, 0:n], sv, sv, ALU.mult)
                for c in range(nchunk):
                    f.mm(ps[:, 0:n], ones_f[:], sq[:, c, 0:n], start=c == 0, stop=c == nchunk - 1)
                rs = rsp.next()
                f.act(rs[:, 0:n], ps[:, 0:n], AF.Sqrt, bias=EPS_T[:, 0:1], scale=1.0 / (128 * nchunk))
                f.recip(rs[:, 0:n], rs[:, 0:n])
                for c in range(nchunk):
                    sv = cq[:, c, 0:n] if nchunk == 2 else ckv[:, 0:n]
                    dv = cqn[:, c, 0:n] if nchunk == 2 else ckvn[:, 0:n]
                    gv = gains[:, l, c:c + 1] if nchunk == 2 else gains[:, l:l + 1]
                    f.stt(dv, sv, gv, rs[:, 0:n], ALU.mult, ALU.mult)
            for h in range(8):
                pa = psq.next()
                pb = psq.next()
                for kc in range(2):
                    f.mm(pa[0:96, 0:n], wuq[:, kc, h * 96:(h + 1) * 96], cqn[:, kc, 0:n], start=kc == 0, stop=kc == 1)
                for kc in range(2):
                    f.mm(pb[0:96, 0:n], wuqs[:, kc, h * 96:(h + 1) * 96], cqn[:, kc, 0:n], start=kc == 0, stop=kc == 1)
                q = qst.next()
                ta = tmpa.next()
                tb_ = tmpb.next()
                f.cp(q[0:64, 0:n], pa[0:64, 0:n], eng="act")
                f.tt(ta[64:96, 0:n], pa[64:96, 0:n], cosb[64:96, 0:n], ALU.mult)
                f.tt(tb_[64:96, 0:n], pb[64:96, 0:n], sinb[64:96, 0:n], ALU.mult)
                f.tt(q[64:96, 0:n], ta[64:96, 0:n], tb_[64:96, 0:n], ALU.add, eng="pool")
                f.dma("sp", qT_d[h, :, t0:t0 + n], q[:, 0:n])
                pk = psq.next()
                f.mm(pk[0:64, 0:n], wk_k[:, h * 64:(h + 1) * 64], ckvn[:, 0:n])
                k = kst.next()
                f.cp(k[0:64, 0:n], pk[0:64, 0:n], eng="act")
                f.cp(k[64:96, 0:n], krb[64:96, 0:n], eng="pool")
                f.dma("sp", kT_d[h, :, t0:t0 + n], k[:, 0:n])
            for ti in range(n // 128):
                tt = t0 // 128 + ti
                pv = psq.next()
                f.mm(pv[:, :], ckvn[:, ti * 128:(ti + 1) * 128], wk_v[:, :])
                vs = vst.next()
                f.cp(vs[:, :, 0:64], pv[:, :].re("p (h e) -> p h e", e=64), eng="act")
                f.dma("sp", V_d[tt * 128:(tt + 1) * 128, :], vs[:].re("p h e -> p (h e)"))

        wa = loadw(1952, 32)
        wa2 = f.sbuf("wa2", [16, 512], F32)
        ba = f.sbuf("ba", [1, 512], F32)
        f.dma("sp", wa2[:], I["w_a2"][l])
        f.dma("sp", ba[:], I["b_a"][l])
        aft = Pool([f.sbuf("aft%d" % i, [16, 512], F32) for i in range(2)])
        abt = Pool([f.sbuf("abt%d" % i, [16, 512], F32) for i in range(2)])
        ggp = Pool([f.sbuf("ggs%d" % i, [128, 512], F32) for i in range(2)])
        for bi, (t0, n) in enumerate(TB):
            pa = psp.next()
            pb = psp.next()
            for kc in range(8):
                f.mm(pa[0:16, 0:n], wa[:, kc, 0:16], hT[:, kc, t0:t0 + n], start=kc == 0, stop=kc == 7)
            for kc in range(8):
                f.mm(pb[0:16, 0:n], wa[:, kc, 16:32], hT[:, kc, t0:t0 + n], start=kc == 0, stop=kc == 7)
            af = aft.next()
            ab = abt.next()
            f.cp(af[:, 0:n], pa[0:16, 0:n], eng="act")
            f.cp(ab[:, 0:n], pb[0:16, 0:n], eng="act")
            for ti in range(n // 128):
                pg = psq.next()
                f.mm(pg[:, 0:256], af[:, ti * 128:(ti + 1) * 128], wa2[:, 0:256], start=True, stop=False)
                f.mm(pg[:, 0:256], ones_f[0:1, :], ba[:, 0:256], start=False, stop=True)
                f.mm(pg[:, 256:512], ab[:, ti * 128:(ti + 1) * 128], wa2[:, 256:512], start=True, stop=False)
                f.mm(pg[:, 256:512], ones_f[0:1, :], ba[:, 256:512], start=False, stop=True)
                gs = ggp.next()
                f.act(gs[:], pg[:], AF.Exp, scale=-1.0)
                f.act(gs[:], gs[:], AF.Ln, bias=ONE_T[:, 0:1])
                f.ts(gs[:], gs[:], -1.0 / 16.0, ALU.mult)
                tt = t0 // 128 + ti
                f.dma("sp", gg_d[tt * 128:(tt + 1) * 128, :], gs[:])

        st32 = Pool([f.sbuf("pst32_%d" % i, [128, 512], F32) for i in range(3)])
        st16 = Pool([f.sbuf("pst16_%d" % i, [128, 512], BF16) for i in range(3)])

        def ev_qk(ps, tt):
            s = st32.next()
            f.act(s[:, 0:256], ps[:, 0:256], AF.Copy, scale=0.125)
            f.cp(s[:, 256:512], ps[:, 256:512], eng="dve")
            f.dma("sp", gqk_d[tt * 128:(tt + 1) * 128, :], s[:])

        def ev_to(dst, c0, dt16):
            def ev(ps, tt):
                s = (st16 if dt16 else st32).next()
                f.cp(s[:], ps[:], eng="act" if tt % 2 else "dve")
                f.dma("sp", dst[tt * 128:(tt + 1) * 128, c0:c0 + 512], s[:])
            return ev

        tm(loadw(416, 512), 512, ev_qk)
        tm(loadw(928, 512), 512, ev_to(gv_d, 0, True))
        tm(loadw(1440, 512), 512, ev_to(gr_d, 0, False))
        for sc in range(3):
            w = loadw(1984 + sc * 512, 512)
            for c in range(4):
                ch = sc * 4 + c

                def ev_h(ps, bi, t0, n, ch=ch):
                    s = st16.next()
                    f.cp(s[:, 0:n], ps[:, 0:n], eng="act" if (bi + ch) % 2 else "dve")
                    f.dma("sp", zhyT_d[ch * 128:(ch + 1) * 128, t0:t0 + n], s[:, 0:n])
                fm(w, c * 128, 128, ev_h)

        for sc in range(6):
            w = loadw(3520 + sc * 512, 512)
            for c in range(4):
                ch = sc * 4 + c

                def ev_g(ps, bi, t0, n, ch=ch):
                    s = st16.next()
                    f.act(s[:, 0:n], ps[:, 0:n], AF.Sigmoid)
                    f.dma("sp", gatesT_d[ch * 128:(ch + 1) * 128, t0:t0 + n], s[:, 0:n])
                fm(w, c * 128, 128, ev_g)
        f.release(m)

    def phase_att(l):
        m = f.mark()
        Vaug = f.sbuf("Vaug", [128, NT, 520], BF16)
        f.dma("sp", Vaug[:], V_d[:].re("(t p) e -> p t e", p=128))
        qp = Pool([f.sbuf("aq%d" % i, [96, T], BF16) for i in range(2)])
        kp = Pool([f.sbuf("ak%d" % i, [96, T], BF16) for i in range(2)])
        pp = Pool([f.sbuf("ap%d" % i, [128, 512], BF16) for i in range(4)])
        sps = Pool([f.psum("as%d" % i, [128, 512], F32) for i in range(4)])
        ops = Pool([f.psum("ao%d" % i, [128, 512], F32) for i in range(2)])
        bps = f.psum("abc", [128, 512], F32)
        rec = Pool([f.sbuf("arec%d" % i, [65, 512], F32) for i in range(2)])
        osb = Pool([f.sbuf("aosb%d" % i, [64, 512], F32) for i in range(2)])
        yst = Pool([f.sbuf("ayst%d" % i, [64, 512], BF16) for i in range(3)])
        for h in range(8):
            q = qp.next()
            k = kp.next()
            f.dma("sp", q[:], qT_d[h])
            f.dma("sp", k[:], kT_d[h])
            for bi, (t0, n) in enumerate(TB):
                if bi == 0 and l == DEPTH - 1:
                    continue
                nk = 2 if bi == 0 else NT
                o = ops.next()
                pend = None
                for j in range(nk + 1):
                    p = None
                    if j < nk:
                        s = sps.next()
                        f.mm(s[:, 0:n], k[:, j * 128:(j + 1) * 128], q[:, t0:t0 + n])
                        p = pp.next()
                        f.act(p[:, 0:n], s[:, 0:n], AF.Exp, scale=MLA_SCALE)
                    if pend is not None:
                        jj, pv = pend
                        f.mm(o[0:65, 0:n], Vaug[:, jj, h * 65:(h + 1) * 65], pv[:, 0:n], start=jj == 0, stop=jj == nk - 1)
                    pend = (j, p) if j < nk else None
                r = rec.next()
                f.recip(r[64:65, 0:n], o[64:65, 0:n])
                f.mm(bps[0:64, 0:n], ones_f[64:65, 0:64], r[64:65, 0:n])
                os_ = osb.next()
                f.cp(os_[:, 0:n], o[0:64, 0:n], eng="act")
                y = yst.next()
                f.tt(y[:, 0:n], os_[:, 0:n], bps[0:64, 0:n], ALU.mult)
                f.dma("sp", ymlaT_d[h * 64:(h + 1) * 64, t0:t0 + n], y[:, 0:n])
        f.release(m)

    def phase_gla(l):
        m = f.mark()
        Mm = f.sbuf("glaM", [128, 2, 128], F32)
        mask = f.sbuf("glamask", [128, 2, 128], F32)
        ind = f.sbuf("glaind", [128, 2], F32)
        gon = f.sbuf("gon", [128, 128], F32)
        f.dma("sp", Mm[:], I["gla_M"][:])
        f.dma("sp", mask[:], I["gla_mask"][:])
        f.dma("sp", ind[:], I["gla_ind"][:])
        f.dma("sp", gon[:], I["gla_on"][l])
        S = f.sbuf("glaS", [64, 4, 128], F32)
        qkp = Pool([f.sbuf("gqk%d" % i, [128, 512], F32) for i in range(2)])
        gp = Pool([f.sbuf("gg%d" % i, [128, 256], F32) for i in range(2)])
        vp = Pool([f.sbuf("gv%d" % i, [128, 512], BF16) for i in range(2)])
        ePp = Pool([f.sbuf("geP%d" % i, [128, 256], F32) for i in range(2)])
        eNp = Pool([f.sbuf("geN%d" % i, [128, 256], F32) for i in range(2)])
        qpp = Pool([f.sbuf("gqp%d" % i, [128, 256], BF16) for i in range(2)])
        kpp = Pool([f.sbuf("gkp%d" % i, [128, 256], BF16) for i in range(2)])
        decp = Pool([f.sbuf("gdec%d" % i, [64, 4, 2], F32) for i in range(2)])
        qkTp = Pool([f.sbuf("gqkT%d" % i, [64, 8, 128], BF16) for i in range(2)])
        Amp = Pool([f.sbuf("gAm%d" % i, [128, 4, 128], BF16) for i in range(2)])
        Smidp = Pool([f.sbuf("gSm%d" % i, [64, 4, 128], F32) for i in range(2)])
        Smbp = Pool([f.sbuf("gSb%d" % i, [64, 4, 128], BF16) for i in range(2)])
        osp = Pool([f.sbuf("gos%d" % i, [128, 4, 128], F32) for i in range(2)])
        opp = Pool([f.sbuf("gop%d" % i, [128, 4, 128], F32) for i in range(2)])
        sqp = Pool([f.sbuf("gsq%d" % i, [128, 4, 128], F32) for i in range(2)])
        stp = Pool([f.sbuf("gst%d" % i, [128, 8], F32) for i in range(2)])
        rp = Pool([f.sbuf("gr%d" % i, [128, 512], F32) for i in range(2)])
        yp = Pool([f.sbuf("gy%d" % i, [128, 512], BF16) for i in range(2)])
        yTp = Pool([f.sbuf("gyT%d" % i, [128, 4, 128], BF16) for i in range(2)])
        pE = f.psum("gpE", [128, 512], F32)
        pcs = f.psum("gpcs", [128, 512], F32)
        ptr = f.psum("gptr", [128, 8, 128], BF16)
        pA = f.psum("gpA", [128, 4, 128], F32)
        po = Pool([f.psum("gpo%d" % i, [128, 4, 128], F32) for i in range(2)])
        pU = f.psum("gpU", [128, 4, 128], F32)
        pT = f.psum("gpT", [128, 8, 128], BF16)
        for d in (0, 1):
            order = list(range(NT)) if d == 0 else [1, 0] + list(range(NT - 1, 1, -1))
            f.memset(S[:], 0.0)
            for tt in order:
                r0 = tt * 128
                qk = qkp.next()
                g = gp.next()
                v = vp.next()
                f.dma("sp", qk[:], gqk_d[r0:r0 + 128, :])
                f.dma("sp", g[:], gg_d[r0:r0 + 128, d * 256:(d + 1) * 256])
                f.dma("sp", v[:], gv_d[r0:r0 + 128, :])
                f.mm(pE[:, 0:256], Mm[:, d, :], g[:])
                eP = ePp.next()
                eN = eNp.next()
                f.act(eP[:], pE[:, 0:256], AF.Exp)
                f.act(eN[:], pE[:, 0:256], AF.Exp, scale=-1.0)
                for h in range(4):
                    f.mm(pcs[0:64, h * 2:(h + 1) * 2], g[:, h * 64:(h + 1) * 64], ind[:])
                dec = decp.next()
                f.act(dec[:].re("p h s -> p (h s)"), pcs[0:64, 0:8], AF.Exp)
                dmid = dec[:, :, d:d + 1]
                dend = dec[:, :, 1 - d:2 - d]
                qp_ = qpp.next()
                kp_ = kpp.next()
                f.tt(qp_[:], qk[:, 0:256], eP[:], ALU.mult)
                f.tt(kp_[:], qk[:, 256:512], eN[:], ALU.mult, eng="pool")
                for h in range(4):
                    f.tr(ptr[0:64, h, :], qp_[:, h * 64:(h + 1) * 64], ident[:])
                for h in range(4):
                    f.tr(ptr[0:64, 4 + h, :], kp_[:, h * 64:(h + 1) * 64], ident[:])
                qkT = qkTp.next()
                f.cp(qkT[:], ptr[0:64], eng="act")
                for h in range(4):
                    f.mm(pA[:, h, :], qkT[:, 4 + h, :], qkT[:, h, :])
                Am = Amp.next()
                f.tt(Am[:], pA[:], mask[:, d:d + 1, :].bc([128, 4, 128]), ALU.mult)
                Smid = Smidp.next()
                f.tt(Smid[:], S[:], dmid.bc([64, 4, 128]), ALU.mult)
                Smb = Smbp.next()
                f.cp(Smb[:], Smid[:], eng="act")
                o_ps = po.next()
                for h in range(4):
                    f.mm(o_ps[:, h, :], qkT[:, h, :], Smb[:, h, :], start=True, stop=False)
                    f.mm(o_ps[:, h, :], Am[:, h, :], v[:, h * 128:(h + 1) * 128], start=False, stop=True)
                for h in range(4):
                    f.mm(pU[0:64, h, :], kp_[:, h * 64:(h + 1) * 64], v[:, h * 128:(h + 1) * 128])
                f.tt(S[:], Smid[:], pU[0:64], ALU.add)
                f.tt(S[:], S[:], dend.bc([64, 4, 128]), ALU.mult)
                if d == 0:
                    os_ = osp.next()
                    f.cp(os_[:], o_ps[:], eng="act")
                    f.dma("sp", og_d[r0:r0 + 128, :], os_[:].re("p h v -> p (h v)"))
                else:
                    if tt < 2 and l == DEPTH - 1:
                        continue
                    op_ = opp.next()
                    f.dma("sp", op_[:].re("p h v -> p (h v)"), og_d[r0:r0 + 128, :])
                    rr = rp.next()
                    f.dma("sp", rr[:], gr_d[r0:r0 + 128, :])
                    os_ = osp.next()
                    f.tt(os_[:], o_ps[:], op_[:], ALU.add)
                    sq = sqp.next()
                    f.tt(sq[:], os_[:], os_[:], ALU.mult, eng="pool")
                    st = stp.next()
                    f.op("dve", lambda e: e.tensor_reduce(out=st[:, 0:4].ap, in_=sq[:].ap, axis=mybir.AxisListType.X, op=ALU.add),
                         reads=[sq], writes=[st])
                    f.act(st[:, 4:8], st[:, 0:4], AF.Sqrt, bias=EPS_T[:, 0:1], scale=1.0 / 128)
                    f.recip(st[:, 4:8], st[:, 4:8])
                    f.tt(os_[:], os_[:], st[:, 4:8].un(2).bc([128, 4, 128]), ALU.mult)
                    f.tt(os_[:], os_[:], gon[:].un(1).bc([128, 4, 128]), ALU.mult, eng="pool")
                    f.act(rr[:], rr[:], AF.Silu)
                    y = yp.next()
                    f.tt(y[:], os_[:].re("p h v -> p (h v)"), rr[:], ALU.mult)
                    for c in range(4):
                        f.tr(pT[:, c, :], y[:, c * 128:(c + 1) * 128], ident[:])
                    yT = yTp.next()
                    f.cp(yT[:], pT[:, 0:4, :], eng="act")
                    f.dma("sp", yglaT_d[:, r0:r0 + 128].re("(c p) t -> p c t", p=128), yT[:])
        f.release(m)

    def phase_merge(l):
        m = f.mark()
        wo = [f.sbuf("wo%d" % i, [128, 4, D], BF16) for i in range(3)]
        for i, nm in enumerate(("w_o_mla", "w_o_gla", "w_o_hy")):
            f.dma("pool", wo[i][:], I[nm][l].re("(kc p) n -> p kc n", p=128))
        wout = f.sbuf("wout", [128, 8, D], BF16)
        for c in range(2):
            f.dma("pool", wout[:, :, c * 512:(c + 1) * 512], I["w_out"][l].re("(kc p) n -> p kc n", p=128)[:, :, c * 512:(c + 1) * 512])
        gx = f.sbuf("gx", [128, 2, D], F32)
        f.dma("sp", gx[:, 0, :], modrow[0, l, 0].pb(128))
        f.dma("sp", gx[:, 1, :], modrow[1, l, 0].pb(128))
        ybp = Pool([f.sbuf("mby%d" % i, [128, 3, 4, 512], BF16) for i in range(2)])
        gtp = Pool([f.sbuf("mgt%d" % i, [128, 3, 512], BF16) for i in range(3)])
        mTp = Pool([f.sbuf("mT%d" % i, [128, 8, 512], BF16) for i in range(2)])
        accp = Pool([f.sbuf("macc%d" % i, [128, 512], F32) for i in range(2)])
        tmpp = Pool([f.sbuf("mtmp%d" % i, [128, 512], F32) for i in range(3)])
        xp = Pool([f.sbuf("mx%d" % i, [128, D], F32) for i in range(3)])
        ps3 = [Pool([f.psum("mps%d_%d" % (i, j), [128, 512], F32) for j in range(2)]) for i in range(3)]
        pso = Pool([f.psum("mpo%d" % i, [128, 512], F32) for i in range(2)])
        gview = gatesT_d[:].re("(b c p) t -> p b c t", b=3, c=8, p=128)
        for bi, (t0, n) in enumerate(TB):
            if bi == 0 and l == DEPTH - 1:
                continue
            yb = ybp.next()
            for i, srcT in enumerate((ymlaT_d, yglaT_d, yhyT_d)):
                f.dma("sp", yb[:, i, :, 0:n], srcT[:, t0:t0 + n].re("(kc p) t -> p kc t", p=128))
            mT = mTp.next()
            for oc in range(8):
                gt = gtp.next()
                f.dma("sp", gt[:, :, 0:n], gview[:, :, oc, t0:t0 + n])
                pss = []
                for i in range(3):
                    ps = ps3[i].next()
                    for kc in range(4):
                        f.mm(ps[:, 0:n], wo[i][:, kc, oc * 128:(oc + 1) * 128], yb[:, i, kc, 0:n], start=kc == 0, stop=kc == 3)
                    pss.append(ps)
                acc = accp.next()
                t1 = tmpp.next()
                t2 = tmpp.next()
                f.tt(acc[:, 0:n], pss[0][:, 0:n], gt[:, 0, 0:n], ALU.mult)
                f.tt(t1[:, 0:n], pss[1][:, 0:n], gt[:, 1, 0:n], ALU.mult)
                f.tt(t2[:, 0:n], pss[2][:, 0:n], gt[:, 2, 0:n], ALU.mult)
                f.tt(acc[:, 0:n], acc[:, 0:n], t1[:, 0:n], ALU.add, eng="pool")
                f.tt(mT[:, oc, 0:n], acc[:, 0:n], t2[:, 0:n], ALU.add, eng="pool")
            s = 1 if bi == 0 else 0
            for ti in range(n // 128):
                tt = t0 // 128 + ti
                x = xp.next()
                f.dma("sp", x[:], xres_t[tt][:])
                for half in range(2):
                    ps = pso.next()
                    for kc in range(8):
                        f.mm(ps[:, :], mT[:, kc, ti * 128:(ti + 1) * 128], wout[:, kc, half * 512:(half + 1) * 512],
                             start=kc == 0, stop=kc == 7)
                    t1 = tmpp.next()
                    f.tt(t1[:], ps[:], gx[:, s, half * 512:(half + 1) * 512], ALU.mult)
                    f.tt(x[:, half * 512:(half + 1) * 512], x[:, half * 512:(half + 1) * 512], t1[:], ALU.add, eng="pool")
                f.dma("sp", xres_t[tt][:], x[:])
        f.release(m)

    def phase_ffn(l):
        m = f.mark()
        w1 = f.sbuf("ffw1", [128, 8, 4096], BF16)
        w2 = f.sbuf("ffw2", [128, 32, D], BF16)
        w1v = I["ff_w1"][l].re("(kc p) n -> p kc n", p=128)
        w2v = I["ff_w2"][l].re("(kc p) n -> p kc n", p=128)
        for c in range(8):
            f.dma("pool", w1[:, :, c * 512:(c + 1) * 512], w1v[:, :, c * 512:(c + 1) * 512])
        for c in range(8):
            f.dma("pool", w2[:, c * 4:(c + 1) * 4, :], w2v[:, c * 4:(c + 1) * 4, :])
        gx = f.sbuf("fgx", [128, 2, D], F32)
        f.dma("sp", gx[:, 0, :], modrow[0, l, 1].pb(128))
        f.dma("sp", gx[:, 1, :], modrow[1, l, 1].pb(128))
        nctx = NormCtx()
        hTp = Pool([f.sbuf("fhT%d" % i, [128, 8, 256], BF16) for i in range(2)])
        aTp = Pool([f.sbuf("faT%d" % i, [128, 32, 256], BF16) for i in range(1)])
        rp = Pool([f.sbuf("fr%d" % i, [128, 256], F32) for i in range(2)])
        tmpp = Pool([f.sbuf("ftmp%d" % i, [128, 512], F32) for i in range(2)])
        xp = Pool([f.sbuf("fx%d" % i, [128, D], F32) for i in range(2)])
        psA = Pool([f.psum("fpa%d" % i, [128, 512], F32) for i in range(3)])
        pso = Pool([f.psum("fpo%d" % i, [128, 512], F32) for i in range(3)])
        for blk in range(T // 256):
            if blk == 0 and l == DEPTH - 1:
                continue
            s = 1 if blk == 0 else 0
            hTb = hTp.next()
            for ti in range(2):
                nctx.emit(l, 1, hTb, blk * 2 + ti, ti * 128)
            aT = aTp.next()
            for fc in range(32):
                ps = psA.next()
                for kc in range(8):
                    f.mm(ps[:, 0:256], w1[:, kc, fc * 128:(fc + 1) * 128], hTb[:, kc, :], start=kc == 0, stop=kc == 7)
                r = rp.next()
                f.act(r[:], ps[:, 0:256], AF.Relu)
                f.tt(aT[:, fc, :], r[:], r[:], ALU.mult, eng="pool" if fc % 2 else "dve")
            for ti in range(2):
                tt = blk * 2 + ti
                x = xp.next()
                f.dma("sp", x[:], xres_t[tt][:])
                for half in range(2):
                    ps = pso.next()
                    for fc in range(32):
                        f.mm(ps[:, :], aT[:, fc, ti * 128:(ti + 1) * 128], w2[:, fc, half * 512:(half + 1) * 512],
                             start=fc == 0, stop=fc == 31)
                    t1 = tmpp.next()
                    f.tt(t1[:], ps[:], gx[:, s, half * 512:(half + 1) * 512], ALU.mult)
                    f.tt(x[:, half * 512:(half + 1) * 512], x[:, half * 512:(half + 1) * 512], t1[:], ALU.add, eng="pool")
                f.dma("sp", xres_t[tt][:], x[:])
        f.release(m)

    def phase_final():
        m = f.mark()
        fg = f.sbuf("fing", [128, D], F32)
        f.dma("sp", fg[:], I["fin_g"][:])
        xp = Pool([f.sbuf("zx%d" % i, [128, D], F32) for i in range(3)])
        jp = Pool([f.sbuf("zj%d" % i, [128, D], F32) for i in range(2)])
        stp = Pool([f.sbuf("zst%d" % i, [128, 4], F32) for i in range(3)])
        for tt in range(2, NT):
            x = xp.next()
            j = jp.next()
            st = stp.next()
            f.dma("sp", x[:], xres_t[tt][:])
            f.act(j[:], x[:], AF.Square, accum=st[:, 0:1])
            f.act(st[:, 1:2], st[:, 0:1], AF.Sqrt, bias=EPS_T[:, 0:1], scale=1.0 / D)
            f.recip(st[:, 2:3], st[:, 1:2])
            f.act(j[:], x[:], AF.Identity, scale=st[:, 2:3])
            f.tt(x[:], j[:], fg[:], ALU.mult)
            f.dma("sp", out_y[(tt - 2) * 128:(tt - 1) * 128, :], x[:])
        f.release(m)

    def phase_hy(l, ctx_seg):
        m = f.mark()
        r0, nrow = (0, LC) if ctx_seg else (LC, L)
        na = nrow // 64
        NA = 2 * na
        NF = NA * 64
        NFA = NA // 2 + 1
        sfx = "_c" if ctx_seg else ""

        mA = f.mark()
        swt = f.sbuf("hsw", [128, 12, 4], F32)
        f.dma("sp", swt[:], I["hy_swb"][l])
        zp = Pool([f.sbuf("hz%d" % i, [128, L], BF16) for i in range(2)])
        accp = Pool([f.sbuf("hacc%d" % i, [128, L], F32) for i in range(2)])
        op_ = Pool([f.sbuf("hso%d" % i, [128, L], BF16) for i in range(2)])
        for ch in range(12):
            z = zp.next()
            acc = accp.next()
            o = op_.next()
            f.dma("sp", z[:, 0:nrow], zhyT_d[ch * 128:(ch + 1) * 128, r0:r0 + nrow])
            f.act(acc[:, 0:nrow], z[:, 0:nrow], AF.Identity, bias=swt[:, ch, 3:4], scale=swt[:, ch, 1:2])
            f.stt(acc[:, 1:nrow], z[:, 0:nrow - 1], swt[:, ch, 0:1], acc[:, 1:nrow], ALU.mult, ALU.add)
            f.stt(acc[:, 0:nrow - 1], z[:, 1:nrow], swt[:, ch, 2:3], acc[:, 0:nrow - 1], ALU.mult, ALU.add)
            f.cp(o[:, 0:nrow], acc[:, 0:nrow], eng="act")
            f.dma("sp", scT_d[ch * 128:(ch + 1) * 128, r0:r0 + nrow], o[:, 0:nrow])
        f.release(mA)

        F1 = f.sbuf("hF1", [NA, 3 * NFA], BF16)
        E2r = f.sbuf("hE2r", [128, NFA, 128], BF16)
        E2i = f.sbuf("hE2i", [128, NFA, 128], BF16)
        f.dma("sp", F1[:], I["hy_F1" + sfx][:])
        f.dma("sp", E2r[:], I["hy_E2r" + sfx][:])
        f.dma("sp", E2i[:], I["hy_E2i" + sfx][:])
        Yp = Pool([f.sbuf("hY%d" % i, [128, 32, 3 * NFA], BF16) for i in range(1)])
        psY = Pool([f.psum("hpY%d" % i, [128, 512], F32) for i in range(2)])
        psX = Pool([f.psum("hpX%d" % i, [128, 2, 8, 32], F32) for i in range(2)])
        pst = Pool([f.psum("hpt%d" % i, [128, 8, 64], BF16) for i in range(1)])

        def spectrum(ut, Kp, consume, ypool=None):
            Y = (ypool or Yp).next()
            for q in range(32):
                ps = psY.next()
                f.mm(ps[:, 0:3 * NFA], ut[0:Kp, q, :], F1[0:Kp, :])
                f.cp(Y[:, q, :], ps[:, 0:3 * NFA], eng="act" if q % 2 else "dve")
            for fa0 in range(0, NFA, 8):
                nfa = min(8, NFA - fa0)
                px = psX.next()
                pr = px[:, 0]
                pi = px[:, 1]
                for i in range(nfa):
                    fa = fa0 + i
                    f.mm(pr[:, i, :], E2r[:, fa, :], Y[:, :, fa], start=True, stop=False)
                    f.mm(pr[:, i, :], E2i[:, fa, :], Y[:, :, 2 * NFA + fa], start=False, stop=True)
                for i in range(nfa):
                    fa = fa0 + i
                    f.mm(pi[:, i, :], E2i[:, fa, :], Y[:, :, fa], start=True, stop=False)
                    f.mm(pi[:, i, :], E2r[:, fa, :], Y[:, :, NFA + fa], start=False, stop=True)
                consume(fa0, nfa, pr, pi)

        mB = f.mark()
        hd2 = f.sbuf("hhd2", [64, NF], F32)
        fw3 = f.sbuf("hfw3", [64, 2048], F32)
        fb3T = f.sbuf("hfb3T", [128, 16], F32)
        nd = f.sbuf("hnd", [128, 4], F32)
        hbT = f.sbuf("hhbT", [128, 2, 4], F32)
        f.dma("sp", fw3[:], I["hy_f_w3"][l])
        f.dma("sp", fb3T[:], I["hy_fb3T"][l])
        f.dma("sp", nd[:], I["hy_negdelta"][:])
        f.dma("sp", hbT[:], I["hy_biasT"][l])
        mB1 = f.mark()
        featT = f.sbuf("hfeat", [33, NF], F32)
        f.dma("sp", featT[:], I["hy_featT" + sfx][:])
        fw1 = f.sbuf("hfw1", [33, 64], F32)
        fw2 = f.sbuf("hfw2", [64, 64], F32)
        fb12 = f.sbuf("hfb12", [64, 2], F32)
        f.dma("sp", fw1[:], I["hy_f_w1"][l])
        f.dma("sp", fw2[:], I["hy_f_w2"][l])
        f.dma("sp", fb12[:], I["hy_fb12"][l])
        hd1p = Pool([f.sbuf("hhd1_%d" % i, [64, 512], F32) for i in range(2)])
        ap_ = Pool([f.sbuf("ha%d" % i, [64, 512], F32) for i in range(2)])
        m1p = Pool([f.sbuf("hm1_%d" % i, [64, 512], F32) for i in range(2)])
        m2p = Pool([f.sbuf("hm2_%d" % i, [64, 512], F32) for i in range(2)])
        psm = Pool([f.psum("hpm%d" % i, [128, 512], F32) for i in range(3)])
        nblk = NF // 512

        def sin_wrap(dst, ps, bias):
            a = ap_.next()
            m1 = m1p.next()
            m2 = m2p.next()
            f.act(a[:], ps[0:64, :], AF.Identity, bias=bias)
            f.ts(m1[:], a[:], math.pi, ALU.is_gt, 2 * math.pi, ALU.mult)
            f.ts(m2[:], a[:], -math.pi, ALU.is_lt, 2 * math.pi, ALU.mult)
            f.tt(a[:], a[:], m1[:], ALU.subtract)
            f.tt(a[:], a[:], m2[:], ALU.add)
            f.act(dst, a[:], AF.Sin)

        for blk in range(nblk):
            ps = psm.next()
            f.mm(ps[0:64, :], fw1[:], featT[:, blk * 512:(blk + 1) * 512])
            hd1 = hd1p.next()
            sin_wrap(hd1[:], ps, fb12[:, 0:1])
            ps2 = psm.next()
            f.mm(ps2[0:64, :], fw2[:], hd1[:])
            sin_wrap(hd2[:, blk * 512:(blk + 1) * 512], ps2, fb12[:, 1:2])
        f.release(mB1)
        tn2 = f.sbuf("htn2", [128, NF], F32)
        f.dma("sp", tn2[:], I["hy_tn2" + sfx][0].pb(128))
        kT = f.sbuf("hkT", [128, NF], F32)
        kTb = f.sbuf("hkTb", [128, NF], BF16)
        kbp = Pool([f.sbuf("hkb%d" % i, [128, 512], F32) for i in range(2)])
        wbp = Pool([f.sbuf("hwb%d" % i, [128, 512], F32) for i in range(2)])
        jk = f.sbuf("hjk", [128, 512], BF16)
        asum = f.sbuf("hasum", [128, 20], F32)
        kup = Pool([f.sbuf("hku%d" % i, [128, 32, 128], BF16) for i in range(1)])
        Hst = Pool([f.sbuf("hHst%d" % i, [128, 2, NFA, 32], BF16) for i in range(1)])
        psm = Pool([f.psum("hpm2_%d" % i, [128, 512], F32) for i in range(2)])
        bs = min(512, NF // 2)
        nb2 = NF // bs
        for cc in range(4):
            for n_ in range(2):
                for blk in range(nb2):
                    dr = 0 if blk < nb2 // 2 else 1
                    col = (dr * 2 + n_) * 4 + cc
                    cs_ = slice(blk * bs, (blk + 1) * bs)
                    ps = psm.next()
                    f.mm(ps[:, 0:bs], fw3[:, col * 128:(col + 1) * 128], hd2[:, cs_])
                    kb = kbp.next()
                    wb = wbp.next()
                    f.act(wb[:, 0:bs], tn2[:, cs_], AF.Exp, scale=nd[:, cc:cc + 1])
                    f.act(kb[:, 0:bs], ps[:, 0:bs], AF.Identity, bias=fb3T[:, col:col + 1])
                    f.tt(kT[:, cs_], kb[:, 0:bs], wb[:, 0:bs], ALU.mult)
                    f.act(jk[:, 0:bs], kT[:, cs_], AF.Abs, accum=asum[:, blk:blk + 1])
                f.op("dve", lambda e: e.tensor_reduce(out=asum[:, 16:17].ap, in_=asum[:, 0:nb2].ap, axis=mybir.AxisListType.X, op=ALU.add),
                     reads=[asum], writes=[asum])
                f.recip(asum[:, 17:18], asum[:, 16:17])
                f.act(kTb[:], kT[:], AF.Identity, scale=asum[:, 17:18])
                f.ts(kTb[:, 0:1], kT[:, 0:1], asum[:, 17:18], ALU.mult, hbT[:, n_, cc:cc + 1], ALU.add)
                for gg in range(2):
                    g = cc * 2 + gg
                    kv = kTb[gg * 64:(gg + 1) * 64, :].re("c (a b) -> c b a", b=64)
                    ut = kup.next()
                    for b0 in range(0, 64, 8):
                        pt = pst.next()
                        for i in range(8):
                            f.tr(pt[0:NA, i, :], kv[:, b0 + i, :], ident[gg * 64:(gg + 1) * 64, gg * 64:(gg + 1) * 64])
                        f.cp(ut[0:NA].re("a q (b cp) -> a b q cp", cp=2)[:, b0:b0 + 8], pt[0:NA, :, :].re("a b (q cp) -> a b q cp", cp=2),
                             eng="act" if (b0 // 8) % 2 else "dve")
                    hs = Hst.next()

                    def cons(fa0, nfa, pr, pi, hs=hs):
                        f.cp(hs[:, 0, fa0:fa0 + nfa, :], pr[:, 0:nfa, :], eng="act")
                        f.cp(hs[:, 1, fa0:fa0 + nfa, :], pi[:, 0:nfa, :], eng="dve")
                    spectrum(ut, NA, cons)
                    f.dma("sp", H_d[n_, g, :, 0:2 * NFA * 32], hs[:].re("p r f q -> p (r f q)"))
        f.release(mB)

        CA = f.sbuf("hCA", [128, 3, 128], BF16)
        DBr = f.sbuf("hDBr", [NFA, 64, na], BF16)
        DBni = f.sbuf("hDBni", [NFA, 64, na], BF16)
        f.dma("sp", CA[:], I["hy_CA"][:])
        f.dma("sp", DBr[:], I["hy_DBr" + sfx][:])
        f.dma("sp", DBni[:], I["hy_DBni" + sfx][:])
        YpD = Pool([f.sbuf("hYD%d" % i, [128, 32, 3 * NFA], BF16) for i in range(1)] + Yp.bufs)
        uTp = Pool([f.sbuf("huT%d" % i, [64, L], BF16) for i in range(1)])
        gTp = Pool([f.sbuf("hgT%d" % i, [64, L], BF16) for i in range(2)])
        yTp = Pool([f.sbuf("hyT%d" % i, [64, L], BF16) for i in range(1)])
        up = Pool([f.sbuf("hu%d" % i, [64, 32, 128], BF16) for i in range(2)])
        Hp = Pool([f.sbuf("hH%d" % i, [128, 2, NFA, 32], BF16) for i in range(2)])
        Pp = Pool([f.sbuf("hP%d" % i, [128, 2, 32, NFA], BF16) for i in range(2)])
        Z0p = Pool([f.sbuf("hZ0_%d" % i, [NFA, 2, 64, 64], BF16) for i in range(1)])
        tp = Pool([f.sbuf("ht%d" % i, [128, 8, 32], F32) for i in range(6)])
        psZ = Pool([f.psum("hpZ%d" % i, [128, 4, 128], F32) for i in range(2)])
        psO = Pool([f.psum("hpO%d" % i, [128, 8, 64], F32) for i in range(1)])
        for n_ in range(2):
            srcT = scT_d[1024:1536] if n_ == 0 else y1T_d
            gateT = scT_d[0:512] if n_ == 0 else scT_d[512:1024]
            dstT = y1T_d if n_ == 0 else yhyT_d
            for g in range(8):
                uT = uTp.next()
                gT = gTp.next()
                f.dma("sp", uT[:, 0:nrow], srcT[g * 64:(g + 1) * 64, r0:r0 + nrow])
                f.dma("sp", gT[:, 0:nrow], gateT[g * 64:(g + 1) * 64, r0:r0 + nrow])
                H = Hp.next()
                f.dma("sp", H[:].re("p r f q -> p (r f q)"), H_d[n_, g, :, 0:2 * NFA * 32])
                u = up.next()
                uv = uT[:, 0:nrow].re("c (a b) -> c b a", b=64)
                for b0 in range(0, 64, 8):
                    pt = pst.next()
                    for i in range(8):
                        f.tr(pt[0:na, i, :], uv[:, b0 + i, :], ident[0:64, 0:64])
                    f.cp(u[0:na].re("a q (b cp) -> a b q cp", cp=2)[:, b0:b0 + 8], pt[0:na, :, :].re("a b (q cp) -> a b q cp", cp=2),
                         eng="act" if (b0 // 8) % 2 else "dve")
                P = Pp.next()

                def cons(fa0, nfa, pr, pi, H=H, P=P):
                    t1, t2, t3, t4 = tp.next(), tp.next(), tp.next(), tp.next()
                    f.tt(t1[:, 0:nfa], pr[:, 0:nfa, :], H[:, 0, fa0:fa0 + nfa, :], ALU.mult)
                    f.tt(t2[:, 0:nfa], pi[:, 0:nfa, :], H[:, 1, fa0:fa0 + nfa, :], ALU.mult)
                    f.tt(t3[:, 0:nfa], pr[:, 0:nfa, :], H[:, 1, fa0:fa0 + nfa, :], ALU.mult)
                    f.tt(t4[:, 0:nfa], pi[:, 0:nfa, :], H[:, 0, fa0:fa0 + nfa, :], ALU.mult)
                    f.tt(P[:, 0, :, fa0:fa0 + nfa].re("p q f -> p f q"), t1[:, 0:nfa], t2[:, 0:nfa], ALU.subtract, eng="pool")
                    f.tt(P[:, 1, :, fa0:fa0 + nfa].re("p q f -> p f q"), t3[:, 0:nfa], t4[:, 0:nfa], ALU.add, eng="pool")
                spectrum(u, na, cons, YpD)
                Z0 = Z0p.next()
                for q0 in range(0, 32, 4):
                    zr = psZ.next()
                    zi = psZ.next()
                    for i in range(4):
                        q = q0 + i
                        f.mm(zr[0:NFA, i, :], P[:, 0, q, :], CA[:, 0, :], start=True, stop=False)
                        f.mm(zr[0:NFA, i, :], P[:, 1, q, :], CA[:, 2, :], start=False, stop=True)
                    for i in range(4):
                        q = q0 + i
                        f.mm(zi[0:NFA, i, :], P[:, 0, q, :], CA[:, 1, :], start=True, stop=False)
                        f.mm(zi[0:NFA, i, :], P[:, 1, q, :], CA[:, 0, :], start=False, stop=True)
                    f.cp(Z0[:, 0].re("f b (q cp) -> f q b cp", cp=2)[:, q0:q0 + 4], zr[0:NFA, :, :].re("f q (b cp) -> f q b cp", cp=2), eng="act")
                    f.cp(Z0[:, 1].re("f b (q cp) -> f q b cp", cp=2)[:, q0:q0 + 4], zi[0:NFA, :, :].re("f q (b cp) -> f q b cp", cp=2), eng="dve")
                yT = yTp.next()
                yv = yT[:, 0:nrow].re("c (a b) -> c a b", b=64)
                gv = gT[:, 0:nrow].re("c (a b) -> c a b", b=64)
                for b0 in range(0, 64, 8):
                    po_ = psO.next()
                    for i in range(8):
                        b = b0 + i
                        f.mm(po_[0:64, i, 0:na], Z0[:, 0, b, :], DBr[:, b, :], start=True, stop=False)
                        f.mm(po_[0:64, i, 0:na], Z0[:, 1, b, :], DBni[:, b, :], start=False, stop=True)
                    f.tt(yv[:, :, b0:b0 + 8], po_[0:64, :, 0:na].re("c b a -> c a b"), gv[:, :, b0:b0 + 8], ALU.mult)
                f.dma("sp", dstT[g * 64:(g + 1) * 64, r0:r0 + nrow], yT[:, 0:nrow])
            f.barrier()
        f.release(m)

    ONE_T = f.sbuf("one_t", [128, 1], F32)
    f.memset(ONE_T[:], 1.0)

    phase_mod()
    done = False
    if "only_hy" in dbg:
        phase_hy(0, False)
        phase_hy(0, True)
        f.barrier()
        f.barrier(["sp"])
        f.release(0)
        return nc
    for l in range(DEPTH):
        if "from_merge" not in dbg:
            mk = f.mark()
            hT = f.sbuf("hT", [128, 8, T], BF16)
            norm_tiles(l, 0, hT, range(NT))
            if hT_d is not None and l == 0:
                for kc in range(8):
                    f.dma("sp", hT_d[kc * 128:(kc + 1) * 128, :], hT[:, kc, :])
            if stop_after == "norm":
                f.release(mk)
                break
            phase_proj(l, hT)
            f.release(mk)
            if stop_after == "proj":
                break
            if "skip_att" not in dbg:
                phase_att(l)
            if stop_after == "att":
                break
            if "skip_gla" not in dbg:
                phase_gla(l)
            if stop_after == "gla":
                break
            if "skip_hy" not in dbg:
                phase_hy(l, False)
                if l < DEPTH - 1:
                    phase_hy(l, True)
            if stop_after == "hy":
                break
        phase_merge(l)
        if stop_after == "merge":
            break
        phase_ffn(l)
        if stop_after == "ffn":
            break
    else:
        done = True
    f.barrier()
    if done:
        phase_final()
    f.barrier(["sp"])
    f.release(0)
    return nc


def _fm(v, chunks):
    return np.ascontiguousarray(np.asarray(v, np.float32).reshape(chunks, 128).T)


def make_in_maps(inputs):
    g = {k: np.asarray(v) for k, v in inputs.items()}
    hc = host_constants()
    perm = rope_swap_perm()
    sh = {}
    sh["ada_w"] = np.ascontiguousarray(g["ada_w"], np.float32)
    sh["ada_bf"] = np.ascontiguousarray(np.stack([_fm(g["ada_b"][l], 48) for l in range(DEPTH)], 1))
    sh["ada_br"] = np.ascontiguousarray(np.broadcast_to(g["ada_b"][None], (2, DEPTH, 6 * D)), np.float32)
    sh["n1g"] = np.ascontiguousarray(np.stack([_fm(g["norm1_g"][l], 8) for l in range(DEPTH)], 1))
    sh["n2g"] = np.ascontiguousarray(np.stack([_fm(g["norm2_g"][l], 8) for l in range(DEPTH)], 1))
    sh["w_in"] = np.ascontiguousarray(g["w_in"], np.float32)
    wkr = np.zeros((DEPTH, D, 2, 96), np.float32)
    wkr[:, :, 0, 64:96] = g["w_in"][:, :, 384:416]
    wkr[:, :, 1, 64:96] = g["w_in"][:, :, 384:416][:, :, perm]
    sh["w_kr2"] = wkr
    sh["qng"] = np.ascontiguousarray(np.stack([_fm(g["mla_q_norm"][l], 2) for l in range(DEPTH)], 1))
    sh["kvng"] = np.ascontiguousarray(np.stack([g["mla_kv_norm"][l] for l in range(DEPTH)], 1), np.float32)
    sh["w_uq"] = np.ascontiguousarray(g["mla_w_uq"], np.float32)
    wsw = g["mla_w_uq"].reshape(DEPTH, 256, 8, 96).copy()
    wsw[:, :, :, 64:96] = wsw[:, :, :, 64:96][:, :, :, perm]
    sh["w_uq_sw"] = np.ascontiguousarray(wsw.reshape(DEPTH, 256, 768), np.float32)
    ukv = g["mla_w_ukv"].reshape(DEPTH, 128, 8, 128)
    sh["w_ukv_k"] = np.ascontiguousarray(ukv[:, :, :, 0:64].reshape(DEPTH, 128, 512), np.float32)
    sh["w_ukv_v"] = np.ascontiguousarray(ukv[:, :, :, 64:128].reshape(DEPTH, 128, 512), np.float32)
    sh["w_a2"] = np.ascontiguousarray(np.concatenate([g["gla_w_a2"][:, 0], g["gla_w_a2"][:, 1]], -1), np.float32)
    sh["b_a"] = np.ascontiguousarray(np.concatenate([g["gla_b_a"][:, 0], g["gla_b_a"][:, 1]], -1)[:, None, :], np.float32)
    for nm_ in ("w_o_mla", "w_o_gla", "w_o_hy", "w_out", "ff_w1", "ff_w2"):
        sh[nm_] = np.ascontiguousarray(g[nm_], np.float32)
    sw = g["hy_short_w"]
    swb = np.concatenate([sw, g["hy_short_b"][:, None, :]], 1)
    sh["hy_swb"] = np.ascontiguousarray(swb.reshape(DEPTH, 4, 12, 128).transpose(0, 3, 2, 1), np.float32)
    sh["hy_f_w1"] = np.ascontiguousarray(g["hy_f_w1"], np.float32)
    sh["hy_f_w2"] = np.ascontiguousarray(g["hy_f_w2"], np.float32)
    sh["hy_f_w3"] = np.ascontiguousarray(g["hy_f_w3"], np.float32)
    sh["hy_fb12"] = np.ascontiguousarray(np.stack([g["hy_f_b1"], g["hy_f_b2"]], -1), np.float32)
    sh["hy_fb3T"] = np.ascontiguousarray(g["hy_f_b3"].reshape(DEPTH, 16, 128).transpose(0, 2, 1), np.float32)
    sh["hy_biasT"] = np.ascontiguousarray(g["hy_bias"].reshape(DEPTH, 2, 4, 128).transpose(0, 3, 1, 2), np.float32)
    sh["fin_g"] = np.ascontiguousarray(np.broadcast_to(g["final_norm_g"][None, :], (128, D)), np.float32)
    sh["gla_on"] = np.ascontiguousarray(np.broadcast_to(g["gla_out_norm"][:, None, :], (DEPTH, 128, 128)), np.float32)
    for k, v in hc.items():
        sh[k] = v
    maps = []
    for b in range(8):
        m = dict(sh)
        m["xc"] = np.ascontiguousarray(np.concatenate([g["ctx"][b], g["x"][b]], 0), np.float32)
        m["cs"] = np.ascontiguousarray(np.stack([_fm(g["c"][b], 8), _fm(g["c_ctx"], 8)], -1))
        maps.append(m)
    return maps


_NC_CACHE = {}


def kernel(**inputs):
    if "nc" not in _NC_CACHE:
        _NC_CACHE["nc"] = build()
    nc = _NC_CACHE["nc"]
    maps = make_in_maps(inputs)
    res = run_bass_kernel_spmd(nc, maps, core_ids=list(range(8)))
    return np.stack([np.asarray(r["y"], np.float32) for r in res.results], 0)
```

```python
import math
import numpy as np
import ml_dtypes
import concourse.bass as bass
import concourse.mybir as mybir
from concourse.bass_utils import run_bass_kernel_spmd

F32 = mybir.dt.float32
BF16 = mybir.dt.bfloat16
AF = mybir.ActivationFunctionType
ALU = mybir.AluOpType

D = 1024
L = 4096
LC = 256
T = L + LC
NT = T // 128
DEPTH = 2
DIN = 6592
EPS = 1e-6
MLA_SCALE = 96 ** -0.5
TB = [(0, 256)] + [(256 + 512 * i, 512) for i in range(8)]
NFFT = 8192


class V:
    __slots__ = ("b", "ap")

    def __init__(self, b, ap):
        self.b = b
        self.ap = ap

    def __getitem__(self, idx):
        return V(self.b, self.ap[idx])

    def re(self, pat, **kw):
        return V(self.b, self.ap.rearrange(pat, **kw))

    def bc(self, shape):
        return V(self.b, self.ap.broadcast_to(list(shape)))

    def un(self, axis):
        return V(self.b, self.ap.unsqueeze(axis))

    def pb(self, n):
        return V(self.b, self.ap.partition_broadcast(n))


class Buf:
    __slots__ = ("t", "name", "lw", "rd", "psum", "dram")

    def __init__(self, t, name, psum=False, dram=False):
        self.t = t
        self.name = name
        self.lw = []
        self.rd = []
        self.psum = psum
        self.dram = dram

    def __getitem__(self, idx):
        return V(self, self.t[idx])

    @property
    def v(self):
        return V(self, self.t[:] if not hasattr(self.t, "ap") or True else self.t)


class FW:
    NDMA_SEM = 36
    NDMA_HW = 24

    def __init__(self, nc):
        self.nc = nc
        self.eng = {"pe": nc.tensor, "act": nc.scalar, "dve": nc.vector, "pool": nc.gpsimd, "sp": nc.sync}
        self.sem = {}
        self.cnt = {}
        for e in self.eng:
            self.sem[e] = nc.alloc_semaphore("s_" + e)
            self.cnt[e] = 0
        self.dsem = [nc.alloc_semaphore("d%d" % i) for i in range(self.NDMA_SEM)]
        self.dcnt = [0] * self.NDMA_SEM
        self.dnext = 0
        self.dnext_sw = 0
        self.seen = {e: {} for e in self.eng}
        self.ninst = 0
        self._ctx = []
        self._uid = 0
        self.deferred = []

    def _nm(self, name):
        self._uid += 1
        return "%s_%d" % (name, self._uid)

    def sbuf(self, name, shape, dt):
        g = self.nc.sbuf_tensor(self._nm(name), list(shape), dt)
        t = g.__enter__()
        self._ctx.append(g)
        return Buf(t, name)

    def psum(self, name, shape, dt=F32):
        g = self.nc.psum_tensor(self._nm(name), list(shape), dt)
        t = g.__enter__()
        self._ctx.append(g)
        return Buf(t, name, psum=True)

    def dram(self, name, shape, dt, kind="Internal"):
        t = self.nc.dram_tensor(name, list(shape), dt, kind=kind)
        return Buf(t.ap(), name, dram=True)

    def _wait(self, e, tok):
        if tok is None:
            return
        key, val = tok
        if e == "pe" and key == "pe":
            return
        if self.seen[e].get(key, 0) >= val:
            return
        self.seen[e][key] = val
        sem = self.sem[key] if isinstance(key, str) else self.dsem[key]
        self.eng[e].wait_ge(sem, val)

    def _deps(self, e, reads, writes, dma_write=False):
        for b in reads:
            for tok in b.lw:
                self._wait(e, tok)
        for b in writes:
            if not (dma_write and all(isinstance(t[0], int) for t in b.lw)):
                for tok in b.lw:
                    self._wait(e, tok)
            for tok in b.rd:
                self._wait(e, tok)

    @staticmethod
    def _compact(toks):
        best = {}
        for k, v in toks:
            if best.get(k, 0) < v:
                best[k] = v
        return list(best.items())

    def _commit(self, tok, reads, writes, dma_write=False):
        for b in reads:
            b.rd.append(tok)
            if len(b.rd) > 48:
                b.rd = self._compact(b.rd)
        for b in writes:
            if dma_write and b.lw and all(isinstance(t[0], int) for t in b.lw):
                b.lw.append(tok)
                if len(b.lw) > 48:
                    b.lw = self._compact(b.lw)
            else:
                b.lw = [tok]
            b.rd = []

    def flush(self):
        d, self.deferred = self.deferred, []
        for (q, out, in_, kw) in d:
            self._dma_now(q, out, in_, **kw)

    def op(self, e, fn, reads=(), writes=()):
        if self.deferred:
            self.flush()
        rd = [b for b in reads if not b.psum]
        wr = list(writes) + [b for b in reads if b.psum]
        self._deps(e, rd, wr)
        ins = fn(self.eng[e])
        self.cnt[e] += 1
        ins.then_inc(self.sem[e], 1)
        self._commit((e, self.cnt[e]), rd, wr)
        self.ninst += 1
        return ins

    def dma(self, q, out, in_, **kw):
        if out.b.dram and not in_.b.dram:
            self.deferred.append((q, out, in_, kw))
            return
        for (_, so, si, _) in self.deferred:
            if so.b is in_.b or si.b is out.b or so.b is out.b:
                self.flush()
                break
        self._dma_now(q, out, in_, **kw)

    def _dma_now(self, q, out, in_, **kw):
        if q == "pool":
            slot = self.NDMA_HW + self.dnext_sw
            self.dnext_sw = (self.dnext_sw + 1) % (self.NDMA_SEM - self.NDMA_HW)
        else:
            slot = self.dnext
            self.dnext = (self.dnext + 1) % self.NDMA_HW
        if self.dcnt[slot] > 0:
            self._wait(q, (slot, self.dcnt[slot]))
        self._deps(q, [in_.b], [out.b], dma_write=True)
        ins = self.eng[q].dma_start(out=out.ap, in_=in_.ap, **kw)
        self.dcnt[slot] += 16
        ins.then_inc(self.dsem[slot], 16)
        self._commit((slot, self.dcnt[slot]), [in_.b], [out.b], dma_write=True)
        self.ninst += 1

    def barrier(self, engines=None):
        self.flush()
        for e in (engines or self.eng):
            for e2 in self.eng:
                if e2 != e and self.cnt[e2] > 0:
                    self._wait(e, (e2, self.cnt[e2]))
            for s in range(self.NDMA_SEM):
                if self.dcnt[s] > 0:
                    self._wait(e, (s, self.dcnt[s]))

    def mark(self):
        return len(self._ctx)

    def release(self, mark):
        self.barrier()
        while len(self._ctx) > mark:
            self._ctx.pop().__exit__(None, None, None)

    def mm(self, out, lhsT, rhs, start=True, stop=True):
        return self.op("pe", lambda e: e.matmul(out.ap, lhsT=lhsT.ap, rhs=rhs.ap, start=start, stop=stop),
                       reads=[lhsT.b, rhs.b], writes=[out.b])

    def tr(self, out, in_, ident):
        return self.op("pe", lambda e: e.transpose(out.ap, in_.ap, ident.ap), reads=[in_.b, ident.b], writes=[out.b])

    def act(self, out, in_, func, bias=None, scale=None, accum=None, eng="act"):
        kw = {}
        rd = [in_.b]
        wr = [out.b]
        if bias is not None:
            if isinstance(bias, V):
                kw["bias"] = bias.ap
                rd.append(bias.b)
            else:
                kw["bias"] = bias
        if scale is not None:
            if isinstance(scale, V):
                kw["scale"] = scale.ap
                rd.append(scale.b)
            else:
                kw["scale"] = scale
        if accum is not None:
            kw["accum_out"] = accum.ap
            wr.append(accum.b)
        return self.op("act", lambda e: e.activation(out=out.ap, in_=in_.ap, func=func, **kw), reads=rd, writes=wr)

    def tt(self, out, a, b, op, eng="dve"):
        return self.op(eng, lambda e: e.tensor_tensor(out=out.ap, in0=a.ap, in1=b.ap, op=op),
                       reads=[a.b, b.b], writes=[out.b])

    def ts(self, out, a, s1, op0, s2=None, op1=None, eng="dve"):
        rd = [a.b]
        s1a = s1
        s2a = s2
        if isinstance(s1, V):
            rd.append(s1.b)
            s1a = s1.ap
        if isinstance(s2, V):
            rd.append(s2.b)
            s2a = s2.ap
        kw = {}
        if op1 is not None:
            kw["op1"] = op1
        return self.op(eng, lambda e: e.tensor_scalar(out=out.ap, in0=a.ap, scalar1=s1a, scalar2=s2a, op0=op0, **kw),
                       reads=rd, writes=[out.b])

    def stt(self, out, a, s, b, op0, op1, eng="dve"):
        rd = [a.b, b.b]
        sa = s
        if isinstance(s, V):
            rd.append(s.b)
            sa = s.ap
        return self.op(eng, lambda e: e.scalar_tensor_tensor(out=out.ap, in0=a.ap, scalar=sa, in1=b.ap, op0=op0, op1=op1),
                       reads=rd, writes=[out.b])

    def cp(self, out, in_, eng="dve"):
        if eng == "act":
            return self.op("act", lambda e: e.copy(out=out.ap, in_=in_.ap), reads=[in_.b], writes=[out.b])
        return self.op(eng, lambda e: e.tensor_copy(out=out.ap, in_=in_.ap), reads=[in_.b], writes=[out.b])

    def memset(self, out, val, eng="pool"):
        return self.op(eng, lambda e: e.memset(out.ap, val), writes=[out.b])

    def recip(self, out, in_):
        return self.op("dve", lambda e: e.reciprocal(out=out.ap, in_=in_.ap), reads=[in_.b], writes=[out.b])


class Pool:
    def __init__(self, bufs):
        self.bufs = bufs
        self.i = 0

    def next(self):
        b = self.bufs[self.i % len(self.bufs)]
        self.i += 1
        return b


def _bf(a):
    return np.ascontiguousarray(a.astype(ml_dtypes.bfloat16))


def host_constants():
    c = {}
    c["ident_bf"] = _bf(np.eye(128, dtype=np.float32))
    c["ident_f"] = np.eye(128, dtype=np.float32)
    c["ones_f"] = np.ones((128, 128), np.float32)
    rows = L // 64
    row = np.repeat(np.arange(rows, dtype=np.float32), 64)
    col = np.tile(np.arange(64, dtype=np.float32), rows)
    inv = (10000.0 ** (-np.arange(8, dtype=np.float32) / 8)).astype(np.float32)
    ang = np.concatenate([row[:, None] * inv, col[:, None] * inv], axis=-1)
    cos, sin = np.cos(ang), np.sin(ang)
    cosT = np.ones((96, T), np.float32)
    sinT = np.zeros((96, T), np.float32)
    for r in range(32):
        g, j = r // 16, r % 16
        half, i = j // 8, j % 8
        cosT[64 + r, LC:] = cos[:, g * 8 + i]
        sinT[64 + r, LC:] = (-sin[:, g * 8 + i]) if half == 0 else sin[:, g * 8 + i]
    c["cosT"] = cosT
    c["sinT"] = sinT
    i_ = np.arange(128)[None, :]
    j_ = np.arange(128)[:, None]
    Mf = ((j_ >= 64) & (j_ <= i_)).astype(np.float32) - ((j_ > i_) & (j_ <= 63)).astype(np.float32)
    Mb = ((j_ >= i_) & (j_ <= 63)).astype(np.float32) - ((j_ >= 64) & (j_ < i_)).astype(np.float32)
    c["gla_M"] = np.stack([Mf, Mb], 1).astype(np.float32)
    c["gla_mask"] = np.stack([(j_ <= i_), (j_ >= i_)], 1).astype(np.float32)
    ind = np.zeros((128, 2), np.float32)
    ind[:64, 0] = 1
    ind[64:, 1] = 1
    c["gla_ind"] = ind
    deltas = np.linspace(math.log(1e-2) / 0.3, math.log(1e-2) / 1.5, 512)
    fb = np.linspace(1e-4, 15.0, 16)
    b_ = np.arange(64)
    fbb = np.arange(64)
    for sfx, Ls in (("", L), ("_c", LC)):
        NF = 2 * Ls
        NA = NF // 64
        NFA = NA // 2 + 1
        pos = np.arange(Ls, dtype=np.float64)
        tn = pos / max(Ls - 1, 1)
        ang = (2.0 * math.pi / Ls) * pos[:, None] * fb
        feat = np.concatenate([tn[:, None], np.cos(ang), np.sin(ang)], -1)
        win = np.exp(-tn[:, None] * np.abs(deltas))
        feat2 = np.zeros((NF, 33))
        win2 = np.zeros((NF, 512))
        feat2[:Ls] = feat
        win2[:Ls] = win
        idx = np.arange(NF - Ls + 1, NF)
        feat2[idx] = feat[NF - idx]
        win2[idx] = win[NF - idx]
        c["hy_featT" + sfx] = np.ascontiguousarray(feat2.T.astype(np.float32))
        tn2 = np.full((1, NF), 1.0e4)
        tn2[0, :Ls] = tn
        tn2[0, idx] = tn[NF - idx]
        c["hy_tn2" + sfx] = tn2.astype(np.float32)
        a_ = np.arange(NA)[:, None]
        fa = np.arange(NFA)[None, :]
        th = 2 * math.pi * ((fa * a_) % NA) / NA
        c["hy_F1" + sfx] = _bf(np.concatenate([np.cos(th), -np.sin(th), np.sin(th)], 1))
        E2r = np.zeros((64, 2, NFA, 64, 2))
        E2i = np.zeros((64, 2, NFA, 64, 2))
        ph = 2 * math.pi * (((np.arange(NFA)[None, :, None] + NA * fbb[None, None, :]) * b_[:, None, None]) % NF) / NF
        for cp in range(2):
            E2r[:, cp, :, :, cp] = np.cos(ph)
            E2i[:, cp, :, :, cp] = -np.sin(ph)
        c["hy_E2r" + sfx] = _bf(E2r.reshape(128, NFA, 128))
        c["hy_E2i" + sfx] = _bf(E2i.reshape(128, NFA, 128))
        tt_ = 64 * np.arange(NA // 2)[None, None, :] + b_[None, :, None]
        th2 = 2 * math.pi * ((np.arange(NFA)[:, None, None] * tt_) % NF) / NF
        wgt = np.full((NFA, 1, 1), 2.0 / NF)
        wgt[0] = wgt[NFA - 1] = 1.0 / NF
        c["hy_DBr" + sfx] = _bf(wgt * np.cos(th2))
        c["hy_DBni" + sfx] = _bf(-wgt * np.sin(th2))
    c["hy_negdelta"] = np.ascontiguousarray((-np.abs(deltas)).reshape(4, 128).T.astype(np.float32))
    psi = 2 * math.pi * ((fbb[:, None] * b_[None, :]) % 64) / 64
    CA = np.zeros((64, 2, 3, 64, 2))
    for cp in range(2):
        CA[:, cp, 0, :, cp] = np.cos(psi)
        CA[:, cp, 1, :, cp] = np.sin(psi)
        CA[:, cp, 2, :, cp] = -np.sin(psi)
    c["hy_CA"] = _bf(CA.reshape(128, 3, 128))
    return c


def rope_swap_perm():
    p = np.zeros(32, np.int64)
    for r in range(32):
        g, j = r // 16, r % 16
        p[r] = g * 16 + (j + 8) % 16
    return p


def build(dbg=None):
    dbg = dbg or {}
    stop_after = dbg.get("stop_after", None)
    ext = dbg.get("ext", ())
    nc = bass.Bass("TRN2", target_bir_lowering=False)
    f = FW(nc)
    hc = host_constants()

    def inp(name, shape, dt=F32):
        return Buf(nc.dram_tensor(name, list(shape), dt, kind="ExternalInput").ap(), name, dram=True)

    def scratch(name, shape, dt):
        if name in dbg.get("inject", ()):
            return f.dram(name, shape, dt, kind="ExternalInput")
        return f.dram(name, shape, dt, kind="ExternalOutput" if name in ext else "Internal")

    I = {}
    I["xc"] = inp("xc", [T, D])
    I["cs"] = inp("cs", [128, 8, 2])
    I["ada_w"] = inp("ada_w", [DEPTH, D, 6 * D])
    I["ada_bf"] = inp("ada_bf", [128, DEPTH, 48])
    I["ada_br"] = inp("ada_br", [2, DEPTH, 6 * D])
    I["n1g"] = inp("n1g", [128, DEPTH, 8])
    I["n2g"] = inp("n2g", [128, DEPTH, 8])
    I["w_in"] = inp("w_in", [DEPTH, D, DIN])
    I["w_kr2"] = inp("w_kr2", [DEPTH, D, 2, 96])
    I["qng"] = inp("qng", [128, DEPTH, 2])
    I["kvng"] = inp("kvng", [128, DEPTH])
    I["w_uq"] = inp("w_uq", [DEPTH, 256, 768])
    I["w_uq_sw"] = inp("w_uq_sw", [DEPTH, 256, 768])
    I["w_ukv_k"] = inp("w_ukv_k", [DEPTH, 128, 512])
    I["w_ukv_v"] = inp("w_ukv_v", [DEPTH, 128, 512])
    I["w_a2"] = inp("w_a2", [DEPTH, 16, 512])
    I["b_a"] = inp("b_a", [DEPTH, 1, 512])
    I["gla_on"] = inp("gla_on", [DEPTH, 128, 128])
    for nm_ in ("w_o_mla", "w_o_gla", "w_o_hy"):
        I[nm_] = inp(nm_, [DEPTH, 512, D])
    I["w_out"] = inp("w_out", [DEPTH, D, D])
    I["ff_w1"] = inp("ff_w1", [DEPTH, D, 4 * D])
    I["ff_w2"] = inp("ff_w2", [DEPTH, 4 * D, D])
    I["fin_g"] = inp("fin_g", [128, D])
    I["hy_swb"] = inp("hy_swb", [DEPTH, 128, 12, 4])
    I["hy_f_w1"] = inp("hy_f_w1", [DEPTH, 33, 64])
    I["hy_f_w2"] = inp("hy_f_w2", [DEPTH, 64, 64])
    I["hy_f_w3"] = inp("hy_f_w3", [DEPTH, 64, 2048])
    I["hy_fb12"] = inp("hy_fb12", [DEPTH, 64, 2])
    I["hy_fb3T"] = inp("hy_fb3T", [DEPTH, 128, 16])
    I["hy_biasT"] = inp("hy_biasT", [DEPTH, 128, 2, 4])
    for k, v in hc.items():
        I[k] = inp(k, list(v.shape), BF16 if v.dtype == ml_dtypes.bfloat16 else F32)
    out_y = Buf(nc.dram_tensor("y", [L, D], F32, kind="ExternalOutput").ap(), "y", dram=True)

    xres = scratch("xres", [T, D], F32)
    modrow = scratch("modrow", [2, DEPTH, 2, D], F32)
    qT_d = scratch("qT_d", [8, 96, T], BF16)
    kT_d = scratch("kT_d", [8, 96, T], BF16)
    V_d = scratch("V_d", [T, 520], BF16)
    gqk_d = scratch("gqk_d", [T, 512], F32)
    gv_d = scratch("gv_d", [T, 512], BF16)
    gr_d = scratch("gr_d", [T, 512], F32)
    gg_d = scratch("gg_d", [T, 512], F32)
    zhyT_d = scratch("zhyT_d", [1536, T], BF16)
    scT_d = scratch("scT_d", [1536, T], BF16)
    H_d = scratch("H_d", [2, 8, 128, 2 * 65 * 32], BF16)
    y1T_d = scratch("y1T_d", [512, T], BF16)
    gatesT_d = scratch("gatesT_d", [3072, T], BF16)
    hT_d = scratch("hT_d", [D, T], BF16) if "hT_d" in ext else None
    ymlaT_d = scratch("ymlaT_d", [512, T], BF16)
    yglaT_d = scratch("yglaT_d", [512, T], BF16)
    yhyT_d = scratch("yhyT_d", [512, T], BF16)
    og_d = scratch("og_d", [2, T, 512], F32)

    ident = f.sbuf("ident", [128, 128], BF16)
    ones_f = f.sbuf("ones_f", [128, 128], F32)
    modF = f.sbuf("modF", [128, DEPTH, 48, 2], F32)
    AB = f.sbuf("AB", [128, DEPTH, 2, 2, 8, 2], F32)
    f.dma("sp", ident[:], I["ident_bf"][:])
    f.dma("sp", ones_f[:], I["ones_f"][:])
    xres_t = [Buf(xres.t[tt * 128:(tt + 1) * 128, :], "xres%d" % tt, dram=True) for tt in range(NT)]
    for tt in range(NT):
        f.dma("sp", xres_t[tt][:], I["xc"][tt * 128:(tt + 1) * 128, :])

    def phase_mod():
        m = f.mark()
        cs = f.sbuf("cs", [128, 8, 2], F32)
        scs = f.sbuf("scs", [128, 8, 2], F32)
        abf = f.sbuf("abf", [128, DEPTH, 48], F32)
        abr = f.sbuf("abr", [2, DEPTH, 6 * D], F32)
        g1 = f.sbuf("g1", [128, DEPTH, 8], F32)
        g2 = f.sbuf("g2", [128, DEPTH, 8], F32)
        wp = Pool([f.sbuf("adaw%d" % i, [128, 8, 512], F32) for i in range(2)])
        rowst = f.sbuf("rowst", [2, 512], F32)
        psF = f.psum("psF", [128, 512], F32)
        psR = Pool([f.psum("psR%d" % i, [128, 512], F32) for i in range(2)])
        f.dma("sp", cs[:], I["cs"][:])
        f.dma("sp", abf[:], I["ada_bf"][:])
        f.dma("sp", abr[:], I["ada_br"][:])
        f.dma("sp", g1[:], I["n1g"][:])
        f.dma("sp", g2[:], I["n2g"][:])
        f.act(scs[:], cs[:], AF.Silu)
        for l in range(DEPTH):
            wv = I["ada_w"][l].re("(kc p) j -> p kc j", p=128)
            for jb in range(12):
                w = wp.next()
                f.dma("sp", w[:], wv[:, :, jb * 512:(jb + 1) * 512])
                which = jb // 2
                if which in (2, 5):
                    pr = psR.next()
                    for kc in range(8):
                        f.mm(pr[0:2, :], scs[:, kc, :], w[:, kc, :], start=kc == 0, stop=kc == 7)
                    f.tt(rowst[:], pr[0:2, :], abr[:, l, jb * 512:(jb + 1) * 512], ALU.add)
                    f.dma("sp", modrow[:, l, 0 if which == 2 else 1, (jb % 2) * 512:(jb % 2 + 1) * 512], rowst[:])
                else:
                    for jc in range(4):
                        ch = jb * 4 + jc
                        for kc in range(8):
                            f.mm(psF[:, ch * 2:ch * 2 + 2], w[:, kc, jc * 128:(jc + 1) * 128], scs[:, kc, :],
                                 start=kc == 0, stop=kc == 7)
            for (c0, c1) in ((0, 16), (24, 40)):
                pv = psF[:, c0 * 2:c1 * 2].re("p (c s) -> p c s", s=2)
                f.tt(modF[:, l, c0:c1, :], pv, abf[:, l, c0:c1].un(2).bc([128, c1 - c0, 2]), ALU.add)
            for n_i, (sh0, sc0, g) in enumerate(((0, 8, g1), (24, 32, g2))):
                f.stt(AB[:, l, n_i, 0], modF[:, l, sc0:sc0 + 8, :], 1.0, g[:, l, :].un(2).bc([128, 8, 2]), ALU.add, ALU.mult)
                f.cp(AB[:, l, n_i, 1], modF[:, l, sh0:sh0 + 8, :])
        f.release(m)

    class NormCtx:
        def __init__(self):
            self.xp = Pool([f.sbuf("nx%d" % i, [128, D], F32) for i in range(2)])
            self.xnp = Pool([f.sbuf("nxn%d" % i, [128, D], BF16) for i in range(2)])
            self.stp = Pool([f.sbuf("nst%d" % i, [128, 4], F32) for i in range(3)])
            self.pp = Pool([f.psum("nps%d" % i, [128, 8, 128], BF16) for i in range(2)])

        def emit(self, l, n_i, hT, tt, c0):
            s = 0 if tt >= 2 else 1
            x = self.xp.next()
            st = self.stp.next()
            xn = self.xnp.next()
            f.dma("sp", x[:], xres_t[tt][:])
            f.act(xn[:], x[:], AF.Square, accum=st[:, 0:1])
            f.act(st[:, 1:2], st[:, 0:1], AF.Sqrt, bias=EPS_T[:, 0:1], scale=1.0 / D)
            f.recip(st[:, 2:3], st[:, 1:2])
            f.act(xn[:], x[:], AF.Identity, scale=st[:, 2:3])
            ps = self.pp.next()
            for kc in range(8):
                f.tr(ps[:, kc, :], xn[:, kc * 128:(kc + 1) * 128], ident[:])
            for kc in range(8):
                if kc % 2 == 0:
                    f.act(hT[:, kc, c0:c0 + 128], ps[:, kc, :], AF.Identity,
                          bias=AB[:, l, n_i, 1, kc, s:s + 1], scale=AB[:, l, n_i, 0, kc, s:s + 1])
                else:
                    f.ts(hT[:, kc, c0:c0 + 128], ps[:, kc, :], AB[:, l, n_i, 0, kc, s:s + 1], ALU.mult,
                         AB[:, l, n_i, 1, kc, s:s + 1], ALU.add)

    def norm_tiles(l, n_i, hT, tiles):
        m = f.mark()
        nctx = NormCtx()
        for tt in tiles:
            nctx.emit(l, n_i, hT, tt, tt * 128)
        f.release(m)

    EPS_T = f.sbuf("eps_t", [128, 1], F32)
    f.memset(EPS_T[:], EPS)

    def phase_proj(l, hT):
        m = f.mark()
        wp = Pool([f.sbuf("pw%d" % i, [128, 8, 512], BF16) for i in range(3)])
        psp = Pool([f.psum("pps%d" % i, [128, 512], F32) for i in range(4)])
        psq = Pool([f.psum("ppq%d" % i, [128, 512], F32) for i in range(3)])
        w_l = I["w_in"][l].re("(kc p) n -> p kc n", p=128)

        def loadw(c0, n):
            w = wp.next()
            f.dma("pool", w[:, :, 0:n], w_l[:, :, c0:c0 + n])
            return w

        def fm(w, wc0, ncol, evac):
            for bi, (t0, n) in enumerate(TB):
                ps = psp.next()
                for kc in range(8):
                    f.mm(ps[0:ncol, 0:n], w[:, kc, wc0:wc0 + ncol], hT[:, kc, t0:t0 + n], start=kc == 0, stop=kc == 7)
                evac(ps, bi, t0, n)

        def tm(w, ncol, evac):
            for tt in range(NT):
                ps = psp.next()
                for kc in range(8):
                    f.mm(ps[:, 0:ncol], hT[:, kc, tt * 128:(tt + 1) * 128], w[:, kc, 0:ncol], start=kc == 0, stop=kc == 7)
                evac(ps, tt)

        w0 = loadw(0, 384)
        wk = wp.next()
        f.dma("pool", wk[:, :, 0:192], I["w_kr2"][l].re("(kc p) a n -> p kc (a n)", p=128))
        qng = f.sbuf("qng", [128, DEPTH, 2], F32)
        kvng = f.sbuf("kvng", [128, DEPTH], F32)
        f.dma("sp", qng[:], I["qng"][:])
        f.dma("sp", kvng[:], I["kvng"][:])
        wuq = f.sbuf("wuq", [128, 2, 768], BF16)
        wuqs = f.sbuf("wuqs", [128, 2, 768], BF16)
        wk_k = f.sbuf("wukv_k", [128, 512], BF16)
        wk_v = f.sbuf("wukv_v", [128, 512], BF16)
        f.dma("pool", wuq[:], I["w_uq"][l].re("(kc p) n -> p kc n", p=128))
        f.dma("pool", wuqs[:], I["w_uq_sw"][l].re("(kc p) n -> p kc n", p=128))
        f.dma("pool", wk_k[:], I["w_ukv_k"][l])
        f.dma("pool", wk_v[:], I["w_ukv_v"][l])
        cosp = Pool([f.sbuf("cosb%d" % i, [96, 512], F32) for i in range(2)])
        sinp = Pool([f.sbuf("sinb%d" % i, [96, 512], F32) for i in range(2)])
        cqp = Pool([f.sbuf("cqb%d" % i, [128, 2, 512], F32) for i in range(2)])
        ckvp = Pool([f.sbuf("ckvb%d" % i, [128, 512], F32) for i in range(2)])
        cqnp = Pool([f.sbuf("cqnb%d" % i, [128, 2, 512], BF16) for i in range(2)])
        ckvnp = Pool([f.sbuf("ckvnb%d" % i, [128, 512], BF16) for i in range(2)])
        krp = Pool([f.sbuf("krb%d" % i, [96, 512], BF16) for i in range(2)])
        tmpa = Pool([f.sbuf("ptmpa%d" % i, [128, 512], F32) for i in range(2)])
        tmpb = Pool([f.sbuf("ptmpb%d" % i, [128, 512], F32) for i in range(2)])
        sqp = Pool([f.sbuf("psq%d" % i, [128, 2, 512], F32) for i in range(2)])
        rsp = Pool([f.sbuf("prs%d" % i, [128, 512], F32) for i in range(2)])
        qst = Pool([f.sbuf("qst%d" % i, [96, 512], BF16) for i in range(3)])
        kst = Pool([f.sbuf("kst%d" % i, [96, 512], BF16) for i in range(3)])
        vst = Pool([f.sbuf("vst%d" % i, [128, 8, 65], BF16) for i in range(2)])
        for vb in vst.bufs:
            f.memset(vb[:], 1.0)
        for bi, (t0, n) in enumerate(TB):
            cosb = cosp.next()
            sinb = sinp.next()
            f.dma("sp", cosb[64:96, 0:n], I["cosT"][64:96, t0:t0 + n])
            f.dma("sp", sinb[64:96, 0:n], I["sinT"][64:96, t0:t0 + n])
            cq = cqp.next()
            ckv = ckvp.next()
            for c in range(3):
                ps = psp.next()
                for kc in range(8):
                    f.mm(ps[:, 0:n], w0[:, kc, c * 128:(c + 1) * 128], hT[:, kc, t0:t0 + n], start=kc == 0, stop=kc == 7)
                f.cp(cq[:, c, 0:n] if c < 2 else ckv[:, 0:n], ps[:, 0:n], eng="act")
            pa = psp.next()
            pb = psp.next()
            for kc in range(8):
                f.mm(pa[0:96, 0:n], wk[:, kc, 0:96], hT[:, kc, t0:t0 + n], start=kc == 0, stop=kc == 7)
            for kc in range(8):
                f.mm(pb[0:96, 0:n], wk[:, kc, 96:192], hT[:, kc, t0:t0 + n], start=kc == 0, stop=kc == 7)
            ta = tmpa.next()
            tb_ = tmpb.next()
            krb = krp.next()
            f.tt(ta[64:96, 0:n], pa[64:96, 0:n], cosb[64:96, 0:n], ALU.mult)
            f.tt(tb_[64:96, 0:n], pb[64:96, 0:n], sinb[64:96, 0:n], ALU.mult)
            f.tt(krb[64:96, 0:n], ta[64:96, 0:n], tb_[64:96, 0:n], ALU.add, eng="pool")
            cqn = cqnp.next()
            ckvn = ckvnp.next()
            for (nchunk, gains) in ((2, qng), (1, kvng)):
                sq = sqp.next()
                ps = psp.next()
                for c in range(nchunk):
                    sv = cq[:, c, 0:n] if nchunk == 2 else ckv[:, 0:n]
                    f.tt(sq[:, c, 0:n], sv, sv, ALU.mult)
                for c in range(nchunk):
                    f.mm(ps[:, 0:n], ones_f[:], sq[:, c, 0:n], start=c == 0, stop=c == nchunk - 1)
                rs = rsp.next()
                f.act(rs[:, 0:n], ps[:, 0:n], AF.Sqrt, bias=EPS_T[:, 0:1], scale=1.0 / (128 * nchunk))
                f.recip(rs[:, 0:n], rs[:, 0:n])
                for c in range(nchunk):
                    sv = cq[:, c, 0:n] if nchunk == 2 else ckv[:, 0:n]
                    dv = cqn[:, c, 0:n] if nchunk == 2 else ckvn[:, 0:n]
                    gv = gains[:, l, c:c + 1] if nchunk == 2 else gains[:, l:l + 1]
                    f.stt(dv, sv, gv, rs[:, 0:n], ALU.mult, ALU.mult)
            for h in range(8):
                pa = psq.next()
                pb = psq.next()
                for kc in range(2):
                    f.mm(pa[0:96, 0:n], wuq[:, kc, h * 96:(h + 1) * 96], cqn[:, kc, 0:n], start=kc == 0, stop=kc == 1)
                for kc in range(2):
                    f.mm(pb[0:96, 0:n], wuqs[:, kc, h * 96:(h + 1) * 96], cqn[:, kc, 0:n], start=kc == 0, stop=kc == 1)
                q = qst.next()
                ta = tmpa.next()
                tb_ = tmpb.next()
                f.cp(q[0:64, 0:n], pa[0:64, 0:n], eng="act")
                f.tt(ta[64:96, 0:n], pa[64:96, 0:n], cosb[64:96, 0:n], ALU.mult)
                f.tt(tb_[64:96, 0:n], pb[64:96, 0:n], sinb[64:96, 0:n], ALU.mult)
                f.tt(q[64:96, 0:n], ta[64:96, 0:n], tb_[64:96, 0:n], ALU.add, eng="pool")
                f.dma("sp", qT_d[h, :, t0:t0 + n], q[:, 0:n])
                pk = psq.next()
                f.mm(pk[0:64, 0:n], wk_k[:, h * 64:(h + 1) * 64], ckvn[:, 0:n])
                k = kst.next()
                f.cp(k[0:64, 0:n], pk[0:64, 0:n], eng="act")
                f.cp(k[64:96, 0:n], krb[64:96, 0:n], eng="pool")
                f.dma("sp", kT_d[h, :, t0:t0 + n], k[:, 0:n])
            for ti in range(n // 128):
                tt = t0 // 128 + ti
                pv = psq.next()
                f.mm(pv[:, :], ckvn[:, ti * 128:(ti + 1) * 128], wk_v[:, :])
                vs = vst.next()
                f.cp(vs[:, :, 0:64], pv[:, :].re("p (h e) -> p h e", e=64), eng="act")
                f.dma("sp", V_d[tt * 128:(tt + 1) * 128, :], vs[:].re("p h e -> p (h e)"))

        wa = loadw(1952, 32)
        wa2 = f.sbuf("wa2", [16, 512], F32)
        ba = f.sbuf("ba", [1, 512], F32)
        f.dma("sp", wa2[:], I["w_a2"][l])
        f.dma("sp", ba[:], I["b_a"][l])
        aft = Pool([f.sbuf("aft%d" % i, [16, 512], F32) for i in range(2)])
        abt = Pool([f.sbuf("abt%d" % i, [16, 512], F32) for i in range(2)])
        ggp = Pool([f.sbuf("ggs%d" % i, [128, 512], F32) for i in range(2)])
        for bi, (t0, n) in enumerate(TB):
            pa = psp.next()
            pb = psp.next()
            for kc in range(8):
                f.mm(pa[0:16, 0:n], wa[:, kc, 0:16], hT[:, kc, t0:t0 + n], start=kc == 0, stop=kc == 7)
            for kc in range(8):
                f.mm(pb[0:16, 0:n], wa[:, kc, 16:32], hT[:, kc, t0:t0 + n], start=kc == 0, stop=kc == 7)
            af = aft.next()
            ab = abt.next()
            f.cp(af[:, 0:n], pa[0:16, 0:n], eng="act")
            f.cp(ab[:, 0:n], pb[0:16, 0:n], eng="act")
            for ti in range(n // 128):
                pg = psq.next()
                f.mm(pg[:, 0:256], af[:, ti * 128:(ti + 1) * 128], wa2[:, 0:256], start=True, stop=False)
                f.mm(pg[:, 0:256], ones_f[0:1, :], ba[:, 0:256], start=False, stop=True)
                f.mm(pg[:, 256:512], ab[:, ti * 128:(ti + 1) * 128], wa2[:, 256:512], start=True, stop=False)
                f.mm(pg[:, 256:512], ones_f[0:1, :], ba[:, 256:512], start=False, stop=True)
                gs = ggp.next()
                f.act(gs[:], pg[:], AF.Exp, scale=-1.0)
                f.act(gs[:], gs[:], AF.Ln, bias=ONE_T[:, 0:1])
                f.ts(gs[:], gs[:], -1.0 / 16.0, ALU.mult)
                tt = t0 // 128 + ti
                f.dma("sp", gg_d[tt * 128:(tt + 1) * 128, :], gs[:])

        st32 = Pool([f.sbuf("pst32_%d" % i, [128, 512], F32) for i in range(3)])
        st16 = Pool([f.sbuf("pst16_%d" % i, [128, 512], BF16) for i in range(3)])

        def ev_qk(ps, tt):
            s = st32.next()
            f.act(s[:, 0:256], ps[:, 0:256], AF.Copy, scale=0.125)
            f.cp(s[:, 256:512], ps[:, 256:512], eng="dve")
            f.dma("sp", gqk_d[tt * 128:(tt + 1) * 128, :], s[:])

        def ev_to(dst, c0, dt16):
            def ev(ps, tt):
                s = (st16 if dt16 else st32).next()
                f.cp(s[:], ps[:], eng="act" if tt % 2 else "dve")
                f.dma("sp", dst[tt * 128:(tt + 1) * 128, c0:c0 + 512], s[:])
            return ev

        tm(loadw(416, 512), 512, ev_qk)
        tm(loadw(928, 512), 512, ev_to(gv_d, 0, True))
        tm(loadw(1440, 512), 512, ev_to(gr_d, 0, False))
        for sc in range(3):
            w = loadw(1984 + sc * 512, 512)
            for c in range(4):
                ch = sc * 4 + c

                def ev_h(ps, bi, t0, n, ch=ch):
                    s = st16.next()
                    f.cp(s[:, 0:n], ps[:, 0:n], eng="act" if (bi + ch) % 2 else "dve")
                    f.dma("sp", zhyT_d[ch * 128:(ch + 1) * 128, t0:t0 + n], s[:, 0:n])
                fm(w, c * 128, 128, ev_h)

        for sc in range(6):
            w = loadw(3520 + sc * 512, 512)
            for c in range(4):
                ch = sc * 4 + c

                def ev_g(ps, bi, t0, n, ch=ch):
                    s = st16.next()
                    f.act(s[:, 0:n], ps[:, 0:n], AF.Sigmoid)
                    f.dma("sp", gatesT_d[ch * 128:(ch + 1) * 128, t0:t0 + n], s[:, 0:n])
                fm(w, c * 128, 128, ev_g)
        f.release(m)

    def phase_att(l):
        m = f.mark()
        Vaug = f.sbuf("Vaug", [128, NT, 520], BF16)
        f.dma("sp", Vaug[:], V_d[:].re("(t p) e -> p t e", p=128))
        qp = Pool([f.sbuf("aq%d" % i, [96, T], BF16) for i in range(2)])
        kp = Pool([f.sbuf("ak%d" % i, [96, T], BF16) for i in range(2)])
        pp = Pool([f.sbuf("ap%d" % i, [128, 2, 512], BF16) for i in range(3)])
        sps = Pool([f.psum("as%d" % i, [128, 2, 512], F32) for i in range(2)])
        ops = Pool([f.psum("ao%d" % i, [128, 512], F32) for i in range(2)])
        bps = f.psum("abc", [128, 512], F32)
        rec = Pool([f.sbuf("arec%d" % i, [65, 512], F32) for i in range(2)])
        osb = Pool([f.sbuf("aosb%d" % i, [64, 512], F32) for i in range(2)])
        yst = Pool([f.sbuf("ayst%d" % i, [64, 512], BF16) for i in range(3)])
        for h in range(8):
            q = qp.next()
            k = kp.next()
            f.dma("sp", q[:], qT_d[h])
            f.dma("sp", k[:], kT_d[h])
            for bi, (t0, n) in enumerate(TB):
                if bi == 0 and l == DEPTH - 1:
                    continue
                nk = 2 if bi == 0 else NT
                npair = nk // 2
                o = ops.next()
                pend = None
                for jp in range(npair + 1):
                    p = None
                    if jp < npair:
                        s = sps.next()
                        for u_ in range(2):
                            j = jp * 2 + u_
                            f.mm(s[:, u_, 0:n], k[:, j * 128:(j + 1) * 128], q[:, t0:t0 + n])
                        p = pp.next()
                        f.act(p[:, :, 0:n], s[:, :, 0:n], AF.Exp, scale=MLA_SCALE)
                    if pend is not None:
                        jq, pv = pend
                        for u_ in range(2):
                            jj = jq * 2 + u_
                            f.mm(o[0:65, 0:n], Vaug[:, jj, h * 65:(h + 1) * 65], pv[:, u_, 0:n], start=jj == 0, stop=jj == nk - 1)
                    pend = (jp, p) if jp < npair else None
                r = rec.next()
                f.recip(r[64:65, 0:n], o[64:65, 0:n])
                f.mm(bps[0:64, 0:n], ones_f[64:65, 0:64], r[64:65, 0:n])
                os_ = osb.next()
                f.cp(os_[:, 0:n], o[0:64, 0:n], eng="act")
                y = yst.next()
                f.tt(y[:, 0:n], os_[:, 0:n], bps[0:64, 0:n], ALU.mult)
                f.dma("sp", ymlaT_d[h * 64:(h + 1) * 64, t0:t0 + n], y[:, 0:n])
        f.release(m)

    def phase_gla(l):
        m = f.mark()
        Mm = f.sbuf("glaM", [128, 2, 128], F32)
        mask = f.sbuf("glamask", [128, 2, 128], F32)
        ind = f.sbuf("glaind", [128, 2], F32)
        gon = f.sbuf("gon", [128, 128], F32)
        f.dma("sp", Mm[:], I["gla_M"][:])
        f.dma("sp", mask[:], I["gla_mask"][:])
        f.dma("sp", ind[:], I["gla_ind"][:])
        f.dma("sp", gon[:], I["gla_on"][l])
        mA = f.mark()
        ptr = f.psum("gptr", [128, 8, 128], BF16)
        pU = f.psum("gpU", [128, 4, 128], F32)

        def chain(d):
            S = f.sbuf("glaS%d" % d, [64, 4, 128], F32)
            P2 = lambda nm, shp, dt, k=2: Pool([f.sbuf("%s%d_%d" % (nm, d, i), shp, dt) for i in range(k)])
            qkp, gp, vp = P2("gqk", [128, 512], F32, 3), P2("gg", [128, 256], F32, 3), P2("gv", [128, 512], BF16, 3)
            ePp, eNp = P2("geP", [128, 256], F32), P2("geN", [128, 256], F32)
            qpp, kpp = P2("gqp", [128, 256], BF16), P2("gkp", [128, 256], BF16)
            decp, qkTp = P2("gdec", [64, 4, 2], F32), P2("gqkT", [64, 8, 128], BF16)
            Amp, Smidp, Smbp = P2("gAm", [128, 4, 128], BF16), P2("gSm", [64, 4, 128], F32), P2("gSb", [64, 4, 128], BF16)
            osp = P2("gos", [128, 4, 128], F32)
            pE = f.psum("gpE%d" % d, [128, 512], F32)
            pA = f.psum("gpA%d" % d, [128, 4, 128], F32)
            po = f.psum("gpo%d" % d, [128, 4, 128], F32)
            order = list(range(NT)) if d == 0 else [1, 0] + list(range(NT - 1, 1, -1))
            f.memset(S[:], 0.0)
            yield
            loaded = {}

            def load(tt):
                r0 = tt * 128
                qk, g, v = qkp.next(), gp.next(), vp.next()
                f.dma("sp", qk[:], gqk_d[r0:r0 + 128, :])
                f.dma("sp", g[:], gg_d[r0:r0 + 128, d * 256:(d + 1) * 256])
                f.dma("sp", v[:], gv_d[r0:r0 + 128, :])
                loaded[tt] = (qk, g, v)
            load(order[0])
            for oi, tt in enumerate(order):
                r0 = tt * 128
                if oi + 1 < len(order):
                    load(order[oi + 1])
                qk, g, v = loaded.pop(tt)
                f.mm(pE[:, 0:256], Mm[:, d, :], g[:])
                for h in range(4):
                    f.mm(pE[0:64, 256 + h * 2:258 + h * 2], g[:, h * 64:(h + 1) * 64], ind[:])
                yield
                eP, eN, dec = ePp.next(), eNp.next(), decp.next()
                f.act(eP[:], pE[:, 0:256], AF.Exp)
                f.act(eN[:], pE[:, 0:256], AF.Exp, scale=-1.0)
                f.act(dec[:].re("p h s -> p (h s)"), pE[0:64, 256:264], AF.Exp)
                dmid = dec[:, :, d:d + 1]
                dend = dec[:, :, 1 - d:2 - d]
                yield
                qp_, kp_ = qpp.next(), kpp.next()
                f.tt(qp_[:], qk[:, 0:256], eP[:], ALU.mult)
                f.tt(kp_[:], qk[:, 256:512], eN[:], ALU.mult, eng="pool")
                Smid = Smidp.next()
                f.tt(Smid[:], S[:], dmid.bc([64, 4, 128]), ALU.mult)
                Smb = Smbp.next()
                f.cp(Smb[:], Smid[:], eng="act")
                yield
                for h in range(4):
                    f.tr(ptr[0:64, h, :], qp_[:, h * 64:(h + 1) * 64], ident[:])
                for h in range(4):
                    f.tr(ptr[0:64, 4 + h, :], kp_[:, h * 64:(h + 1) * 64], ident[:])
                for h in range(4):
                    f.mm(pU[0:64, h, :], kp_[:, h * 64:(h + 1) * 64], v[:, h * 128:(h + 1) * 128])
                qkT = qkTp.next()
                f.cp(qkT[:], ptr[0:64], eng="act")
                f.tt(S[:], Smid[:], pU[0:64], ALU.add)
                f.tt(S[:], S[:], dend.bc([64, 4, 128]), ALU.mult)
                yield
                for h in range(4):
                    f.mm(pA[:, h, :], qkT[:, 4 + h, :], qkT[:, h, :])
                yield
                Am = Amp.next()
                f.tt(Am[:], pA[:], mask[:, d:d + 1, :].bc([128, 4, 128]), ALU.mult)
                yield
                for h in range(4):
                    f.mm(po[:, h, :], qkT[:, h, :], Smb[:, h, :], start=True, stop=False)
                    f.mm(po[:, h, :], Am[:, h, :], v[:, h * 128:(h + 1) * 128], start=False, stop=True)
                yield
                os_ = osp.next()
                f.cp(os_[:], po[:], eng="act")
                f.dma("sp", og_d[d, r0:r0 + 128, :], os_[:].re("p h v -> p (h v)"))
                yield

        gens = [chain(0), chain(1)]
        live = list(gens)
        while live:
            for g_ in list(live):
                try:
                    next(g_)
                except StopIteration:
                    live.remove(g_)
        f.release(mA)

        ofp = Pool([f.sbuf("gof%d" % i, [128, 4, 128], F32) for i in range(2)])
        obp = Pool([f.sbuf("gob%d" % i, [128, 4, 128], F32) for i in range(2)])
        sqp = Pool([f.sbuf("gsq%d" % i, [128, 4, 128], F32) for i in range(2)])
        stp = Pool([f.sbuf("gst%d" % i, [128, 8], F32) for i in range(2)])
        rp = Pool([f.sbuf("gr%d" % i, [128, 512], F32) for i in range(2)])
        yp = Pool([f.sbuf("gy%d" % i, [128, 512], BF16) for i in range(2)])
        yTp = Pool([f.sbuf("gyT%d" % i, [128, 4, 128], BF16) for i in range(2)])
        pTp = Pool([f.psum("gpT%d" % i, [128, 8, 128], BF16) for i in range(2)])
        for tt in range(NT):
            if tt < 2 and l == DEPTH - 1:
                continue
            r0 = tt * 128
            of_, ob_, rr = ofp.next(), obp.next(), rp.next()
            f.dma("sp", of_[:].re("p h v -> p (h v)"), og_d[0, r0:r0 + 128, :])
            f.dma("sp", ob_[:].re("p h v -> p (h v)"), og_d[1, r0:r0 + 128, :])
            f.dma("sp", rr[:], gr_d[r0:r0 + 128, :])
            f.tt(of_[:], of_[:], ob_[:], ALU.add)
            sq = sqp.next()
            f.tt(sq[:], of_[:], of_[:], ALU.mult, eng="pool")
            st = stp.next()
            f.op("dve", lambda e: e.tensor_reduce(out=st[:, 0:4].ap, in_=sq[:].ap, axis=mybir.AxisListType.X, op=ALU.add),
                 reads=[sq], writes=[st])
            f.act(st[:, 4:8], st[:, 0:4], AF.Sqrt, bias=EPS_T[:, 0:1], scale=1.0 / 128)
            f.recip(st[:, 4:8], st[:, 4:8])
            f.tt(of_[:], of_[:], st[:, 4:8].un(2).bc([128, 4, 128]), ALU.mult)
            f.tt(of_[:], of_[:], gon[:].un(1).bc([128, 4, 128]), ALU.mult, eng="pool")
            f.act(rr[:], rr[:], AF.Silu)
            y = yp.next()
            f.tt(y[:], of_[:].re("p h v -> p (h v)"), rr[:], ALU.mult)
            pT = pTp.next()
            for c in range(4):
                f.tr(pT[:, c, :], y[:, c * 128:(c + 1) * 128], ident[:])
            yT = yTp.next()
            f.cp(yT[:], pT[:, 0:4, :], eng="act")
            f.dma("sp", yglaT_d[:, r0:r0 + 128].re("(c p) t -> p c t", p=128), yT[:])
        f.release(m)

    def phase_merge(l):
        m = f.mark()
        wo = [f.sbuf("wo%d" % i, [128, 4, D], BF16) for i in range(3)]
        for i, nm in enumerate(("w_o_mla", "w_o_gla", "w_o_hy")):
            f.dma("pool", wo[i][:], I[nm][l].re("(kc p) n -> p kc n", p=128))
        wout = f.sbuf("wout", [128, 8, D], BF16)
        for c in range(2):
            f.dma("pool", wout[:, :, c * 512:(c + 1) * 512], I["w_out"][l].re("(kc p) n -> p kc n", p=128)[:, :, c * 512:(c + 1) * 512])
        gx = f.sbuf("gx", [128, 2, D], F32)
        f.dma("sp", gx[:, 0, :], modrow[0, l, 0].pb(128))
        f.dma("sp", gx[:, 1, :], modrow[1, l, 0].pb(128))
        ybp = Pool([f.sbuf("mby%d" % i, [128, 3, 4, 512], BF16) for i in range(2)])
        gtp = Pool([f.sbuf("mgt%d" % i, [128, 3, 512], BF16) for i in range(3)])
        mTp = Pool([f.sbuf("mT%d" % i, [128, 8, 512], BF16) for i in range(2)])
        accp = Pool([f.sbuf("macc%d" % i, [128, 512], F32) for i in range(2)])
        tmpp = Pool([f.sbuf("mtmp%d" % i, [128, 512], F32) for i in range(3)])
        xp = Pool([f.sbuf("mx%d" % i, [128, D], F32) for i in range(3)])
        ps3 = [Pool([f.psum("mps%d_%d" % (i, j), [128, 512], F32) for j in range(2)]) for i in range(3)]
        pso = Pool([f.psum("mpo%d" % i, [128, 512], F32) for i in range(2)])
        gview = gatesT_d[:].re("(b c p) t -> p b c t", b=3, c=8, p=128)
        for bi, (t0, n) in enumerate(TB):
            if bi == 0 and l == DEPTH - 1:
                continue
            yb = ybp.next()
            for i, srcT in enumerate((ymlaT_d, yglaT_d, yhyT_d)):
                f.dma("sp", yb[:, i, :, 0:n], srcT[:, t0:t0 + n].re("(kc p) t -> p kc t", p=128))
            mT = mTp.next()
            for oc in range(8):
                gt = gtp.next()
                f.dma("sp", gt[:, :, 0:n], gview[:, :, oc, t0:t0 + n])
                pss = []
                for i in range(3):
                    ps = ps3[i].next()
                    for kc in range(4):
                        f.mm(ps[:, 0:n], wo[i][:, kc, oc * 128:(oc + 1) * 128], yb[:, i, kc, 0:n], start=kc == 0, stop=kc == 3)
                    pss.append(ps)
                acc = accp.next()
                t1 = tmpp.next()
                t2 = tmpp.next()
                f.tt(acc[:, 0:n], pss[0][:, 0:n], gt[:, 0, 0:n], ALU.mult)
                f.tt(t1[:, 0:n], pss[1][:, 0:n], gt[:, 1, 0:n], ALU.mult)
                f.tt(t2[:, 0:n], pss[2][:, 0:n], gt[:, 2, 0:n], ALU.mult)
                f.tt(acc[:, 0:n], acc[:, 0:n], t1[:, 0:n], ALU.add, eng="pool")
                f.tt(mT[:, oc, 0:n], acc[:, 0:n], t2[:, 0:n], ALU.add, eng="pool")
            s = 1 if bi == 0 else 0
            for ti in range(n // 128):
                tt = t0 // 128 + ti
                x = xp.next()
                f.dma("sp", x[:], xres_t[tt][:])
                for half in range(2):
                    ps = pso.next()
                    for kc in range(8):
                        f.mm(ps[:, :], mT[:, kc, ti * 128:(ti + 1) * 128], wout[:, kc, half * 512:(half + 1) * 512],
                             start=kc == 0, stop=kc == 7)
                    t1 = tmpp.next()
                    f.tt(t1[:], ps[:], gx[:, s, half * 512:(half + 1) * 512], ALU.mult)
                    f.tt(x[:, half * 512:(half + 1) * 512], x[:, half * 512:(half + 1) * 512], t1[:], ALU.add, eng="pool")
                f.dma("sp", xres_t[tt][:], x[:])
        f.release(m)

    def phase_ffn(l):
        m = f.mark()
        w1 = f.sbuf("ffw1", [128, 8, 4096], BF16)
        w2 = f.sbuf("ffw2", [128, 32, D], BF16)
        w1v = I["ff_w1"][l].re("(kc p) n -> p kc n", p=128)
        w2v = I["ff_w2"][l].re("(kc p) n -> p kc n", p=128)
        for c in range(8):
            f.dma("pool", w1[:, :, c * 512:(c + 1) * 512], w1v[:, :, c * 512:(c + 1) * 512])
        for c in range(8):
            f.dma("pool", w2[:, c * 4:(c + 1) * 4, :], w2v[:, c * 4:(c + 1) * 4, :])
        gx = f.sbuf("fgx", [128, 2, D], F32)
        f.dma("sp", gx[:, 0, :], modrow[0, l, 1].pb(128))
        f.dma("sp", gx[:, 1, :], modrow[1, l, 1].pb(128))
        nctx = NormCtx()
        hTp = Pool([f.sbuf("fhT%d" % i, [128, 8, 256], BF16) for i in range(2)])
        aTp = Pool([f.sbuf("faT%d" % i, [128, 32, 256], BF16) for i in range(1)])
        rp = Pool([f.sbuf("fr%d" % i, [128, 256], F32) for i in range(2)])
        tmpp = Pool([f.sbuf("ftmp%d" % i, [128, 512], F32) for i in range(2)])
        xp = Pool([f.sbuf("fx%d" % i, [128, D], F32) for i in range(2)])
        psA = Pool([f.psum("fpa%d" % i, [128, 512], F32) for i in range(3)])
        pso = Pool([f.psum("fpo%d" % i, [128, 512], F32) for i in range(3)])
        for blk in range(T // 256):
            if blk == 0 and l == DEPTH - 1:
                continue
            s = 1 if blk == 0 else 0
            hTb = hTp.next()
            for ti in range(2):
                nctx.emit(l, 1, hTb, blk * 2 + ti, ti * 128)
            aT = aTp.next()
            for fc in range(32):
                ps = psA.next()
                for kc in range(8):
                    f.mm(ps[:, 0:256], w1[:, kc, fc * 128:(fc + 1) * 128], hTb[:, kc, :], start=kc == 0, stop=kc == 7)
                r = rp.next()
                f.act(r[:], ps[:, 0:256], AF.Relu)
                f.tt(aT[:, fc, :], r[:], r[:], ALU.mult, eng="pool" if fc % 2 else "dve")
            for ti in range(2):
                tt = blk * 2 + ti
                x = xp.next()
                f.dma("sp", x[:], xres_t[tt][:])
                for half in range(2):
                    ps = pso.next()
                    for fc in range(32):
                        f.mm(ps[:, :], aT[:, fc, ti * 128:(ti + 1) * 128], w2[:, fc, half * 512:(half + 1) * 512],
                             start=fc == 0, stop=fc == 31)
                    t1 = tmpp.next()
                    f.tt(t1[:], ps[:], gx[:, s, half * 512:(half + 1) * 512], ALU.mult)
                    f.tt(x[:, half * 512:(half + 1) * 512], x[:, half * 512:(half + 1) * 512], t1[:], ALU.add, eng="pool")
                f.dma("sp", xres_t[tt][:], x[:])
        f.release(m)

    def phase_final():
        m = f.mark()
        fg = f.sbuf("fing", [128, D], F32)
        f.dma("sp", fg[:], I["fin_g"][:])
        xp = Pool([f.sbuf("zx%d" % i, [128, D], F32) for i in range(3)])
        jp = Pool([f.sbuf("zj%d" % i, [128, D], F32) for i in range(2)])
        stp = Pool([f.sbuf("zst%d" % i, [128, 4], F32) for i in range(3)])
        for tt in range(2, NT):
            x = xp.next()
            j = jp.next()
            st = stp.next()
            f.dma("sp", x[:], xres_t[tt][:])
            f.act(j[:], x[:], AF.Square, accum=st[:, 0:1])
            f.act(st[:, 1:2], st[:, 0:1], AF.Sqrt, bias=EPS_T[:, 0:1], scale=1.0 / D)
            f.recip(st[:, 2:3], st[:, 1:2])
            f.act(j[:], x[:], AF.Identity, scale=st[:, 2:3])
            f.tt(x[:], j[:], fg[:], ALU.mult)
            f.dma("sp", out_y[(tt - 2) * 128:(tt - 1) * 128, :], x[:])
        f.release(m)

    def phase_hy(l, ctx_seg):
        m = f.mark()
        r0, nrow = (0, LC) if ctx_seg else (LC, L)
        na = nrow // 64
        NA = 2 * na
        NF = NA * 64
        NFA = NA // 2 + 1
        sfx = "_c" if ctx_seg else ""

        mA = f.mark()
        swt = f.sbuf("hsw", [128, 12, 4], F32)
        f.dma("sp", swt[:], I["hy_swb"][l])
        zp = Pool([f.sbuf("hz%d" % i, [128, L], BF16) for i in range(2)])
        accp = Pool([f.sbuf("hacc%d" % i, [128, L], F32) for i in range(2)])
        op_ = Pool([f.sbuf("hso%d" % i, [128, L], BF16) for i in range(2)])
        for ch in range(12):
            z = zp.next()
            acc = accp.next()
            o = op_.next()
            f.dma("sp", z[:, 0:nrow], zhyT_d[ch * 128:(ch + 1) * 128, r0:r0 + nrow])
            f.act(acc[:, 0:nrow], z[:, 0:nrow], AF.Identity, bias=swt[:, ch, 3:4], scale=swt[:, ch, 1:2])
            f.stt(acc[:, 1:nrow], z[:, 0:nrow - 1], swt[:, ch, 0:1], acc[:, 1:nrow], ALU.mult, ALU.add)
            f.stt(acc[:, 0:nrow - 1], z[:, 1:nrow], swt[:, ch, 2:3], acc[:, 0:nrow - 1], ALU.mult, ALU.add)
            f.cp(o[:, 0:nrow], acc[:, 0:nrow], eng="act")
            f.dma("sp", scT_d[ch * 128:(ch + 1) * 128, r0:r0 + nrow], o[:, 0:nrow])
        f.release(mA)

        F1 = f.sbuf("hF1", [NA, 3 * NFA], BF16)
        E2r = f.sbuf("hE2r", [128, NFA, 128], BF16)
        E2i = f.sbuf("hE2i", [128, NFA, 128], BF16)
        f.dma("sp", F1[:], I["hy_F1" + sfx][:])
        f.dma("sp", E2r[:], I["hy_E2r" + sfx][:])
        f.dma("sp", E2i[:], I["hy_E2i" + sfx][:])
        Yp = Pool([f.sbuf("hY%d" % i, [128, 32, 3 * NFA], BF16) for i in range(1)])
        psY = Pool([f.psum("hpY%d" % i, [128, 512], F32) for i in range(2)])
        psX = Pool([f.psum("hpX%d" % i, [128, 2, 8, 32], F32) for i in range(2)])
        pst = Pool([f.psum("hpt%d" % i, [128, 8, 64], BF16) for i in range(1)])

        def spectrum(ut, Kp, consume, ypool=None):
            Y = (ypool or Yp).next()
            for q in range(32):
                ps = psY.next()
                f.mm(ps[:, 0:3 * NFA], ut[0:Kp, q, :], F1[0:Kp, :])
                f.cp(Y[:, q, :], ps[:, 0:3 * NFA], eng="act" if q % 2 else "dve")
            for fa0 in range(0, NFA, 8):
                nfa = min(8, NFA - fa0)
                px = psX.next()
                pr = px[:, 0]
                pi = px[:, 1]
                for i in range(nfa):
                    fa = fa0 + i
                    f.mm(pr[:, i, :], E2r[:, fa, :], Y[:, :, fa], start=True, stop=False)
                    f.mm(pr[:, i, :], E2i[:, fa, :], Y[:, :, 2 * NFA + fa], start=False, stop=True)
                for i in range(nfa):
                    fa = fa0 + i
                    f.mm(pi[:, i, :], E2i[:, fa, :], Y[:, :, fa], start=True, stop=False)
                    f.mm(pi[:, i, :], E2r[:, fa, :], Y[:, :, NFA + fa], start=False, stop=True)
                consume(fa0, nfa, pr, pi)

        mB = f.mark()
        hd2 = f.sbuf("hhd2", [64, NF], F32)
        fw3 = f.sbuf("hfw3", [64, 2048], F32)
        fb3T = f.sbuf("hfb3T", [128, 16], F32)
        nd = f.sbuf("hnd", [128, 4], F32)
        hbT = f.sbuf("hhbT", [128, 2, 4], F32)
        f.dma("sp", fw3[:], I["hy_f_w3"][l])
        f.dma("sp", fb3T[:], I["hy_fb3T"][l])
        f.dma("sp", nd[:], I["hy_negdelta"][:])
        f.dma("sp", hbT[:], I["hy_biasT"][l])
        mB1 = f.mark()
        featT = f.sbuf("hfeat", [33, NF], F32)
        f.dma("sp", featT[:], I["hy_featT" + sfx][:])
        fw1 = f.sbuf("hfw1", [33, 64], F32)
        fw2 = f.sbuf("hfw2", [64, 64], F32)
        fb12 = f.sbuf("hfb12", [64, 2], F32)
        f.dma("sp", fw1[:], I["hy_f_w1"][l])
        f.dma("sp", fw2[:], I["hy_f_w2"][l])
        f.dma("sp", fb12[:], I["hy_fb12"][l])
        hd1p = Pool([f.sbuf("hhd1_%d" % i, [64, 512], F32) for i in range(2)])
        ap_ = Pool([f.sbuf("ha%d" % i, [64, 512], F32) for i in range(2)])
        m1p = Pool([f.sbuf("hm1_%d" % i, [64, 512], F32) for i in range(2)])
        m2p = Pool([f.sbuf("hm2_%d" % i, [64, 512], F32) for i in range(2)])
        psm = Pool([f.psum("hpm%d" % i, [128, 512], F32) for i in range(3)])
        nblk = NF // 512

        def sin_wrap(dst, ps, bias):
            a = ap_.next()
            m1 = m1p.next()
            m2 = m2p.next()
            f.act(a[:], ps[0:64, :], AF.Identity, bias=bias)
            f.ts(m1[:], a[:], math.pi, ALU.is_gt, 2 * math.pi, ALU.mult)
            f.ts(m2[:], a[:], -math.pi, ALU.is_lt, 2 * math.pi, ALU.mult)
            f.tt(a[:], a[:], m1[:], ALU.subtract)
            f.tt(a[:], a[:], m2[:], ALU.add)
            f.act(dst, a[:], AF.Sin)

        for blk in range(nblk):
            ps = psm.next()
            f.mm(ps[0:64, :], fw1[:], featT[:, blk * 512:(blk + 1) * 512])
            hd1 = hd1p.next()
            sin_wrap(hd1[:], ps, fb12[:, 0:1])
            ps2 = psm.next()
            f.mm(ps2[0:64, :], fw2[:], hd1[:])
            sin_wrap(hd2[:, blk * 512:(blk + 1) * 512], ps2, fb12[:, 1:2])
        f.release(mB1)
        tn2 = f.sbuf("htn2", [128, NF], F32)
        f.dma("sp", tn2[:], I["hy_tn2" + sfx][0].pb(128))
        kT = f.sbuf("hkT", [128, NF], F32)
        kTb = f.sbuf("hkTb", [128, NF], BF16)
        kbp = Pool([f.sbuf("hkb%d" % i, [128, 512], F32) for i in range(2)])
        wbp = Pool([f.sbuf("hwb%d" % i, [128, 512], F32) for i in range(2)])
        jk = f.sbuf("hjk", [128, 512], BF16)
        asum = f.sbuf("hasum", [128, 20], F32)
        kup = Pool([f.sbuf("hku%d" % i, [128, 32, 128], BF16) for i in range(1)])
        Hst = Pool([f.sbuf("hHst%d" % i, [128, 2, NFA, 32], BF16) for i in range(1)])
        psm = Pool([f.psum("hpm2_%d" % i, [128, 512], F32) for i in range(2)])
        bs = min(512, NF // 2)
        nb2 = NF // bs
        for cc in range(4):
            for n_ in range(2):
                for blk in range(nb2):
                    dr = 0 if blk < nb2 // 2 else 1
                    col = (dr * 2 + n_) * 4 + cc
                    cs_ = slice(blk * bs, (blk + 1) * bs)
                    ps = psm.next()
                    f.mm(ps[:, 0:bs], fw3[:, col * 128:(col + 1) * 128], hd2[:, cs_])
                    kb = kbp.next()
                    wb = wbp.next()
                    f.act(wb[:, 0:bs], tn2[:, cs_], AF.Exp, scale=nd[:, cc:cc + 1])
                    f.stt(kT[:, cs_], ps[:, 0:bs], fb3T[:, col:col + 1], wb[:, 0:bs], ALU.add, ALU.mult)
                    f.act(jk[:, 0:bs], kT[:, cs_], AF.Abs, accum=asum[:, blk:blk + 1])
                f.op("dve", lambda e: e.tensor_reduce(out=asum[:, 16:17].ap, in_=asum[:, 0:nb2].ap, axis=mybir.AxisListType.X, op=ALU.add),
                     reads=[asum], writes=[asum])
                f.recip(asum[:, 17:18], asum[:, 16:17])
                f.act(kTb[:], kT[:], AF.Identity, scale=asum[:, 17:18])
                f.ts(kTb[:, 0:1], kT[:, 0:1], asum[:, 17:18], ALU.mult, hbT[:, n_, cc:cc + 1], ALU.add)
                for gg in range(2):
                    g = cc * 2 + gg
                    kv = kTb[gg * 64:(gg + 1) * 64, :].re("c (a b) -> c b a", b=64)
                    ut = kup.next()
                    for b0 in range(0, 64, 8):
                        pt = pst.next()
                        for i in range(8):
                            f.tr(pt[0:NA, i, :], kv[:, b0 + i, :], ident[gg * 64:(gg + 1) * 64, gg * 64:(gg + 1) * 64])
                        f.cp(ut[0:NA].re("a q (b cp) -> a b q cp", cp=2)[:, b0:b0 + 8], pt[0:NA, :, :].re("a b (q cp) -> a b q cp", cp=2),
                             eng="act" if (b0 // 8) % 2 else "dve")
                    hs = Hst.next()

                    def cons(fa0, nfa, pr, pi, hs=hs):
                        f.cp(hs[:, 0, fa0:fa0 + nfa, :], pr[:, 0:nfa, :], eng="act")
                        f.cp(hs[:, 1, fa0:fa0 + nfa, :], pi[:, 0:nfa, :], eng="dve")
                    spectrum(ut, NA, cons)
                    f.dma("sp", H_d[n_, g, :, 0:2 * NFA * 32], hs[:].re("p r f q -> p (r f q)"))
        f.release(mB)

        CA = f.sbuf("hCA", [128, 3, 128], BF16)
        DBr = f.sbuf("hDBr", [NFA, 64, na], BF16)
        DBni = f.sbuf("hDBni", [NFA, 64, na], BF16)
        f.dma("sp", CA[:], I["hy_CA"][:])
        f.dma("sp", DBr[:], I["hy_DBr" + sfx][:])
        f.dma("sp", DBni[:], I["hy_DBni" + sfx][:])
        YpD = Pool([f.sbuf("hYD%d" % i, [128, 32, 3 * NFA], BF16) for i in range(1)] + Yp.bufs)
        uTp = Pool([f.sbuf("huT%d" % i, [64, L], BF16) for i in range(1)])
        gTp = Pool([f.sbuf("hgT%d" % i, [64, L], BF16) for i in range(2)])
        yTp = Pool([f.sbuf("hyT%d" % i, [64, L], BF16) for i in range(1)])
        up = Pool([f.sbuf("hu%d" % i, [64, 32, 128], BF16) for i in range(2)])
        Hp = Pool([f.sbuf("hH%d" % i, [128, 2, NFA, 32], BF16) for i in range(2)])
        Pp = Pool([f.sbuf("hP%d" % i, [128, 2, 32, NFA], BF16) for i in range(2)])
        Z0p = Pool([f.sbuf("hZ0_%d" % i, [NFA, 2, 64, 64], BF16) for i in range(1)])
        tp = Pool([f.sbuf("ht%d" % i, [128, 8, 32], F32) for i in range(6)])
        psZ = Pool([f.psum("hpZ%d" % i, [128, 4, 128], F32) for i in range(2)])
        psO = Pool([f.psum("hpO%d" % i, [128, 8, 64], F32) for i in range(1)])
        for n_ in range(2):
            srcT = scT_d[1024:1536] if n_ == 0 else y1T_d
            gateT = scT_d[0:512] if n_ == 0 else scT_d[512:1024]
            dstT = y1T_d if n_ == 0 else yhyT_d
            for g in range(8):
                uT = uTp.next()
                gT = gTp.next()
                f.dma("sp", uT[:, 0:nrow], srcT[g * 64:(g + 1) * 64, r0:r0 + nrow])
                f.dma("sp", gT[:, 0:nrow], gateT[g * 64:(g + 1) * 64, r0:r0 + nrow])
                H = Hp.next()
                f.dma("sp", H[:].re("p r f q -> p (r f q)"), H_d[n_, g, :, 0:2 * NFA * 32])
                u = up.next()
                uv = uT[:, 0:nrow].re("c (a b) -> c b a", b=64)
                for b0 in range(0, 64, 8):
                    pt = pst.next()
                    for i in range(8):
                        f.tr(pt[0:na, i, :], uv[:, b0 + i, :], ident[0:64, 0:64])
                    f.cp(u[0:na].re("a q (b cp) -> a b q cp", cp=2)[:, b0:b0 + 8], pt[0:na, :, :].re("a b (q cp) -> a b q cp", cp=2),
                         eng="act" if (b0 // 8) % 2 else "dve")
                P = Pp.next()

                def cons(fa0, nfa, pr, pi, H=H, P=P):
                    t1, t2, t3, t4 = tp.next(), tp.next(), tp.next(), tp.next()
                    f.tt(t1[:, 0:nfa], pr[:, 0:nfa, :], H[:, 0, fa0:fa0 + nfa, :], ALU.mult)
                    f.tt(t2[:, 0:nfa], pi[:, 0:nfa, :], H[:, 1, fa0:fa0 + nfa, :], ALU.mult)
                    f.tt(t3[:, 0:nfa], pr[:, 0:nfa, :], H[:, 1, fa0:fa0 + nfa, :], ALU.mult)
                    f.tt(t4[:, 0:nfa], pi[:, 0:nfa, :], H[:, 0, fa0:fa0 + nfa, :], ALU.mult)
                    f.tt(P[:, 0, :, fa0:fa0 + nfa].re("p q f -> p f q"), t1[:, 0:nfa], t2[:, 0:nfa], ALU.subtract, eng="pool")
                    f.tt(P[:, 1, :, fa0:fa0 + nfa].re("p q f -> p f q"), t3[:, 0:nfa], t4[:, 0:nfa], ALU.add, eng="pool")
                spectrum(u, na, cons, YpD)
                Z0 = Z0p.next()
                for q0 in range(0, 32, 4):
                    zr = psZ.next()
                    zi = psZ.next()
                    for i in range(4):
                        q = q0 + i
                        f.mm(zr[0:NFA, i, :], P[:, 0, q, :], CA[:, 0, :], start=True, stop=False)
                        f.mm(zr[0:NFA, i, :], P[:, 1, q, :], CA[:, 2, :], start=False, stop=True)
                    for i in range(4):
                        q = q0 + i
                        f.mm(zi[0:NFA, i, :], P[:, 0, q, :], CA[:, 1, :], start=True, stop=False)
                        f.mm(zi[0:NFA, i, :], P[:, 1, q, :], CA[:, 0, :], start=False, stop=True)
                    f.cp(Z0[:, 0].re("f b (q cp) -> f q b cp", cp=2)[:, q0:q0 + 4], zr[0:NFA, :, :].re("f q (b cp) -> f q b cp", cp=2), eng="act")
                    f.cp(Z0[:, 1].re("f b (q cp) -> f q b cp", cp=2)[:, q0:q0 + 4], zi[0:NFA, :, :].re("f q (b cp) -> f q b cp", cp=2), eng="dve")
                yT = yTp.next()
                yv = yT[:, 0:nrow].re("c (a b) -> c a b", b=64)
                gv = gT[:, 0:nrow].re("c (a b) -> c a b", b=64)
                for b0 in range(0, 64, 8):
                    po_ = psO.next()
                    for i in range(8):
                        b = b0 + i
                        f.mm(po_[0:64, i, 0:na], Z0[:, 0, b, :], DBr[:, b, :], start=True, stop=False)
                        f.mm(po_[0:64, i, 0:na], Z0[:, 1, b, :], DBni[:, b, :], start=False, stop=True)
                    f.tt(yv[:, :, b0:b0 + 8], po_[0:64, :, 0:na].re("c b a -> c a b"), gv[:, :, b0:b0 + 8], ALU.mult)
                f.dma("sp", dstT[g * 64:(g + 1) * 64, r0:r0 + nrow], yT[:, 0:nrow])
            f.barrier()
        f.release(m)

    ONE_T = f.sbuf("one_t", [128, 1], F32)
    f.memset(ONE_T[:], 1.0)

    phase_mod()
    done = False
    if "only_hy" in dbg:
        phase_hy(0, False)
        phase_hy(0, True)
        f.barrier()
        f.barrier(["sp"])
        f.release(0)
        return nc
    for l in range(DEPTH):
        if "from_merge" not in dbg:
            mk = f.mark()
            hT = f.sbuf("hT", [128, 8, T], BF16)
            norm_tiles(l, 0, hT, range(NT))
            if hT_d is not None and l == 0:
                for kc in range(8):
                    f.dma("sp", hT_d[kc * 128:(kc + 1) * 128, :], hT[:, kc, :])
            if stop_after == "norm":
                f.release(mk)
                break
            phase_proj(l, hT)
            f.release(mk)
            if stop_after == "proj":
                break
            if "skip_att" not in dbg:
                phase_att(l)
            if stop_after == "att":
                break
            if "skip_gla" not in dbg:
                phase_gla(l)
            if stop_after == "gla":
                break
            if "skip_hy" not in dbg:
                phase_hy(l, False)
                if l < DEPTH - 1:
                    phase_hy(l, True)
            if stop_after == "hy":
                break
        phase_merge(l)
        if stop_after == "merge":
            break
        phase_ffn(l)
        if stop_after == "ffn":
            break
    else:
        done = True
    f.barrier()
    if done:
        phase_final()
    f.barrier(["sp"])
    f.release(0)
    return nc


def _fm(v, chunks):
    return np.ascontiguousarray(np.asarray(v, np.float32).reshape(chunks, 128).T)


def make_in_maps(inputs):
    g = {k: np.asarray(v) for k, v in inputs.items()}
    hc = host_constants()
    perm = rope_swap_perm()
    sh = {}
    sh["ada_w"] = np.ascontiguousarray(g["ada_w"], np.float32)
    sh["ada_bf"] = np.ascontiguousarray(np.stack([_fm(g["ada_b"][l], 48) for l in range(DEPTH)], 1))
    sh["ada_br"] = np.ascontiguousarray(np.broadcast_to(g["ada_b"][None], (2, DEPTH, 6 * D)), np.float32)
    sh["n1g"] = np.ascontiguousarray(np.stack([_fm(g["norm1_g"][l], 8) for l in range(DEPTH)], 1))
    sh["n2g"] = np.ascontiguousarray(np.stack([_fm(g["norm2_g"][l], 8) for l in range(DEPTH)], 1))
    sh["w_in"] = np.ascontiguousarray(g["w_in"], np.float32)
    wkr = np.zeros((DEPTH, D, 2, 96), np.float32)
    wkr[:, :, 0, 64:96] = g["w_in"][:, :, 384:416]
    wkr[:, :, 1, 64:96] = g["w_in"][:, :, 384:416][:, :, perm]
    sh["w_kr2"] = wkr
    sh["qng"] = np.ascontiguousarray(np.stack([_fm(g["mla_q_norm"][l], 2) for l in range(DEPTH)], 1))
    sh["kvng"] = np.ascontiguousarray(np.stack([g["mla_kv_norm"][l] for l in range(DEPTH)], 1), np.float32)
    sh["w_uq"] = np.ascontiguousarray(g["mla_w_uq"], np.float32)
    wsw = g["mla_w_uq"].reshape(DEPTH, 256, 8, 96).copy()
    wsw[:, :, :, 64:96] = wsw[:, :, :, 64:96][:, :, :, perm]
    sh["w_uq_sw"] = np.ascontiguousarray(wsw.reshape(DEPTH, 256, 768), np.float32)
    ukv = g["mla_w_ukv"].reshape(DEPTH, 128, 8, 128)
    sh["w_ukv_k"] = np.ascontiguousarray(ukv[:, :, :, 0:64].reshape(DEPTH, 128, 512), np.float32)
    sh["w_ukv_v"] = np.ascontiguousarray(ukv[:, :, :, 64:128].reshape(DEPTH, 128, 512), np.float32)
    sh["w_a2"] = np.ascontiguousarray(np.concatenate([g["gla_w_a2"][:, 0], g["gla_w_a2"][:, 1]], -1), np.float32)
    sh["b_a"] = np.ascontiguousarray(np.concatenate([g["gla_b_a"][:, 0], g["gla_b_a"][:, 1]], -1)[:, None, :], np.float32)
    for nm_ in ("w_o_mla", "w_o_gla", "w_o_hy", "w_out", "ff_w1", "ff_w2"):
        sh[nm_] = np.ascontiguousarray(g[nm_], np.float32)
    sw = g["hy_short_w"]
    swb = np.concatenate([sw, g["hy_short_b"][:, None, :]], 1)
    sh["hy_swb"] = np.ascontiguousarray(swb.reshape(DEPTH, 4, 12, 128).transpose(0, 3, 2, 1), np.float32)
    sh["hy_f_w1"] = np.ascontiguousarray(g["hy_f_w1"], np.float32)
    sh["hy_f_w2"] = np.ascontiguousarray(g["hy_f_w2"], np.float32)
    sh["hy_f_w3"] = np.ascontiguousarray(g["hy_f_w3"], np.float32)
    sh["hy_fb12"] = np.ascontiguousarray(np.stack([g["hy_f_b1"], g["hy_f_b2"]], -1), np.float32)
    sh["hy_fb3T"] = np.ascontiguousarray(g["hy_f_b3"].reshape(DEPTH, 16, 128).transpose(0, 2, 1), np.float32)
    sh["hy_biasT"] = np.ascontiguousarray(g["hy_bias"].reshape(DEPTH, 2, 4, 128).transpose(0, 3, 1, 2), np.float32)
    sh["fin_g"] = np.ascontiguousarray(np.broadcast_to(g["final_norm_g"][None, :], (128, D)), np.float32)
    sh["gla_on"] = np.ascontiguousarray(np.broadcast_to(g["gla_out_norm"][:, None, :], (DEPTH, 128, 128)), np.float32)
    for k, v in hc.items():
        sh[k] = v
    maps = []
    for b in range(8):
        m = dict(sh)
        m["xc"] = np.ascontiguousarray(np.concatenate([g["ctx"][b], g["x"][b]], 0), np.float32)
        m["cs"] = np.ascontiguousarray(np.stack([_fm(g["c"][b], 8), _fm(g["c_ctx"], 8)], -1))
        maps.append(m)
    return maps


_NC_CACHE = {}


def kernel(**inputs):
    if "nc" not in _NC_CACHE:
        _NC_CACHE["nc"] = build()
    nc = _NC_CACHE["nc"]
    maps = make_in_maps(inputs)
    res = run_bass_kernel_spmd(nc, maps, core_ids=list(range(8)))
    return np.stack([np.asarray(r["y"], np.float32) for r in res.results], 0)
```

```python
import math
import numpy as np
import ml_dtypes
import concourse.bass as bass
import concourse.mybir as mybir
from concourse.bass_utils import run_bass_kernel_spmd

F32 = mybir.dt.float32
BF16 = mybir.dt.bfloat16
AF = mybir.ActivationFunctionType
ALU = mybir.AluOpType

D = 1024
L = 4096
LC = 256
T = L + LC
NT = T // 128
DEPTH = 2
DIN = 6592
EPS = 1e-6
MLA_SCALE = 96 ** -0.5
TB = [(0, 256)] + [(256 + 512 * i, 512) for i in range(8)]
NFFT = 8192


class V:
    __slots__ = ("b", "ap")

    def __init__(self, b, ap):
        self.b = b
        self.ap = ap

    def __getitem__(self, idx):
        return V(self.b, self.ap[idx])

    def re(self, pat, **kw):
        return V(self.b, self.ap.rearrange(pat, **kw))

    def bc(self, shape):
        return V(self.b, self.ap.broadcast_to(list(shape)))

    def un(self, axis):
        return V(self.b, self.ap.unsqueeze(axis))

    def pb(self, n):
        return V(self.b, self.ap.partition_broadcast(n))


class Buf:
    __slots__ = ("t", "name", "lw", "rd", "psum", "dram")

    def __init__(self, t, name, psum=False, dram=False):
        self.t = t
        self.name = name
        self.lw = []
        self.rd = []
        self.psum = psum
        self.dram = dram

    def __getitem__(self, idx):
        return V(self, self.t[idx])

    @property
    def v(self):
        return V(self, self.t[:] if not hasattr(self.t, "ap") or True else self.t)


class FW:
    NDMA_SEM = 36
    NDMA_HW = 24

    def __init__(self, nc):
        self.nc = nc
        self.eng = {"pe": nc.tensor, "act": nc.scalar, "dve": nc.vector, "pool": nc.gpsimd, "sp": nc.sync}
        self.sem = {}
        self.cnt = {}
        for e in self.eng:
            self.sem[e] = nc.alloc_semaphore("s_" + e)
            self.cnt[e] = 0
        self.dsem = [nc.alloc_semaphore("d%d" % i) for i in range(self.NDMA_SEM)]
        self.dcnt = [0] * self.NDMA_SEM
        self.dnext = 0
        self.dnext_sw = 0
        self.seen = {e: {} for e in self.eng}
        self.ninst = 0
        self._ctx = []
        self._uid = 0
        self.deferred = []

    def _nm(self, name):
        self._uid += 1
        return "%s_%d" % (name, self._uid)

    def sbuf(self, name, shape, dt):
        g = self.nc.sbuf_tensor(self._nm(name), list(shape), dt)
        t = g.__enter__()
        self._ctx.append(g)
        return Buf(t, name)

    def psum(self, name, shape, dt=F32):
        g = self.nc.psum_tensor(self._nm(name), list(shape), dt)
        t = g.__enter__()
        self._ctx.append(g)
        return Buf(t, name, psum=True)

    def dram(self, name, shape, dt, kind="Internal"):
        t = self.nc.dram_tensor(name, list(shape), dt, kind=kind)
        return Buf(t.ap(), name, dram=True)

    def _wait(self, e, tok):
        if tok is None:
            return
        key, val = tok
        if e == "pe" and key == "pe":
            return
        if self.seen[e].get(key, 0) >= val:
            return
        self.seen[e][key] = val
        sem = self.sem[key] if isinstance(key, str) else self.dsem[key]
        self.eng[e].wait_ge(sem, val)

    def _deps(self, e, reads, writes, dma_write=False):
        for b in reads:
            for tok in b.lw:
                self._wait(e, tok)
        for b in writes:
            if not (dma_write and all(isinstance(t[0], int) for t in b.lw)):
                for tok in b.lw:
                    self._wait(e, tok)
            for tok in b.rd:
                self._wait(e, tok)

    @staticmethod
    def _compact(toks):
        best = {}
        for k, v in toks:
            if best.get(k, 0) < v:
                best[k] = v
        return list(best.items())

    def _commit(self, tok, reads, writes, dma_write=False):
        for b in reads:
            b.rd.append(tok)
            if len(b.rd) > 48:
                b.rd = self._compact(b.rd)
        for b in writes:
            if dma_write and b.lw and all(isinstance(t[0], int) for t in b.lw):
                b.lw.append(tok)
                if len(b.lw) > 48:
                    b.lw = self._compact(b.lw)
            else:
                b.lw = [tok]
            b.rd = []

    def flush(self):
        d, self.deferred = self.deferred, []
        for (q, out, in_, kw) in d:
            self._dma_now(q, out, in_, **kw)

    def op(self, e, fn, reads=(), writes=()):
        if self.deferred:
            self.flush()
        rd = [b for b in reads if not b.psum]
        wr = list(writes) + [b for b in reads if b.psum]
        self._deps(e, rd, wr)
        ins = fn(self.eng[e])
        self.cnt[e] += 1
        ins.then_inc(self.sem[e], 1)
        self._commit((e, self.cnt[e]), rd, wr)
        self.ninst += 1
        return ins

    def dma(self, q, out, in_, **kw):
        if out.b.dram and not in_.b.dram:
            self.deferred.append((q, out, in_, kw))
            return
        for (_, so, si, _) in self.deferred:
            if so.b is in_.b or si.b is out.b or so.b is out.b:
                self.flush()
                break
        self._dma_now(q, out, in_, **kw)

    def _dma_now(self, q, out, in_, **kw):
        if q == "pool":
            slot = self.NDMA_HW + self.dnext_sw
            self.dnext_sw = (self.dnext_sw + 1) % (self.NDMA_SEM - self.NDMA_HW)
        else:
            slot = self.dnext
            self.dnext = (self.dnext + 1) % self.NDMA_HW
        if self.dcnt[slot] > 0:
            self._wait(q, (slot, self.dcnt[slot]))
        self._deps(q, [in_.b], [out.b], dma_write=True)
        ins = self.eng[q].dma_start(out=out.ap, in_=in_.ap, **kw)
        self.dcnt[slot] += 16
        ins.then_inc(self.dsem[slot], 16)
        self._commit((slot, self.dcnt[slot]), [in_.b], [out.b], dma_write=True)
        self.ninst += 1

    def barrier(self, engines=None):
        self.flush()
        for e in (engines or self.eng):
            for e2 in self.eng:
                if e2 != e and self.cnt[e2] > 0:
                    self._wait(e, (e2, self.cnt[e2]))
            for s in range(self.NDMA_SEM):
                if self.dcnt[s] > 0:
                    self._wait(e, (s, self.dcnt[s]))

    def mark(self):
        return len(self._ctx)

    def release(self, mark):
        self.barrier()
        while len(self._ctx) > mark:
            self._ctx.pop().__exit__(None, None, None)

    def mm(self, out, lhsT, rhs, start=True, stop=True):
        return self.op("pe", lambda e: e.matmul(out.ap, lhsT=lhsT.ap, rhs=rhs.ap, start=start, stop=stop),
                       reads=[lhsT.b, rhs.b], writes=[out.b])

    def tr(self, out, in_, ident):
        return self.op("pe", lambda e: e.transpose(out.ap, in_.ap, ident.ap), reads=[in_.b, ident.b], writes=[out.b])

    def act(self, out, in_, func, bias=None, scale=None, accum=None, eng="act"):
        kw = {}
        rd = [in_.b]
        wr = [out.b]
        if bias is not None:
            if isinstance(bias, V):
                kw["bias"] = bias.ap
                rd.append(bias.b)
            else:
                kw["bias"] = bias
        if scale is not None:
            if isinstance(scale, V):
                kw["scale"] = scale.ap
                rd.append(scale.b)
            else:
                kw["scale"] = scale
        if accum is not None:
            kw["accum_out"] = accum.ap
            wr.append(accum.b)
        return self.op("act", lambda e: e.activation(out=out.ap, in_=in_.ap, func=func, **kw), reads=rd, writes=wr)

    def tt(self, out, a, b, op, eng="dve"):
        return self.op(eng, lambda e: e.tensor_tensor(out=out.ap, in0=a.ap, in1=b.ap, op=op),
                       reads=[a.b, b.b], writes=[out.b])

    def ts(self, out, a, s1, op0, s2=None, op1=None, eng="dve"):
        rd = [a.b]
        s1a = s1
        s2a = s2
        if isinstance(s1, V):
            rd.append(s1.b)
            s1a = s1.ap
        if isinstance(s2, V):
            rd.append(s2.b)
            s2a = s2.ap
        kw = {}
        if op1 is not None:
            kw["op1"] = op1
        return self.op(eng, lambda e: e.tensor_scalar(out=out.ap, in0=a.ap, scalar1=s1a, scalar2=s2a, op0=op0, **kw),
                       reads=rd, writes=[out.b])

    def stt(self, out, a, s, b, op0, op1, eng="dve"):
        rd = [a.b, b.b]
        sa = s
        if isinstance(s, V):
            rd.append(s.b)
            sa = s.ap
        return self.op(eng, lambda e: e.scalar_tensor_tensor(out=out.ap, in0=a.ap, scalar=sa, in1=b.ap, op0=op0, op1=op1),
                       reads=rd, writes=[out.b])

    def cp(self, out, in_, eng="dve"):
        if eng == "act":
            return self.op("act", lambda e: e.copy(out=out.ap, in_=in_.ap), reads=[in_.b], writes=[out.b])
        return self.op(eng, lambda e: e.tensor_copy(out=out.ap, in_=in_.ap), reads=[in_.b], writes=[out.b])

    def memset(self, out, val, eng="pool"):
        return self.op(eng, lambda e: e.memset(out.ap, val), writes=[out.b])

    def recip(self, out, in_):
        return self.op("dve", lambda e: e.reciprocal(out=out.ap, in_=in_.ap), reads=[in_.b], writes=[out.b])


class Pool:
    def __init__(self, bufs):
        self.bufs = bufs
        self.i = 0

    def next(self):
        b = self.bufs[self.i % len(self.bufs)]
        self.i += 1
        return b


def _bf(a):
    return np.ascontiguousarray(a.astype(ml_dtypes.bfloat16))


def host_constants():
    c = {}
    c["ident_bf"] = _bf(np.eye(128, dtype=np.float32))
    c["ident_f"] = np.eye(128, dtype=np.float32)
    c["ones_f"] = np.ones((128, 128), np.float32)
    rows = L // 64
    row = np.repeat(np.arange(rows, dtype=np.float32), 64)
    col = np.tile(np.arange(64, dtype=np.float32), rows)
    inv = (10000.0 ** (-np.arange(8, dtype=np.float32) / 8)).astype(np.float32)
    ang = np.concatenate([row[:, None] * inv, col[:, None] * inv], axis=-1)
    cos, sin = np.cos(ang), np.sin(ang)
    cosT = np.ones((96, T), np.float32)
    sinT = np.zeros((96, T), np.float32)
    for r in range(32):
        g, j = r // 16, r % 16
        half, i = j // 8, j % 8
        cosT[64 + r, LC:] = cos[:, g * 8 + i]
        sinT[64 + r, LC:] = (-sin[:, g * 8 + i]) if half == 0 else sin[:, g * 8 + i]
    c["cosT"] = cosT
    c["sinT"] = sinT
    i_ = np.arange(128)[None, :]
    j_ = np.arange(128)[:, None]
    Mf = ((j_ >= 64) & (j_ <= i_)).astype(np.float32) - ((j_ > i_) & (j_ <= 63)).astype(np.float32)
    Mb = ((j_ >= i_) & (j_ <= 63)).astype(np.float32) - ((j_ >= 64) & (j_ < i_)).astype(np.float32)
    c["gla_M"] = np.stack([Mf, Mb], 1).astype(np.float32)
    c["gla_mask"] = np.stack([(j_ <= i_), (j_ >= i_)], 1).astype(np.float32)
    ind = np.zeros((128, 2), np.float32)
    ind[:64, 0] = 1
    ind[64:, 1] = 1
    c["gla_ind"] = ind
    deltas = np.linspace(math.log(1e-2) / 0.3, math.log(1e-2) / 1.5, 512)
    fb = np.linspace(1e-4, 15.0, 16)
    b_ = np.arange(64)
    fbb = np.arange(64)
    for sfx, Ls in (("", L), ("_c", LC)):
        NF = 2 * Ls
        NA = NF // 64
        NFA = NA // 2 + 1
        pos = np.arange(Ls, dtype=np.float64)
        tn = pos / max(Ls - 1, 1)
        ang = (2.0 * math.pi / Ls) * pos[:, None] * fb
        feat = np.concatenate([tn[:, None], np.cos(ang), np.sin(ang)], -1)
        win = np.exp(-tn[:, None] * np.abs(deltas))
        feat2 = np.zeros((NF, 33))
        win2 = np.zeros((NF, 512))
        feat2[:Ls] = feat
        win2[:Ls] = win
        idx = np.arange(NF - Ls + 1, NF)
        feat2[idx] = feat[NF - idx]
        win2[idx] = win[NF - idx]
        c["hy_featT" + sfx] = np.ascontiguousarray(feat2.T.astype(np.float32))
        tn2 = np.full((1, NF), 1.0e4)
        tn2[0, :Ls] = tn
        tn2[0, idx] = tn[NF - idx]
        c["hy_tn2" + sfx] = tn2.astype(np.float32)
        a_ = np.arange(NA)[:, None]
        fa = np.arange(NFA)[None, :]
        th = 2 * math.pi * ((fa * a_) % NA) / NA
        c["hy_F1" + sfx] = _bf(np.concatenate([np.cos(th), -np.sin(th), np.sin(th)], 1))
        E2r = np.zeros((64, 2, NFA, 64, 2))
        E2i = np.zeros((64, 2, NFA, 64, 2))
        ph = 2 * math.pi * (((np.arange(NFA)[None, :, None] + NA * fbb[None, None, :]) * b_[:, None, None]) % NF) / NF
        for cp in range(2):
            E2r[:, cp, :, :, cp] = np.cos(ph)
            E2i[:, cp, :, :, cp] = -np.sin(ph)
        c["hy_E2r" + sfx] = _bf(E2r.reshape(128, NFA, 128))
        c["hy_E2i" + sfx] = _bf(E2i.reshape(128, NFA, 128))
        tt_ = 64 * np.arange(NA // 2)[None, None, :] + b_[None, :, None]
        th2 = 2 * math.pi * ((np.arange(NFA)[:, None, None] * tt_) % NF) / NF
        wgt = np.full((NFA, 1, 1), 2.0 / NF)
        wgt[0] = wgt[NFA - 1] = 1.0 / NF
        c["hy_DBr" + sfx] = _bf(wgt * np.cos(th2))
        c["hy_DBni" + sfx] = _bf(-wgt * np.sin(th2))
    c["hy_negdelta"] = np.ascontiguousarray((-np.abs(deltas)).reshape(4, 128).T.astype(np.float32))
    psi = 2 * math.pi * ((fbb[:, None] * b_[None, :]) % 64) / 64
    CA = np.zeros((64, 2, 3, 64, 2))
    for cp in range(2):
        CA[:, cp, 0, :, cp] = np.cos(psi)
        CA[:, cp, 1, :, cp] = np.sin(psi)
        CA[:, cp, 2, :, cp] = -np.sin(psi)
    c["hy_CA"] = _bf(CA.reshape(128, 3, 128))
    return c


def rope_swap_perm():
    p = np.zeros(32, np.int64)
    for r in range(32):
        g, j = r // 16, r % 16
        p[r] = g * 16 + (j + 8) % 16
    return p


def build(dbg=None):
    dbg = dbg or {}
    stop_after = dbg.get("stop_after", None)
    ext = dbg.get("ext", ())
    nc = bass.Bass("TRN2", target_bir_lowering=False)
    f = FW(nc)
    hc = host_constants()

    def inp(name, shape, dt=F32):
        return Buf(nc.dram_tensor(name, list(shape), dt, kind="ExternalInput").ap(), name, dram=True)

    def scratch(name, shape, dt):
        if name in dbg.get("inject", ()):
            return f.dram(name, shape, dt, kind="ExternalInput")
        return f.dram(name, shape, dt, kind="ExternalOutput" if name in ext else "Internal")

    I = {}
    I["xc"] = inp("xc", [T, D])
    I["cs"] = inp("cs", [128, 8, 2])
    I["ada_w"] = inp("ada_w", [DEPTH, D, 6 * D])
    I["ada_bf"] = inp("ada_bf", [128, DEPTH, 48])
    I["ada_br"] = inp("ada_br", [2, DEPTH, 6 * D])
    I["n1g"] = inp("n1g", [128, DEPTH, 8])
    I["n2g"] = inp("n2g", [128, DEPTH, 8])
    I["w_in"] = inp("w_in", [DEPTH, D, DIN])
    I["w_kr2"] = inp("w_kr2", [DEPTH, D, 2, 96])
    I["qng"] = inp("qng", [128, DEPTH, 2])
    I["kvng"] = inp("kvng", [128, DEPTH])
    I["w_uq"] = inp("w_uq", [DEPTH, 256, 768])
    I["w_uq_sw"] = inp("w_uq_sw", [DEPTH, 256, 768])
    I["w_ukv_k"] = inp("w_ukv_k", [DEPTH, 128, 512])
    I["w_ukv_v"] = inp("w_ukv_v", [DEPTH, 128, 512])
    I["w_a2"] = inp("w_a2", [DEPTH, 16, 512])
    I["b_a"] = inp("b_a", [DEPTH, 1, 512])
    I["gla_on"] = inp("gla_on", [DEPTH, 128, 128])
    for nm_ in ("w_o_mla", "w_o_gla", "w_o_hy"):
        I[nm_] = inp(nm_, [DEPTH, 512, D])
    I["w_out"] = inp("w_out", [DEPTH, D, D])
    I["ff_w1"] = inp("ff_w1", [DEPTH, D, 4 * D])
    I["ff_w2"] = inp("ff_w2", [DEPTH, 4 * D, D])
    I["fin_g"] = inp("fin_g", [128, D])
    I["hy_swb"] = inp("hy_swb", [DEPTH, 128, 12, 4])
    I["hy_f_w1"] = inp("hy_f_w1", [DEPTH, 33, 64])
    I["hy_f_w2"] = inp("hy_f_w2", [DEPTH, 64, 64])
    I["hy_f_w3"] = inp("hy_f_w3", [DEPTH, 64, 2048])
    I["hy_fb12"] = inp("hy_fb12", [DEPTH, 64, 2])
    I["hy_fb3T"] = inp("hy_fb3T", [DEPTH, 128, 16])
    I["hy_biasT"] = inp("hy_biasT", [DEPTH, 128, 2, 4])
    for k, v in hc.items():
        I[k] = inp(k, list(v.shape), BF16 if v.dtype == ml_dtypes.bfloat16 else F32)
    out_y = Buf(nc.dram_tensor("y", [L, D], F32, kind="ExternalOutput").ap(), "y", dram=True)

    xres = scratch("xres", [T, D], F32)
    modrow = scratch("modrow", [2, DEPTH, 2, D], F32)
    qT_d = scratch("qT_d", [8, 96, T], BF16)
    kT_d = scratch("kT_d", [8, 96, T], BF16)
    V_d = scratch("V_d", [T, 520], BF16)
    gqk_d = scratch("gqk_d", [T, 512], F32)
    gv_d = scratch("gv_d", [T, 512], BF16)
    gr_d = scratch("gr_d", [T, 512], F32)
    gg_d = scratch("gg_d", [T, 512], F32)
    zhyT_d = scratch("zhyT_d", [1536, T], BF16)
    scT_d = scratch("scT_d", [1536, T], BF16)
    H_d = scratch("H_d", [2, 8, 128, 2 * 65 * 32], BF16)
    y1T_d = scratch("y1T_d", [512, T], BF16)
    gatesT_d = scratch("gatesT_d", [3072, T], BF16)
    hT_d = scratch("hT_d", [D, T], BF16) if "hT_d" in ext else None
    ymlaT_d = scratch("ymlaT_d", [512, T], BF16)
    yglaT_d = scratch("yglaT_d", [512, T], BF16)
    yhyT_d = scratch("yhyT_d", [512, T], BF16)
    og_d = scratch("og_d", [2, T, 512], F32)

    ident = f.sbuf("ident", [128, 128], BF16)
    ones_f = f.sbuf("ones_f", [128, 128], F32)
    modF = f.sbuf("modF", [128, DEPTH, 48, 2], F32)
    AB = f.sbuf("AB", [128, DEPTH, 2, 2, 8, 2], F32)
    f.dma("sp", ident[:], I["ident_bf"][:])
    f.dma("sp", ones_f[:], I["ones_f"][:])
    xres_t = [Buf(xres.t[tt * 128:(tt + 1) * 128, :], "xres%d" % tt, dram=True) for tt in range(NT)]
    for tt in range(NT):
        f.dma("sp", xres_t[tt][:], I["xc"][tt * 128:(tt + 1) * 128, :])

    def phase_mod():
        m = f.mark()
        cs = f.sbuf("cs", [128, 8, 2], F32)
        scs = f.sbuf("scs", [128, 8, 2], F32)
        abf = f.sbuf("abf", [128, DEPTH, 48], F32)
        abr = f.sbuf("abr", [2, DEPTH, 6 * D], F32)
        g1 = f.sbuf("g1", [128, DEPTH, 8], F32)
        g2 = f.sbuf("g2", [128, DEPTH, 8], F32)
        wp = Pool([f.sbuf("adaw%d" % i, [128, 8, 512], F32) for i in range(2)])
        rowst = f.sbuf("rowst", [2, 512], F32)
        psF = f.psum("psF", [128, 512], F32)
        psR = Pool([f.psum("psR%d" % i, [128, 512], F32) for i in range(2)])
        f.dma("sp", cs[:], I["cs"][:])
        f.dma("sp", abf[:], I["ada_bf"][:])
        f.dma("sp", abr[:], I["ada_br"][:])
        f.dma("sp", g1[:], I["n1g"][:])
        f.dma("sp", g2[:], I["n2g"][:])
        f.act(scs[:], cs[:], AF.Silu)
        for l in range(DEPTH):
            wv = I["ada_w"][l].re("(kc p) j -> p kc j", p=128)
            for jb in range(12):
                w = wp.next()
                f.dma("sp", w[:], wv[:, :, jb * 512:(jb + 1) * 512])
                which = jb // 2
                if which in (2, 5):
                    pr = psR.next()
                    for kc in range(8):
                        f.mm(pr[0:2, :], scs[:, kc, :], w[:, kc, :], start=kc == 0, stop=kc == 7)
                    f.tt(rowst[:], pr[0:2, :], abr[:, l, jb * 512:(jb + 1) * 512], ALU.add)
                    f.dma("sp", modrow[:, l, 0 if which == 2 else 1, (jb % 2) * 512:(jb % 2 + 1) * 512], rowst[:])
                else:
                    for jc in range(4):
                        ch = jb * 4 + jc
                        for kc in range(8):
                            f.mm(psF[:, ch * 2:ch * 2 + 2], w[:, kc, jc * 128:(jc + 1) * 128], scs[:, kc, :],
                                 start=kc == 0, stop=kc == 7)
            for (c0, c1) in ((0, 16), (24, 40)):
                pv = psF[:, c0 * 2:c1 * 2].re("p (c s) -> p c s", s=2)
                f.tt(modF[:, l, c0:c1, :], pv, abf[:, l, c0:c1].un(2).bc([128, c1 - c0, 2]), ALU.add)
            for n_i, (sh0, sc0, g) in enumerate(((0, 8, g1), (24, 32, g2))):
                f.stt(AB[:, l, n_i, 0], modF[:, l, sc0:sc0 + 8, :], 1.0, g[:, l, :].un(2).bc([128, 8, 2]), ALU.add, ALU.mult)
                f.cp(AB[:, l, n_i, 1], modF[:, l, sh0:sh0 + 8, :])
        f.release(m)

    class NormCtx:
        def __init__(self):
            self.xp = Pool([f.sbuf("nx%d" % i, [128, D], F32) for i in range(2)])
            self.xnp = Pool([f.sbuf("nxn%d" % i, [128, D], BF16) for i in range(2)])
            self.stp = Pool([f.sbuf("nst%d" % i, [128, 4], F32) for i in range(3)])
            self.pp = Pool([f.psum("nps%d" % i, [128, 8, 128], BF16) for i in range(2)])

        def emit(self, l, n_i, hT, tt, c0):
            s = 0 if tt >= 2 else 1
            x = self.xp.next()
            st = self.stp.next()
            xn = self.xnp.next()
            f.dma("sp", x[:], xres_t[tt][:])
            f.act(xn[:], x[:], AF.Square, accum=st[:, 0:1])
            f.act(st[:, 1:2], st[:, 0:1], AF.Sqrt, bias=EPS_T[:, 0:1], scale=1.0 / D)
            f.recip(st[:, 2:3], st[:, 1:2])
            f.act(xn[:], x[:], AF.Identity, scale=st[:, 2:3])
            ps = self.pp.next()
            for kc in range(8):
                f.tr(ps[:, kc, :], xn[:, kc * 128:(kc + 1) * 128], ident[:])
            for kc in range(8):
                if kc % 2 == 0:
                    f.act(hT[:, kc, c0:c0 + 128], ps[:, kc, :], AF.Identity,
                          bias=AB[:, l, n_i, 1, kc, s:s + 1], scale=AB[:, l, n_i, 0, kc, s:s + 1])
                else:
                    f.ts(hT[:, kc, c0:c0 + 128], ps[:, kc, :], AB[:, l, n_i, 0, kc, s:s + 1], ALU.mult,
                         AB[:, l, n_i, 1, kc, s:s + 1], ALU.add)

    def norm_tiles(l, n_i, hT, tiles):
        m = f.mark()
        nctx = NormCtx()
        for tt in tiles:
            nctx.emit(l, n_i, hT, tt, tt * 128)
        f.release(m)

    EPS_T = f.sbuf("eps_t", [128, 1], F32)
    f.memset(EPS_T[:], EPS)

    def phase_proj(l, hT):
        m = f.mark()
        wp = Pool([f.sbuf("pw%d" % i, [128, 8, 512], BF16) for i in range(3)])
        psp = Pool([f.psum("pps%d" % i, [128, 512], F32) for i in range(4)])
        psq = Pool([f.psum("ppq%d" % i, [128, 512], F32) for i in range(3)])
        w_l = I["w_in"][l].re("(kc p) n -> p kc n", p=128)

        def loadw(c0, n):
            w = wp.next()
            f.dma("pool", w[:, :, 0:n], w_l[:, :, c0:c0 + n])
            return w

        def fm(w, wc0, ncol, evac):
            for bi, (t0, n) in enumerate(TB):
                ps = psp.next()
                for kc in range(8):
                    f.mm(ps[0:ncol, 0:n], w[:, kc, wc0:wc0 + ncol], hT[:, kc, t0:t0 + n], start=kc == 0, stop=kc == 7)
                evac(ps, bi, t0, n)

        def tm(w, ncol, evac):
            for tt in range(NT):
                ps = psp.next()
                for kc in range(8):
                    f.mm(ps[:, 0:ncol], hT[:, kc, tt * 128:(tt + 1) * 128], w[:, kc, 0:ncol], start=kc == 0, stop=kc == 7)
                evac(ps, tt)

        w0 = loadw(0, 384)
        wk = wp.next()
        f.dma("pool", wk[:, :, 0:192], I["w_kr2"][l].re("(kc p) a n -> p kc (a n)", p=128))
        qng = f.sbuf("qng", [128, DEPTH, 2], F32)
        kvng = f.sbuf("kvng", [128, DEPTH], F32)
        f.dma("sp", qng[:], I["qng"][:])
        f.dma("sp", kvng[:], I["kvng"][:])
        wuq = f.sbuf("wuq", [128, 2, 768], BF16)
        wuqs = f.sbuf("wuqs", [128, 2, 768], BF16)
        wk_k = f.sbuf("wukv_k", [128, 512], BF16)
        wk_v = f.sbuf("wukv_v", [128, 512], BF16)
        f.dma("pool", wuq[:], I["w_uq"][l].re("(kc p) n -> p kc n", p=128))
        f.dma("pool", wuqs[:], I["w_uq_sw"][l].re("(kc p) n -> p kc n", p=128))
        f.dma("pool", wk_k[:], I["w_ukv_k"][l])
        f.dma("pool", wk_v[:], I["w_ukv_v"][l])
        cosp = Pool([f.sbuf("cosb%d" % i, [96, 512], F32) for i in range(2)])
        sinp = Pool([f.sbuf("sinb%d" % i, [96, 512], F32) for i in range(2)])
        cqp = Pool([f.sbuf("cqb%d" % i, [128, 2, 512], F32) for i in range(2)])
        ckvp = Pool([f.sbuf("ckvb%d" % i, [128, 512], F32) for i in range(2)])
        cqnp = Pool([f.sbuf("cqnb%d" % i, [128, 2, 512], BF16) for i in range(2)])
        ckvnp = Pool([f.sbuf("ckvnb%d" % i, [128, 512], BF16) for i in range(2)])
        krp = Pool([f.sbuf("krb%d" % i, [96, 512], BF16) for i in range(2)])
        tmpa = Pool([f.sbuf("ptmpa%d" % i, [128, 512], F32) for i in range(2)])
        tmpb = Pool([f.sbuf("ptmpb%d" % i, [128, 512], F32) for i in range(2)])
        sqp = Pool([f.sbuf("psq%d" % i, [128, 2, 512], F32) for i in range(2)])
        rsp = Pool([f.sbuf("prs%d" % i, [128, 512], F32) for i in range(2)])
        qst = Pool([f.sbuf("qst%d" % i, [96, 512], BF16) for i in range(3)])
        kst = Pool([f.sbuf("kst%d" % i, [96, 512], BF16) for i in range(3)])
        vst = Pool([f.sbuf("vst%d" % i, [128, 8, 65], BF16) for i in range(2)])
        for vb in vst.bufs:
            f.memset(vb[:], 1.0)
        for bi, (t0, n) in enumerate(TB):
            cosb = cosp.next()
            sinb = sinp.next()
            f.dma("sp", cosb[64:96, 0:n], I["cosT"][64:96, t0:t0 + n])
            f.dma("sp", sinb[64:96, 0:n], I["sinT"][64:96, t0:t0 + n])
            cq = cqp.next()
            ckv = ckvp.next()
            for c in range(3):
                ps = psp.next()
                for kc in range(8):
                    f.mm(ps[:, 0:n], w0[:, kc, c * 128:(c + 1) * 128], hT[:, kc, t0:t0 + n], start=kc == 0, stop=kc == 7)
                f.cp(cq[:, c, 0:n] if c < 2 else ckv[:, 0:n], ps[:, 0:n], eng="act")
            pa = psp.next()
            pb = psp.next()
            for kc in range(8):
                f.mm(pa[0:96, 0:n], wk[:, kc, 0:96], hT[:, kc, t0:t0 + n], start=kc == 0, stop=kc == 7)
            for kc in range(8):
                f.mm(pb[0:96, 0:n], wk[:, kc, 96:192], hT[:, kc, t0:t0 + n], start=kc == 0, stop=kc == 7)
            ta = tmpa.next()
            tb_ = tmpb.next()
            krb = krp.next()
            f.tt(ta[64:96, 0:n], pa[64:96, 0:n], cosb[64:96, 0:n], ALU.mult)
            f.tt(tb_[64:96, 0:n], pb[64:96, 0:n], sinb[64:96, 0:n], ALU.mult)
            f.tt(krb[64:96, 0:n], ta[64:96, 0:n], tb_[64:96, 0:n], ALU.add, eng="pool")
            cqn = cqnp.next()
            ckvn = ckvnp.next()
            for (nchunk, gains) in ((2, qng), (1, kvng)):
                sq = sqp.next()
                ps = psp.next()
                for c in range(nchunk):
                    sv = cq[:, c, 0:n] if nchunk == 2 else ckv[:, 0:n]
                    f.tt(sq[:, c, 0:n], sv, sv, ALU.mult)
                for c in range(nchunk):
                    f.mm(ps[:, 0:n], ones_f[:], sq[:, c, 0:n], start=c == 0, stop=c == nchunk - 1)
                rs = rsp.next()
                f.act(rs[:, 0:n], ps[:, 0:n], AF.Sqrt, bias=EPS_T[:, 0:1], scale=1.0 / (128 * nchunk))
                f.recip(rs[:, 0:n], rs[:, 0:n])
                for c in range(nchunk):
                    sv = cq[:, c, 0:n] if nchunk == 2 else ckv[:, 0:n]
                    dv = cqn[:, c, 0:n] if nchunk == 2 else ckvn[:, 0:n]
                    gv = gains[:, l, c:c + 1] if nchunk == 2 else gains[:, l:l + 1]
                    f.stt(dv, sv, gv, rs[:, 0:n], ALU.mult, ALU.mult)
            for h in range(8):
                pa = psq.next()
                pb = psq.next()
                for kc in range(2):
                    f.mm(pa[0:96, 0:n], wuq[:, kc, h * 96:(h + 1) * 96], cqn[:, kc, 0:n], start=kc == 0, stop=kc == 1)
                for kc in range(2):
                    f.mm(pb[0:96, 0:n], wuqs[:, kc, h * 96:(h + 1) * 96], cqn[:, kc, 0:n], start=kc == 0, stop=kc == 1)
                q = qst.next()
                ta = tmpa.next()
                tb_ = tmpb.next()
                f.cp(q[0:64, 0:n], pa[0:64, 0:n], eng="act")
                f.tt(ta[64:96, 0:n], pa[64:96, 0:n], cosb[64:96, 0:n], ALU.mult)
                f.tt(tb_[64:96, 0:n], pb[64:96, 0:n], sinb[64:96, 0:n], ALU.mult)
                f.tt(q[64:96, 0:n], ta[64:96, 0:n], tb_[64:96, 0:n], ALU.add, eng="pool")
                f.dma("sp", qT_d[h, :, t0:t0 + n], q[:, 0:n])
                pk = psq.next()
                f.mm(pk[0:64, 0:n], wk_k[:, h * 64:(h + 1) * 64], ckvn[:, 0:n])
                k = kst.next()
                f.cp(k[0:64, 0:n], pk[0:64, 0:n], eng="act")
                f.cp(k[64:96, 0:n], krb[64:96, 0:n], eng="pool")
                f.dma("sp", kT_d[h, :, t0:t0 + n], k[:, 0:n])
            for ti in range(n // 128):
                tt = t0 // 128 + ti
                pv = psq.next()
                f.mm(pv[:, :], ckvn[:, ti * 128:(ti + 1) * 128], wk_v[:, :])
                vs = vst.next()
                f.cp(vs[:, :, 0:64], pv[:, :].re("p (h e) -> p h e", e=64), eng="act")
                f.dma("sp", V_d[tt * 128:(tt + 1) * 128, :], vs[:].re("p h e -> p (h e)"))

        wa = loadw(1952, 32)
        wa2 = f.sbuf("wa2", [16, 512], F32)
        ba = f.sbuf("ba", [1, 512], F32)
        f.dma("sp", wa2[:], I["w_a2"][l])
        f.dma("sp", ba[:], I["b_a"][l])
        aft = Pool([f.sbuf("aft%d" % i, [16, 512], F32) for i in range(2)])
        abt = Pool([f.sbuf("abt%d" % i, [16, 512], F32) for i in range(2)])
        ggp = Pool([f.sbuf("ggs%d" % i, [128, 512], F32) for i in range(2)])
        for bi, (t0, n) in enumerate(TB):
            pa = psp.next()
            pb = psp.next()
            for kc in range(8):
                f.mm(pa[0:16, 0:n], wa[:, kc, 0:16], hT[:, kc, t0:t0 + n], start=kc == 0, stop=kc == 7)
            for kc in range(8):
                f.mm(pb[0:16, 0:n], wa[:, kc, 16:32], hT[:, kc, t0:t0 + n], start=kc == 0, stop=kc == 7)
            af = aft.next()
            ab = abt.next()
            f.cp(af[:, 0:n], pa[0:16, 0:n], eng="act")
            f.cp(ab[:, 0:n], pb[0:16, 0:n], eng="act")
            for ti in range(n // 128):
                pg = psq.next()
                f.mm(pg[:, 0:256], af[:, ti * 128:(ti + 1) * 128], wa2[:, 0:256], start=True, stop=False)
                f.mm(pg[:, 0:256], ones_f[0:1, :], ba[:, 0:256], start=False, stop=True)
                f.mm(pg[:, 256:512], ab[:, ti * 128:(ti + 1) * 128], wa2[:, 256:512], start=True, stop=False)
                f.mm(pg[:, 256:512], ones_f[0:1, :], ba[:, 256:512], start=False, stop=True)
                gs = ggp.next()
                f.act(gs[:], pg[:], AF.Exp, scale=-1.0)
                f.act(gs[:], gs[:], AF.Ln, bias=ONE_T[:, 0:1])
                f.ts(gs[:], gs[:], -1.0 / 16.0, ALU.mult)
                tt = t0 // 128 + ti
                f.dma("sp", gg_d[tt * 128:(tt + 1) * 128, :], gs[:])

        st32 = Pool([f.sbuf("pst32_%d" % i, [128, 512], F32) for i in range(3)])
        st16 = Pool([f.sbuf("pst16_%d" % i, [128, 512], BF16) for i in range(3)])

        def ev_qk(ps, tt):
            s = st32.next()
            f.act(s[:, 0:256], ps[:, 0:256], AF.Copy, scale=0.125)
            f.cp(s[:, 256:512], ps[:, 256:512], eng="dve")
            f.dma("sp", gqk_d[tt * 128:(tt + 1) * 128, :], s[:])

        def ev_to(dst, c0, dt16):
            def ev(ps, tt):
                s = (st16 if dt16 else st32).next()
                f.cp(s[:], ps[:], eng="act" if tt % 2 else "dve")
                f.dma("sp", dst[tt * 128:(tt + 1) * 128, c0:c0 + 512], s[:])
            return ev

        tm(loadw(416, 512), 512, ev_qk)
        tm(loadw(928, 512), 512, ev_to(gv_d, 0, True))
        tm(loadw(1440, 512), 512, ev_to(gr_d, 0, False))
        for sc in range(3):
            w = loadw(1984 + sc * 512, 512)
            for c in range(4):
                ch = sc * 4 + c

                def ev_h(ps, bi, t0, n, ch=ch):
                    s = st16.next()
                    f.cp(s[:, 0:n], ps[:, 0:n], eng="act" if (bi + ch) % 2 else "dve")
                    f.dma("sp", zhyT_d[ch * 128:(ch + 1) * 128, t0:t0 + n], s[:, 0:n])
                fm(w, c * 128, 128, ev_h)

        for sc in range(6):
            w = loadw(3520 + sc * 512, 512)
            for c in range(4):
                ch = sc * 4 + c

                def ev_g(ps, bi, t0, n, ch=ch):
                    s = st16.next()
                    f.act(s[:, 0:n], ps[:, 0:n], AF.Sigmoid)
                    f.dma("sp", gatesT_d[ch * 128:(ch + 1) * 128, t0:t0 + n], s[:, 0:n])
                fm(w, c * 128, 128, ev_g)
        f.release(m)

    def phase_att(l):
        m = f.mark()
        Vaug = f.sbuf("Vaug", [128, NT, 520], BF16)
        f.dma("sp", Vaug[:], V_d[:].re("(t p) e -> p t e", p=128))
        qp = Pool([f.sbuf("aq%d" % i, [96, T], BF16) for i in range(2)])
        kp = Pool([f.sbuf("ak%d" % i, [96, T], BF16) for i in range(2)])
        pp = Pool([f.sbuf("ap%d" % i, [128, 2, 512], BF16) for i in range(3)])
        sps = Pool([f.psum("as%d" % i, [128, 2, 512], F32) for i in range(2)])
        ops = Pool([f.psum("ao%d" % i, [128, 512], F32) for i in range(2)])
        bps = f.psum("abc", [128, 512], F32)
        rec = Pool([f.sbuf("arec%d" % i, [65, 512], F32) for i in range(2)])
        osb = Pool([f.sbuf("aosb%d" % i, [64, 512], F32) for i in range(2)])
        yst = Pool([f.sbuf("ayst%d" % i, [64, 512], BF16) for i in range(3)])
        for h in range(8):
            q = qp.next()
            k = kp.next()
            f.dma("sp", q[:], qT_d[h])
            f.dma("sp", k[:], kT_d[h])
            for bi, (t0, n) in enumerate(TB):
                if bi == 0 and l == DEPTH - 1:
                    continue
                nk = 2 if bi == 0 else NT
                npair = nk // 2
                o = ops.next()
                pend = None
                for jp in range(npair + 1):
                    p = None
                    if jp < npair:
                        s = sps.next()
                        for u_ in range(2):
                            j = jp * 2 + u_
                            f.mm(s[:, u_, 0:n], k[:, j * 128:(j + 1) * 128], q[:, t0:t0 + n])
                        p = pp.next()
                        f.act(p[:, :, 0:n], s[:, :, 0:n], AF.Exp, scale=MLA_SCALE)
                    if pend is not None:
                        jq, pv = pend
                        for u_ in range(2):
                            jj = jq * 2 + u_
                            f.mm(o[0:65, 0:n], Vaug[:, jj, h * 65:(h + 1) * 65], pv[:, u_, 0:n], start=jj == 0, stop=jj == nk - 1)
                    pend = (jp, p) if jp < npair else None
                r = rec.next()
                f.recip(r[64:65, 0:n], o[64:65, 0:n])
                f.mm(bps[0:64, 0:n], ones_f[64:65, 0:64], r[64:65, 0:n])
                os_ = osb.next()
                f.cp(os_[:, 0:n], o[0:64, 0:n], eng="act")
                y = yst.next()
                f.tt(y[:, 0:n], os_[:, 0:n], bps[0:64, 0:n], ALU.mult)
                f.dma("sp", ymlaT_d[h * 64:(h + 1) * 64, t0:t0 + n], y[:, 0:n])
        f.release(m)

    def phase_gla(l):
        m = f.mark()
        Mm = f.sbuf("glaM", [128, 2, 128], F32)
        mask = f.sbuf("glamask", [128, 2, 128], F32)
        ind = f.sbuf("glaind", [128, 2], F32)
        gon = f.sbuf("gon", [128, 128], F32)
        f.dma("sp", Mm[:], I["gla_M"][:])
        f.dma("sp", mask[:], I["gla_mask"][:])
        f.dma("sp", ind[:], I["gla_ind"][:])
        f.dma("sp", gon[:], I["gla_on"][l])
        mA = f.mark()
        ptr = f.psum("gptr", [128, 8, 128], BF16)
        pU = f.psum("gpU", [128, 4, 128], F32)

        def chain(d):
            S = f.sbuf("glaS%d" % d, [64, 4, 128], F32)
            P2 = lambda nm, shp, dt, k=2: Pool([f.sbuf("%s%d_%d" % (nm, d, i), shp, dt) for i in range(k)])
            qkp, gp, vp = P2("gqk", [128, 512], F32, 3), P2("gg", [128, 256], F32, 3), P2("gv", [128, 512], BF16, 3)
            ePp, eNp = P2("geP", [128, 256], F32), P2("geN", [128, 256], F32)
            qpp, kpp = P2("gqp", [128, 256], BF16), P2("gkp", [128, 256], BF16)
            decp, qkTp = P2("gdec", [64, 4, 2], F32), P2("gqkT", [64, 8, 128], BF16)
            Amp, Smidp, Smbp = P2("gAm", [128, 4, 128], BF16), P2("gSm", [64, 4, 128], F32), P2("gSb", [64, 4, 128], BF16)
            osp = P2("gos", [128, 4, 128], F32)
            pE = f.psum("gpE%d" % d, [128, 512], F32)
            pA = f.psum("gpA%d" % d, [128, 4, 128], F32)
            po = f.psum("gpo%d" % d, [128, 4, 128], F32)
            order = list(range(NT)) if d == 0 else [1, 0] + list(range(NT - 1, 1, -1))
            f.memset(S[:], 0.0)
            yield
            loaded = {}

            def load(tt):
                r0 = tt * 128
                qk, g, v = qkp.next(), gp.next(), vp.next()
                f.dma("sp", qk[:], gqk_d[r0:r0 + 128, :])
                f.dma("sp", g[:], gg_d[r0:r0 + 128, d * 256:(d + 1) * 256])
                f.dma("sp", v[:], gv_d[r0:r0 + 128, :])
                loaded[tt] = (qk, g, v)
            load(order[0])
            for oi, tt in enumerate(order):
                r0 = tt * 128
                if oi + 1 < len(order):
                    load(order[oi + 1])
                qk, g, v = loaded.pop(tt)
                f.mm(pE[:, 0:256], Mm[:, d, :], g[:])
                for h in range(4):
                    f.mm(pE[0:64, 256 + h * 2:258 + h * 2], g[:, h * 64:(h + 1) * 64], ind[:])
                yield
                eP, eN, dec = ePp.next(), eNp.next(), decp.next()
                f.act(eP[:], pE[:, 0:256], AF.Exp)
                f.act(eN[:], pE[:, 0:256], AF.Exp, scale=-1.0)
                f.act(dec[:].re("p h s -> p (h s)"), pE[0:64, 256:264], AF.Exp)
                dmid = dec[:, :, d:d + 1]
                dend = dec[:, :, 1 - d:2 - d]
                yield
                qp_, kp_ = qpp.next(), kpp.next()
                f.tt(qp_[:], qk[:, 0:256], eP[:], ALU.mult)
                f.tt(kp_[:], qk[:, 256:512], eN[:], ALU.mult, eng="pool")
                Smid = Smidp.next()
                f.tt(Smid[:], S[:], dmid.bc([64, 4, 128]), ALU.mult)
                Smb = Smbp.next()
                f.cp(Smb[:], Smid[:], eng="act")
                yield
                for h in range(4):
                    f.tr(ptr[0:64, h, :], qp_[:, h * 64:(h + 1) * 64], ident[:])
                for h in range(4):
                    f.tr(ptr[0:64, 4 + h, :], kp_[:, h * 64:(h + 1) * 64], ident[:])
                for h in range(4):
                    f.mm(pU[0:64, h, :], kp_[:, h * 64:(h + 1) * 64], v[:, h * 128:(h + 1) * 128])
                qkT = qkTp.next()
                f.cp(qkT[:], ptr[0:64], eng="act")
                f.tt(S[:], Smid[:], pU[0:64], ALU.add)
                f.tt(S[:], S[:], dend.bc([64, 4, 128]), ALU.mult)
                yield
                for h in range(4):
                    f.mm(pA[:, h, :], qkT[:, 4 + h, :], qkT[:, h, :])
                yield
                Am = Amp.next()
                f.tt(Am[:], pA[:], mask[:, d:d + 1, :].bc([128, 4, 128]), ALU.mult)
                yield
                for h in range(4):
                    f.mm(po[:, h, :], qkT[:, h, :], Smb[:, h, :], start=True, stop=False)
                    f.mm(po[:, h, :], Am[:, h, :], v[:, h * 128:(h + 1) * 128], start=False, stop=True)
                yield
                os_ = osp.next()
                f.cp(os_[:], po[:], eng="act")
                f.dma("sp", og_d[d, r0:r0 + 128, :], os_[:].re("p h v -> p (h v)"))
                yield

        gens = [chain(0), chain(1)]
        live = list(gens)
        while live:
            for g_ in list(live):
                try:
                    next(g_)
                except StopIteration:
                    live.remove(g_)
        f.release(mA)

        ofp = Pool([f.sbuf("gof%d" % i, [128, 4, 128], F32) for i in range(2)])
        obp = Pool([f.sbuf("gob%d" % i, [128, 4, 128], F32) for i in range(2)])
        sqp = Pool([f.sbuf("gsq%d" % i, [128, 4, 128], F32) for i in range(2)])
        stp = Pool([f.sbuf("gst%d" % i, [128, 8], F32) for i in range(2)])
        rp = Pool([f.sbuf("gr%d" % i, [128, 512], F32) for i in range(2)])
        yp = Pool([f.sbuf("gy%d" % i, [128, 512], BF16) for i in range(2)])
        yTp = Pool([f.sbuf("gyT%d" % i, [128, 4, 128], BF16) for i in range(2)])
        pTp = Pool([f.psum("gpT%d" % i, [128, 8, 128], BF16) for i in range(2)])
        for tt in range(NT):
            if tt < 2 and l == DEPTH - 1:
                continue
            r0 = tt * 128
            of_, ob_, rr = ofp.next(), obp.next(), rp.next()
            f.dma("sp", of_[:].re("p h v -> p (h v)"), og_d[0, r0:r0 + 128, :])
            f.dma("sp", ob_[:].re("p h v -> p (h v)"), og_d[1, r0:r0 + 128, :])
            f.dma("sp", rr[:], gr_d[r0:r0 + 128, :])
            f.tt(of_[:], of_[:], ob_[:], ALU.add)
            sq = sqp.next()
            f.tt(sq[:], of_[:], of_[:], ALU.mult, eng="pool")
            st = stp.next()
            f.op("dve", lambda e: e.tensor_reduce(out=st[:, 0:4].ap, in_=sq[:].ap, axis=mybir.AxisListType.X, op=ALU.add),
                 reads=[sq], writes=[st])
            f.act(st[:, 4:8], st[:, 0:4], AF.Sqrt, bias=EPS_T[:, 0:1], scale=1.0 / 128)
            f.recip(st[:, 4:8], st[:, 4:8])
            f.tt(of_[:], of_[:], st[:, 4:8].un(2).bc([128, 4, 128]), ALU.mult)
            f.tt(of_[:], of_[:], gon[:].un(1).bc([128, 4, 128]), ALU.mult, eng="pool")
            f.act(rr[:], rr[:], AF.Silu)
            y = yp.next()
            f.tt(y[:], of_[:].re("p h v -> p (h v)"), rr[:], ALU.mult)
            pT = pTp.next()
            for c in range(4):
                f.tr(pT[:, c, :], y[:, c * 128:(c + 1) * 128], ident[:])
            yT = yTp.next()
            f.cp(yT[:], pT[:, 0:4, :], eng="act")
            f.dma("sp", yglaT_d[:, r0:r0 + 128].re("(c p) t -> p c t", p=128), yT[:])
        f.release(m)

    def phase_merge(l, prefetch=None):
        m = f.mark()
        wo = [f.sbuf("wo%d" % i, [128, 4, D], BF16) for i in range(3)]
        for i, nm in enumerate(("w_o_mla", "w_o_gla", "w_o_hy")):
            f.dma("pool", wo[i][:], I[nm][l].re("(kc p) n -> p kc n", p=128))
        wout = f.sbuf("wout", [128, 8, D], BF16)
        for c in range(2):
            f.dma("pool", wout[:, :, c * 512:(c + 1) * 512], I["w_out"][l].re("(kc p) n -> p kc n", p=128)[:, :, c * 512:(c + 1) * 512])
        if prefetch is not None:
            prefetch()
        gx = f.sbuf("gx", [128, 2, D], F32)
        f.dma("sp", gx[:, 0, :], modrow[0, l, 0].pb(128))
        f.dma("sp", gx[:, 1, :], modrow[1, l, 0].pb(128))
        ybp = Pool([f.sbuf("mby%d" % i, [128, 3, 4, 512], BF16) for i in range(2)])
        gtp = Pool([f.sbuf("mgt%d" % i, [128, 3, 512], BF16) for i in range(3)])
        mTp = Pool([f.sbuf("mT%d" % i, [128, 8, 512], BF16) for i in range(2)])
        accp = Pool([f.sbuf("macc%d" % i, [128, 512], F32) for i in range(2)])
        tmpp = Pool([f.sbuf("mtmp%d" % i, [128, 512], F32) for i in range(3)])
        xp = Pool([f.sbuf("mx%d" % i, [128, D], F32) for i in range(3)])
        ps3 = [Pool([f.psum("mps%d_%d" % (i, j), [128, 512], F32) for j in range(2)]) for i in range(3)]
        pso = Pool([f.psum("mpo%d" % i, [128, 512], F32) for i in range(2)])
        gview = gatesT_d[:].re("(b c p) t -> p b c t", b=3, c=8, p=128)
        for bi, (t0, n) in enumerate(TB):
            if bi == 0 and l == DEPTH - 1:
                continue
            yb = ybp.next()
            for i, srcT in enumerate((ymlaT_d, yglaT_d, yhyT_d)):
                f.dma("sp", yb[:, i, :, 0:n], srcT[:, t0:t0 + n].re("(kc p) t -> p kc t", p=128))
            mT = mTp.next()
            for oc in range(8):
                gt = gtp.next()
                f.dma("sp", gt[:, :, 0:n], gview[:, :, oc, t0:t0 + n])
                pss = []
                for i in range(3):
                    ps = ps3[i].next()
                    for kc in range(4):
                        f.mm(ps[:, 0:n], wo[i][:, kc, oc * 128:(oc + 1) * 128], yb[:, i, kc, 0:n], start=kc == 0, stop=kc == 3)
                    pss.append(ps)
                acc = accp.next()
                t1 = tmpp.next()
                t2 = tmpp.next()
                f.tt(acc[:, 0:n], pss[0][:, 0:n], gt[:, 0, 0:n], ALU.mult)
                f.tt(t1[:, 0:n], pss[1][:, 0:n], gt[:, 1, 0:n], ALU.mult)
                f.tt(t2[:, 0:n], pss[2][:, 0:n], gt[:, 2, 0:n], ALU.mult)
                f.tt(acc[:, 0:n], acc[:, 0:n], t1[:, 0:n], ALU.add, eng="pool")
                f.tt(mT[:, oc, 0:n], acc[:, 0:n], t2[:, 0:n], ALU.add, eng="pool")
            s = 1 if bi == 0 else 0
            for ti in range(n // 128):
                tt = t0 // 128 + ti
                x = xp.next()
                f.dma("sp", x[:], xres_t[tt][:])
                for half in range(2):
                    ps = pso.next()
                    for kc in range(8):
                        f.mm(ps[:, :], mT[:, kc, ti * 128:(ti + 1) * 128], wout[:, kc, half * 512:(half + 1) * 512],
                             start=kc == 0, stop=kc == 7)
                    t1 = tmpp.next()
                    f.tt(t1[:], ps[:], gx[:, s, half * 512:(half + 1) * 512], ALU.mult)
                    f.tt(x[:, half * 512:(half + 1) * 512], x[:, half * 512:(half + 1) * 512], t1[:], ALU.add, eng="pool")
                f.dma("sp", xres_t[tt][:], x[:])
        f.release(m)

    def ffn_w1_alloc():
        return f.sbuf("ffw1", [128, 8, 4096], BF16)

    def ffn_w1_load(l, w1):
        w1v = I["ff_w1"][l].re("(kc p) n -> p kc n", p=128)
        for c in range(8):
            f.dma("pool", w1[:, :, c * 512:(c + 1) * 512], w1v[:, :, c * 512:(c + 1) * 512])

    def phase_ffn(l, w1):
        m = f.mark()
        w2 = f.sbuf("ffw2", [128, 32, D], BF16)
        w2v = I["ff_w2"][l].re("(kc p) n -> p kc n", p=128)
        for c in range(8):
            f.dma("pool", w2[:, c * 4:(c + 1) * 4, :], w2v[:, c * 4:(c + 1) * 4, :])
        gx = f.sbuf("fgx", [128, 2, D], F32)
        f.dma("sp", gx[:, 0, :], modrow[0, l, 1].pb(128))
        f.dma("sp", gx[:, 1, :], modrow[1, l, 1].pb(128))
        nctx = NormCtx()
        hTp = Pool([f.sbuf("fhT%d" % i, [128, 8, 256], BF16) for i in range(2)])
        aTp = Pool([f.sbuf("faT%d" % i, [128, 32, 256], BF16) for i in range(1)])
        rp = Pool([f.sbuf("fr%d" % i, [128, 256], F32) for i in range(2)])
        tmpp = Pool([f.sbuf("ftmp%d" % i, [128, 512], F32) for i in range(2)])
        xp = Pool([f.sbuf("fx%d" % i, [128, D], F32) for i in range(2)])
        psA = Pool([f.psum("fpa%d" % i, [128, 512], F32) for i in range(3)])
        pso = Pool([f.psum("fpo%d" % i, [128, 512], F32) for i in range(3)])
        for blk in range(T // 256):
            if blk == 0 and l == DEPTH - 1:
                continue
            s = 1 if blk == 0 else 0
            hTb = hTp.next()
            for ti in range(2):
                nctx.emit(l, 1, hTb, blk * 2 + ti, ti * 128)
            aT = aTp.next()
            for fc in range(32):
                ps = psA.next()
                for kc in range(8):
                    f.mm(ps[:, 0:256], w1[:, kc, fc * 128:(fc + 1) * 128], hTb[:, kc, :], start=kc == 0, stop=kc == 7)
                r = rp.next()
                f.act(r[:], ps[:, 0:256], AF.Relu)
                f.tt(aT[:, fc, :], r[:], r[:], ALU.mult, eng="pool" if fc % 2 else "dve")
            for ti in range(2):
                tt = blk * 2 + ti
                x = xp.next()
                f.dma("sp", x[:], xres_t[tt][:])
                for half in range(2):
                    ps = pso.next()
                    for fc in range(32):
                        f.mm(ps[:, :], aT[:, fc, ti * 128:(ti + 1) * 128], w2[:, fc, half * 512:(half + 1) * 512],
                             start=fc == 0, stop=fc == 31)
                    t1 = tmpp.next()
                    f.tt(t1[:], ps[:], gx[:, s, half * 512:(half + 1) * 512], ALU.mult)
                    f.tt(x[:, half * 512:(half + 1) * 512], x[:, half * 512:(half + 1) * 512], t1[:], ALU.add, eng="pool")
                f.dma("sp", xres_t[tt][:], x[:])
        f.release(m)

    def phase_final():
        m = f.mark()
        fg = f.sbuf("fing", [128, D], F32)
        f.dma("sp", fg[:], I["fin_g"][:])
        xp = Pool([f.sbuf("zx%d" % i, [128, D], F32) for i in range(3)])
        jp = Pool([f.sbuf("zj%d" % i, [128, D], F32) for i in range(2)])
        stp = Pool([f.sbuf("zst%d" % i, [128, 4], F32) for i in range(3)])
        for tt in range(2, NT):
            x = xp.next()
            j = jp.next()
            st = stp.next()
            f.dma("sp", x[:], xres_t[tt][:])
            f.act(j[:], x[:], AF.Square, accum=st[:, 0:1])
            f.act(st[:, 1:2], st[:, 0:1], AF.Sqrt, bias=EPS_T[:, 0:1], scale=1.0 / D)
            f.recip(st[:, 2:3], st[:, 1:2])
            f.act(j[:], x[:], AF.Identity, scale=st[:, 2:3])
            f.tt(x[:], j[:], fg[:], ALU.mult)
            f.dma("sp", out_y[(tt - 2) * 128:(tt - 1) * 128, :], x[:])
        f.release(m)

    def phase_hy(l, ctx_seg):
        m = f.mark()
        r0, nrow = (0, LC) if ctx_seg else (LC, L)
        na = nrow // 64
        NA = 2 * na
        NF = NA * 64
        NFA = NA // 2 + 1
        sfx = "_c" if ctx_seg else ""

        mA = f.mark()
        swt = f.sbuf("hsw", [128, 12, 4], F32)
        f.dma("sp", swt[:], I["hy_swb"][l])
        zp = Pool([f.sbuf("hz%d" % i, [128, L], BF16) for i in range(2)])
        accp = Pool([f.sbuf("hacc%d" % i, [128, L], F32) for i in range(2)])
        op_ = Pool([f.sbuf("hso%d" % i, [128, L], BF16) for i in range(2)])
        for ch in range(12):
            z = zp.next()
            acc = accp.next()
            o = op_.next()
            f.dma("sp", z[:, 0:nrow], zhyT_d[ch * 128:(ch + 1) * 128, r0:r0 + nrow])
            f.act(acc[:, 0:nrow], z[:, 0:nrow], AF.Identity, bias=swt[:, ch, 3:4], scale=swt[:, ch, 1:2])
            f.stt(acc[:, 1:nrow], z[:, 0:nrow - 1], swt[:, ch, 0:1], acc[:, 1:nrow], ALU.mult, ALU.add)
            f.stt(acc[:, 0:nrow - 1], z[:, 1:nrow], swt[:, ch, 2:3], acc[:, 0:nrow - 1], ALU.mult, ALU.add)
            f.cp(o[:, 0:nrow], acc[:, 0:nrow], eng="act")
            f.dma("sp", scT_d[ch * 128:(ch + 1) * 128, r0:r0 + nrow], o[:, 0:nrow])
        f.release(mA)

        F1 = f.sbuf("hF1", [NA, 3 * NFA], BF16)
        E2r = f.sbuf("hE2r", [128, NFA, 128], BF16)
        E2i = f.sbuf("hE2i", [128, NFA, 128], BF16)
        f.dma("sp", F1[:], I["hy_F1" + sfx][:])
        f.dma("sp", E2r[:], I["hy_E2r" + sfx][:])
        f.dma("sp", E2i[:], I["hy_E2i" + sfx][:])
        psY = Pool([f.psum("hpY%d" % i, [128, 512], F32) for i in range(2)])
        psX = Pool([f.psum("hpX%d" % i, [128, 2, 8, 32], F32) for i in range(2)])
        pst = Pool([f.psum("hpt%d" % i, [128, 8, 64], BF16) for i in range(1)])

        def spectrum_gen(ut, Kp, consume, Y):
            for q in range(32):
                ps = psY.next()
                f.mm(ps[:, 0:3 * NFA], ut[0:Kp, q, :], F1[0:Kp, :])
                f.cp(Y[:, q, :], ps[:, 0:3 * NFA], eng="act" if q % 2 else "dve")
                if q % 4 == 3:
                    yield
            for fa0 in range(0, NFA, 8):
                nfa = min(8, NFA - fa0)
                px = psX.next()
                pr = px[:, 0]
                pi = px[:, 1]
                for i in range(nfa):
                    fa = fa0 + i
                    f.mm(pr[:, i, :], E2r[:, fa, :], Y[:, :, fa], start=True, stop=False)
                    f.mm(pr[:, i, :], E2i[:, fa, :], Y[:, :, 2 * NFA + fa], start=False, stop=True)
                for i in range(nfa):
                    fa = fa0 + i
                    f.mm(pi[:, i, :], E2i[:, fa, :], Y[:, :, fa], start=True, stop=False)
                    f.mm(pi[:, i, :], E2r[:, fa, :], Y[:, :, NFA + fa], start=False, stop=True)
                consume(fa0, nfa, pr, pi)
                yield

        mB = f.mark()
        hd2 = f.sbuf("hhd2", [64, NF], F32)
        fw3 = f.sbuf("hfw3", [64, 2048], F32)
        fb3T = f.sbuf("hfb3T", [128, 16], F32)
        nd = f.sbuf("hnd", [128, 4], F32)
        hbT = f.sbuf("hhbT", [128, 2, 4], F32)
        f.dma("sp", fw3[:], I["hy_f_w3"][l])
        f.dma("sp", fb3T[:], I["hy_fb3T"][l])
        f.dma("sp", nd[:], I["hy_negdelta"][:])
        f.dma("sp", hbT[:], I["hy_biasT"][l])
        mB1 = f.mark()
        featT = f.sbuf("hfeat", [33, NF], F32)
        f.dma("sp", featT[:], I["hy_featT" + sfx][:])
        fw1 = f.sbuf("hfw1", [33, 64], F32)
        fw2 = f.sbuf("hfw2", [64, 64], F32)
        fb12 = f.sbuf("hfb12", [64, 2], F32)
        f.dma("sp", fw1[:], I["hy_f_w1"][l])
        f.dma("sp", fw2[:], I["hy_f_w2"][l])
        f.dma("sp", fb12[:], I["hy_fb12"][l])
        hd1p = Pool([f.sbuf("hhd1_%d" % i, [64, 512], F32) for i in range(2)])
        ap_ = Pool([f.sbuf("ha%d" % i, [64, 512], F32) for i in range(2)])
        m1p = Pool([f.sbuf("hm1_%d" % i, [64, 512], F32) for i in range(2)])
        m2p = Pool([f.sbuf("hm2_%d" % i, [64, 512], F32) for i in range(2)])
        psm = Pool([f.psum("hpm%d" % i, [128, 512], F32) for i in range(3)])
        nblk = NF // 512

        def sin_wrap(dst, ps, bias):
            a = ap_.next()
            m1 = m1p.next()
            m2 = m2p.next()
            f.act(a[:], ps[0:64, :], AF.Identity, bias=bias)
            f.ts(m1[:], a[:], math.pi, ALU.is_gt, 2 * math.pi, ALU.mult)
            f.ts(m2[:], a[:], -math.pi, ALU.is_lt, 2 * math.pi, ALU.mult)
            f.tt(a[:], a[:], m1[:], ALU.subtract)
            f.tt(a[:], a[:], m2[:], ALU.add)
            f.act(dst, a[:], AF.Sin)

        for blk in range(nblk):
            ps = psm.next()
            f.mm(ps[0:64, :], fw1[:], featT[:, blk * 512:(blk + 1) * 512])
            hd1 = hd1p.next()
            sin_wrap(hd1[:], ps, fb12[:, 0:1])
            ps2 = psm.next()
            f.mm(ps2[0:64, :], fw2[:], hd1[:])
            sin_wrap(hd2[:, blk * 512:(blk + 1) * 512], ps2, fb12[:, 1:2])
        f.release(mB1)
        tn2 = f.sbuf("htn2", [128, NF], F32)
        f.dma("sp", tn2[:], I["hy_tn2" + sfx][0].pb(128))
        kT = f.sbuf("hkT", [128, NF], F32)
        kTb = f.sbuf("hkTb", [128, NF], BF16)
        kbp = Pool([f.sbuf("hkb%d" % i, [128, 512], F32) for i in range(2)])
        wbp = Pool([f.sbuf("hwb%d" % i, [128, 512], F32) for i in range(2)])
        jk = f.sbuf("hjk", [128, 512], BF16)
        asum = f.sbuf("hasum", [128, 20], F32)
        kup = Pool([f.sbuf("hku%d" % i, [128, 32, 128], BF16) for i in range(1)])
        Ybc = f.sbuf("hYbc", [128, 32, 3 * NFA], BF16)
        Hst = Pool([f.sbuf("hHst%d" % i, [128, 2, NFA, 32], BF16) for i in range(1)])
        psm = Pool([f.psum("hpm2_%d" % i, [128, 512], F32) for i in range(2)])
        bs = min(512, NF // 2)
        nb2 = NF // bs
        for cc in range(4):
            for n_ in range(2):
                for blk in range(nb2):
                    dr = 0 if blk < nb2 // 2 else 1
                    col = (dr * 2 + n_) * 4 + cc
                    cs_ = slice(blk * bs, (blk + 1) * bs)
                    ps = psm.next()
                    f.mm(ps[:, 0:bs], fw3[:, col * 128:(col + 1) * 128], hd2[:, cs_])
                    kb = kbp.next()
                    wb = wbp.next()
                    f.act(wb[:, 0:bs], tn2[:, cs_], AF.Exp, scale=nd[:, cc:cc + 1])
                    f.stt(kT[:, cs_], ps[:, 0:bs], fb3T[:, col:col + 1], wb[:, 0:bs], ALU.add, ALU.mult)
                    f.act(jk[:, 0:bs], kT[:, cs_], AF.Abs, accum=asum[:, blk:blk + 1])
                f.op("dve", lambda e: e.tensor_reduce(out=asum[:, 16:17].ap, in_=asum[:, 0:nb2].ap, axis=mybir.AxisListType.X, op=ALU.add),
                     reads=[asum], writes=[asum])
                f.recip(asum[:, 17:18], asum[:, 16:17])
                f.act(kTb[:], kT[:], AF.Identity, scale=asum[:, 17:18])
                f.ts(kTb[:, 0:1], kT[:, 0:1], asum[:, 17:18], ALU.mult, hbT[:, n_, cc:cc + 1], ALU.add)
                for gg in range(2):
                    g = cc * 2 + gg
                    kv = kTb[gg * 64:(gg + 1) * 64, :].re("c (a b) -> c b a", b=64)
                    ut = kup.next()
                    for b0 in range(0, 64, 8):
                        pt = pst.next()
                        for i in range(8):
                            f.tr(pt[0:NA, i, :], kv[:, b0 + i, :], ident[gg * 64:(gg + 1) * 64, gg * 64:(gg + 1) * 64])
                        f.cp(ut[0:NA].re("a q (b cp) -> a b q cp", cp=2)[:, b0:b0 + 8], pt[0:NA, :, :].re("a b (q cp) -> a b q cp", cp=2),
                             eng="act" if (b0 // 8) % 2 else "dve")
                    hs = Hst.next()

                    def cons(fa0, nfa, pr, pi, hs=hs):
                        f.cp(hs[:, 0, fa0:fa0 + nfa, :], pr[:, 0:nfa, :], eng="act")
                        f.cp(hs[:, 1, fa0:fa0 + nfa, :], pi[:, 0:nfa, :], eng="dve")
                    for _ in spectrum_gen(ut, NA, cons, Ybc):
                        pass
                    f.dma("sp", H_d[n_, g, :, 0:2 * NFA * 32], hs[:].re("p r f q -> p (r f q)"))
        f.release(mB)

        CA = f.sbuf("hCA", [128, 3, 128], BF16)
        DBr = f.sbuf("hDBr", [NFA, 64, na], BF16)
        DBni = f.sbuf("hDBni", [NFA, 64, na], BF16)
        f.dma("sp", CA[:], I["hy_CA"][:])
        f.dma("sp", DBr[:], I["hy_DBr" + sfx][:])
        f.dma("sp", DBni[:], I["hy_DBni" + sfx][:])
        tp = Pool([f.sbuf("ht%d" % i, [128, 8, 32], F32) for i in range(8)])
        psZ = Pool([f.psum("hpZ%d" % i, [128, 4, 128], F32) for i in range(2)])
        psO = Pool([f.psum("hpO%d" % i, [128, 8, 64], F32) for i in range(1)])
        RES = []
        for ci in range(2):
            RES.append(dict(
                uT=f.sbuf("huT%d" % ci, [64, L], BF16), gT=f.sbuf("hgT%d" % ci, [64, L], BF16),
                u=f.sbuf("hu%d" % ci, [64, 32, 128], BF16), H=f.sbuf("hH%d" % ci, [128, 2, NFA, 32], BF16),
                P=f.sbuf("hP%d" % ci, [128, 2, 32, NFA], BF16), Z0=f.sbuf("hZ0_%d" % ci, [NFA, 2, 64, 64], BF16),
                Y=f.sbuf("hYD%d" % ci, [128, 32, 3 * NFA], BF16)))

        def conv_group(n_, g, R):
            srcT = scT_d[1024:1536] if n_ == 0 else y1T_d
            gateT = scT_d[0:512] if n_ == 0 else scT_d[512:1024]
            dstT = y1T_d if n_ == 0 else yhyT_d
            uT, gT, u, H, P, Z0, Y = R["uT"], R["gT"], R["u"], R["H"], R["P"], R["Z0"], R["Y"]
            f.dma("sp", uT[:, 0:nrow], srcT[g * 64:(g + 1) * 64, r0:r0 + nrow])
            f.dma("sp", gT[:, 0:nrow], gateT[g * 64:(g + 1) * 64, r0:r0 + nrow])
            f.dma("sp", H[:].re("p r f q -> p (r f q)"), H_d[n_, g, :, 0:2 * NFA * 32])
            yield
            uv = uT[:, 0:nrow].re("c (a b) -> c b a", b=64)
            for b0 in range(0, 64, 8):
                pt = pst.next()
                for i in range(8):
                    f.tr(pt[0:na, i, :], uv[:, b0 + i, :], ident[0:64, 0:64])
                f.cp(u[0:na].re("a q (b cp) -> a b q cp", cp=2)[:, b0:b0 + 8], pt[0:na, :, :].re("a b (q cp) -> a b q cp", cp=2),
                     eng="act" if (b0 // 8) % 2 else "dve")
                yield

            def cons(fa0, nfa, pr, pi):
                t1, t2, t3, t4 = tp.next(), tp.next(), tp.next(), tp.next()
                f.tt(t1[:, 0:nfa], pr[:, 0:nfa, :], H[:, 0, fa0:fa0 + nfa, :], ALU.mult)
                f.tt(t2[:, 0:nfa], pi[:, 0:nfa, :], H[:, 1, fa0:fa0 + nfa, :], ALU.mult)
                f.tt(t3[:, 0:nfa], pr[:, 0:nfa, :], H[:, 1, fa0:fa0 + nfa, :], ALU.mult)
                f.tt(t4[:, 0:nfa], pi[:, 0:nfa, :], H[:, 0, fa0:fa0 + nfa, :], ALU.mult)
                f.tt(P[:, 0, :, fa0:fa0 + nfa].re("p q f -> p f q"), t1[:, 0:nfa], t2[:, 0:nfa], ALU.subtract, eng="pool")
                f.tt(P[:, 1, :, fa0:fa0 + nfa].re("p q f -> p f q"), t3[:, 0:nfa], t4[:, 0:nfa], ALU.add, eng="pool")
            for _ in spectrum_gen(u, na, cons, Y):
                yield
            for q0 in range(0, 32, 4):
                zr = psZ.next()
                zi = psZ.next()
                for i in range(4):
                    q = q0 + i
                    f.mm(zr[0:NFA, i, :], P[:, 0, q, :], CA[:, 0, :], start=True, stop=False)
                    f.mm(zr[0:NFA, i, :], P[:, 1, q, :], CA[:, 2, :], start=False, stop=True)
                for i in range(4):
                    q = q0 + i
                    f.mm(zi[0:NFA, i, :], P[:, 0, q, :], CA[:, 1, :], start=True, stop=False)
                    f.mm(zi[0:NFA, i, :], P[:, 1, q, :], CA[:, 0, :], start=False, stop=True)
                f.cp(Z0[:, 0].re("f b (q cp) -> f q b cp", cp=2)[:, q0:q0 + 4], zr[0:NFA, :, :].re("f q (b cp) -> f q b cp", cp=2), eng="act")
                f.cp(Z0[:, 1].re("f b (q cp) -> f q b cp", cp=2)[:, q0:q0 + 4], zi[0:NFA, :, :].re("f q (b cp) -> f q b cp", cp=2), eng="dve")
                yield
            yv = uT[:, 0:nrow].re("c (a b) -> c a b", b=64)
            gv = gT[:, 0:nrow].re("c (a b) -> c a b", b=64)
            for b0 in range(0, 64, 8):
                po_ = psO.next()
                for i in range(8):
                    b = b0 + i
                    f.mm(po_[0:64, i, 0:na], Z0[:, 0, b, :], DBr[:, b, :], start=True, stop=False)
                    f.mm(po_[0:64, i, 0:na], Z0[:, 1, b, :], DBni[:, b, :], start=False, stop=True)
                f.tt(yv[:, :, b0:b0 + 8], po_[0:64, :, 0:na].re("c b a -> c a b"), gv[:, :, b0:b0 + 8], ALU.mult)
                yield
            f.dma("sp", dstT[g * 64:(g + 1) * 64, r0:r0 + nrow], uT[:, 0:nrow])
            yield

        for n_ in range(2):
            for g0 in range(0, 8, 2):
                live = [conv_group(n_, g0, RES[0]), conv_group(n_, g0 + 1, RES[1])]
                while live:
                    for g_ in list(live):
                        try:
                            next(g_)
                        except StopIteration:
                            live.remove(g_)
            f.barrier()
        f.release(m)

    ONE_T = f.sbuf("one_t", [128, 1], F32)
    f.memset(ONE_T[:], 1.0)

    phase_mod()
    done = False
    if "only_hy" in dbg:
        phase_hy(0, False)
        phase_hy(0, True)
        f.barrier()
        f.barrier(["sp"])
        f.release(0)
        return nc
    for l in range(DEPTH):
        if "from_merge" not in dbg:
            mk = f.mark()
            hT = f.sbuf("hT", [128, 8, T], BF16)
            norm_tiles(l, 0, hT, range(NT))
            if hT_d is not None and l == 0:
                for kc in range(8):
                    f.dma("sp", hT_d[kc * 128:(kc + 1) * 128, :], hT[:, kc, :])
            if stop_after == "norm":
                f.release(mk)
                break
            phase_proj(l, hT)
            f.release(mk)
            if stop_after == "proj":
                break
            if "skip_att" not in dbg:
                phase_att(l)
            if stop_after == "att":
                break
            if "skip_gla" not in dbg:
                phase_gla(l)
            if stop_after == "gla":
                break
            if "skip_hy" not in dbg:
                phase_hy(l, False)
                if l < DEPTH - 1:
                    phase_hy(l, True)
            if stop_after == "hy":
                break
        mk2 = f.mark()
        w1 = ffn_w1_alloc()
        phase_merge(l, prefetch=lambda: ffn_w1_load(l, w1))
        if stop_after == "merge":
            break
        phase_ffn(l, w1)
        f.release(mk2)
        if stop_after == "ffn":
            break
    else:
        done = True
    f.barrier()
    if done:
        phase_final()
    f.barrier(["sp"])
    f.release(0)
    return nc


def _fm(v, chunks):
    return np.ascontiguousarray(np.asarray(v, np.float32).reshape(chunks, 128).T)


def make_in_maps(inputs):
    g = {k: np.asarray(v) for k, v in inputs.items()}
    hc = host_constants()
    perm = rope_swap_perm()
    sh = {}
    sh["ada_w"] = np.ascontiguousarray(g["ada_w"], np.float32)
    sh["ada_bf"] = np.ascontiguousarray(np.stack([_fm(g["ada_b"][l], 48) for l in range(DEPTH)], 1))
    sh["ada_br"] = np.ascontiguousarray(np.broadcast_to(g["ada_b"][None], (2, DEPTH, 6 * D)), np.float32)
    sh["n1g"] = np.ascontiguousarray(np.stack([_fm(g["norm1_g"][l], 8) for l in range(DEPTH)], 1))
    sh["n2g"] = np.ascontiguousarray(np.stack([_fm(g["norm2_g"][l], 8) for l in range(DEPTH)], 1))
    sh["w_in"] = np.ascontiguousarray(g["w_in"], np.float32)
    wkr = np.zeros((DEPTH, D, 2, 96), np.float32)
    wkr[:, :, 0, 64:96] = g["w_in"][:, :, 384:416]
    wkr[:, :, 1, 64:96] = g["w_in"][:, :, 384:416][:, :, perm]
    sh["w_kr2"] = wkr
    sh["qng"] = np.ascontiguousarray(np.stack([_fm(g["mla_q_norm"][l], 2) for l in range(DEPTH)], 1))
    sh["kvng"] = np.ascontiguousarray(np.stack([g["mla_kv_norm"][l] for l in range(DEPTH)], 1), np.float32)
    sh["w_uq"] = np.ascontiguousarray(g["mla_w_uq"], np.float32)
    wsw = g["mla_w_uq"].reshape(DEPTH, 256, 8, 96).copy()
    wsw[:, :, :, 64:96] = wsw[:, :, :, 64:96][:, :, :, perm]
    sh["w_uq_sw"] = np.ascontiguousarray(wsw.reshape(DEPTH, 256, 768), np.float32)
    ukv = g["mla_w_ukv"].reshape(DEPTH, 128, 8, 128)
    sh["w_ukv_k"] = np.ascontiguousarray(ukv[:, :, :, 0:64].reshape(DEPTH, 128, 512), np.float32)
    sh["w_ukv_v"] = np.ascontiguousarray(ukv[:, :, :, 64:128].reshape(DEPTH, 128, 512), np.float32)
    sh["w_a2"] = np.ascontiguousarray(np.concatenate([g["gla_w_a2"][:, 0], g["gla_w_a2"][:, 1]], -1), np.float32)
    sh["b_a"] = np.ascontiguousarray(np.concatenate([g["gla_b_a"][:, 0], g["gla_b_a"][:, 1]], -1)[:, None, :], np.float32)
    for nm_ in ("w_o_mla", "w_o_gla", "w_o_hy", "w_out", "ff_w1", "ff_w2"):
        sh[nm_] = np.ascontiguousarray(g[nm_], np.float32)
    sw = g["hy_short_w"]
    swb = np.concatenate([sw, g["hy_short_b"][:, None, :]], 1)
    sh["hy_swb"] = np.ascontiguousarray(swb.reshape(DEPTH, 4, 12, 128).transpose(0, 3, 2, 1), np.float32)
    sh["hy_f_w1"] = np.ascontiguousarray(g["hy_f_w1"], np.float32)
    sh["hy_f_w2"] = np.ascontiguousarray(g["hy_f_w2"], np.float32)
    sh["hy_f_w3"] = np.ascontiguousarray(g["hy_f_w3"], np.float32)
    sh["hy_fb12"] = np.ascontiguousarray(np.stack([g["hy_f_b1"], g["hy_f_b2"]], -1), np.float32)
    sh["hy_fb3T"] = np.ascontiguousarray(g["hy_f_b3"].reshape(DEPTH, 16, 128).transpose(0, 2, 1), np.float32)
    sh["hy_biasT"] = np.ascontiguousarray(g["hy_bias"].reshape(DEPTH, 2, 4, 128).transpose(0, 3, 1, 2), np.float32)
    sh["fin_g"] = np.ascontiguousarray(np.broadcast_to(g["final_norm_g"][None, :], (128, D)), np.float32)
    sh["gla_on"] = np.ascontiguousarray(np.broadcast_to(g["gla_out_norm"][:, None, :], (DEPTH, 128, 128)), np.float32)
    for k, v in hc.items():
        sh[k] = v
    maps = []
    for b in range(8):
        m = dict(sh)
        m["xc"] = np.ascontiguousarray(np.concatenate([g["ctx"][b], g["x"][b]], 0), np.float32)
        m["cs"] = np.ascontiguousarray(np.stack([_fm(g["c"][b], 8), _fm(g["c_ctx"], 8)], -1))
        maps.append(m)
    return maps


_NC_CACHE = {}


def kernel(**inputs):
    if "nc" not in _NC_CACHE:
        _NC_CACHE["nc"] = build()
    nc = _NC_CACHE["nc"]
    maps = make_in_maps(inputs)
    res = run_bass_kernel_spmd(nc, maps, core_ids=list(range(8)))
    return np.stack([np.asarray(r["y"], np.float32) for r in res.results], 0)
```

```python
import math
import numpy as np
import ml_dtypes
import concourse.bass as bass
import concourse.mybir as mybir
from concourse.bass_utils import run_bass_kernel_spmd

F32 = mybir.dt.float32
BF16 = mybir.dt.bfloat16
AF = mybir.ActivationFunctionType
ALU = mybir.AluOpType

D = 1024
L = 4096
LC = 256
T = L + LC
NT = T // 128
DEPTH = 2
DIN = 6592
EPS = 1e-6
MLA_SCALE = 96 ** -0.5
TB = [(0, 256)] + [(256 + 512 * i, 512) for i in range(8)]
NFFT = 8192


class V:
    __slots__ = ("b", "ap")

    def __init__(self, b, ap):
        self.b = b
        self.ap = ap

    def __getitem__(self, idx):
        return V(self.b, self.ap[idx])

    def re(self, pat, **kw):
        return V(self.b, self.ap.rearrange(pat, **kw))

    def bc(self, shape):
        return V(self.b, self.ap.broadcast_to(list(shape)))

    def un(self, axis):
        return V(self.b, self.ap.unsqueeze(axis))

    def pb(self, n):
        return V(self.b, self.ap.partition_broadcast(n))


class Buf:
    __slots__ = ("t", "name", "lw", "rd", "psum", "dram")

    def __init__(self, t, name, psum=False, dram=False):
        self.t = t
        self.name = name
        self.lw = []
        self.rd = []
        self.psum = psum
        self.dram = dram

    def __getitem__(self, idx):
        return V(self, self.t[idx])

    @property
    def v(self):
        return V(self, self.t[:] if not hasattr(self.t, "ap") or True else self.t)


class FW:
    NDMA_SEM = 36
    NDMA_HW = 24

    def __init__(self, nc):
        self.nc = nc
        self.eng = {"pe": nc.tensor, "act": nc.scalar, "dve": nc.vector, "pool": nc.gpsimd, "sp": nc.sync}
        self.sem = {}
        self.cnt = {}
        for e in self.eng:
            self.sem[e] = nc.alloc_semaphore("s_" + e)
            self.cnt[e] = 0
        self.dsem = [nc.alloc_semaphore("d%d" % i) for i in range(self.NDMA_SEM)]
        self.dcnt = [0] * self.NDMA_SEM
        self.dnext = 0
        self.dnext_sw = 0
        self.seen = {e: {} for e in self.eng}
        self.ninst = 0
        self._ctx = []
        self._uid = 0
        self.deferred = []

    def _nm(self, name):
        self._uid += 1
        return "%s_%d" % (name, self._uid)

    def sbuf(self, name, shape, dt):
        g = self.nc.sbuf_tensor(self._nm(name), list(shape), dt)
        t = g.__enter__()
        self._ctx.append(g)
        return Buf(t, name)

    def psum(self, name, shape, dt=F32):
        g = self.nc.psum_tensor(self._nm(name), list(shape), dt)
        t = g.__enter__()
        self._ctx.append(g)
        return Buf(t, name, psum=True)

    def dram(self, name, shape, dt, kind="Internal"):
        t = self.nc.dram_tensor(name, list(shape), dt, kind=kind)
        return Buf(t.ap(), name, dram=True)

    def _wait(self, e, tok):
        if tok is None:
            return
        key, val = tok
        if e == "pe" and key == "pe":
            return
        if self.seen[e].get(key, 0) >= val:
            return
        self.seen[e][key] = val
        sem = self.sem[key] if isinstance(key, str) else self.dsem[key]
        self.eng[e].wait_ge(sem, val)

    def _deps(self, e, reads, writes, dma_write=False):
        for b in reads:
            for tok in b.lw:
                self._wait(e, tok)
        for b in writes:
            if not (dma_write and all(isinstance(t[0], int) for t in b.lw)):
                for tok in b.lw:
                    self._wait(e, tok)
            for tok in b.rd:
                self._wait(e, tok)

    @staticmethod
    def _compact(toks):
        best = {}
        for k, v in toks:
            if best.get(k, 0) < v:
                best[k] = v
        return list(best.items())

    def _commit(self, tok, reads, writes, dma_write=False):
        for b in reads:
            b.rd.append(tok)
            if len(b.rd) > 48:
                b.rd = self._compact(b.rd)
        for b in writes:
            if dma_write and b.lw and all(isinstance(t[0], int) for t in b.lw):
                b.lw.append(tok)
                if len(b.lw) > 48:
                    b.lw = self._compact(b.lw)
            else:
                b.lw = [tok]
            b.rd = []

    def flush(self):
        d, self.deferred = self.deferred, []
        for (q, out, in_, kw) in d:
            self._dma_now(q, out, in_, **kw)

    def op(self, e, fn, reads=(), writes=()):
        if self.deferred:
            self.flush()
        rd = [b for b in reads if not b.psum]
        wr = list(writes) + [b for b in reads if b.psum]
        self._deps(e, rd, wr)
        ins = fn(self.eng[e])
        self.cnt[e] += 1
        ins.then_inc(self.sem[e], 1)
        self._commit((e, self.cnt[e]), rd, wr)
        self.ninst += 1
        return ins

    def dma(self, q, out, in_, **kw):
        if out.b.dram and not in_.b.dram:
            self.deferred.append((q, out, in_, kw))
            return
        for (_, so, si, _) in self.deferred:
            if so.b is in_.b or si.b is out.b or so.b is out.b:
                self.flush()
                break
        self._dma_now(q, out, in_, **kw)

    def _dma_now(self, q, out, in_, **kw):
        if q == "pool":
            slot = self.NDMA_HW + self.dnext_sw
            self.dnext_sw = (self.dnext_sw + 1) % (self.NDMA_SEM - self.NDMA_HW)
        else:
            slot = self.dnext
            self.dnext = (self.dnext + 1) % self.NDMA_HW
        if self.dcnt[slot] > 0:
            self._wait(q, (slot, self.dcnt[slot]))
        self._deps(q, [in_.b], [out.b], dma_write=True)
        ins = self.eng[q].dma_start(out=out.ap, in_=in_.ap, **kw)
        self.dcnt[slot] += 16
        ins.then_inc(self.dsem[slot], 16)
        self._commit((slot, self.dcnt[slot]), [in_.b], [out.b], dma_write=True)
        self.ninst += 1

    def barrier(self, engines=None):
        self.flush()
        for e in (engines or self.eng):
            for e2 in self.eng:
                if e2 != e and self.cnt[e2] > 0:
                    self._wait(e, (e2, self.cnt[e2]))
            for s in range(self.NDMA_SEM):
                if self.dcnt[s] > 0:
                    self._wait(e, (s, self.dcnt[s]))

    def mark(self):
        return len(self._ctx)

    def release(self, mark):
        self.barrier()
        while len(self._ctx) > mark:
            self._ctx.pop().__exit__(None, None, None)

    def mm(self, out, lhsT, rhs, start=True, stop=True):
        return self.op("pe", lambda e: e.matmul(out.ap, lhsT=lhsT.ap, rhs=rhs.ap, start=start, stop=stop),
                       reads=[lhsT.b, rhs.b], writes=[out.b])

    def tr(self, out, in_, ident):
        return self.op("pe", lambda e: e.transpose(out.ap, in_.ap, ident.ap), reads=[in_.b, ident.b], writes=[out.b])

    def act(self, out, in_, func, bias=None, scale=None, accum=None, eng="act"):
        kw = {}
        rd = [in_.b]
        wr = [out.b]
        if bias is not None:
            if isinstance(bias, V):
                kw["bias"] = bias.ap
                rd.append(bias.b)
            else:
                kw["bias"] = bias
        if scale is not None:
            if isinstance(scale, V):
                kw["scale"] = scale.ap
                rd.append(scale.b)
            else:
                kw["scale"] = scale
        if accum is not None:
            kw["accum_out"] = accum.ap
            wr.append(accum.b)
        return self.op("act", lambda e: e.activation(out=out.ap, in_=in_.ap, func=func, **kw), reads=rd, writes=wr)

    def tt(self, out, a, b, op, eng="dve"):
        return self.op(eng, lambda e: e.tensor_tensor(out=out.ap, in0=a.ap, in1=b.ap, op=op),
                       reads=[a.b, b.b], writes=[out.b])

    def ts(self, out, a, s1, op0, s2=None, op1=None, eng="dve"):
        rd = [a.b]
        s1a = s1
        s2a = s2
        if isinstance(s1, V):
            rd.append(s1.b)
            s1a = s1.ap
        if isinstance(s2, V):
            rd.append(s2.b)
            s2a = s2.ap
        kw = {}
        if op1 is not None:
            kw["op1"] = op1
        return self.op(eng, lambda e: e.tensor_scalar(out=out.ap, in0=a.ap, scalar1=s1a, scalar2=s2a, op0=op0, **kw),
                       reads=rd, writes=[out.b])

    def stt(self, out, a, s, b, op0, op1, eng="dve"):
        rd = [a.b, b.b]
        sa = s
        if isinstance(s, V):
            rd.append(s.b)
            sa = s.ap
        return self.op(eng, lambda e: e.scalar_tensor_tensor(out=out.ap, in0=a.ap, scalar=sa, in1=b.ap, op0=op0, op1=op1),
                       reads=rd, writes=[out.b])

    def cp(self, out, in_, eng="dve"):
        if eng == "act":
            return self.op("act", lambda e: e.copy(out=out.ap, in_=in_.ap), reads=[in_.b], writes=[out.b])
        return self.op(eng, lambda e: e.tensor_copy(out=out.ap, in_=in_.ap), reads=[in_.b], writes=[out.b])

    def memset(self, out, val, eng="pool"):
        return self.op(eng, lambda e: e.memset(out.ap, val), writes=[out.b])

    def recip(self, out, in_):
        return self.op("dve", lambda e: e.reciprocal(out=out.ap, in_=in_.ap), reads=[in_.b], writes=[out.b])


class Pool:
    def __init__(self, bufs):
        self.bufs = bufs
        self.i = 0

    def next(self):
        b = self.bufs[self.i % len(self.bufs)]
        self.i += 1
        return b


def _bf(a):
    return np.ascontiguousarray(a.astype(ml_dtypes.bfloat16))


def host_constants():
    c = {}
    c["ident_bf"] = _bf(np.eye(128, dtype=np.float32))
    c["ident_f"] = np.eye(128, dtype=np.float32)
    c["ones_f"] = np.ones((128, 128), np.float32)
    rows = L // 64
    row = np.repeat(np.arange(rows, dtype=np.float32), 64)
    col = np.tile(np.arange(64, dtype=np.float32), rows)
    inv = (10000.0 ** (-np.arange(8, dtype=np.float32) / 8)).astype(np.float32)
    ang = np.concatenate([row[:, None] * inv, col[:, None] * inv], axis=-1)
    cos, sin = np.cos(ang), np.sin(ang)
    cosT = np.ones((96, T), np.float32)
    sinT = np.zeros((96, T), np.float32)
    for r in range(32):
        g, j = r // 16, r % 16
        half, i = j // 8, j % 8
        cosT[64 + r, LC:] = cos[:, g * 8 + i]
        sinT[64 + r, LC:] = (-sin[:, g * 8 + i]) if half == 0 else sin[:, g * 8 + i]
    c["cosT"] = cosT
    c["sinT"] = sinT
    i_ = np.arange(128)[None, :]
    j_ = np.arange(128)[:, None]
    Mf = ((j_ >= 64) & (j_ <= i_)).astype(np.float32) - ((j_ > i_) & (j_ <= 63)).astype(np.float32)
    Mb = ((j_ >= i_) & (j_ <= 63)).astype(np.float32) - ((j_ >= 64) & (j_ < i_)).astype(np.float32)
    c["gla_M"] = np.stack([Mf, Mb], 1).astype(np.float32)
    c["gla_mask"] = np.stack([(j_ <= i_), (j_ >= i_)], 1).astype(np.float32)
    ind = np.zeros((128, 2), np.float32)
    ind[:64, 0] = 1
    ind[64:, 1] = 1
    c["gla_ind"] = ind
    deltas = np.linspace(math.log(1e-2) / 0.3, math.log(1e-2) / 1.5, 512)
    fb = np.linspace(1e-4, 15.0, 16)
    b_ = np.arange(64)
    fbb = np.arange(64)
    for sfx, Ls in (("", L), ("_c", LC)):
        NF = 2 * Ls
        NA = NF // 64
        NFA = NA // 2 + 1
        pos = np.arange(Ls, dtype=np.float64)
        tn = pos / max(Ls - 1, 1)
        ang = (2.0 * math.pi / Ls) * pos[:, None] * fb
        feat = np.concatenate([tn[:, None], np.cos(ang), np.sin(ang)], -1)
        win = np.exp(-tn[:, None] * np.abs(deltas))
        feat2 = np.zeros((NF, 33))
        win2 = np.zeros((NF, 512))
        feat2[:Ls] = feat
        win2[:Ls] = win
        idx = np.arange(NF - Ls + 1, NF)
        feat2[idx] = feat[NF - idx]
        win2[idx] = win[NF - idx]
        c["hy_featT" + sfx] = np.ascontiguousarray(feat2.T.astype(np.float32))
        tn2 = np.full((1, NF), 1.0e4)
        tn2[0, :Ls] = tn
        tn2[0, idx] = tn[NF - idx]
        c["hy_tn2" + sfx] = tn2.astype(np.float32)
        a_ = np.arange(NA)[:, None]
        fa = np.arange(NFA)[None, :]
        th = 2 * math.pi * ((fa * a_) % NA) / NA
        c["hy_F1" + sfx] = _bf(np.concatenate([np.cos(th), -np.sin(th), np.sin(th)], 1))
        E2r = np.zeros((64, 2, NFA, 64, 2))
        E2i = np.zeros((64, 2, NFA, 64, 2))
        ph = 2 * math.pi * (((np.arange(NFA)[None, :, None] + NA * fbb[None, None, :]) * b_[:, None, None]) % NF) / NF
        for cp in range(2):
            E2r[:, cp, :, :, cp] = np.cos(ph)
            E2i[:, cp, :, :, cp] = -np.sin(ph)
        c["hy_E2r" + sfx] = _bf(E2r.reshape(128, NFA, 128))
        c["hy_E2i" + sfx] = _bf(E2i.reshape(128, NFA, 128))
        tt_ = 64 * np.arange(NA // 2)[None, None, :] + b_[None, :, None]
        th2 = 2 * math.pi * ((np.arange(NFA)[:, None, None] * tt_) % NF) / NF
        wgt = np.full((NFA, 1, 1), 2.0 / NF)
        wgt[0] = wgt[NFA - 1] = 1.0 / NF
        c["hy_DBr" + sfx] = _bf(wgt * np.cos(th2))
        c["hy_DBni" + sfx] = _bf(-wgt * np.sin(th2))
    c["hy_negdelta"] = np.ascontiguousarray((-np.abs(deltas)).reshape(4, 128).T.astype(np.float32))
    psi = 2 * math.pi * ((fbb[:, None] * b_[None, :]) % 64) / 64
    CA = np.zeros((64, 2, 3, 64, 2))
    for cp in range(2):
        CA[:, cp, 0, :, cp] = np.cos(psi)
        CA[:, cp, 1, :, cp] = np.sin(psi)
        CA[:, cp, 2, :, cp] = -np.sin(psi)
    c["hy_CA"] = _bf(CA.reshape(128, 3, 128))
    return c


def rope_swap_perm():
    p = np.zeros(32, np.int64)
    for r in range(32):
        g, j = r // 16, r % 16
        p[r] = g * 16 + (j + 8) % 16
    return p


def build(dbg=None):
    dbg = dbg or {}
    stop_after = dbg.get("stop_after", None)
    ext = dbg.get("ext", ())
    nc = bass.Bass("TRN2", target_bir_lowering=False)
    f = FW(nc)
    hc = host_constants()

    def inp(name, shape, dt=F32):
        return Buf(nc.dram_tensor(name, list(shape), dt, kind="ExternalInput").ap(), name, dram=True)

    def scratch(name, shape, dt):
        if name in dbg.get("inject", ()):
            return f.dram(name, shape, dt, kind="ExternalInput")
        return f.dram(name, shape, dt, kind="ExternalOutput" if name in ext else "Internal")

    I = {}
    I["xc"] = inp("xc", [T, D])
    I["cs"] = inp("cs", [128, 8, 2])
    I["ada_w"] = inp("ada_w", [DEPTH, D, 6 * D])
    I["ada_bf"] = inp("ada_bf", [128, DEPTH, 48])
    I["ada_br"] = inp("ada_br", [2, DEPTH, 6 * D])
    I["n1g"] = inp("n1g", [128, DEPTH, 8])
    I["n2g"] = inp("n2g", [128, DEPTH, 8])
    I["w_in"] = inp("w_in", [DEPTH, D, DIN])
    I["w_kr2"] = inp("w_kr2", [DEPTH, D, 2, 96])
    I["qng"] = inp("qng", [128, DEPTH, 2])
    I["kvng"] = inp("kvng", [128, DEPTH])
    I["w_uq"] = inp("w_uq", [DEPTH, 256, 768])
    I["w_uq_sw"] = inp("w_uq_sw", [DEPTH, 256, 768])
    I["w_ukv_k"] = inp("w_ukv_k", [DEPTH, 128, 512])
    I["w_ukv_v"] = inp("w_ukv_v", [DEPTH, 128, 512])
    I["w_a2"] = inp("w_a2", [DEPTH, 16, 512])
    I["b_a"] = inp("b_a", [DEPTH, 1, 512])
    I["gla_on"] = inp("gla_on", [DEPTH, 128, 128])
    for nm_ in ("w_o_mla", "w_o_gla", "w_o_hy"):
        I[nm_] = inp(nm_, [DEPTH, 512, D])
    I["w_out"] = inp("w_out", [DEPTH, D, D])
    I["ff_w1"] = inp("ff_w1", [DEPTH, D, 4 * D])
    I["ff_w2"] = inp("ff_w2", [DEPTH, 4 * D, D])
    I["fin_g"] = inp("fin_g", [128, D])
    I["hy_swb"] = inp("hy_swb", [DEPTH, 128, 12, 4])
    I["hy_f_w1"] = inp("hy_f_w1", [DEPTH, 33, 64])
    I["hy_f_w2"] = inp("hy_f_w2", [DEPTH, 64, 64])
    I["hy_f_w3"] = inp("hy_f_w3", [DEPTH, 64, 2048])
    I["hy_fb12"] = inp("hy_fb12", [DEPTH, 64, 2])
    I["hy_fb3T"] = inp("hy_fb3T", [DEPTH, 128, 16])
    I["hy_biasT"] = inp("hy_biasT", [DEPTH, 128, 2, 4])
    for k, v in hc.items():
        I[k] = inp(k, list(v.shape), BF16 if v.dtype == ml_dtypes.bfloat16 else F32)
    out_y = Buf(nc.dram_tensor("y", [L, D], F32, kind="ExternalOutput").ap(), "y", dram=True)

    xres = scratch("xres", [T, D], F32)
    modrow = scratch("modrow", [2, DEPTH, 2, D], F32)
    qT_d = scratch("qT_d", [8, 96, T], BF16)
    kT_d = scratch("kT_d", [8, 96, T], BF16)
    V_d = scratch("V_d", [T, 520], BF16)
    gqk_d = scratch("gqk_d", [T, 512], F32)
    gv_d = scratch("gv_d", [T, 512], BF16)
    gr_d = scratch("gr_d", [T, 512], F32)
    gg_d = scratch("gg_d", [T, 512], F32)
    zhyT_d = scratch("zhyT_d", [1536, T], BF16)
    scT_d = scratch("scT_d", [1536, T], BF16)
    H_d = scratch("H_d", [2, 8, 128, 2 * 65 * 32], BF16)
    y1T_d = scratch("y1T_d", [512, T], BF16)
    gatesT_d = scratch("gatesT_d", [3072, T], BF16)
    hT_d = scratch("hT_d", [D, T], BF16) if "hT_d" in ext else None
    ymlaT_d = scratch("ymlaT_d", [512, T], BF16)
    yglaT_d = scratch("yglaT_d", [512, T], BF16)
    yhyT_d = scratch("yhyT_d", [512, T], BF16)
    og_d = scratch("og_d", [2, T, 512], F32)

    ident = f.sbuf("ident", [128, 128], BF16)
    ones_f = f.sbuf("ones_f", [128, 128], F32)
    modF = f.sbuf("modF", [128, DEPTH, 48, 2], F32)
    AB = f.sbuf("AB", [128, DEPTH, 2, 2, 8, 2], F32)
    f.dma("sp", ident[:], I["ident_bf"][:])
    f.dma("sp", ones_f[:], I["ones_f"][:])
    xres_t = [Buf(xres.t[tt * 128:(tt + 1) * 128, :], "xres%d" % tt, dram=True) for tt in range(NT)]
    for tt in range(NT):
        f.dma("sp", xres_t[tt][:], I["xc"][tt * 128:(tt + 1) * 128, :])

    def phase_mod():
        m = f.mark()
        cs = f.sbuf("cs", [128, 8, 2], F32)
        scs = f.sbuf("scs", [128, 8, 2], F32)
        abf = f.sbuf("abf", [128, DEPTH, 48], F32)
        abr = f.sbuf("abr", [2, DEPTH, 6 * D], F32)
        g1 = f.sbuf("g1", [128, DEPTH, 8], F32)
        g2 = f.sbuf("g2", [128, DEPTH, 8], F32)
        wp = Pool([f.sbuf("adaw%d" % i, [128, 8, 512], F32) for i in range(2)])
        rowst = f.sbuf("rowst", [2, 512], F32)
        psF = f.psum("psF", [128, 512], F32)
        psR = Pool([f.psum("psR%d" % i, [128, 512], F32) for i in range(2)])
        f.dma("sp", cs[:], I["cs"][:])
        f.dma("sp", abf[:], I["ada_bf"][:])
        f.dma("sp", abr[:], I["ada_br"][:])
        f.dma("sp", g1[:], I["n1g"][:])
        f.dma("sp", g2[:], I["n2g"][:])
        f.act(scs[:], cs[:], AF.Silu)
        for l in range(DEPTH):
            wv = I["ada_w"][l].re("(kc p) j -> p kc j", p=128)
            for jb in range(12):
                w = wp.next()
                f.dma("sp", w[:], wv[:, :, jb * 512:(jb + 1) * 512])
                which = jb // 2
                if which in (2, 5):
                    pr = psR.next()
                    for kc in range(8):
                        f.mm(pr[0:2, :], scs[:, kc, :], w[:, kc, :], start=kc == 0, stop=kc == 7)
                    f.tt(rowst[:], pr[0:2, :], abr[:, l, jb * 512:(jb + 1) * 512], ALU.add)
                    f.dma("sp", modrow[:, l, 0 if which == 2 else 1, (jb % 2) * 512:(jb % 2 + 1) * 512], rowst[:])
                else:
                    for jc in range(4):
                        ch = jb * 4 + jc
                        for kc in range(8):
                            f.mm(psF[:, ch * 2:ch * 2 + 2], w[:, kc, jc * 128:(jc + 1) * 128], scs[:, kc, :],
                                 start=kc == 0, stop=kc == 7)
            for (c0, c1) in ((0, 16), (24, 40)):
                pv = psF[:, c0 * 2:c1 * 2].re("p (c s) -> p c s", s=2)
                f.tt(modF[:, l, c0:c1, :], pv, abf[:, l, c0:c1].un(2).bc([128, c1 - c0, 2]), ALU.add)
            for n_i, (sh0, sc0, g) in enumerate(((0, 8, g1), (24, 32, g2))):
                f.stt(AB[:, l, n_i, 0], modF[:, l, sc0:sc0 + 8, :], 1.0, g[:, l, :].un(2).bc([128, 8, 2]), ALU.add, ALU.mult)
                f.cp(AB[:, l, n_i, 1], modF[:, l, sh0:sh0 + 8, :])
        f.release(m)

    class NormCtx:
        def __init__(self):
            self.xp = Pool([f.sbuf("nx%d" % i, [128, D], F32) for i in range(2)])
            self.xnp = Pool([f.sbuf("nxn%d" % i, [128, D], BF16) for i in range(2)])
            self.stp = Pool([f.sbuf("nst%d" % i, [128, 4], F32) for i in range(3)])
            self.pp = Pool([f.psum("nps%d" % i, [128, 8, 128], BF16) for i in range(2)])

        def emit(self, l, n_i, hT, tt, c0):
            s = 0 if tt >= 2 else 1
            x = self.xp.next()
            st = self.stp.next()
            xn = self.xnp.next()
            f.dma("sp", x[:], xres_t[tt][:])
            f.act(xn[:], x[:], AF.Square, accum=st[:, 0:1])
            f.act(st[:, 1:2], st[:, 0:1], AF.Sqrt, bias=EPS_T[:, 0:1], scale=1.0 / D)
            f.recip(st[:, 2:3], st[:, 1:2])
            f.act(xn[:], x[:], AF.Identity, scale=st[:, 2:3])
            ps = self.pp.next()
            for kc in range(8):
                f.tr(ps[:, kc, :], xn[:, kc * 128:(kc + 1) * 128], ident[:])
            for kc in range(8):
                if kc % 2 == 0:
                    f.act(hT[:, kc, c0:c0 + 128], ps[:, kc, :], AF.Identity,
                          bias=AB[:, l, n_i, 1, kc, s:s + 1], scale=AB[:, l, n_i, 0, kc, s:s + 1])
                else:
                    f.ts(hT[:, kc, c0:c0 + 128], ps[:, kc, :], AB[:, l, n_i, 0, kc, s:s + 1], ALU.mult,
                         AB[:, l, n_i, 1, kc, s:s + 1], ALU.add)

    def norm_tiles(l, n_i, hT, tiles):
        m = f.mark()
        nctx = NormCtx()
        for tt in tiles:
            nctx.emit(l, n_i, hT, tt, tt * 128)
        f.release(m)

    EPS_T = f.sbuf("eps_t", [128, 1], F32)
    f.memset(EPS_T[:], EPS)

    def phase_proj(l, hT):
        m = f.mark()
        wp = Pool([f.sbuf("pw%d" % i, [128, 8, 512], BF16) for i in range(3)])
        psp = Pool([f.psum("pps%d" % i, [128, 512], F32) for i in range(4)])
        psq = Pool([f.psum("ppq%d" % i, [128, 512], F32) for i in range(3)])
        w_l = I["w_in"][l].re("(kc p) n -> p kc n", p=128)

        def loadw(c0, n):
            w = wp.next()
            f.dma("pool", w[:, :, 0:n], w_l[:, :, c0:c0 + n])
            return w

        def fm(w, wc0, ncol, evac):
            for bi, (t0, n) in enumerate(TB):
                ps = psp.next()
                for kc in range(8):
                    f.mm(ps[0:ncol, 0:n], w[:, kc, wc0:wc0 + ncol], hT[:, kc, t0:t0 + n], start=kc == 0, stop=kc == 7)
                evac(ps, bi, t0, n)

        def tm(w, ncol, evac):
            for tt in range(NT):
                ps = psp.next()
                for kc in range(8):
                    f.mm(ps[:, 0:ncol], hT[:, kc, tt * 128:(tt + 1) * 128], w[:, kc, 0:ncol], start=kc == 0, stop=kc == 7)
                evac(ps, tt)

        w0 = loadw(0, 384)
        wk = wp.next()
        f.dma("pool", wk[:, :, 0:192], I["w_kr2"][l].re("(kc p) a n -> p kc (a n)", p=128))
        qng = f.sbuf("qng", [128, DEPTH, 2], F32)
        kvng = f.sbuf("kvng", [128, DEPTH], F32)
        f.dma("sp", qng[:], I["qng"][:])
        f.dma("sp", kvng[:], I["kvng"][:])
        wuq = f.sbuf("wuq", [128, 2, 768], BF16)
        wuqs = f.sbuf("wuqs", [128, 2, 768], BF16)
        wk_k = f.sbuf("wukv_k", [128, 512], BF16)
        wk_v = f.sbuf("wukv_v", [128, 512], BF16)
        f.dma("pool", wuq[:], I["w_uq"][l].re("(kc p) n -> p kc n", p=128))
        f.dma("pool", wuqs[:], I["w_uq_sw"][l].re("(kc p) n -> p kc n", p=128))
        f.dma("pool", wk_k[:], I["w_ukv_k"][l])
        f.dma("pool", wk_v[:], I["w_ukv_v"][l])
        cosp = Pool([f.sbuf("cosb%d" % i, [96, 512], F32) for i in range(2)])
        sinp = Pool([f.sbuf("sinb%d" % i, [96, 512], F32) for i in range(2)])
        cqp = Pool([f.sbuf("cqb%d" % i, [128, 2, 512], F32) for i in range(2)])
        ckvp = Pool([f.sbuf("ckvb%d" % i, [128, 512], F32) for i in range(2)])
        cqnp = Pool([f.sbuf("cqnb%d" % i, [128, 2, 512], BF16) for i in range(2)])
        ckvnp = Pool([f.sbuf("ckvnb%d" % i, [128, 512], BF16) for i in range(2)])
        krp = Pool([f.sbuf("krb%d" % i, [96, 512], BF16) for i in range(2)])
        tmpa = Pool([f.sbuf("ptmpa%d" % i, [128, 512], F32) for i in range(2)])
        tmpb = Pool([f.sbuf("ptmpb%d" % i, [128, 512], F32) for i in range(2)])
        sqp = Pool([f.sbuf("psq%d" % i, [128, 2, 512], F32) for i in range(2)])
        rsp = Pool([f.sbuf("prs%d" % i, [128, 512], F32) for i in range(2)])
        qst = Pool([f.sbuf("qst%d" % i, [96, 512], BF16) for i in range(3)])
        kst = Pool([f.sbuf("kst%d" % i, [96, 512], BF16) for i in range(3)])
        vst = Pool([f.sbuf("vst%d" % i, [128, 8, 65], BF16) for i in range(2)])
        for vb in vst.bufs:
            f.memset(vb[:], 1.0)
        for bi, (t0, n) in enumerate(TB):
            cosb = cosp.next()
            sinb = sinp.next()
            f.dma("sp", cosb[64:96, 0:n], I["cosT"][64:96, t0:t0 + n])
            f.dma("sp", sinb[64:96, 0:n], I["sinT"][64:96, t0:t0 + n])
            cq = cqp.next()
            ckv = ckvp.next()
            for c in range(3):
                ps = psp.next()
                for kc in range(8):
                    f.mm(ps[:, 0:n], w0[:, kc, c * 128:(c + 1) * 128], hT[:, kc, t0:t0 + n], start=kc == 0, stop=kc == 7)
                f.cp(cq[:, c, 0:n] if c < 2 else ckv[:, 0:n], ps[:, 0:n], eng="act")
            pa = psp.next()
            pb = psp.next()
            for kc in range(8):
                f.mm(pa[0:96, 0:n], wk[:, kc, 0:96], hT[:, kc, t0:t0 + n], start=kc == 0, stop=kc == 7)
            for kc in range(8):
                f.mm(pb[0:96, 0:n], wk[:, kc, 96:192], hT[:, kc, t0:t0 + n], start=kc == 0, stop=kc == 7)
            ta = tmpa.next()
            tb_ = tmpb.next()
            krb = krp.next()
            f.tt(ta[64:96, 0:n], pa[64:96, 0:n], cosb[64:96, 0:n], ALU.mult)
            f.tt(tb_[64:96, 0:n], pb[64:96, 0:n], sinb[64:96, 0:n], ALU.mult)
            f.tt(krb[64:96, 0:n], ta[64:96, 0:n], tb_[64:96, 0:n], ALU.add, eng="pool")
            cqn = cqnp.next()
            ckvn = ckvnp.next()
            for (nchunk, gains) in ((2, qng), (1, kvng)):
                sq = sqp.next()
                ps = psp.next()
                for c in range(nchunk):
                    sv = cq[:, c, 0:n] if nchunk == 2 else ckv[:, 0:n]
                    f.tt(sq[:, c, 0:n], sv, sv, ALU.mult)
                for c in range(nchunk):
                    f.mm(ps[:, 0:n], ones_f[:], sq[:, c, 0:n], start=c == 0, stop=c == nchunk - 1)
                rs = rsp.next()
                f.act(rs[:, 0:n], ps[:, 0:n], AF.Sqrt, bias=EPS_T[:, 0:1], scale=1.0 / (128 * nchunk))
                f.recip(rs[:, 0:n], rs[:, 0:n])
                for c in range(nchunk):
                    sv = cq[:, c, 0:n] if nchunk == 2 else ckv[:, 0:n]
                    dv = cqn[:, c, 0:n] if nchunk == 2 else ckvn[:, 0:n]
                    gv = gains[:, l, c:c + 1] if nchunk == 2 else gains[:, l:l + 1]
                    f.stt(dv, sv, gv, rs[:, 0:n], ALU.mult, ALU.mult)
            for h in range(8):
                pa = psq.next()
                pb = psq.next()
                for kc in range(2):
                    f.mm(pa[0:96, 0:n], wuq[:, kc, h * 96:(h + 1) * 96], cqn[:, kc, 0:n], start=kc == 0, stop=kc == 1)
                for kc in range(2):
                    f.mm(pb[0:96, 0:n], wuqs[:, kc, h * 96:(h + 1) * 96], cqn[:, kc, 0:n], start=kc == 0, stop=kc == 1)
                q = qst.next()
                ta = tmpa.next()
                tb_ = tmpb.next()
                f.cp(q[0:64, 0:n], pa[0:64, 0:n], eng="act")
                f.tt(ta[64:96, 0:n], pa[64:96, 0:n], cosb[64:96, 0:n], ALU.mult)
                f.tt(tb_[64:96, 0:n], pb[64:96, 0:n], sinb[64:96, 0:n], ALU.mult)
                f.tt(q[64:96, 0:n], ta[64:96, 0:n], tb_[64:96, 0:n], ALU.add, eng="pool")
                f.dma("sp", qT_d[h, :, t0:t0 + n], q[:, 0:n])
                pk = psq.next()
                f.mm(pk[0:64, 0:n], wk_k[:, h * 64:(h + 1) * 64], ckvn[:, 0:n])
                k = kst.next()
                f.cp(k[0:64, 0:n], pk[0:64, 0:n], eng="act")
                f.cp(k[64:96, 0:n], krb[64:96, 0:n], eng="pool")
                f.dma("sp", kT_d[h, :, t0:t0 + n], k[:, 0:n])
            for ti in range(n // 128):
                tt = t0 // 128 + ti
                pv = psq.next()
                f.mm(pv[:, :], ckvn[:, ti * 128:(ti + 1) * 128], wk_v[:, :])
                vs = vst.next()
                f.cp(vs[:, :, 0:64], pv[:, :].re("p (h e) -> p h e", e=64), eng="act")
                f.dma("sp", V_d[tt * 128:(tt + 1) * 128, :], vs[:].re("p h e -> p (h e)"))

        wa = loadw(1952, 32)
        wa2 = f.sbuf("wa2", [16, 512], F32)
        ba = f.sbuf("ba", [1, 512], F32)
        f.dma("sp", wa2[:], I["w_a2"][l])
        f.dma("sp", ba[:], I["b_a"][l])
        aft = Pool([f.sbuf("aft%d" % i, [16, 512], F32) for i in range(2)])
        abt = Pool([f.sbuf("abt%d" % i, [16, 512], F32) for i in range(2)])
        ggp = Pool([f.sbuf("ggs%d" % i, [128, 512], F32) for i in range(2)])
        for bi, (t0, n) in enumerate(TB):
            pa = psp.next()
            pb = psp.next()
            for kc in range(8):
                f.mm(pa[0:16, 0:n], wa[:, kc, 0:16], hT[:, kc, t0:t0 + n], start=kc == 0, stop=kc == 7)
            for kc in range(8):
                f.mm(pb[0:16, 0:n], wa[:, kc, 16:32], hT[:, kc, t0:t0 + n], start=kc == 0, stop=kc == 7)
            af = aft.next()
            ab = abt.next()
            f.cp(af[:, 0:n], pa[0:16, 0:n], eng="act")
            f.cp(ab[:, 0:n], pb[0:16, 0:n], eng="act")
            for ti in range(n // 128):
                pg = psq.next()
                f.mm(pg[:, 0:256], af[:, ti * 128:(ti + 1) * 128], wa2[:, 0:256], start=True, stop=False)
                f.mm(pg[:, 0:256], ones_f[0:1, :], ba[:, 0:256], start=False, stop=True)
                f.mm(pg[:, 256:512], ab[:, ti * 128:(ti + 1) * 128], wa2[:, 256:512], start=True, stop=False)
                f.mm(pg[:, 256:512], ones_f[0:1, :], ba[:, 256:512], start=False, stop=True)
                gs = ggp.next()
                f.act(gs[:], pg[:], AF.Exp, scale=-1.0)
                f.act(gs[:], gs[:], AF.Ln, bias=ONE_T[:, 0:1])
                f.ts(gs[:], gs[:], -1.0 / 16.0, ALU.mult)
                tt = t0 // 128 + ti
                f.dma("sp", gg_d[tt * 128:(tt + 1) * 128, :], gs[:])

        st32 = Pool([f.sbuf("pst32_%d" % i, [128, 512], F32) for i in range(3)])
        st16 = Pool([f.sbuf("pst16_%d" % i, [128, 512], BF16) for i in range(3)])

        def ev_qk(ps, tt):
            s = st32.next()
            f.act(s[:, 0:256], ps[:, 0:256], AF.Copy, scale=0.125)
            f.cp(s[:, 256:512], ps[:, 256:512], eng="dve")
            f.dma("sp", gqk_d[tt * 128:(tt + 1) * 128, :], s[:])

        def ev_to(dst, c0, dt16):
            def ev(ps, tt):
                s = (st16 if dt16 else st32).next()
                f.cp(s[:], ps[:], eng="act" if tt % 2 else "dve")
                f.dma("sp", dst[tt * 128:(tt + 1) * 128, c0:c0 + 512], s[:])
            return ev

        tm(loadw(416, 512), 512, ev_qk)
        tm(loadw(928, 512), 512, ev_to(gv_d, 0, True))
        tm(loadw(1440, 512), 512, ev_to(gr_d, 0, False))
        for sc in range(3):
            w = loadw(1984 + sc * 512, 512)
            for c in range(4):
                ch = sc * 4 + c

                def ev_h(ps, bi, t0, n, ch=ch):
                    s = st16.next()
                    f.cp(s[:, 0:n], ps[:, 0:n], eng="act" if (bi + ch) % 2 else "dve")
                    f.dma("sp", zhyT_d[ch * 128:(ch + 1) * 128, t0:t0 + n], s[:, 0:n])
                fm(w, c * 128, 128, ev_h)

        for sc in range(6):
            w = loadw(3520 + sc * 512, 512)
            for c in range(4):
                ch = sc * 4 + c

                def ev_g(ps, bi, t0, n, ch=ch):
                    s = st16.next()
                    f.act(s[:, 0:n], ps[:, 0:n], AF.Sigmoid)
                    f.dma("sp", gatesT_d[ch * 128:(ch + 1) * 128, t0:t0 + n], s[:, 0:n])
                fm(w, c * 128, 128, ev_g)
        f.release(m)

    def phase_att(l):
        m = f.mark()
        Vaug = f.sbuf("Vaug", [128, NT, 520], BF16)
        f.dma("sp", Vaug[:], V_d[:].re("(t p) e -> p t e", p=128))
        qp = Pool([f.sbuf("aq%d" % i, [96, T], BF16) for i in range(2)])
        kp = Pool([f.sbuf("ak%d" % i, [96, T], BF16) for i in range(2)])
        pp = Pool([f.sbuf("ap%d" % i, [128, 2, 512], BF16) for i in range(3)])
        sps = Pool([f.psum("as%d" % i, [128, 2, 512], F32) for i in range(2)])
        ops = Pool([f.psum("ao%d" % i, [128, 512], F32) for i in range(2)])
        bps = f.psum("abc", [128, 512], F32)
        rec = Pool([f.sbuf("arec%d" % i, [65, 512], F32) for i in range(2)])
        osb = Pool([f.sbuf("aosb%d" % i, [64, 512], F32) for i in range(2)])
        yst = Pool([f.sbuf("ayst%d" % i, [64, 512], BF16) for i in range(3)])
        for h in range(8):
            q = qp.next()
            k = kp.next()
            f.dma("sp", q[:], qT_d[h])
            f.dma("sp", k[:], kT_d[h])
            for bi, (t0, n) in enumerate(TB):
                if bi == 0 and l == DEPTH - 1:
                    continue
                nk = 2 if bi == 0 else NT
                npair = nk // 2
                o = ops.next()
                pend = None
                for jp in range(npair + 1):
                    p = None
                    if jp < npair:
                        s = sps.next()
                        for u_ in range(2):
                            j = jp * 2 + u_
                            f.mm(s[:, u_, 0:n], k[:, j * 128:(j + 1) * 128], q[:, t0:t0 + n])
                        p = pp.next()
                        f.act(p[:, :, 0:n], s[:, :, 0:n], AF.Exp, scale=MLA_SCALE)
                    if pend is not None:
                        jq, pv = pend
                        for u_ in range(2):
                            jj = jq * 2 + u_
                            f.mm(o[0:65, 0:n], Vaug[:, jj, h * 65:(h + 1) * 65], pv[:, u_, 0:n], start=jj == 0, stop=jj == nk - 1)
                    pend = (jp, p) if jp < npair else None
                r = rec.next()
                f.recip(r[64:65, 0:n], o[64:65, 0:n])
                f.mm(bps[0:64, 0:n], ones_f[64:65, 0:64], r[64:65, 0:n])
                os_ = osb.next()
                f.cp(os_[:, 0:n], o[0:64, 0:n], eng="act")
                y = yst.next()
                f.tt(y[:, 0:n], os_[:, 0:n], bps[0:64, 0:n], ALU.mult)
                f.dma("sp", ymlaT_d[h * 64:(h + 1) * 64, t0:t0 + n], y[:, 0:n])
        f.release(m)

    def phase_gla(l):
        m = f.mark()
        Mm = f.sbuf("glaM", [128, 2, 128], F32)
        mask = f.sbuf("glamask", [128, 2, 128], F32)
        ind = f.sbuf("glaind", [128, 2], F32)
        gon = f.sbuf("gon", [128, 128], F32)
        f.dma("sp", Mm[:], I["gla_M"][:])
        f.dma("sp", mask[:], I["gla_mask"][:])
        f.dma("sp", ind[:], I["gla_ind"][:])
        f.dma("sp", gon[:], I["gla_on"][l])
        mA = f.mark()
        ptr = f.psum("gptr", [128, 8, 128], BF16)
        pU = f.psum("gpU", [128, 4, 128], F32)

        def chain(d):
            S = f.sbuf("glaS%d" % d, [64, 4, 128], F32)
            P2 = lambda nm, shp, dt, k=2: Pool([f.sbuf("%s%d_%d" % (nm, d, i), shp, dt) for i in range(k)])
            qkp, gp, vp = P2("gqk", [128, 512], F32, 3), P2("gg", [128, 256], F32, 3), P2("gv", [128, 512], BF16, 3)
            ePp, eNp = P2("geP", [128, 256], F32), P2("geN", [128, 256], F32)
            qpp, kpp = P2("gqp", [128, 256], BF16), P2("gkp", [128, 256], BF16)
            decp, qkTp = P2("gdec", [64, 4, 2], F32), P2("gqkT", [64, 8, 128], BF16)
            Amp, Smidp, Smbp = P2("gAm", [128, 4, 128], BF16), P2("gSm", [64, 4, 128], F32), P2("gSb", [64, 4, 128], BF16)
            osp = P2("gos", [128, 4, 128], F32)
            pE = f.psum("gpE%d" % d, [128, 512], F32)
            pA = f.psum("gpA%d" % d, [128, 4, 128], F32)
            po = f.psum("gpo%d" % d, [128, 4, 128], F32)
            order = list(range(NT)) if d == 0 else [1, 0] + list(range(NT - 1, 1, -1))
            f.memset(S[:], 0.0)
            yield
            loaded = {}

            def load(tt):
                r0 = tt * 128
                qk, g, v = qkp.next(), gp.next(), vp.next()
                f.dma("sp", qk[:], gqk_d[r0:r0 + 128, :])
                f.dma("sp", g[:], gg_d[r0:r0 + 128, d * 256:(d + 1) * 256])
                f.dma("sp", v[:], gv_d[r0:r0 + 128, :])
                loaded[tt] = (qk, g, v)
            load(order[0])
            for oi, tt in enumerate(order):
                r0 = tt * 128
                if oi + 1 < len(order):
                    load(order[oi + 1])
                qk, g, v = loaded.pop(tt)
                f.mm(pE[:, 0:256], Mm[:, d, :], g[:])
                for h in range(4):
                    f.mm(pE[0:64, 256 + h * 2:258 + h * 2], g[:, h * 64:(h + 1) * 64], ind[:])
                yield
                eP, eN, dec = ePp.next(), eNp.next(), decp.next()
                f.act(eP[:], pE[:, 0:256], AF.Exp)
                f.act(eN[:], pE[:, 0:256], AF.Exp, scale=-1.0)
                f.act(dec[:].re("p h s -> p (h s)"), pE[0:64, 256:264], AF.Exp)
                dmid = dec[:, :, d:d + 1]
                dend = dec[:, :, 1 - d:2 - d]
                yield
                qp_, kp_ = qpp.next(), kpp.next()
                f.tt(qp_[:], qk[:, 0:256], eP[:], ALU.mult)
                f.tt(kp_[:], qk[:, 256:512], eN[:], ALU.mult)
                Smid = Smidp.next()
                f.tt(Smid[:], S[:], dmid.bc([64, 4, 128]), ALU.mult)
                Smb = Smbp.next()
                f.cp(Smb[:], Smid[:], eng="act")
                yield
                for h in range(4):
                    f.tr(ptr[0:64, h, :], qp_[:, h * 64:(h + 1) * 64], ident[:])
                for h in range(4):
                    f.tr(ptr[0:64, 4 + h, :], kp_[:, h * 64:(h + 1) * 64], ident[:])
                for h in range(4):
                    f.mm(pU[0:64, h, :], kp_[:, h * 64:(h + 1) * 64], v[:, h * 128:(h + 1) * 128])
                qkT = qkTp.next()
                f.cp(qkT[:], ptr[0:64], eng="act")
                f.tt(S[:], Smid[:], pU[0:64], ALU.add)
                f.tt(S[:], S[:], dend.bc([64, 4, 128]), ALU.mult)
                yield
                for h in range(4):
                    f.mm(pA[:, h, :], qkT[:, 4 + h, :], qkT[:, h, :])
                yield
                Am = Amp.next()
                f.tt(Am[:], pA[:], mask[:, d:d + 1, :].bc([128, 4, 128]), ALU.mult)
                yield
                for h in range(4):
                    f.mm(po[:, h, :], qkT[:, h, :], Smb[:, h, :], start=True, stop=False)
                    f.mm(po[:, h, :], Am[:, h, :], v[:, h * 128:(h + 1) * 128], start=False, stop=True)
                yield
                os_ = osp.next()
                f.cp(os_[:], po[:], eng="act")
                f.dma("sp", og_d[d, r0:r0 + 128, :], os_[:].re("p h v -> p (h v)"))
                yield

        gens = [chain(0), chain(1)]
        live = list(gens)
        while live:
            for g_ in list(live):
                try:
                    next(g_)
                except StopIteration:
                    live.remove(g_)
        f.release(mA)

        ofp = Pool([f.sbuf("gof%d" % i, [128, 4, 128], F32) for i in range(2)])
        obp = Pool([f.sbuf("gob%d" % i, [128, 4, 128], F32) for i in range(2)])
        sqp = Pool([f.sbuf("gsq%d" % i, [128, 4, 128], F32) for i in range(2)])
        stp = Pool([f.sbuf("gst%d" % i, [128, 8], F32) for i in range(2)])
        rp = Pool([f.sbuf("gr%d" % i, [128, 512], F32) for i in range(2)])
        yp = Pool([f.sbuf("gy%d" % i, [128, 512], BF16) for i in range(2)])
        yTp = Pool([f.sbuf("gyT%d" % i, [128, 4, 128], BF16) for i in range(2)])
        pTp = Pool([f.psum("gpT%d" % i, [128, 8, 128], BF16) for i in range(2)])
        for tt in range(NT):
            if tt < 2 and l == DEPTH - 1:
                continue
            r0 = tt * 128
            of_, ob_, rr = ofp.next(), obp.next(), rp.next()
            f.dma("sp", of_[:].re("p h v -> p (h v)"), og_d[0, r0:r0 + 128, :])
            f.dma("sp", ob_[:].re("p h v -> p (h v)"), og_d[1, r0:r0 + 128, :])
            f.dma("sp", rr[:], gr_d[r0:r0 + 128, :])
            f.tt(of_[:], of_[:], ob_[:], ALU.add)
            sq = sqp.next()
            f.tt(sq[:], of_[:], of_[:], ALU.mult)
            st = stp.next()
            f.op("dve", lambda e: e.tensor_reduce(out=st[:, 0:4].ap, in_=sq[:].ap, axis=mybir.AxisListType.X, op=ALU.add),
                 reads=[sq], writes=[st])
            f.act(st[:, 4:8], st[:, 0:4], AF.Sqrt, bias=EPS_T[:, 0:1], scale=1.0 / 128)
            f.recip(st[:, 4:8], st[:, 4:8])
            f.tt(of_[:], of_[:], st[:, 4:8].un(2).bc([128, 4, 128]), ALU.mult)
            f.tt(of_[:], of_[:], gon[:].un(1).bc([128, 4, 128]), ALU.mult, eng="pool")
            f.act(rr[:], rr[:], AF.Silu)
            y = yp.next()
            f.tt(y[:], of_[:].re("p h v -> p (h v)"), rr[:], ALU.mult)
            pT = pTp.next()
            for c in range(4):
                f.tr(pT[:, c, :], y[:, c * 128:(c + 1) * 128], ident[:])
            yT = yTp.next()
            f.cp(yT[:], pT[:, 0:4, :], eng="act")
            f.dma("sp", yglaT_d[:, r0:r0 + 128].re("(c p) t -> p c t", p=128), yT[:])
        f.release(m)

    def phase_merge(l, prefetch=None):
        m = f.mark()
        wo = [f.sbuf("wo%d" % i, [128, 4, D], BF16) for i in range(3)]
        for i, nm in enumerate(("w_o_mla", "w_o_gla", "w_o_hy")):
            f.dma("pool", wo[i][:], I[nm][l].re("(kc p) n -> p kc n", p=128))
        wout = f.sbuf("wout", [128, 8, D], BF16)
        for c in range(2):
            f.dma("pool", wout[:, :, c * 512:(c + 1) * 512], I["w_out"][l].re("(kc p) n -> p kc n", p=128)[:, :, c * 512:(c + 1) * 512])
        if prefetch is not None:
            prefetch()
        gx = f.sbuf("gx", [128, 2, D], F32)
        f.dma("sp", gx[:, 0, :], modrow[0, l, 0].pb(128))
        f.dma("sp", gx[:, 1, :], modrow[1, l, 0].pb(128))
        ybp = Pool([f.sbuf("mby%d" % i, [128, 3, 4, 512], BF16) for i in range(2)])
        gtp = Pool([f.sbuf("mgt%d" % i, [128, 3, 512], BF16) for i in range(3)])
        mTp = Pool([f.sbuf("mT%d" % i, [128, 8, 512], BF16) for i in range(2)])
        accp = Pool([f.sbuf("macc%d" % i, [128, 512], F32) for i in range(2)])
        tmpp = Pool([f.sbuf("mtmp%d" % i, [128, 512], F32) for i in range(3)])
        xp = Pool([f.sbuf("mx%d" % i, [128, D], F32) for i in range(3)])
        ps3 = [Pool([f.psum("mps%d_%d" % (i, j), [128, 512], F32) for j in range(2)]) for i in range(3)]
        pso = Pool([f.psum("mpo%d" % i, [128, 512], F32) for i in range(2)])
        gview = gatesT_d[:].re("(b c p) t -> p b c t", b=3, c=8, p=128)
        for bi, (t0, n) in enumerate(TB):
            if bi == 0 and l == DEPTH - 1:
                continue
            yb = ybp.next()
            for i, srcT in enumerate((ymlaT_d, yglaT_d, yhyT_d)):
                f.dma("sp", yb[:, i, :, 0:n], srcT[:, t0:t0 + n].re("(kc p) t -> p kc t", p=128))
            mT = mTp.next()
            for oc in range(8):
                gt = gtp.next()
                f.dma("sp", gt[:, :, 0:n], gview[:, :, oc, t0:t0 + n])
                pss = []
                for i in range(3):
                    ps = ps3[i].next()
                    for kc in range(4):
                        f.mm(ps[:, 0:n], wo[i][:, kc, oc * 128:(oc + 1) * 128], yb[:, i, kc, 0:n], start=kc == 0, stop=kc == 3)
                    pss.append(ps)
                acc = accp.next()
                t1 = tmpp.next()
                t2 = tmpp.next()
                f.tt(acc[:, 0:n], pss[0][:, 0:n], gt[:, 0, 0:n], ALU.mult)
                f.tt(t1[:, 0:n], pss[1][:, 0:n], gt[:, 1, 0:n], ALU.mult)
                f.tt(t2[:, 0:n], pss[2][:, 0:n], gt[:, 2, 0:n], ALU.mult)
                f.tt(acc[:, 0:n], acc[:, 0:n], t1[:, 0:n], ALU.add, eng="pool" if oc % 4 == 3 else "dve")
                f.tt(mT[:, oc, 0:n], acc[:, 0:n], t2[:, 0:n], ALU.add)
            s = 1 if bi == 0 else 0
            for ti in range(n // 128):
                tt = t0 // 128 + ti
                x = xp.next()
                f.dma("sp", x[:], xres_t[tt][:])
                for half in range(2):
                    ps = pso.next()
                    for kc in range(8):
                        f.mm(ps[:, :], mT[:, kc, ti * 128:(ti + 1) * 128], wout[:, kc, half * 512:(half + 1) * 512],
                             start=kc == 0, stop=kc == 7)
                    t1 = tmpp.next()
                    f.tt(t1[:], ps[:], gx[:, s, half * 512:(half + 1) * 512], ALU.mult)
                    f.tt(x[:, half * 512:(half + 1) * 512], x[:, half * 512:(half + 1) * 512], t1[:], ALU.add)
                f.dma("sp", xres_t[tt][:], x[:])
        f.release(m)

    def ffn_w1_alloc():
        return f.sbuf("ffw1", [128, 8, 4096], BF16)

    def ffn_w1_load(l, w1):
        w1v = I["ff_w1"][l].re("(kc p) n -> p kc n", p=128)
        for c in range(8):
            f.dma("pool", w1[:, :, c * 512:(c + 1) * 512], w1v[:, :, c * 512:(c + 1) * 512])

    def phase_ffn(l, w1):
        m = f.mark()
        w2 = f.sbuf("ffw2", [128, 32, D], BF16)
        w2v = I["ff_w2"][l].re("(kc p) n -> p kc n", p=128)
        for c in range(8):
            f.dma("pool", w2[:, c * 4:(c + 1) * 4, :], w2v[:, c * 4:(c + 1) * 4, :])
        gx = f.sbuf("fgx", [128, 2, D], F32)
        f.dma("sp", gx[:, 0, :], modrow[0, l, 1].pb(128))
        f.dma("sp", gx[:, 1, :], modrow[1, l, 1].pb(128))
        nctx = NormCtx()
        hTp = Pool([f.sbuf("fhT%d" % i, [128, 8, 256], BF16) for i in range(2)])
        aTp = Pool([f.sbuf("faT%d" % i, [128, 32, 256], BF16) for i in range(1)])
        rp = Pool([f.sbuf("fr%d" % i, [128, 256], F32) for i in range(2)])
        tmpp = Pool([f.sbuf("ftmp%d" % i, [128, 512], F32) for i in range(2)])
        xp = Pool([f.sbuf("fx%d" % i, [128, D], F32) for i in range(2)])
        psA = Pool([f.psum("fpa%d" % i, [128, 512], F32) for i in range(3)])
        pso = Pool([f.psum("fpo%d" % i, [128, 512], F32) for i in range(3)])
        for blk in range(T // 256):
            if blk == 0 and l == DEPTH - 1:
                continue
            s = 1 if blk == 0 else 0
            hTb = hTp.next()
            for ti in range(2):
                nctx.emit(l, 1, hTb, blk * 2 + ti, ti * 128)
            aT = aTp.next()
            for fc in range(32):
                ps = psA.next()
                for kc in range(8):
                    f.mm(ps[:, 0:256], w1[:, kc, fc * 128:(fc + 1) * 128], hTb[:, kc, :], start=kc == 0, stop=kc == 7)
                r = rp.next()
                f.act(r[:], ps[:, 0:256], AF.Relu)
                f.tt(aT[:, fc, :], r[:], r[:], ALU.mult, eng="pool" if fc % 4 == 3 else "dve")
            for ti in range(2):
                tt = blk * 2 + ti
                x = xp.next()
                f.dma("sp", x[:], xres_t[tt][:])
                for half in range(2):
                    ps = pso.next()
                    for fc in range(32):
                        f.mm(ps[:, :], aT[:, fc, ti * 128:(ti + 1) * 128], w2[:, fc, half * 512:(half + 1) * 512],
                             start=fc == 0, stop=fc == 31)
                    t1 = tmpp.next()
                    f.tt(t1[:], ps[:], gx[:, s, half * 512:(half + 1) * 512], ALU.mult)
                    f.tt(x[:, half * 512:(half + 1) * 512], x[:, half * 512:(half + 1) * 512], t1[:], ALU.add)
                f.dma("sp", xres_t[tt][:], x[:])
        f.release(m)

    def phase_final():
        m = f.mark()
        fg = f.sbuf("fing", [128, D], F32)
        f.dma("sp", fg[:], I["fin_g"][:])
        xp = Pool([f.sbuf("zx%d" % i, [128, D], F32) for i in range(3)])
        jp = Pool([f.sbuf("zj%d" % i, [128, D], F32) for i in range(2)])
        stp = Pool([f.sbuf("zst%d" % i, [128, 4], F32) for i in range(3)])
        for tt in range(2, NT):
            x = xp.next()
            j = jp.next()
            st = stp.next()
            f.dma("sp", x[:], xres_t[tt][:])
            f.act(j[:], x[:], AF.Square, accum=st[:, 0:1])
            f.act(st[:, 1:2], st[:, 0:1], AF.Sqrt, bias=EPS_T[:, 0:1], scale=1.0 / D)
            f.recip(st[:, 2:3], st[:, 1:2])
            f.act(j[:], x[:], AF.Identity, scale=st[:, 2:3])
            f.tt(x[:], j[:], fg[:], ALU.mult)
            f.dma("sp", out_y[(tt - 2) * 128:(tt - 1) * 128, :], x[:])
        f.release(m)

    def phase_hy(l, ctx_seg):
        m = f.mark()
        r0, nrow = (0, LC) if ctx_seg else (LC, L)
        na = nrow // 64
        NA = 2 * na
        NF = NA * 64
        NFA = NA // 2 + 1
        sfx = "_c" if ctx_seg else ""

        mA = f.mark()
        swt = f.sbuf("hsw", [128, 12, 4], F32)
        f.dma("sp", swt[:], I["hy_swb"][l])
        zp = Pool([f.sbuf("hz%d" % i, [128, L], BF16) for i in range(2)])
        accp = Pool([f.sbuf("hacc%d" % i, [128, L], F32) for i in range(2)])
        op_ = Pool([f.sbuf("hso%d" % i, [128, L], BF16) for i in range(2)])
        for ch in range(12):
            z = zp.next()
            acc = accp.next()
            o = op_.next()
            f.dma("sp", z[:, 0:nrow], zhyT_d[ch * 128:(ch + 1) * 128, r0:r0 + nrow])
            f.act(acc[:, 0:nrow], z[:, 0:nrow], AF.Identity, bias=swt[:, ch, 3:4], scale=swt[:, ch, 1:2])
            f.stt(acc[:, 1:nrow], z[:, 0:nrow - 1], swt[:, ch, 0:1], acc[:, 1:nrow], ALU.mult, ALU.add)
            f.stt(acc[:, 0:nrow - 1], z[:, 1:nrow], swt[:, ch, 2:3], acc[:, 0:nrow - 1], ALU.mult, ALU.add)
            f.cp(o[:, 0:nrow], acc[:, 0:nrow], eng="act")
            f.dma("sp", scT_d[ch * 128:(ch + 1) * 128, r0:r0 + nrow], o[:, 0:nrow])
        f.release(mA)

        F1 = f.sbuf("hF1", [NA, 3 * NFA], BF16)
        E2r = f.sbuf("hE2r", [128, NFA, 128], BF16)
        E2i = f.sbuf("hE2i", [128, NFA, 128], BF16)
        f.dma("sp", F1[:], I["hy_F1" + sfx][:])
        f.dma("sp", E2r[:], I["hy_E2r" + sfx][:])
        f.dma("sp", E2i[:], I["hy_E2i" + sfx][:])
        psY = Pool([f.psum("hpY%d" % i, [128, 512], F32) for i in range(2)])
        psX = Pool([f.psum("hpX%d" % i, [128, 2, 8, 32], F32) for i in range(2)])
        pst = Pool([f.psum("hpt%d" % i, [128, 8, 64], BF16) for i in range(1)])

        def spectrum_gen(ut, Kp, consume, Y):
            for q in range(32):
                ps = psY.next()
                f.mm(ps[:, 0:3 * NFA], ut[0:Kp, q, :], F1[0:Kp, :])
                f.cp(Y[:, q, :], ps[:, 0:3 * NFA], eng="act" if q % 2 else "dve")
                if q % 4 == 3:
                    yield
            for fa0 in range(0, NFA, 8):
                nfa = min(8, NFA - fa0)
                px = psX.next()
                pr = px[:, 0]
                pi = px[:, 1]
                for i in range(nfa):
                    fa = fa0 + i
                    f.mm(pr[:, i, :], E2r[:, fa, :], Y[:, :, fa], start=True, stop=False)
                    f.mm(pr[:, i, :], E2i[:, fa, :], Y[:, :, 2 * NFA + fa], start=False, stop=True)
                for i in range(nfa):
                    fa = fa0 + i
                    f.mm(pi[:, i, :], E2i[:, fa, :], Y[:, :, fa], start=True, stop=False)
                    f.mm(pi[:, i, :], E2r[:, fa, :], Y[:, :, NFA + fa], start=False, stop=True)
                consume(fa0, nfa, pr, pi)
                yield

        mB = f.mark()
        hd2 = f.sbuf("hhd2", [64, NF], F32)
        fw3 = f.sbuf("hfw3", [64, 2048], F32)
        fb3T = f.sbuf("hfb3T", [128, 16], F32)
        nd = f.sbuf("hnd", [128, 4], F32)
        hbT = f.sbuf("hhbT", [128, 2, 4], F32)
        f.dma("sp", fw3[:], I["hy_f_w3"][l])
        f.dma("sp", fb3T[:], I["hy_fb3T"][l])
        f.dma("sp", nd[:], I["hy_negdelta"][:])
        f.dma("sp", hbT[:], I["hy_biasT"][l])
        mB1 = f.mark()
        featT = f.sbuf("hfeat", [33, NF], F32)
        f.dma("sp", featT[:], I["hy_featT" + sfx][:])
        fw1 = f.sbuf("hfw1", [33, 64], F32)
        fw2 = f.sbuf("hfw2", [64, 64], F32)
        fb12 = f.sbuf("hfb12", [64, 2], F32)
        f.dma("sp", fw1[:], I["hy_f_w1"][l])
        f.dma("sp", fw2[:], I["hy_f_w2"][l])
        f.dma("sp", fb12[:], I["hy_fb12"][l])
        hd1p = Pool([f.sbuf("hhd1_%d" % i, [64, 512], F32) for i in range(2)])
        ap_ = Pool([f.sbuf("ha%d" % i, [64, 512], F32) for i in range(2)])
        m1p = Pool([f.sbuf("hm1_%d" % i, [64, 512], F32) for i in range(2)])
        m2p = Pool([f.sbuf("hm2_%d" % i, [64, 512], F32) for i in range(2)])
        psm = Pool([f.psum("hpm%d" % i, [128, 512], F32) for i in range(3)])
        nblk = NF // 512

        def sin_wrap(dst, ps, bias):
            a = ap_.next()
            m1 = m1p.next()
            m2 = m2p.next()
            f.act(a[:], ps[0:64, :], AF.Identity, bias=bias)
            f.ts(m1[:], a[:], math.pi, ALU.is_gt, 2 * math.pi, ALU.mult)
            f.ts(m2[:], a[:], -math.pi, ALU.is_lt, 2 * math.pi, ALU.mult)
            f.tt(a[:], a[:], m1[:], ALU.subtract)
            f.tt(a[:], a[:], m2[:], ALU.add)
            f.act(dst, a[:], AF.Sin)

        for blk in range(nblk):
            ps = psm.next()
            f.mm(ps[0:64, :], fw1[:], featT[:, blk * 512:(blk + 1) * 512])
            hd1 = hd1p.next()
            sin_wrap(hd1[:], ps, fb12[:, 0:1])
            ps2 = psm.next()
            f.mm(ps2[0:64, :], fw2[:], hd1[:])
            sin_wrap(hd2[:, blk * 512:(blk + 1) * 512], ps2, fb12[:, 1:2])
        f.release(mB1)
        tn2 = f.sbuf("htn2", [128, NF], F32)
        f.dma("sp", tn2[:], I["hy_tn2" + sfx][0].pb(128))
        kT = f.sbuf("hkT", [128, NF], F32)
        kTb = f.sbuf("hkTb", [128, NF], BF16)
        kbp = Pool([f.sbuf("hkb%d" % i, [128, 512], F32) for i in range(2)])
        wbp = Pool([f.sbuf("hwb%d" % i, [128, 512], F32) for i in range(2)])
        jk = f.sbuf("hjk", [128, 512], BF16)
        asum = f.sbuf("hasum", [128, 20], F32)
        kup = Pool([f.sbuf("hku%d" % i, [128, 32, 128], BF16) for i in range(1)])
        Ybc = f.sbuf("hYbc", [128, 32, 3 * NFA], BF16)
        Hst = Pool([f.sbuf("hHst%d" % i, [128, 2, NFA, 32], BF16) for i in range(1)])
        psm = Pool([f.psum("hpm2_%d" % i, [128, 512], F32) for i in range(2)])
        bs = min(512, NF // 2)
        nb2 = NF // bs
        for cc in range(4):
            for n_ in range(2):
                for blk in range(nb2):
                    dr = 0 if blk < nb2 // 2 else 1
                    col = (dr * 2 + n_) * 4 + cc
                    cs_ = slice(blk * bs, (blk + 1) * bs)
                    ps = psm.next()
                    f.mm(ps[:, 0:bs], fw3[:, col * 128:(col + 1) * 128], hd2[:, cs_])
                    kb = kbp.next()
                    wb = wbp.next()
                    f.act(wb[:, 0:bs], tn2[:, cs_], AF.Exp, scale=nd[:, cc:cc + 1])
                    f.stt(kT[:, cs_], ps[:, 0:bs], fb3T[:, col:col + 1], wb[:, 0:bs], ALU.add, ALU.mult)
                    f.act(jk[:, 0:bs], kT[:, cs_], AF.Abs, accum=asum[:, blk:blk + 1])
                f.op("dve", lambda e: e.tensor_reduce(out=asum[:, 16:17].ap, in_=asum[:, 0:nb2].ap, axis=mybir.AxisListType.X, op=ALU.add),
                     reads=[asum], writes=[asum])
                f.recip(asum[:, 17:18], asum[:, 16:17])
                f.act(kTb[:], kT[:], AF.Identity, scale=asum[:, 17:18])
                f.ts(kTb[:, 0:1], kT[:, 0:1], asum[:, 17:18], ALU.mult, hbT[:, n_, cc:cc + 1], ALU.add)
                for gg in range(2):
                    g = cc * 2 + gg
                    kv = kTb[gg * 64:(gg + 1) * 64, :].re("c (a b) -> c b a", b=64)
                    ut = kup.next()
                    for b0 in range(0, 64, 8):
                        pt = pst.next()
                        for i in range(8):
                            f.tr(pt[0:NA, i, :], kv[:, b0 + i, :], ident[gg * 64:(gg + 1) * 64, gg * 64:(gg + 1) * 64])
                        f.cp(ut[0:NA].re("a q (b cp) -> a b q cp", cp=2)[:, b0:b0 + 8], pt[0:NA, :, :].re("a b (q cp) -> a b q cp", cp=2),
                             eng="act" if (b0 // 8) % 2 else "dve")
                    hs = Hst.next()

                    def cons(fa0, nfa, pr, pi, hs=hs):
                        f.cp(hs[:, 0, fa0:fa0 + nfa, :], pr[:, 0:nfa, :], eng="act")
                        f.cp(hs[:, 1, fa0:fa0 + nfa, :], pi[:, 0:nfa, :], eng="dve")
                    for _ in spectrum_gen(ut, NA, cons, Ybc):
                        pass
                    f.dma("sp", H_d[n_, g, :, 0:2 * NFA * 32], hs[:].re("p r f q -> p (r f q)"))
        f.release(mB)

        CA = f.sbuf("hCA", [128, 3, 128], BF16)
        DBr = f.sbuf("hDBr", [NFA, 64, na], BF16)
        DBni = f.sbuf("hDBni", [NFA, 64, na], BF16)
        f.dma("sp", CA[:], I["hy_CA"][:])
        f.dma("sp", DBr[:], I["hy_DBr" + sfx][:])
        f.dma("sp", DBni[:], I["hy_DBni" + sfx][:])
        tp = Pool([f.sbuf("ht%d" % i, [128, 8, 32], F32) for i in range(8)])
        psZ = Pool([f.psum("hpZ%d" % i, [128, 4, 128], F32) for i in range(2)])
        psO = Pool([f.psum("hpO%d" % i, [128, 8, 64], F32) for i in range(1)])
        RES = []
        for ci in range(2):
            RES.append(dict(
                uT=f.sbuf("huT%d" % ci, [64, L], BF16), gT=f.sbuf("hgT%d" % ci, [64, L], BF16),
                u=f.sbuf("hu%d" % ci, [64, 32, 128], BF16), H=f.sbuf("hH%d" % ci, [128, 2, NFA, 32], BF16),
                P=f.sbuf("hP%d" % ci, [128, 2, 32, NFA], BF16), Z0=f.sbuf("hZ0_%d" % ci, [NFA, 2, 64, 64], BF16),
                Y=f.sbuf("hYD%d" % ci, [128, 32, 3 * NFA], BF16)))

        def conv_group(n_, g, R):
            srcT = scT_d[1024:1536] if n_ == 0 else y1T_d
            gateT = scT_d[0:512] if n_ == 0 else scT_d[512:1024]
            dstT = y1T_d if n_ == 0 else yhyT_d
            uT, gT, u, H, P, Z0, Y = R["uT"], R["gT"], R["u"], R["H"], R["P"], R["Z0"], R["Y"]
            f.dma("sp", uT[:, 0:nrow], srcT[g * 64:(g + 1) * 64, r0:r0 + nrow])
            f.dma("sp", gT[:, 0:nrow], gateT[g * 64:(g + 1) * 64, r0:r0 + nrow])
            f.dma("sp", H[:].re("p r f q -> p (r f q)"), H_d[n_, g, :, 0:2 * NFA * 32])
            yield
            uv = uT[:, 0:nrow].re("c (a b) -> c b a", b=64)
            for b0 in range(0, 64, 8):
                pt = pst.next()
                for i in range(8):
                    f.tr(pt[0:na, i, :], uv[:, b0 + i, :], ident[0:64, 0:64])
                f.cp(u[0:na].re("a q (b cp) -> a b q cp", cp=2)[:, b0:b0 + 8], pt[0:na, :, :].re("a b (q cp) -> a b q cp", cp=2),
                     eng="act" if (b0 // 8) % 2 else "dve")
                yield

            def cons(fa0, nfa, pr, pi):
                t1, t2, t3, t4 = tp.next(), tp.next(), tp.next(), tp.next()
                f.tt(t1[:, 0:nfa], pr[:, 0:nfa, :], H[:, 0, fa0:fa0 + nfa, :], ALU.mult)
                f.tt(t2[:, 0:nfa], pi[:, 0:nfa, :], H[:, 1, fa0:fa0 + nfa, :], ALU.mult)
                f.tt(t3[:, 0:nfa], pr[:, 0:nfa, :], H[:, 1, fa0:fa0 + nfa, :], ALU.mult)
                f.tt(t4[:, 0:nfa], pi[:, 0:nfa, :], H[:, 0, fa0:fa0 + nfa, :], ALU.mult)
                f.tt(P[:, 0, :, fa0:fa0 + nfa].re("p q f -> p f q"), t1[:, 0:nfa], t2[:, 0:nfa], ALU.subtract)
                f.tt(P[:, 1, :, fa0:fa0 + nfa].re("p q f -> p f q"), t3[:, 0:nfa], t4[:, 0:nfa], ALU.add, eng="pool")
            for _ in spectrum_gen(u, na, cons, Y):
                yield
            for q0 in range(0, 32, 4):
                zr = psZ.next()
                zi = psZ.next()
                for i in range(4):
                    q = q0 + i
                    f.mm(zr[0:NFA, i, :], P[:, 0, q, :], CA[:, 0, :], start=True, stop=False)
                    f.mm(zr[0:NFA, i, :], P[:, 1, q, :], CA[:, 2, :], start=False, stop=True)
                for i in range(4):
                    q = q0 + i
                    f.mm(zi[0:NFA, i, :], P[:, 0, q, :], CA[:, 1, :], start=True, stop=False)
                    f.mm(zi[0:NFA, i, :], P[:, 1, q, :], CA[:, 0, :], start=False, stop=True)
                f.cp(Z0[:, 0].re("f b (q cp) -> f q b cp", cp=2)[:, q0:q0 + 4], zr[0:NFA, :, :].re("f q (b cp) -> f q b cp", cp=2), eng="act")
                f.cp(Z0[:, 1].re("f b (q cp) -> f q b cp", cp=2)[:, q0:q0 + 4], zi[0:NFA, :, :].re("f q (b cp) -> f q b cp", cp=2), eng="dve")
                yield
            yv = uT[:, 0:nrow].re("c (a b) -> c a b", b=64)
            gv = gT[:, 0:nrow].re("c (a b) -> c a b", b=64)
            for b0 in range(0, 64, 8):
                po_ = psO.next()
                for i in range(8):
                    b = b0 + i
                    f.mm(po_[0:64, i, 0:na], Z0[:, 0, b, :], DBr[:, b, :], start=True, stop=False)
                    f.mm(po_[0:64, i, 0:na], Z0[:, 1, b, :], DBni[:, b, :], start=False, stop=True)
                f.tt(yv[:, :, b0:b0 + 8], po_[0:64, :, 0:na].re("c b a -> c a b"), gv[:, :, b0:b0 + 8], ALU.mult)
                yield
            f.dma("sp", dstT[g * 64:(g + 1) * 64, r0:r0 + nrow], uT[:, 0:nrow])
            yield

        for n_ in range(2):
            for g0 in range(0, 8, 2):
                live = [conv_group(n_, g0, RES[0]), conv_group(n_, g0 + 1, RES[1])]
                while live:
                    for g_ in list(live):
                        try:
                            next(g_)
                        except StopIteration:
                            live.remove(g_)
            f.barrier()
        f.release(m)

    ONE_T = f.sbuf("one_t", [128, 1], F32)
    f.memset(ONE_T[:], 1.0)

    phase_mod()
    done = False
    if "only_hy" in dbg:
        phase_hy(0, False)
        phase_hy(0, True)
        f.barrier()
        f.barrier(["sp"])
        f.release(0)
        return nc
    for l in range(DEPTH):
        if "from_merge" not in dbg:
            mk = f.mark()
            hT = f.sbuf("hT", [128, 8, T], BF16)
            norm_tiles(l, 0, hT, range(NT))
            if hT_d is not None and l == 0:
                for kc in range(8):
                    f.dma("sp", hT_d[kc * 128:(kc + 1) * 128, :], hT[:, kc, :])
            if stop_after == "norm":
                f.release(mk)
                break
            phase_proj(l, hT)
            f.release(mk)
            if stop_after == "proj":
                break
            if "skip_att" not in dbg:
                phase_att(l)
            if stop_after == "att":
                break
            if "skip_gla" not in dbg:
                phase_gla(l)
            if stop_after == "gla":
                break
            if "skip_hy" not in dbg:
                phase_hy(l, False)
                if l < DEPTH - 1:
                    phase_hy(l, True)
            if stop_after == "hy":
                break
        mk2 = f.mark()
        w1 = ffn_w1_alloc()
        phase_merge(l, prefetch=lambda: ffn_w1_load(l, w1))
        if stop_after == "merge":
            break
        phase_ffn(l, w1)
        f.release(mk2)
        if stop_after == "ffn":
            break
    else:
        done = True
    f.barrier()
    if done:
        phase_final()
    f.barrier(["sp"])
    f.release(0)
    return nc


def _fm(v, chunks):
    return np.ascontiguousarray(np.asarray(v, np.float32).reshape(chunks, 128).T)


def make_in_maps(inputs):
    g = {k: np.asarray(v) for k, v in inputs.items()}
    hc = host_constants()
    perm = rope_swap_perm()
    sh = {}
    sh["ada_w"] = np.ascontiguousarray(g["ada_w"], np.float32)
    sh["ada_bf"] = np.ascontiguousarray(np.stack([_fm(g["ada_b"][l], 48) for l in range(DEPTH)], 1))
    sh["ada_br"] = np.ascontiguousarray(np.broadcast_to(g["ada_b"][None], (2, DEPTH, 6 * D)), np.float32)
    sh["n1g"] = np.ascontiguousarray(np.stack([_fm(g["norm1_g"][l], 8) for l in range(DEPTH)], 1))
    sh["n2g"] = np.ascontiguousarray(np.stack([_fm(g["norm2_g"][l], 8) for l in range(DEPTH)], 1))
    sh["w_in"] = np.ascontiguousarray(g["w_in"], np.float32)
    wkr = np.zeros((DEPTH, D, 2, 96), np.float32)
    wkr[:, :, 0, 64:96] = g["w_in"][:, :, 384:416]
    wkr[:, :, 1, 64:96] = g["w_in"][:, :, 384:416][:, :, perm]
    sh["w_kr2"] = wkr
    sh["qng"] = np.ascontiguousarray(np.stack([_fm(g["mla_q_norm"][l], 2) for l in range(DEPTH)], 1))
    sh["kvng"] = np.ascontiguousarray(np.stack([g["mla_kv_norm"][l] for l in range(DEPTH)], 1), np.float32)
    sh["w_uq"] = np.ascontiguousarray(g["mla_w_uq"], np.float32)
    wsw = g["mla_w_uq"].reshape(DEPTH, 256, 8, 96).copy()
    wsw[:, :, :, 64:96] = wsw[:, :, :, 64:96][:, :, :, perm]
    sh["w_uq_sw"] = np.ascontiguousarray(wsw.reshape(DEPTH, 256, 768), np.float32)
    ukv = g["mla_w_ukv"].reshape(DEPTH, 128, 8, 128)
    sh["w_ukv_k"] = np.ascontiguousarray(ukv[:, :, :, 0:64].reshape(DEPTH, 128, 512), np.float32)
    sh["w_ukv_v"] = np.ascontiguousarray(ukv[:, :, :, 64:128].reshape(DEPTH, 128, 512), np.float32)
    sh["w_a2"] = np.ascontiguousarray(np.concatenate([g["gla_w_a2"][:, 0], g["gla_w_a2"][:, 1]], -1), np.float32)
    sh["b_a"] = np.ascontiguousarray(np.concatenate([g["gla_b_a"][:, 0], g["gla_b_a"][:, 1]], -1)[:, None, :], np.float32)
    for nm_ in ("w_o_mla", "w_o_gla", "w_o_hy", "w_out", "ff_w1", "ff_w2"):
        sh[nm_] = np.ascontiguousarray(g[nm_], np.float32)
    sw = g["hy_short_w"]
    swb = np.concatenate([sw, g["hy_short_b"][:, None, :]], 1)
    sh["hy_swb"] = np.ascontiguousarray(swb.reshape(DEPTH, 4, 12, 128).transpose(0, 3, 2, 1), np.float32)
    sh["hy_f_w1"] = np.ascontiguousarray(g["hy_f_w1"], np.float32)
    sh["hy_f_w2"] = np.ascontiguousarray(g["hy_f_w2"], np.float32)
    sh["hy_f_w3"] = np.ascontiguousarray(g["hy_f_w3"], np.float32)
    sh["hy_fb12"] = np.ascontiguousarray(np.stack([g["hy_f_b1"], g["hy_f_b2"]], -1), np.float32)
    sh["hy_fb3T"] = np.ascontiguousarray(g["hy_f_b3"].reshape(DEPTH, 16, 128).transpose(0, 2, 1), np.float32)
    sh["hy_biasT"] = np.ascontiguousarray(g["hy_bias"].reshape(DEPTH, 2, 4, 128).transpose(0, 3, 1, 2), np.float32)
    sh["fin_g"] = np.ascontiguousarray(np.broadcast_to(g["final_norm_g"][None, :], (128, D)), np.float32)
    sh["gla_on"] = np.ascontiguousarray(np.broadcast_to(g["gla_out_norm"][:, None, :], (DEPTH, 128, 128)), np.float32)
    for k, v in hc.items():
        sh[k] = v
    maps = []
    for b in range(8):
        m = dict(sh)
        m["xc"] = np.ascontiguousarray(np.concatenate([g["ctx"][b], g["x"][b]], 0), np.float32)
        m["cs"] = np.ascontiguousarray(np.stack([_fm(g["c"][b], 8), _fm(g["c_ctx"], 8)], -1))
        maps.append(m)
    return maps


_NC_CACHE = {}


def kernel(**inputs):
    if "nc" not in _NC_CACHE:
        _NC_CACHE["nc"] = build()
    nc = _NC_CACHE["nc"]
    maps = make_in_maps(inputs)
    res = run_bass_kernel_spmd(nc, maps, core_ids=list(range(8)))
    return np.stack([np.asarray(r["y"], np.float32) for r in res.results], 0)
```

```python
import math
import numpy as np
import ml_dtypes
import concourse.bass as bass
import concourse.mybir as mybir
from concourse.bass_utils import run_bass_kernel_spmd

F32 = mybir.dt.float32
BF16 = mybir.dt.bfloat16
AF = mybir.ActivationFunctionType
ALU = mybir.AluOpType

D = 1024
L = 4096
LC = 256
T = L + LC
NT = T // 128
DEPTH = 2
DIN = 6592
EPS = 1e-6
MLA_SCALE = 96 ** -0.5
TB = [(0, 256)] + [(256 + 512 * i, 512) for i in range(8)]
NFFT = 8192


class V:
    __slots__ = ("b", "ap")

    def __init__(self, b, ap):
        self.b = b
        self.ap = ap

    def __getitem__(self, idx):
        return V(self.b, self.ap[idx])

    def re(self, pat, **kw):
        return V(self.b, self.ap.rearrange(pat, **kw))

    def bc(self, shape):
        return V(self.b, self.ap.broadcast_to(list(shape)))

    def un(self, axis):
        return V(self.b, self.ap.unsqueeze(axis))

    def pb(self, n):
        return V(self.b, self.ap.partition_broadcast(n))


class Buf:
    __slots__ = ("t", "name", "lw", "rd", "psum", "dram")

    def __init__(self, t, name, psum=False, dram=False):
        self.t = t
        self.name = name
        self.lw = []
        self.rd = []
        self.psum = psum
        self.dram = dram

    def __getitem__(self, idx):
        return V(self, self.t[idx])

    @property
    def v(self):
        return V(self, self.t[:] if not hasattr(self.t, "ap") or True else self.t)


class FW:
    NDMA_SEM = 36
    NDMA_HW = 24

    def __init__(self, nc):
        self.nc = nc
        self.eng = {"pe": nc.tensor, "act": nc.scalar, "dve": nc.vector, "pool": nc.gpsimd, "sp": nc.sync}
        self.sem = {}
        self.cnt = {}
        for e in self.eng:
            self.sem[e] = nc.alloc_semaphore("s_" + e)
            self.cnt[e] = 0
        self.dsem = [nc.alloc_semaphore("d%d" % i) for i in range(self.NDMA_SEM)]
        self.dcnt = [0] * self.NDMA_SEM
        self.dnext = 0
        self.dnext_sw = 0
        self.seen = {e: {} for e in self.eng}
        self.ninst = 0
        self._ctx = []
        self._uid = 0
        self.deferred = []

    def _nm(self, name):
        self._uid += 1
        return "%s_%d" % (name, self._uid)

    def sbuf(self, name, shape, dt):
        g = self.nc.sbuf_tensor(self._nm(name), list(shape), dt)
        t = g.__enter__()
        self._ctx.append(g)
        return Buf(t, name)

    def psum(self, name, shape, dt=F32):
        g = self.nc.psum_tensor(self._nm(name), list(shape), dt)
        t = g.__enter__()
        self._ctx.append(g)
        return Buf(t, name, psum=True)

    def dram(self, name, shape, dt, kind="Internal"):
        t = self.nc.dram_tensor(name, list(shape), dt, kind=kind)
        return Buf(t.ap(), name, dram=True)

    def _wait(self, e, tok):
        if tok is None:
            return
        key, val = tok
        if e == "pe" and key == "pe":
            return
        if self.seen[e].get(key, 0) >= val:
            return
        self.seen[e][key] = val
        sem = self.sem[key] if isinstance(key, str) else self.dsem[key]
        self.eng[e].wait_ge(sem, val)

    def _deps(self, e, reads, writes, dma_write=False):
        for b in reads:
            for tok in b.lw:
                self._wait(e, tok)
        for b in writes:
            if not (dma_write and all(isinstance(t[0], int) for t in b.lw)):
                for tok in b.lw:
                    self._wait(e, tok)
            for tok in b.rd:
                self._wait(e, tok)

    @staticmethod
    def _compact(toks):
        best = {}
        for k, v in toks:
            if best.get(k, 0) < v:
                best[k] = v
        return list(best.items())

    def _commit(self, tok, reads, writes, dma_write=False):
        for b in reads:
            b.rd.append(tok)
            if len(b.rd) > 48:
                b.rd = self._compact(b.rd)
        for b in writes:
            if dma_write and b.lw and all(isinstance(t[0], int) for t in b.lw):
                b.lw.append(tok)
                if len(b.lw) > 48:
                    b.lw = self._compact(b.lw)
            else:
                b.lw = [tok]
            b.rd = []

    def flush(self):
        d, self.deferred = self.deferred, []
        for (q, out, in_, kw) in d:
            self._dma_now(q, out, in_, **kw)

    def op(self, e, fn, reads=(), writes=()):
        if self.deferred:
            self.flush()
        rd = [b for b in reads if not b.psum]
        wr = list(writes) + [b for b in reads if b.psum]
        self._deps(e, rd, wr)
        ins = fn(self.eng[e])
        self.cnt[e] += 1
        ins.then_inc(self.sem[e], 1)
        self._commit((e, self.cnt[e]), rd, wr)
        self.ninst += 1
        return ins

    def dma(self, q, out, in_, **kw):
        if out.b.dram and not in_.b.dram:
            self.deferred.append((q, out, in_, kw))
            return
        for (_, so, si, _) in self.deferred:
            if so.b is in_.b or si.b is out.b or so.b is out.b:
                self.flush()
                break
        self._dma_now(q, out, in_, **kw)

    def _dma_now(self, q, out, in_, **kw):
        if q == "pool":
            slot = self.NDMA_HW + self.dnext_sw
            self.dnext_sw = (self.dnext_sw + 1) % (self.NDMA_SEM - self.NDMA_HW)
        else:
            slot = self.dnext
            self.dnext = (self.dnext + 1) % self.NDMA_HW
        if self.dcnt[slot] > 0:
            self._wait(q, (slot, self.dcnt[slot]))
        self._deps(q, [in_.b], [out.b], dma_write=True)
        ins = self.eng[q].dma_start(out=out.ap, in_=in_.ap, **kw)
        self.dcnt[slot] += 16
        ins.then_inc(self.dsem[slot], 16)
        self._commit((slot, self.dcnt[slot]), [in_.b], [out.b], dma_write=True)
        self.ninst += 1

    def barrier(self, engines=None):
        self.flush()
        for e in (engines or self.eng):
            for e2 in self.eng:
                if e2 != e and self.cnt[e2] > 0:
                    self._wait(e, (e2, self.cnt[e2]))
            for s in range(self.NDMA_SEM):
                if self.dcnt[s] > 0:
                    self._wait(e, (s, self.dcnt[s]))

    def mark(self):
        return len(self._ctx)

    def release(self, mark):
        self.barrier()
        while len(self._ctx) > mark:
            self._ctx.pop().__exit__(None, None, None)

    def mm(self, out, lhsT, rhs, start=True, stop=True):
        return self.op("pe", lambda e: e.matmul(out.ap, lhsT=lhsT.ap, rhs=rhs.ap, start=start, stop=stop),
                       reads=[lhsT.b, rhs.b], writes=[out.b])

    def tr(self, out, in_, ident):
        return self.op("pe", lambda e: e.transpose(out.ap, in_.ap, ident.ap), reads=[in_.b, ident.b], writes=[out.b])

    def act(self, out, in_, func, bias=None, scale=None, accum=None, eng="act"):
        kw = {}
        rd = [in_.b]
        wr = [out.b]
        if bias is not None:
            if isinstance(bias, V):
                kw["bias"] = bias.ap
                rd.append(bias.b)
            else:
                kw["bias"] = bias
        if scale is not None:
            if isinstance(scale, V):
                kw["scale"] = scale.ap
                rd.append(scale.b)
            else:
                kw["scale"] = scale
        if accum is not None:
            kw["accum_out"] = accum.ap
            wr.append(accum.b)
        return self.op("act", lambda e: e.activation(out=out.ap, in_=in_.ap, func=func, **kw), reads=rd, writes=wr)

    def tt(self, out, a, b, op, eng="dve"):
        return self.op(eng, lambda e: e.tensor_tensor(out=out.ap, in0=a.ap, in1=b.ap, op=op),
                       reads=[a.b, b.b], writes=[out.b])

    def ts(self, out, a, s1, op0, s2=None, op1=None, eng="dve"):
        rd = [a.b]
        s1a = s1
        s2a = s2
        if isinstance(s1, V):
            rd.append(s1.b)
            s1a = s1.ap
        if isinstance(s2, V):
            rd.append(s2.b)
            s2a = s2.ap
        kw = {}
        if op1 is not None:
            kw["op1"] = op1
        return self.op(eng, lambda e: e.tensor_scalar(out=out.ap, in0=a.ap, scalar1=s1a, scalar2=s2a, op0=op0, **kw),
                       reads=rd, writes=[out.b])

    def stt(self, out, a, s, b, op0, op1, eng="dve"):
        rd = [a.b, b.b]
        sa = s
        if isinstance(s, V):
            rd.append(s.b)
            sa = s.ap
        return self.op(eng, lambda e: e.scalar_tensor_tensor(out=out.ap, in0=a.ap, scalar=sa, in1=b.ap, op0=op0, op1=op1),
                       reads=rd, writes=[out.b])

    def cp(self, out, in_, eng="dve"):
        if eng == "act":
            return self.op("act", lambda e: e.copy(out=out.ap, in_=in_.ap), reads=[in_.b], writes=[out.b])
        return self.op(eng, lambda e: e.tensor_copy(out=out.ap, in_=in_.ap), reads=[in_.b], writes=[out.b])

    def memset(self, out, val, eng="pool"):
        return self.op(eng, lambda e: e.memset(out.ap, val), writes=[out.b])

    def recip(self, out, in_):
        return self.op("dve", lambda e: e.reciprocal(out=out.ap, in_=in_.ap), reads=[in_.b], writes=[out.b])


class Pool:
    def __init__(self, bufs):
        self.bufs = bufs
        self.i = 0

    def next(self):
        b = self.bufs[self.i % len(self.bufs)]
        self.i += 1
        return b


def _bf(a):
    return np.ascontiguousarray(a.astype(ml_dtypes.bfloat16))


def host_constants():
    c = {}
    c["ident_bf"] = _bf(np.eye(128, dtype=np.float32))
    c["ident_f"] = np.eye(128, dtype=np.float32)
    c["ones_f"] = np.ones((128, 128), np.float32)
    rows = L // 64
    row = np.repeat(np.arange(rows, dtype=np.float32), 64)
    col = np.tile(np.arange(64, dtype=np.float32), rows)
    inv = (10000.0 ** (-np.arange(8, dtype=np.float32) / 8)).astype(np.float32)
    ang = np.concatenate([row[:, None] * inv, col[:, None] * inv], axis=-1)
    cos, sin = np.cos(ang), np.sin(ang)
    cosT = np.ones((96, T), np.float32)
    sinT = np.zeros((96, T), np.float32)
    for r in range(32):
        g, j = r // 16, r % 16
        half, i = j // 8, j % 8
        cosT[64 + r, LC:] = cos[:, g * 8 + i]
        sinT[64 + r, LC:] = (-sin[:, g * 8 + i]) if half == 0 else sin[:, g * 8 + i]
    c["cosT"] = cosT
    c["sinT"] = sinT
    i_ = np.arange(128)[None, :]
    j_ = np.arange(128)[:, None]
    Mf = ((j_ >= 64) & (j_ <= i_)).astype(np.float32) - ((j_ > i_) & (j_ <= 63)).astype(np.float32)
    Mb = ((j_ >= i_) & (j_ <= 63)).astype(np.float32) - ((j_ >= 64) & (j_ < i_)).astype(np.float32)
    c["gla_M"] = np.stack([Mf, Mb], 1).astype(np.float32)
    c["gla_mask"] = np.stack([(j_ <= i_), (j_ >= i_)], 1).astype(np.float32)
    ind = np.zeros((128, 2), np.float32)
    ind[:64, 0] = 1
    ind[64:, 1] = 1
    c["gla_ind"] = ind
    deltas = np.linspace(math.log(1e-2) / 0.3, math.log(1e-2) / 1.5, 512)
    fb = np.linspace(1e-4, 15.0, 16)
    b_ = np.arange(64)
    fbb = np.arange(64)
    for sfx, Ls in (("", L), ("_c", LC)):
        NF = 2 * Ls
        NA = NF // 64
        NFA = NA // 2 + 1
        pos = np.arange(Ls, dtype=np.float64)
        tn = pos / max(Ls - 1, 1)
        ang = (2.0 * math.pi / Ls) * pos[:, None] * fb
        feat = np.concatenate([tn[:, None], np.cos(ang), np.sin(ang)], -1)
        win = np.exp(-tn[:, None] * np.abs(deltas))
        feat2 = np.zeros((NF, 33))
        win2 = np.zeros((NF, 512))
        feat2[:Ls] = feat
        win2[:Ls] = win
        idx = np.arange(NF - Ls + 1, NF)
        feat2[idx] = feat[NF - idx]
        win2[idx] = win[NF - idx]
        c["hy_featT" + sfx] = np.ascontiguousarray(feat2.T.astype(np.float32))
        tn2 = np.full((1, NF), 1.0e4)
        tn2[0, :Ls] = tn
        tn2[0, idx] = tn[NF - idx]
        c["hy_tn2" + sfx] = tn2.astype(np.float32)
        a_ = np.arange(NA)[:, None]
        fa = np.arange(NFA)[None, :]
        th = 2 * math.pi * ((fa * a_) % NA) / NA
        c["hy_F1" + sfx] = _bf(np.concatenate([np.cos(th), -np.sin(th), np.sin(th)], 1))
        E2r = np.zeros((64, 2, NFA, 64, 2))
        E2i = np.zeros((64, 2, NFA, 64, 2))
        ph = 2 * math.pi * (((np.arange(NFA)[None, :, None] + NA * fbb[None, None, :]) * b_[:, None, None]) % NF) / NF
        for cp in range(2):
            E2r[:, cp, :, :, cp] = np.cos(ph)
            E2i[:, cp, :, :, cp] = -np.sin(ph)
        c["hy_E2r" + sfx] = _bf(E2r.reshape(128, NFA, 128))
        c["hy_E2i" + sfx] = _bf(E2i.reshape(128, NFA, 128))
        tt_ = 64 * np.arange(NA // 2)[None, None, :] + b_[None, :, None]
        th2 = 2 * math.pi * ((np.arange(NFA)[:, None, None] * tt_) % NF) / NF
        wgt = np.full((NFA, 1, 1), 2.0 / NF)
        wgt[0] = wgt[NFA - 1] = 1.0 / NF
        c["hy_DBr" + sfx] = _bf(wgt * np.cos(th2))
        c["hy_DBni" + sfx] = _bf(-wgt * np.sin(th2))
    c["hy_negdelta"] = np.ascontiguousarray((-np.abs(deltas)).reshape(4, 128).T.astype(np.float32))
    psi = 2 * math.pi * ((fbb[:, None] * b_[None, :]) % 64) / 64
    CA = np.zeros((64, 2, 3, 64, 2))
    for cp in range(2):
        CA[:, cp, 0, :, cp] = np.cos(psi)
        CA[:, cp, 1, :, cp] = np.sin(psi)
        CA[:, cp, 2, :, cp] = -np.sin(psi)
    c["hy_CA"] = _bf(CA.reshape(128, 3, 128))
    return c


def rope_swap_perm():
    p = np.zeros(32, np.int64)
    for r in range(32):
        g, j = r // 16, r % 16
        p[r] = g * 16 + (j + 8) % 16
    return p


def build(dbg=None):
    dbg = dbg or {}
    stop_after = dbg.get("stop_after", None)
    ext = dbg.get("ext", ())
    nc = bass.Bass("TRN2", target_bir_lowering=False)
    f = FW(nc)
    hc = host_constants()

    def inp(name, shape, dt=F32):
        return Buf(nc.dram_tensor(name, list(shape), dt, kind="ExternalInput").ap(), name, dram=True)

    def scratch(name, shape, dt):
        if name in dbg.get("inject", ()):
            return f.dram(name, shape, dt, kind="ExternalInput")
        return f.dram(name, shape, dt, kind="ExternalOutput" if name in ext else "Internal")

    I = {}
    I["xc"] = inp("xc", [T, D])
    I["cs"] = inp("cs", [128, 8, 2])
    I["ada_w"] = inp("ada_w", [DEPTH, D, 6 * D])
    I["ada_bf"] = inp("ada_bf", [128, DEPTH, 48])
    I["ada_br"] = inp("ada_br", [2, DEPTH, 6 * D])
    I["n1g"] = inp("n1g", [128, DEPTH, 8])
    I["n2g"] = inp("n2g", [128, DEPTH, 8])
    I["w_in"] = inp("w_in", [DEPTH, D, DIN])
    I["w_kr2"] = inp("w_kr2", [DEPTH, D, 2, 96])
    I["qng"] = inp("qng", [128, DEPTH, 2])
    I["kvng"] = inp("kvng", [128, DEPTH])
    I["w_uq"] = inp("w_uq", [DEPTH, 256, 768])
    I["w_uq_sw"] = inp("w_uq_sw", [DEPTH, 256, 768])
    I["w_ukv_k"] = inp("w_ukv_k", [DEPTH, 128, 512])
    I["w_ukv_v"] = inp("w_ukv_v", [DEPTH, 128, 512])
    I["w_a2"] = inp("w_a2", [DEPTH, 16, 512])
    I["b_a"] = inp("b_a", [DEPTH, 1, 512])
    I["gla_on"] = inp("gla_on", [DEPTH, 128, 128])
    for nm_ in ("w_o_mla", "w_o_gla", "w_o_hy"):
        I[nm_] = inp(nm_, [DEPTH, 512, D])
    I["w_out"] = inp("w_out", [DEPTH, D, D])
    I["ff_w1"] = inp("ff_w1", [DEPTH, D, 4 * D])
    I["ff_w2"] = inp("ff_w2", [DEPTH, 4 * D, D])
    I["fin_g"] = inp("fin_g", [128, D])
    I["hy_swb"] = inp("hy_swb", [DEPTH, 128, 12, 4])
    I["hy_f_w1"] = inp("hy_f_w1", [DEPTH, 33, 64])
    I["hy_f_w2"] = inp("hy_f_w2", [DEPTH, 64, 64])
    I["hy_f_w3"] = inp("hy_f_w3", [DEPTH, 64, 2048])
    I["hy_fb12"] = inp("hy_fb12", [DEPTH, 64, 2])
    I["hy_fb3T"] = inp("hy_fb3T", [DEPTH, 128, 16])
    I["hy_biasT"] = inp("hy_biasT", [DEPTH, 128, 2, 4])
    for k, v in hc.items():
        I[k] = inp(k, list(v.shape), BF16 if v.dtype == ml_dtypes.bfloat16 else F32)
    out_y = Buf(nc.dram_tensor("y", [L, D], F32, kind="ExternalOutput").ap(), "y", dram=True)

    xres = scratch("xres", [T, D], F32)
    modrow = scratch("modrow", [2, DEPTH, 2, D], F32)
    qT_d = scratch("qT_d", [8, 96, T], BF16)
    kT_d = scratch("kT_d", [8, 96, T], BF16)
    V_d = scratch("V_d", [T, 520], BF16)
    gqk_d = scratch("gqk_d", [T, 512], F32)
    gv_d = scratch("gv_d", [T, 512], BF16)
    gr_d = scratch("gr_d", [T, 512], F32)
    gg_d = scratch("gg_d", [T, 512], F32)
    zhyT_d = scratch("zhyT_d", [1536, T], BF16)
    scT_d = scratch("scT_d", [1536, T], BF16)
    H_d = scratch("H_d", [2, 8, 128, 2 * 65 * 32], BF16)
    y1T_d = scratch("y1T_d", [512, T], BF16)
    gatesT_d = scratch("gatesT_d", [3072, T], BF16)
    hT_d = scratch("hT_d", [D, T], BF16) if "hT_d" in ext else None
    ymlaT_d = scratch("ymlaT_d", [512, T], BF16)
    yglaT_d = scratch("yglaT_d", [512, T], BF16)
    yhyT_d = scratch("yhyT_d", [512, T], BF16)
    og_d = scratch("og_d", [2, T, 512], F32)

    ident = f.sbuf("ident", [128, 128], BF16)
    ones_f = f.sbuf("ones_f", [128, 128], F32)
    modF = f.sbuf("modF", [128, DEPTH, 48, 2], F32)
    AB = f.sbuf("AB", [128, DEPTH, 2, 2, 8, 2], F32)
    f.dma("sp", ident[:], I["ident_bf"][:])
    f.dma("sp", ones_f[:], I["ones_f"][:])
    xres_t = [Buf(xres.t[tt * 128:(tt + 1) * 128, :], "xres%d" % tt, dram=True) for tt in range(NT)]
    for tt in range(NT):
        f.dma("sp", xres_t[tt][:], I["xc"][tt * 128:(tt + 1) * 128, :])

    def phase_mod():
        m = f.mark()
        cs = f.sbuf("cs", [128, 8, 2], F32)
        scs = f.sbuf("scs", [128, 8, 2], F32)
        abf = f.sbuf("abf", [128, DEPTH, 48], F32)
        abr = f.sbuf("abr", [2, DEPTH, 6 * D], F32)
        g1 = f.sbuf("g1", [128, DEPTH, 8], F32)
        g2 = f.sbuf("g2", [128, DEPTH, 8], F32)
        wp = Pool([f.sbuf("adaw%d" % i, [128, 8, 512], F32) for i in range(2)])
        rowst = f.sbuf("rowst", [2, 512], F32)
        psF = f.psum("psF", [128, 512], F32)
        psR = Pool([f.psum("psR%d" % i, [128, 512], F32) for i in range(2)])
        f.dma("sp", cs[:], I["cs"][:])
        f.dma("sp", abf[:], I["ada_bf"][:])
        f.dma("sp", abr[:], I["ada_br"][:])
        f.dma("sp", g1[:], I["n1g"][:])
        f.dma("sp", g2[:], I["n2g"][:])
        f.act(scs[:], cs[:], AF.Silu)
        for l in range(DEPTH):
            wv = I["ada_w"][l].re("(kc p) j -> p kc j", p=128)
            for jb in range(12):
                w = wp.next()
                f.dma("sp", w[:], wv[:, :, jb * 512:(jb + 1) * 512])
                which = jb // 2
                if which in (2, 5):
                    pr = psR.next()
                    for kc in range(8):
                        f.mm(pr[0:2, :], scs[:, kc, :], w[:, kc, :], start=kc == 0, stop=kc == 7)
                    f.tt(rowst[:], pr[0:2, :], abr[:, l, jb * 512:(jb + 1) * 512], ALU.add)
                    f.dma("sp", modrow[:, l, 0 if which == 2 else 1, (jb % 2) * 512:(jb % 2 + 1) * 512], rowst[:])
                else:
                    for jc in range(4):
                        ch = jb * 4 + jc
                        for kc in range(8):
                            f.mm(psF[:, ch * 2:ch * 2 + 2], w[:, kc, jc * 128:(jc + 1) * 128], scs[:, kc, :],
                                 start=kc == 0, stop=kc == 7)
            for (c0, c1) in ((0, 16), (24, 40)):
                pv = psF[:, c0 * 2:c1 * 2].re("p (c s) -> p c s", s=2)
                f.tt(modF[:, l, c0:c1, :], pv, abf[:, l, c0:c1].un(2).bc([128, c1 - c0, 2]), ALU.add)
            for n_i, (sh0, sc0, g) in enumerate(((0, 8, g1), (24, 32, g2))):
                f.stt(AB[:, l, n_i, 0], modF[:, l, sc0:sc0 + 8, :], 1.0, g[:, l, :].un(2).bc([128, 8, 2]), ALU.add, ALU.mult)
                f.cp(AB[:, l, n_i, 1], modF[:, l, sh0:sh0 + 8, :])
        f.release(m)

    class NormCtx:
        def __init__(self):
            self.xp = Pool([f.sbuf("nx%d" % i, [128, D], F32) for i in range(2)])
            self.xnp = Pool([f.sbuf("nxn%d" % i, [128, D], BF16) for i in range(2)])
            self.stp = Pool([f.sbuf("nst%d" % i, [128, 4], F32) for i in range(3)])
            self.pp = Pool([f.psum("nps%d" % i, [128, 8, 128], BF16) for i in range(2)])

        def emit(self, l, n_i, hT, tt, c0):
            s = 0 if tt >= 2 else 1
            x = self.xp.next()
            st = self.stp.next()
            xn = self.xnp.next()
            f.dma("sp", x[:], xres_t[tt][:])
            f.act(xn[:], x[:], AF.Square, accum=st[:, 0:1])
            f.act(st[:, 1:2], st[:, 0:1], AF.Sqrt, bias=EPS_T[:, 0:1], scale=1.0 / D)
            f.recip(st[:, 2:3], st[:, 1:2])
            f.act(xn[:], x[:], AF.Identity, scale=st[:, 2:3])
            ps = self.pp.next()
            for kc in range(8):
                f.tr(ps[:, kc, :], xn[:, kc * 128:(kc + 1) * 128], ident[:])
            for kc in range(8):
                if kc % 2 == 0:
                    f.act(hT[:, kc, c0:c0 + 128], ps[:, kc, :], AF.Identity,
                          bias=AB[:, l, n_i, 1, kc, s:s + 1], scale=AB[:, l, n_i, 0, kc, s:s + 1])
                else:
                    f.ts(hT[:, kc, c0:c0 + 128], ps[:, kc, :], AB[:, l, n_i, 0, kc, s:s + 1], ALU.mult,
                         AB[:, l, n_i, 1, kc, s:s + 1], ALU.add)

    def norm_tiles(l, n_i, hT, tiles):
        m = f.mark()
        nctx = NormCtx()
        for tt in tiles:
            nctx.emit(l, n_i, hT, tt, tt * 128)
        f.release(m)

    EPS_T = f.sbuf("eps_t", [128, 1], F32)
    f.memset(EPS_T[:], EPS)

    def phase_proj(l, hT):
        m = f.mark()
        wp = Pool([f.sbuf("pw%d" % i, [128, 8, 512], BF16) for i in range(3)])
        psp = Pool([f.psum("pps%d" % i, [128, 512], F32) for i in range(4)])
        psq = Pool([f.psum("ppq%d" % i, [128, 512], F32) for i in range(3)])
        w_l = I["w_in"][l].re("(kc p) n -> p kc n", p=128)

        def loadw(c0, n):
            w = wp.next()
            f.dma("pool", w[:, :, 0:n], w_l[:, :, c0:c0 + n])
            return w

        def fm(w, wc0, ncol, evac):
            for bi, (t0, n) in enumerate(TB):
                ps = psp.next()
                for kc in range(8):
                    f.mm(ps[0:ncol, 0:n], w[:, kc, wc0:wc0 + ncol], hT[:, kc, t0:t0 + n], start=kc == 0, stop=kc == 7)
                evac(ps, bi, t0, n)

        def tm(w, ncol, evac):
            for tt in range(NT):
                ps = psp.next()
                for kc in range(8):
                    f.mm(ps[:, 0:ncol], hT[:, kc, tt * 128:(tt + 1) * 128], w[:, kc, 0:ncol], start=kc == 0, stop=kc == 7)
                evac(ps, tt)

        w0 = loadw(0, 384)
        wk = wp.next()
        f.dma("pool", wk[:, :, 0:192], I["w_kr2"][l].re("(kc p) a n -> p kc (a n)", p=128))
        qng = f.sbuf("qng", [128, DEPTH, 2], F32)
        kvng = f.sbuf("kvng", [128, DEPTH], F32)
        f.dma("sp", qng[:], I["qng"][:])
        f.dma("sp", kvng[:], I["kvng"][:])
        wuq = f.sbuf("wuq", [128, 2, 768], BF16)
        wuqs = f.sbuf("wuqs", [128, 2, 768], BF16)
        wk_k = f.sbuf("wukv_k", [128, 512], BF16)
        wk_v = f.sbuf("wukv_v", [128, 512], BF16)
        f.dma("pool", wuq[:], I["w_uq"][l].re("(kc p) n -> p kc n", p=128))
        f.dma("pool", wuqs[:], I["w_uq_sw"][l].re("(kc p) n -> p kc n", p=128))
        f.dma("pool", wk_k[:], I["w_ukv_k"][l])
        f.dma("pool", wk_v[:], I["w_ukv_v"][l])
        cosp = Pool([f.sbuf("cosb%d" % i, [96, 512], F32) for i in range(2)])
        sinp = Pool([f.sbuf("sinb%d" % i, [96, 512], F32) for i in range(2)])
        cqp = Pool([f.sbuf("cqb%d" % i, [128, 2, 512], F32) for i in range(2)])
        ckvp = Pool([f.sbuf("ckvb%d" % i, [128, 512], F32) for i in range(2)])
        cqnp = Pool([f.sbuf("cqnb%d" % i, [128, 2, 512], BF16) for i in range(2)])
        ckvnp = Pool([f.sbuf("ckvnb%d" % i, [128, 512], BF16) for i in range(2)])
        krp = Pool([f.sbuf("krb%d" % i, [96, 512], BF16) for i in range(2)])
        tmpa = Pool([f.sbuf("ptmpa%d" % i, [128, 512], F32) for i in range(2)])
        tmpb = Pool([f.sbuf("ptmpb%d" % i, [128, 512], F32) for i in range(2)])
        sqp = Pool([f.sbuf("psq%d" % i, [128, 2, 512], F32) for i in range(2)])
        rsp = Pool([f.sbuf("prs%d" % i, [128, 512], F32) for i in range(2)])
        qst = Pool([f.sbuf("qst%d" % i, [96, 512], BF16) for i in range(3)])
        kst = Pool([f.sbuf("kst%d" % i, [96, 512], BF16) for i in range(3)])
        vst = Pool([f.sbuf("vst%d" % i, [128, 8, 65], BF16) for i in range(2)])
        for vb in vst.bufs:
            f.memset(vb[:], 1.0)
        for bi, (t0, n) in enumerate(TB):
            cosb = cosp.next()
            sinb = sinp.next()
            f.dma("sp", cosb[64:96, 0:n], I["cosT"][64:96, t0:t0 + n])
            f.dma("sp", sinb[64:96, 0:n], I["sinT"][64:96, t0:t0 + n])
            cq = cqp.next()
            ckv = ckvp.next()
            for c in range(3):
                ps = psp.next()
                for kc in range(8):
                    f.mm(ps[:, 0:n], w0[:, kc, c * 128:(c + 1) * 128], hT[:, kc, t0:t0 + n], start=kc == 0, stop=kc == 7)
                f.cp(cq[:, c, 0:n] if c < 2 else ckv[:, 0:n], ps[:, 0:n], eng="act")
            pa = psp.next()
            pb = psp.next()
            for kc in range(8):
                f.mm(pa[0:96, 0:n], wk[:, kc, 0:96], hT[:, kc, t0:t0 + n], start=kc == 0, stop=kc == 7)
            for kc in range(8):
                f.mm(pb[0:96, 0:n], wk[:, kc, 96:192], hT[:, kc, t0:t0 + n], start=kc == 0, stop=kc == 7)
            ta = tmpa.next()
            tb_ = tmpb.next()
            krb = krp.next()
            f.tt(ta[64:96, 0:n], pa[64:96, 0:n], cosb[64:96, 0:n], ALU.mult)
            f.tt(tb_[64:96, 0:n], pb[64:96, 0:n], sinb[64:96, 0:n], ALU.mult)
            f.tt(krb[64:96, 0:n], ta[64:96, 0:n], tb_[64:96, 0:n], ALU.add, eng="pool")
            cqn = cqnp.next()
            ckvn = ckvnp.next()
            for (nchunk, gains) in ((2, qng), (1, kvng)):
                sq = sqp.next()
                ps = psp.next()
                for c in range(nchunk):
                    sv = cq[:, c, 0:n] if nchunk == 2 else ckv[:, 0:n]
                    f.tt(sq[:, c, 0:n], sv, sv, ALU.mult)
                for c in range(nchunk):
                    f.mm(ps[:, 0:n], ones_f[:], sq[:, c, 0:n], start=c == 0, stop=c == nchunk - 1)
                rs = rsp.next()
                f.act(rs[:, 0:n], ps[:, 0:n], AF.Sqrt, bias=EPS_T[:, 0:1], scale=1.0 / (128 * nchunk))
                f.recip(rs[:, 0:n], rs[:, 0:n])
                for c in range(nchunk):
                    sv = cq[:, c, 0:n] if nchunk == 2 else ckv[:, 0:n]
                    dv = cqn[:, c, 0:n] if nchunk == 2 else ckvn[:, 0:n]
                    gv = gains[:, l, c:c + 1] if nchunk == 2 else gains[:, l:l + 1]
                    f.stt(dv, sv, gv, rs[:, 0:n], ALU.mult, ALU.mult)
            for h in range(8):
                pa = psq.next()
                pb = psq.next()
                for kc in range(2):
                    f.mm(pa[0:96, 0:n], wuq[:, kc, h * 96:(h + 1) * 96], cqn[:, kc, 0:n], start=kc == 0, stop=kc == 1)
                for kc in range(2):
                    f.mm(pb[0:96, 0:n], wuqs[:, kc, h * 96:(h + 1) * 96], cqn[:, kc, 0:n], start=kc == 0, stop=kc == 1)
                q = qst.next()
                ta = tmpa.next()
                tb_ = tmpb.next()
                f.cp(q[0:64, 0:n], pa[0:64, 0:n], eng="act")
                f.tt(ta[64:96, 0:n], pa[64:96, 0:n], cosb[64:96, 0:n], ALU.mult)
                f.tt(tb_[64:96, 0:n], pb[64:96, 0:n], sinb[64:96, 0:n], ALU.mult)
                f.tt(q[64:96, 0:n], ta[64:96, 0:n], tb_[64:96, 0:n], ALU.add, eng="pool")
                f.dma("sp", qT_d[h, :, t0:t0 + n], q[:, 0:n])
                pk = psq.next()
                f.mm(pk[0:64, 0:n], wk_k[:, h * 64:(h + 1) * 64], ckvn[:, 0:n])
                k = kst.next()
                f.cp(k[0:64, 0:n], pk[0:64, 0:n], eng="act")
                f.cp(k[64:96, 0:n], krb[64:96, 0:n], eng="pool")
                f.dma("sp", kT_d[h, :, t0:t0 + n], k[:, 0:n])
            for ti in range(n // 128):
                tt = t0 // 128 + ti
                pv = psq.next()
                f.mm(pv[:, :], ckvn[:, ti * 128:(ti + 1) * 128], wk_v[:, :])
                vs = vst.next()
                f.cp(vs[:, :, 0:64], pv[:, :].re("p (h e) -> p h e", e=64), eng="act")
                f.dma("sp", V_d[tt * 128:(tt + 1) * 128, :], vs[:].re("p h e -> p (h e)"))

        wa = loadw(1952, 32)
        wa2 = f.sbuf("wa2", [16, 512], F32)
        ba = f.sbuf("ba", [1, 512], F32)
        f.dma("sp", wa2[:], I["w_a2"][l])
        f.dma("sp", ba[:], I["b_a"][l])
        aft = Pool([f.sbuf("aft%d" % i, [16, 512], F32) for i in range(2)])
        abt = Pool([f.sbuf("abt%d" % i, [16, 512], F32) for i in range(2)])
        ggp = Pool([f.sbuf("ggs%d" % i, [128, 512], F32) for i in range(2)])
        for bi, (t0, n) in enumerate(TB):
            pa = psp.next()
            pb = psp.next()
            for kc in range(8):
                f.mm(pa[0:16, 0:n], wa[:, kc, 0:16], hT[:, kc, t0:t0 + n], start=kc == 0, stop=kc == 7)
            for kc in range(8):
                f.mm(pb[0:16, 0:n], wa[:, kc, 16:32], hT[:, kc, t0:t0 + n], start=kc == 0, stop=kc == 7)
            af = aft.next()
            ab = abt.next()
            f.cp(af[:, 0:n], pa[0:16, 0:n], eng="act")
            f.cp(ab[:, 0:n], pb[0:16, 0:n], eng="act")
            for ti in range(n // 128):
                pg = psq.next()
                f.mm(pg[:, 0:256], af[:, ti * 128:(ti + 1) * 128], wa2[:, 0:256], start=True, stop=False)
                f.mm(pg[:, 0:256], ones_f[0:1, :], ba[:, 0:256], start=False, stop=True)
                f.mm(pg[:, 256:512], ab[:, ti * 128:(ti + 1) * 128], wa2[:, 256:512], start=True, stop=False)
                f.mm(pg[:, 256:512], ones_f[0:1, :], ba[:, 256:512], start=False, stop=True)
                gs = ggp.next()
                f.act(gs[:], pg[:], AF.Exp, scale=-1.0)
                f.act(gs[:], gs[:], AF.Ln, bias=ONE_T[:, 0:1])
                f.ts(gs[:], gs[:], -1.0 / 16.0, ALU.mult)
                tt = t0 // 128 + ti
                f.dma("sp", gg_d[tt * 128:(tt + 1) * 128, :], gs[:])

        st32 = Pool([f.sbuf("pst32_%d" % i, [128, 512], F32) for i in range(3)])
        st16 = Pool([f.sbuf("pst16_%d" % i, [128, 512], BF16) for i in range(3)])

        def ev_qk(ps, tt):
            s = st32.next()
            f.act(s[:, 0:256], ps[:, 0:256], AF.Copy, scale=0.125)
            f.cp(s[:, 256:512], ps[:, 256:512], eng="dve")
            f.dma("sp", gqk_d[tt * 128:(tt + 1) * 128, :], s[:])

        def ev_to(dst, c0, dt16):
            def ev(ps, tt):
                s = (st16 if dt16 else st32).next()
                f.cp(s[:], ps[:], eng="act" if tt % 2 else "dve")
                f.dma("sp", dst[tt * 128:(tt + 1) * 128, c0:c0 + 512], s[:])
            return ev

        tm(loadw(416, 512), 512, ev_qk)
        tm(loadw(928, 512), 512, ev_to(gv_d, 0, True))
        tm(loadw(1440, 512), 512, ev_to(gr_d, 0, False))
        for sc in range(3):
            w = loadw(1984 + sc * 512, 512)
            for c in range(4):
                ch = sc * 4 + c

                def ev_h(ps, bi, t0, n, ch=ch):
                    s = st16.next()
                    f.cp(s[:, 0:n], ps[:, 0:n], eng="act" if (bi + ch) % 2 else "dve")
                    f.dma("sp", zhyT_d[ch * 128:(ch + 1) * 128, t0:t0 + n], s[:, 0:n])
                fm(w, c * 128, 128, ev_h)

        for sc in range(6):
            w = loadw(3520 + sc * 512, 512)
            for c in range(4):
                ch = sc * 4 + c

                def ev_g(ps, bi, t0, n, ch=ch):
                    s = st16.next()
                    f.act(s[:, 0:n], ps[:, 0:n], AF.Sigmoid)
                    f.dma("sp", gatesT_d[ch * 128:(ch + 1) * 128, t0:t0 + n], s[:, 0:n])
                fm(w, c * 128, 128, ev_g)
        f.release(m)

    def phase_att(l):
        m = f.mark()
        Vaug = f.sbuf("Vaug", [128, NT, 520], BF16)
        f.dma("sp", Vaug[:], V_d[:].re("(t p) e -> p t e", p=128))
        qp = Pool([f.sbuf("aq%d" % i, [96, T], BF16) for i in range(2)])
        kp = Pool([f.sbuf("ak%d" % i, [96, T], BF16) for i in range(2)])
        pp = Pool([f.sbuf("ap%d" % i, [128, 2, 512], BF16) for i in range(3)])
        sps = Pool([f.psum("as%d" % i, [128, 2, 512], F32) for i in range(2)])
        ops = Pool([f.psum("ao%d" % i, [128, 512], F32) for i in range(2)])
        bps = f.psum("abc", [128, 512], F32)
        rec = Pool([f.sbuf("arec%d" % i, [65, 512], F32) for i in range(2)])
        osb = Pool([f.sbuf("aosb%d" % i, [64, 512], F32) for i in range(2)])
        yst = Pool([f.sbuf("ayst%d" % i, [64, 512], BF16) for i in range(3)])
        pending = []

        def epilogue(o, h, t0, n):
            r = rec.next()
            f.recip(r[64:65, 0:n], o[64:65, 0:n])
            f.mm(bps[0:64, 0:n], ones_f[64:65, 0:64], r[64:65, 0:n])
            os_ = osb.next()
            f.cp(os_[:, 0:n], o[0:64, 0:n], eng="act")
            y = yst.next()
            f.tt(y[:, 0:n], os_[:, 0:n], bps[0:64, 0:n], ALU.mult)
            f.dma("sp", ymlaT_d[h * 64:(h + 1) * 64, t0:t0 + n], y[:, 0:n])

        for h in range(8):
            q = qp.next()
            k = kp.next()
            f.dma("sp", q[:], qT_d[h])
            f.dma("sp", k[:], kT_d[h])
            for bi, (t0, n) in enumerate(TB):
                if bi == 0 and l == DEPTH - 1:
                    continue
                nk = 2 if bi == 0 else NT
                npair = nk // 2
                o = ops.next()
                pend = None
                for jp in range(npair + 1):
                    p = None
                    if jp < npair:
                        s = sps.next()
                        for u_ in range(2):
                            j = jp * 2 + u_
                            f.mm(s[:, u_, 0:n], k[:, j * 128:(j + 1) * 128], q[:, t0:t0 + n])
                        p = pp.next()
                        f.act(p[:, :, 0:n], s[:, :, 0:n], AF.Exp, scale=MLA_SCALE)
                    if jp == 1 and pending:
                        epilogue(*pending.pop())
                    if pend is not None:
                        jq, pv = pend
                        for u_ in range(2):
                            jj = jq * 2 + u_
                            f.mm(o[0:65, 0:n], Vaug[:, jj, h * 65:(h + 1) * 65], pv[:, u_, 0:n], start=jj == 0, stop=jj == nk - 1)
                    pend = (jp, p) if jp < npair else None
                if pending:
                    epilogue(*pending.pop())
                pending.append((o, h, t0, n))
        if pending:
            epilogue(*pending.pop())
        f.release(m)

    def phase_gla(l):
        m = f.mark()
        Mm = f.sbuf("glaM", [128, 2, 128], F32)
        mask = f.sbuf("glamask", [128, 2, 128], F32)
        ind = f.sbuf("glaind", [128, 2], F32)
        gon = f.sbuf("gon", [128, 128], F32)
        f.dma("sp", Mm[:], I["gla_M"][:])
        f.dma("sp", mask[:], I["gla_mask"][:])
        f.dma("sp", ind[:], I["gla_ind"][:])
        f.dma("sp", gon[:], I["gla_on"][l])
        mA = f.mark()
        ptr = f.psum("gptr", [128, 8, 128], BF16)
        pU = f.psum("gpU", [128, 4, 128], F32)

        def chain(d):
            S = f.sbuf("glaS%d" % d, [64, 4, 128], F32)
            P2 = lambda nm, shp, dt, k=2: Pool([f.sbuf("%s%d_%d" % (nm, d, i), shp, dt) for i in range(k)])
            qkp, gp, vp = P2("gqk", [128, 512], F32, 3), P2("gg", [128, 256], F32, 3), P2("gv", [128, 512], BF16, 3)
            ePp, eNp = P2("geP", [128, 256], F32), P2("geN", [128, 256], F32)
            qpp, kpp = P2("gqp", [128, 256], BF16), P2("gkp", [128, 256], BF16)
            decp, qkTp = P2("gdec", [64, 4, 2], F32), P2("gqkT", [64, 8, 128], BF16)
            Amp, Smidp, Smbp = P2("gAm", [128, 4, 128], BF16), P2("gSm", [64, 4, 128], F32), P2("gSb", [64, 4, 128], BF16)
            osp = P2("gos", [128, 4, 128], F32)
            pE = f.psum("gpE%d" % d, [128, 512], F32)
            pA = f.psum("gpA%d" % d, [128, 4, 128], F32)
            po = f.psum("gpo%d" % d, [128, 4, 128], F32)
            order = list(range(NT)) if d == 0 else [1, 0] + list(range(NT - 1, 1, -1))
            f.memset(S[:], 0.0)
            yield
            loaded = {}

            def load(tt):
                r0 = tt * 128
                qk, g, v = qkp.next(), gp.next(), vp.next()
                f.dma("sp", qk[:], gqk_d[r0:r0 + 128, :])
                f.dma("sp", g[:], gg_d[r0:r0 + 128, d * 256:(d + 1) * 256])
                f.dma("sp", v[:], gv_d[r0:r0 + 128, :])
                loaded[tt] = (qk, g, v)
            load(order[0])
            for oi, tt in enumerate(order):
                r0 = tt * 128
                if oi + 1 < len(order):
                    load(order[oi + 1])
                qk, g, v = loaded.pop(tt)
                f.mm(pE[:, 0:256], Mm[:, d, :], g[:])
                for h in range(4):
                    f.mm(pE[0:64, 256 + h * 2:258 + h * 2], g[:, h * 64:(h + 1) * 64], ind[:])
                yield
                eP, eN, dec = ePp.next(), eNp.next(), decp.next()
                f.act(eP[:], pE[:, 0:256], AF.Exp)
                f.act(eN[:], pE[:, 0:256], AF.Exp, scale=-1.0)
                f.act(dec[:].re("p h s -> p (h s)"), pE[0:64, 256:264], AF.Exp)
                dmid = dec[:, :, d:d + 1]
                dend = dec[:, :, 1 - d:2 - d]
                yield
                qp_, kp_ = qpp.next(), kpp.next()
                f.tt(qp_[:], qk[:, 0:256], eP[:], ALU.mult)
                f.tt(kp_[:], qk[:, 256:512], eN[:], ALU.mult)
                Smid = Smidp.next()
                f.tt(Smid[:], S[:], dmid.bc([64, 4, 128]), ALU.mult)
                Smb = Smbp.next()
                f.cp(Smb[:], Smid[:], eng="act")
                yield
                for h in range(4):
                    f.tr(ptr[0:64, h, :], qp_[:, h * 64:(h + 1) * 64], ident[:])
                for h in range(4):
                    f.tr(ptr[0:64, 4 + h, :], kp_[:, h * 64:(h + 1) * 64], ident[:])
                for h in range(4):
                    f.mm(pU[0:64, h, :], kp_[:, h * 64:(h + 1) * 64], v[:, h * 128:(h + 1) * 128])
                qkT = qkTp.next()
                f.cp(qkT[:], ptr[0:64], eng="act")
                f.tt(S[:], Smid[:], pU[0:64], ALU.add)
                f.tt(S[:], S[:], dend.bc([64, 4, 128]), ALU.mult)
                yield
                for h in range(4):
                    f.mm(pA[:, h, :], qkT[:, 4 + h, :], qkT[:, h, :])
                yield
                Am = Amp.next()
                f.tt(Am[:], pA[:], mask[:, d:d + 1, :].bc([128, 4, 128]), ALU.mult)
                yield
                for h in range(4):
                    f.mm(po[:, h, :], qkT[:, h, :], Smb[:, h, :], start=True, stop=False)
                    f.mm(po[:, h, :], Am[:, h, :], v[:, h * 128:(h + 1) * 128], start=False, stop=True)
                yield
                os_ = osp.next()
                f.cp(os_[:], po[:], eng="act")
                f.dma("sp", og_d[d, r0:r0 + 128, :], os_[:].re("p h v -> p (h v)"))
                yield

        gens = [chain(0), chain(1)]
        live = list(gens)
        while live:
            for g_ in list(live):
                try:
                    next(g_)
                except StopIteration:
                    live.remove(g_)
        f.release(mA)

        ofp = Pool([f.sbuf("gof%d" % i, [128, 4, 128], F32) for i in range(2)])
        obp = Pool([f.sbuf("gob%d" % i, [128, 4, 128], F32) for i in range(2)])
        sqp = Pool([f.sbuf("gsq%d" % i, [128, 4, 128], F32) for i in range(2)])
        stp = Pool([f.sbuf("gst%d" % i, [128, 8], F32) for i in range(2)])
        rp = Pool([f.sbuf("gr%d" % i, [128, 512], F32) for i in range(2)])
        yp = Pool([f.sbuf("gy%d" % i, [128, 512], BF16) for i in range(2)])
        yTp = Pool([f.sbuf("gyT%d" % i, [128, 4, 128], BF16) for i in range(2)])
        pTp = Pool([f.psum("gpT%d" % i, [128, 8, 128], BF16) for i in range(2)])
        for tt in range(NT):
            if tt < 2 and l == DEPTH - 1:
                continue
            r0 = tt * 128
            of_, ob_, rr = ofp.next(), obp.next(), rp.next()
            f.dma("sp", of_[:].re("p h v -> p (h v)"), og_d[0, r0:r0 + 128, :])
            f.dma("sp", ob_[:].re("p h v -> p (h v)"), og_d[1, r0:r0 + 128, :])
            f.dma("sp", rr[:], gr_d[r0:r0 + 128, :])
            f.tt(of_[:], of_[:], ob_[:], ALU.add)
            sq = sqp.next()
            f.tt(sq[:], of_[:], of_[:], ALU.mult)
            st = stp.next()
            f.op("dve", lambda e: e.tensor_reduce(out=st[:, 0:4].ap, in_=sq[:].ap, axis=mybir.AxisListType.X, op=ALU.add),
                 reads=[sq], writes=[st])
            f.act(st[:, 4:8], st[:, 0:4], AF.Sqrt, bias=EPS_T[:, 0:1], scale=1.0 / 128)
            f.recip(st[:, 4:8], st[:, 4:8])
            f.tt(of_[:], of_[:], st[:, 4:8].un(2).bc([128, 4, 128]), ALU.mult)
            f.tt(of_[:], of_[:], gon[:].un(1).bc([128, 4, 128]), ALU.mult, eng="pool")
            f.act(rr[:], rr[:], AF.Silu)
            y = yp.next()
            f.tt(y[:], of_[:].re("p h v -> p (h v)"), rr[:], ALU.mult)
            pT = pTp.next()
            for c in range(4):
                f.tr(pT[:, c, :], y[:, c * 128:(c + 1) * 128], ident[:])
            yT = yTp.next()
            f.cp(yT[:], pT[:, 0:4, :], eng="act")
            f.dma("sp", yglaT_d[:, r0:r0 + 128].re("(c p) t -> p c t", p=128), yT[:])
        f.release(m)

    def phase_merge(l, prefetch=None):
        m = f.mark()
        wo = [f.sbuf("wo%d" % i, [128, 4, D], BF16) for i in range(3)]
        for i, nm in enumerate(("w_o_mla", "w_o_gla", "w_o_hy")):
            f.dma("pool", wo[i][:], I[nm][l].re("(kc p) n -> p kc n", p=128))
        wout = f.sbuf("wout", [128, 8, D], BF16)
        for c in range(2):
            f.dma("pool", wout[:, :, c * 512:(c + 1) * 512], I["w_out"][l].re("(kc p) n -> p kc n", p=128)[:, :, c * 512:(c + 1) * 512])
        if prefetch is not None:
            prefetch()
        gx = f.sbuf("gx", [128, 2, D], F32)
        f.dma("sp", gx[:, 0, :], modrow[0, l, 0].pb(128))
        f.dma("sp", gx[:, 1, :], modrow[1, l, 0].pb(128))
        ybp = Pool([f.sbuf("mby%d" % i, [128, 3, 4, 512], BF16) for i in range(2)])
        gtp = Pool([f.sbuf("mgt%d" % i, [128, 3, 512], BF16) for i in range(3)])
        mTp = Pool([f.sbuf("mT%d" % i, [128, 8, 512], BF16) for i in range(2)])
        accp = Pool([f.sbuf("macc%d" % i, [128, 512], F32) for i in range(2)])
        tmpp = Pool([f.sbuf("mtmp%d" % i, [128, 512], F32) for i in range(3)])
        xp = Pool([f.sbuf("mx%d" % i, [128, D], F32) for i in range(3)])
        ps3 = [Pool([f.psum("mps%d_%d" % (i, j), [128, 512], F32) for j in range(2)]) for i in range(3)]
        pso = Pool([f.psum("mpo%d" % i, [128, 512], F32) for i in range(2)])
        gview = gatesT_d[:].re("(b c p) t -> p b c t", b=3, c=8, p=128)
        for bi, (t0, n) in enumerate(TB):
            if bi == 0 and l == DEPTH - 1:
                continue
            yb = ybp.next()
            for i, srcT in enumerate((ymlaT_d, yglaT_d, yhyT_d)):
                f.dma("sp", yb[:, i, :, 0:n], srcT[:, t0:t0 + n].re("(kc p) t -> p kc t", p=128))
            mT = mTp.next()
            for oc in range(8):
                gt = gtp.next()
                f.dma("sp", gt[:, :, 0:n], gview[:, :, oc, t0:t0 + n])
                pss = []
                for i in range(3):
                    ps = ps3[i].next()
                    for kc in range(4):
                        f.mm(ps[:, 0:n], wo[i][:, kc, oc * 128:(oc + 1) * 128], yb[:, i, kc, 0:n], start=kc == 0, stop=kc == 3)
                    pss.append(ps)
                acc = accp.next()
                t1 = tmpp.next()
                t2 = tmpp.next()
                f.tt(acc[:, 0:n], pss[0][:, 0:n], gt[:, 0, 0:n], ALU.mult)
                f.tt(t1[:, 0:n], pss[1][:, 0:n], gt[:, 1, 0:n], ALU.mult)
                f.tt(t2[:, 0:n], pss[2][:, 0:n], gt[:, 2, 0:n], ALU.mult)
                f.tt(acc[:, 0:n], acc[:, 0:n], t1[:, 0:n], ALU.add, eng="pool" if oc % 4 == 3 else "dve")
                f.tt(mT[:, oc, 0:n], acc[:, 0:n], t2[:, 0:n], ALU.add)
            s = 1 if bi == 0 else 0
            for ti in range(n // 128):
                tt = t0 // 128 + ti
                x = xp.next()
                f.dma("sp", x[:], xres_t[tt][:])
                for half in range(2):
                    ps = pso.next()
                    for kc in range(8):
                        f.mm(ps[:, :], mT[:, kc, ti * 128:(ti + 1) * 128], wout[:, kc, half * 512:(half + 1) * 512],
                             start=kc == 0, stop=kc == 7)
                    t1 = tmpp.next()
                    f.tt(t1[:], ps[:], gx[:, s, half * 512:(half + 1) * 512], ALU.mult)
                    f.tt(x[:, half * 512:(half + 1) * 512], x[:, half * 512:(half + 1) * 512], t1[:], ALU.add)
                f.dma("sp", xres_t[tt][:], x[:])
        f.release(m)

    def ffn_w1_alloc():
        return f.sbuf("ffw1", [128, 8, 4096], BF16)

    def ffn_w1_load(l, w1):
        w1v = I["ff_w1"][l].re("(kc p) n -> p kc n", p=128)
        for c in range(8):
            f.dma("pool", w1[:, :, c * 512:(c + 1) * 512], w1v[:, :, c * 512:(c + 1) * 512])

    def phase_ffn(l, w1):
        m = f.mark()
        w2 = f.sbuf("ffw2", [128, 32, D], BF16)
        w2v = I["ff_w2"][l].re("(kc p) n -> p kc n", p=128)
        for c in range(8):
            f.dma("pool", w2[:, c * 4:(c + 1) * 4, :], w2v[:, c * 4:(c + 1) * 4, :])
        gx = f.sbuf("fgx", [128, 2, D], F32)
        f.dma("sp", gx[:, 0, :], modrow[0, l, 1].pb(128))
        f.dma("sp", gx[:, 1, :], modrow[1, l, 1].pb(128))
        nctx = NormCtx()
        hTp = Pool([f.sbuf("fhT%d" % i, [128, 8, 256], BF16) for i in range(2)])
        aTp = Pool([f.sbuf("faT%d" % i, [128, 32, 256], BF16) for i in range(1)])
        rp = Pool([f.sbuf("fr%d" % i, [128, 256], F32) for i in range(2)])
        tmpp = Pool([f.sbuf("ftmp%d" % i, [128, 512], F32) for i in range(2)])
        xp = Pool([f.sbuf("fx%d" % i, [128, D], F32) for i in range(2)])
        psA = Pool([f.psum("fpa%d" % i, [128, 512], F32) for i in range(3)])
        pso = Pool([f.psum("fpo%d" % i, [128, 512], F32) for i in range(3)])
        for blk in range(T // 256):
            if blk == 0 and l == DEPTH - 1:
                continue
            s = 1 if blk == 0 else 0
            hTb = hTp.next()
            for ti in range(2):
                nctx.emit(l, 1, hTb, blk * 2 + ti, ti * 128)
            aT = aTp.next()
            for fc in range(32):
                ps = psA.next()
                for kc in range(8):
                    f.mm(ps[:, 0:256], w1[:, kc, fc * 128:(fc + 1) * 128], hTb[:, kc, :], start=kc == 0, stop=kc == 7)
                r = rp.next()
                f.act(r[:], ps[:, 0:256], AF.Relu)
                f.tt(aT[:, fc, :], r[:], r[:], ALU.mult, eng="pool" if fc % 4 == 3 else "dve")
            for ti in range(2):
                tt = blk * 2 + ti
                x = xp.next()
                f.dma("sp", x[:], xres_t[tt][:])
                for half in range(2):
                    ps = pso.next()
                    for fc in range(32):
                        f.mm(ps[:, :], aT[:, fc, ti * 128:(ti + 1) * 128], w2[:, fc, half * 512:(half + 1) * 512],
                             start=fc == 0, stop=fc == 31)
                    t1 = tmpp.next()
                    f.tt(t1[:], ps[:], gx[:, s, half * 512:(half + 1) * 512], ALU.mult)
                    f.tt(x[:, half * 512:(half + 1) * 512], x[:, half * 512:(half + 1) * 512], t1[:], ALU.add)
                f.dma("sp", xres_t[tt][:], x[:])
        f.release(m)

    def phase_final():
        m = f.mark()
        fg = f.sbuf("fing", [128, D], F32)
        f.dma("sp", fg[:], I["fin_g"][:])
        xp = Pool([f.sbuf("zx%d" % i, [128, D], F32) for i in range(3)])
        jp = Pool([f.sbuf("zj%d" % i, [128, D], F32) for i in range(2)])
        stp = Pool([f.sbuf("zst%d" % i, [128, 4], F32) for i in range(3)])
        for tt in range(2, NT):
            x = xp.next()
            j = jp.next()
            st = stp.next()
            f.dma("sp", x[:], xres_t[tt][:])
            f.act(j[:], x[:], AF.Square, accum=st[:, 0:1])
            f.act(st[:, 1:2], st[:, 0:1], AF.Sqrt, bias=EPS_T[:, 0:1], scale=1.0 / D)
            f.recip(st[:, 2:3], st[:, 1:2])
            f.act(j[:], x[:], AF.Identity, scale=st[:, 2:3])
            f.tt(x[:], j[:], fg[:], ALU.mult)
            f.dma("sp", out_y[(tt - 2) * 128:(tt - 1) * 128, :], x[:])
        f.release(m)

    def phase_hy(l, ctx_seg):
        m = f.mark()
        r0, nrow = (0, LC) if ctx_seg else (LC, L)
        na = nrow // 64
        NA = 2 * na
        NF = NA * 64
        NFA = NA // 2 + 1
        sfx = "_c" if ctx_seg else ""

        mA = f.mark()
        swt = f.sbuf("hsw", [128, 12, 4], F32)
        f.dma("sp", swt[:], I["hy_swb"][l])
        zp = Pool([f.sbuf("hz%d" % i, [128, L], BF16) for i in range(2)])
        accp = Pool([f.sbuf("hacc%d" % i, [128, L], F32) for i in range(2)])
        op_ = Pool([f.sbuf("hso%d" % i, [128, L], BF16) for i in range(2)])
        for ch in range(12):
            z = zp.next()
            acc = accp.next()
            o = op_.next()
            f.dma("sp", z[:, 0:nrow], zhyT_d[ch * 128:(ch + 1) * 128, r0:r0 + nrow])
            f.act(acc[:, 0:nrow], z[:, 0:nrow], AF.Identity, bias=swt[:, ch, 3:4], scale=swt[:, ch, 1:2])
            f.stt(acc[:, 1:nrow], z[:, 0:nrow - 1], swt[:, ch, 0:1], acc[:, 1:nrow], ALU.mult, ALU.add)
            f.stt(acc[:, 0:nrow - 1], z[:, 1:nrow], swt[:, ch, 2:3], acc[:, 0:nrow - 1], ALU.mult, ALU.add)
            f.cp(o[:, 0:nrow], acc[:, 0:nrow], eng="act")
            f.dma("sp", scT_d[ch * 128:(ch + 1) * 128, r0:r0 + nrow], o[:, 0:nrow])
        f.release(mA)

        F1 = f.sbuf("hF1", [NA, 3 * NFA], BF16)
        E2r = f.sbuf("hE2r", [128, NFA, 128], BF16)
        E2i = f.sbuf("hE2i", [128, NFA, 128], BF16)
        f.dma("sp", F1[:], I["hy_F1" + sfx][:])
        f.dma("sp", E2r[:], I["hy_E2r" + sfx][:])
        f.dma("sp", E2i[:], I["hy_E2i" + sfx][:])
        psY = Pool([f.psum("hpY%d" % i, [128, 512], F32) for i in range(2)])
        psX = Pool([f.psum("hpX%d" % i, [128, 2, 8, 32], F32) for i in range(2)])
        pst = Pool([f.psum("hpt%d" % i, [128, 8, 64], BF16) for i in range(1)])

        def spectrum_gen(ut, Kp, consume, Y):
            for q in range(32):
                ps = psY.next()
                f.mm(ps[:, 0:3 * NFA], ut[0:Kp, q, :], F1[0:Kp, :])
                f.cp(Y[:, q, :], ps[:, 0:3 * NFA], eng="act" if q % 2 else "dve")
                if q % 4 == 3:
                    yield
            for fa0 in range(0, NFA, 8):
                nfa = min(8, NFA - fa0)
                px = psX.next()
                pr = px[:, 0]
                pi = px[:, 1]
                for i in range(nfa):
                    fa = fa0 + i
                    f.mm(pr[:, i, :], E2r[:, fa, :], Y[:, :, fa], start=True, stop=False)
                    f.mm(pr[:, i, :], E2i[:, fa, :], Y[:, :, 2 * NFA + fa], start=False, stop=True)
                for i in range(nfa):
                    fa = fa0 + i
                    f.mm(pi[:, i, :], E2i[:, fa, :], Y[:, :, fa], start=True, stop=False)
                    f.mm(pi[:, i, :], E2r[:, fa, :], Y[:, :, NFA + fa], start=False, stop=True)
                consume(fa0, nfa, pr, pi)
                yield

        mB = f.mark()
        hd2 = f.sbuf("hhd2", [64, NF], F32)
        fw3 = f.sbuf("hfw3", [64, 2048], F32)
        fb3T = f.sbuf("hfb3T", [128, 16], F32)
        nd = f.sbuf("hnd", [128, 4], F32)
        hbT = f.sbuf("hhbT", [128, 2, 4], F32)
        f.dma("sp", fw3[:], I["hy_f_w3"][l])
        f.dma("sp", fb3T[:], I["hy_fb3T"][l])
        f.dma("sp", nd[:], I["hy_negdelta"][:])
        f.dma("sp", hbT[:], I["hy_biasT"][l])
        mB1 = f.mark()
        featT = f.sbuf("hfeat", [33, NF], F32)
        f.dma("sp", featT[:], I["hy_featT" + sfx][:])
        fw1 = f.sbuf("hfw1", [33, 64], F32)
        fw2 = f.sbuf("hfw2", [64, 64], F32)
        fb12 = f.sbuf("hfb12", [64, 2], F32)
        f.dma("sp", fw1[:], I["hy_f_w1"][l])
        f.dma("sp", fw2[:], I["hy_f_w2"][l])
        f.dma("sp", fb12[:], I["hy_fb12"][l])
        hd1p = Pool([f.sbuf("hhd1_%d" % i, [64, 512], F32) for i in range(2)])
        ap_ = Pool([f.sbuf("ha%d" % i, [64, 512], F32) for i in range(2)])
        m1p = Pool([f.sbuf("hm1_%d" % i, [64, 512], F32) for i in range(2)])
        m2p = Pool([f.sbuf("hm2_%d" % i, [64, 512], F32) for i in range(2)])
        psm = Pool([f.psum("hpm%d" % i, [128, 512], F32) for i in range(3)])
        nblk = NF // 512

        def sin_wrap(dst, ps, bias):
            a = ap_.next()
            m1 = m1p.next()
            m2 = m2p.next()
            f.act(a[:], ps[0:64, :], AF.Identity, bias=bias)
            f.ts(m1[:], a[:], math.pi, ALU.is_gt, 2 * math.pi, ALU.mult)
            f.ts(m2[:], a[:], -math.pi, ALU.is_lt, 2 * math.pi, ALU.mult)
            f.tt(a[:], a[:], m1[:], ALU.subtract)
            f.tt(a[:], a[:], m2[:], ALU.add)
            f.act(dst, a[:], AF.Sin)

        for blk in range(nblk):
            ps = psm.next()
            f.mm(ps[0:64, :], fw1[:], featT[:, blk * 512:(blk + 1) * 512])
            hd1 = hd1p.next()
            sin_wrap(hd1[:], ps, fb12[:, 0:1])
            ps2 = psm.next()
            f.mm(ps2[0:64, :], fw2[:], hd1[:])
            sin_wrap(hd2[:, blk * 512:(blk + 1) * 512], ps2, fb12[:, 1:2])
        f.release(mB1)
        tn2 = f.sbuf("htn2", [128, NF], F32)
        f.dma("sp", tn2[:], I["hy_tn2" + sfx][0].pb(128))
        kT = f.sbuf("hkT", [128, NF], F32)
        kTb = f.sbuf("hkTb", [128, NF], BF16)
        kbp = Pool([f.sbuf("hkb%d" % i, [128, 512], F32) for i in range(2)])
        wbp = Pool([f.sbuf("hwb%d" % i, [128, 512], F32) for i in range(2)])
        jk = f.sbuf("hjk", [128, 512], BF16)
        asum = f.sbuf("hasum", [128, 20], F32)
        kup = Pool([f.sbuf("hku%d" % i, [128, 32, 128], BF16) for i in range(1)])
        Ybc = f.sbuf("hYbc", [128, 32, 3 * NFA], BF16)
        Hst = Pool([f.sbuf("hHst%d" % i, [128, 2, NFA, 32], BF16) for i in range(1)])
        psm = Pool([f.psum("hpm2_%d" % i, [128, 512], F32) for i in range(2)])
        bs = min(512, NF // 2)
        nb2 = NF // bs
        for cc in range(4):
            for n_ in range(2):
                for blk in range(nb2):
                    dr = 0 if blk < nb2 // 2 else 1
                    col = (dr * 2 + n_) * 4 + cc
                    cs_ = slice(blk * bs, (blk + 1) * bs)
                    ps = psm.next()
                    f.mm(ps[:, 0:bs], fw3[:, col * 128:(col + 1) * 128], hd2[:, cs_])
                    kb = kbp.next()
                    wb = wbp.next()
                    f.act(wb[:, 0:bs], tn2[:, cs_], AF.Exp, scale=nd[:, cc:cc + 1])
                    f.stt(kT[:, cs_], ps[:, 0:bs], fb3T[:, col:col + 1], wb[:, 0:bs], ALU.add, ALU.mult)
                    f.act(jk[:, 0:bs], kT[:, cs_], AF.Abs, accum=asum[:, blk:blk + 1])
                f.op("dve", lambda e: e.tensor_reduce(out=asum[:, 16:17].ap, in_=asum[:, 0:nb2].ap, axis=mybir.AxisListType.X, op=ALU.add),
                     reads=[asum], writes=[asum])
                f.recip(asum[:, 17:18], asum[:, 16:17])
                f.act(kTb[:], kT[:], AF.Identity, scale=asum[:, 17:18])
                f.ts(kTb[:, 0:1], kT[:, 0:1], asum[:, 17:18], ALU.mult, hbT[:, n_, cc:cc + 1], ALU.add)
                for gg in range(2):
                    g = cc * 2 + gg
                    kv = kTb[gg * 64:(gg + 1) * 64, :].re("c (a b) -> c b a", b=64)
                    ut = kup.next()
                    for b0 in range(0, 64, 8):
                        pt = pst.next()
                        for i in range(8):
                            f.tr(pt[0:NA, i, :], kv[:, b0 + i, :], ident[gg * 64:(gg + 1) * 64, gg * 64:(gg + 1) * 64])
                        f.cp(ut[0:NA].re("a q (b cp) -> a b q cp", cp=2)[:, b0:b0 + 8], pt[0:NA, :, :].re("a b (q cp) -> a b q cp", cp=2),
                             eng="act" if (b0 // 8) % 2 else "dve")
                    hs = Hst.next()

                    def cons(fa0, nfa, pr, pi, hs=hs):
                        f.cp(hs[:, 0, fa0:fa0 + nfa, :], pr[:, 0:nfa, :], eng="act")
                        f.cp(hs[:, 1, fa0:fa0 + nfa, :], pi[:, 0:nfa, :], eng="dve")
                    for _ in spectrum_gen(ut, NA, cons, Ybc):
                        pass
                    f.dma("sp", H_d[n_, g, :, 0:2 * NFA * 32], hs[:].re("p r f q -> p (r f q)"))
        f.release(mB)

        CA = f.sbuf("hCA", [128, 3, 128], BF16)
        DBr = f.sbuf("hDBr", [NFA, 64, na], BF16)
        DBni = f.sbuf("hDBni", [NFA, 64, na], BF16)
        f.dma("sp", CA[:], I["hy_CA"][:])
        f.dma("sp", DBr[:], I["hy_DBr" + sfx][:])
        f.dma("sp", DBni[:], I["hy_DBni" + sfx][:])
        tp = Pool([f.sbuf("ht%d" % i, [128, 8, 32], F32) for i in range(8)])
        psZ = Pool([f.psum("hpZ%d" % i, [128, 4, 128], F32) for i in range(2)])
        psO = Pool([f.psum("hpO%d" % i, [128, 8, 64], F32) for i in range(1)])
        RES = []
        for ci in range(2):
            RES.append(dict(
                uT=f.sbuf("huT%d" % ci, [64, L], BF16), gT=f.sbuf("hgT%d" % ci, [64, L], BF16),
                u=f.sbuf("hu%d" % ci, [64, 32, 128], BF16), H=f.sbuf("hH%d" % ci, [128, 2, NFA, 32], BF16),
                P=f.sbuf("hP%d" % ci, [128, 2, 32, NFA], BF16), Z0=f.sbuf("hZ0_%d" % ci, [NFA, 2, 64, 64], BF16),
                Y=f.sbuf("hYD%d" % ci, [128, 32, 3 * NFA], BF16)))

        def conv_group(n_, g, R):
            srcT = scT_d[1024:1536] if n_ == 0 else y1T_d
            gateT = scT_d[0:512] if n_ == 0 else scT_d[512:1024]
            dstT = y1T_d if n_ == 0 else yhyT_d
            uT, gT, u, H, P, Z0, Y = R["uT"], R["gT"], R["u"], R["H"], R["P"], R["Z0"], R["Y"]
            f.dma("sp", uT[:, 0:nrow], srcT[g * 64:(g + 1) * 64, r0:r0 + nrow])
            f.dma("sp", gT[:, 0:nrow], gateT[g * 64:(g + 1) * 64, r0:r0 + nrow])
            f.dma("sp", H[:].re("p r f q -> p (r f q)"), H_d[n_, g, :, 0:2 * NFA * 32])
            yield
            uv = uT[:, 0:nrow].re("c (a b) -> c b a", b=64)
            for b0 in range(0, 64, 8):
                pt = pst.next()
                for i in range(8):
                    f.tr(pt[0:na, i, :], uv[:, b0 + i, :], ident[0:64, 0:64])
                f.cp(u[0:na].re("a q (b cp) -> a b q cp", cp=2)[:, b0:b0 + 8], pt[0:na, :, :].re("a b (q cp) -> a b q cp", cp=2),
                     eng="act" if (b0 // 8) % 2 else "dve")
                yield

            def cons(fa0, nfa, pr, pi):
                t1, t2, t3, t4 = tp.next(), tp.next(), tp.next(), tp.next()
                f.tt(t1[:, 0:nfa], pr[:, 0:nfa, :], H[:, 0, fa0:fa0 + nfa, :], ALU.mult)
                f.tt(t2[:, 0:nfa], pi[:, 0:nfa, :], H[:, 1, fa0:fa0 + nfa, :], ALU.mult)
                f.tt(t3[:, 0:nfa], pr[:, 0:nfa, :], H[:, 1, fa0:fa0 + nfa, :], ALU.mult)
                f.tt(t4[:, 0:nfa], pi[:, 0:nfa, :], H[:, 0, fa0:fa0 + nfa, :], ALU.mult)
                f.tt(P[:, 0, :, fa0:fa0 + nfa].re("p q f -> p f q"), t1[:, 0:nfa], t2[:, 0:nfa], ALU.subtract)
                f.tt(P[:, 1, :, fa0:fa0 + nfa].re("p q f -> p f q"), t3[:, 0:nfa], t4[:, 0:nfa], ALU.add, eng="pool")
            for _ in spectrum_gen(u, na, cons, Y):
                yield
            for q0 in range(0, 32, 4):
                zr = psZ.next()
                zi = psZ.next()
                for i in range(4):
                    q = q0 + i
                    f.mm(zr[0:NFA, i, :], P[:, 0, q, :], CA[:, 0, :], start=True, stop=False)
                    f.mm(zr[0:NFA, i, :], P[:, 1, q, :], CA[:, 2, :], start=False, stop=True)
                for i in range(4):
                    q = q0 + i
                    f.mm(zi[0:NFA, i, :], P[:, 0, q, :], CA[:, 1, :], start=True, stop=False)
                    f.mm(zi[0:NFA, i, :], P[:, 1, q, :], CA[:, 0, :], start=False, stop=True)
                f.cp(Z0[:, 0].re("f b (q cp) -> f q b cp", cp=2)[:, q0:q0 + 4], zr[0:NFA, :, :].re("f q (b cp) -> f q b cp", cp=2), eng="act")
                f.cp(Z0[:, 1].re("f b (q cp) -> f q b cp", cp=2)[:, q0:q0 + 4], zi[0:NFA, :, :].re("f q (b cp) -> f q b cp", cp=2), eng="dve")
                yield
            yv = uT[:, 0:nrow].re("c (a b) -> c a b", b=64)
            gv = gT[:, 0:nrow].re("c (a b) -> c a b", b=64)
            for b0 in range(0, 64, 8):
                po_ = psO.next()
                for i in range(8):
                    b = b0 + i
                    f.mm(po_[0:64, i, 0:na], Z0[:, 0, b, :], DBr[:, b, :], start=True, stop=False)
                    f.mm(po_[0:64, i, 0:na], Z0[:, 1, b, :], DBni[:, b, :], start=False, stop=True)
                f.tt(yv[:, :, b0:b0 + 8], po_[0:64, :, 0:na].re("c b a -> c a b"), gv[:, :, b0:b0 + 8], ALU.mult)
                yield
            f.dma("sp", dstT[g * 64:(g + 1) * 64, r0:r0 + nrow], uT[:, 0:nrow])
            yield

        for n_ in range(2):
            for g0 in range(0, 8, 2):
                live = [conv_group(n_, g0, RES[0]), conv_group(n_, g0 + 1, RES[1])]
                while live:
                    for g_ in list(live):
                        try:
                            next(g_)
                        except StopIteration:
                            live.remove(g_)
            f.barrier()
        f.release(m)

    ONE_T = f.sbuf("one_t", [128, 1], F32)
    f.memset(ONE_T[:], 1.0)

    phase_mod()
    done = False
    if "only_hy" in dbg:
        phase_hy(0, False)
        phase_hy(0, True)
        f.barrier()
        f.barrier(["sp"])
        f.release(0)
        return nc
    for l in range(DEPTH):
        if "from_merge" not in dbg:
            mk = f.mark()
            hT = f.sbuf("hT", [128, 8, T], BF16)
            norm_tiles(l, 0, hT, range(NT))
            if hT_d is not None and l == 0:
                for kc in range(8):
                    f.dma("sp", hT_d[kc * 128:(kc + 1) * 128, :], hT[:, kc, :])
            if stop_after == "norm":
                f.release(mk)
                break
            phase_proj(l, hT)
            f.release(mk)
            if stop_after == "proj":
                break
            if "skip_att" not in dbg:
                phase_att(l)
            if stop_after == "att":
                break
            if "skip_gla" not in dbg:
                phase_gla(l)
            if stop_after == "gla":
                break
            if "skip_hy" not in dbg:
                phase_hy(l, False)
                if l < DEPTH - 1:
                    phase_hy(l, True)
            if stop_after == "hy":
                break
        mk2 = f.mark()
        w1 = ffn_w1_alloc()
        phase_merge(l, prefetch=lambda: ffn_w1_load(l, w1))
        if stop_after == "merge":
            break
        phase_ffn(l, w1)
        f.release(mk2)
        if stop_after == "ffn":
            break
    else:
        done = True
    f.barrier()
    if done:
        phase_final()
    f.barrier(["sp"])
    f.release(0)
    return nc


def _fm(v, chunks):
    return np.ascontiguousarray(np.asarray(v, np.float32).reshape(chunks, 128).T)


def make_in_maps(inputs):
    g = {k: np.asarray(v) for k, v in inputs.items()}
    hc = host_constants()
    perm = rope_swap_perm()
    sh = {}
    sh["ada_w"] = np.ascontiguousarray(g["ada_w"], np.float32)
    sh["ada_bf"] = np.ascontiguousarray(np.stack([_fm(g["ada_b"][l], 48) for l in range(DEPTH)], 1))
    sh["ada_br"] = np.ascontiguousarray(np.broadcast_to(g["ada_b"][None], (2, DEPTH, 6 * D)), np.float32)
    sh["n1g"] = np.ascontiguousarray(np.stack([_fm(g["norm1_g"][l], 8) for l in range(DEPTH)], 1))
    sh["n2g"] = np.ascontiguousarray(np.stack([_fm(g["norm2_g"][l], 8) for l in range(DEPTH)], 1))
    sh["w_in"] = np.ascontiguousarray(g["w_in"], np.float32)
    wkr = np.zeros((DEPTH, D, 2, 96), np.float32)
    wkr[:, :, 0, 64:96] = g["w_in"][:, :, 384:416]
    wkr[:, :, 1, 64:96] = g["w_in"][:, :, 384:416][:, :, perm]
    sh["w_kr2"] = wkr
    sh["qng"] = np.ascontiguousarray(np.stack([_fm(g["mla_q_norm"][l], 2) for l in range(DEPTH)], 1))
    sh["kvng"] = np.ascontiguousarray(np.stack([g["mla_kv_norm"][l] for l in range(DEPTH)], 1), np.float32)
    sh["w_uq"] = np.ascontiguousarray(g["mla_w_uq"], np.float32)
    wsw = g["mla_w_uq"].reshape(DEPTH, 256, 8, 96).copy()
    wsw[:, :, :, 64:96] = wsw[:, :, :, 64:96][:, :, :, perm]
    sh["w_uq_sw"] = np.ascontiguousarray(wsw.reshape(DEPTH, 256, 768), np.float32)
    ukv = g["mla_w_ukv"].reshape(DEPTH, 128, 8, 128)
    sh["w_ukv_k"] = np.ascontiguousarray(ukv[:, :, :, 0:64].reshape(DEPTH, 128, 512), np.float32)
    sh["w_ukv_v"] = np.ascontiguousarray(ukv[:, :, :, 64:128].reshape(DEPTH, 128, 512), np.float32)
    sh["w_a2"] = np.ascontiguousarray(np.concatenate([g["gla_w_a2"][:, 0], g["gla_w_a2"][:, 1]], -1), np.float32)
    sh["b_a"] = np.ascontiguousarray(np.concatenate([g["gla_b_a"][:, 0], g["gla_b_a"][:, 1]], -1)[:, None, :], np.float32)
    for nm_ in ("w_o_mla", "w_o_gla", "w_o_hy", "w_out", "ff_w1", "ff_w2"):
        sh[nm_] = np.ascontiguousarray(g[nm_], np.float32)
    sw = g["hy_short_w"]
    swb = np.concatenate([sw, g["hy_short_b"][:, None, :]], 1)
    sh["hy_swb"] = np.ascontiguousarray(swb.reshape(DEPTH, 4, 12, 128).transpose(0, 3, 2, 1), np.float32)
    sh["hy_f_w1"] = np.ascontiguousarray(g["hy_f_w1"], np.float32)
    sh["hy_f_w2"] = np.ascontiguousarray(g["hy_f_w2"], np.float32)
    sh["hy_f_w3"] = np.ascontiguousarray(g["hy_f_w3"], np.float32)
    sh["hy_fb12"] = np.ascontiguousarray(np.stack([g["hy_f_b1"], g["hy_f_b2"]], -1), np.float32)
    sh["hy_fb3T"] = np.ascontiguousarray(g["hy_f_b3"].reshape(DEPTH, 16, 128).transpose(0, 2, 1), np.float32)
    sh["hy_biasT"] = np.ascontiguousarray(g["hy_bias"].reshape(DEPTH, 2, 4, 128).transpose(0, 3, 1, 2), np.float32)
    sh["fin_g"] = np.ascontiguousarray(np.broadcast_to(g["final_norm_g"][None, :], (128, D)), np.float32)
    sh["gla_on"] = np.ascontiguousarray(np.broadcast_to(g["gla_out_norm"][:, None, :], (DEPTH, 128, 128)), np.float32)
    for k, v in hc.items():
        sh[k] = v
    maps = []
    for b in range(8):
        m = dict(sh)
        m["xc"] = np.ascontiguousarray(np.concatenate([g["ctx"][b], g["x"][b]], 0), np.float32)
        m["cs"] = np.ascontiguousarray(np.stack([_fm(g["c"][b], 8), _fm(g["c_ctx"], 8)], -1))
        maps.append(m)
    return maps


_NC_CACHE = {}


def kernel(**inputs):
    if "nc" not in _NC_CACHE:
        _NC_CACHE["nc"] = build()
    nc = _NC_CACHE["nc"]
    maps = make_in_maps(inputs)
    res = run_bass_kernel_spmd(nc, maps, core_ids=list(range(8)))
    return np.stack([np.asarray(r["y"], np.float32) for r in res.results], 0)
```
